# Optimizing a Trainium2 kernel written in Bass

```python
import jax, jax.numpy as jnp
from jax import lax
import numpy as np

D_MODEL = 1024
BATCH = 16
SEQ = 2048
DEPTH = 1

CHUNK = 64
LEFT_CHUNKS = 8
BAND = (LEFT_CHUNKS + 1) * CHUNK
Q_BLOCK = 128
MLA_HEADS = 8
MLA_Q_RANK = 256
MLA_KV_RANK = 128
MLA_NOPE = 64
MLA_ROPE = 32
MLA_QK = MLA_NOPE + MLA_ROPE
MLA_V = 64
ROPE_THETA = 10000.0
CA_HEADS = 8
CA_HEAD_DIM = 64
REL_CLIP = 128
D_MIX = MLA_HEADS * MLA_V + CA_HEADS * CA_HEAD_DIM
D_IN = MLA_Q_RANK + MLA_KV_RANK + MLA_ROPE + 3 * CA_HEADS * CA_HEAD_DIM
D_FF = 2816
CONV_WIDTH = 3
N_MOD = 6
EPS = 1e-6
NEG_INF = -1e30

kernel_name = "hybrid_mla_chunkband_convffn_adaln"


def rmsnorm(x, g):
    xf = x.astype(jnp.float32)
    y = xf * lax.rsqrt(jnp.mean(xf * xf, axis=-1, keepdims=True) + EPS)
    return (y * g.astype(jnp.float32)).astype(x.dtype)


def modulate(h, shift, scale):
    return h * (1 + scale[:, None, :]) + shift[:, None, :]


def rope(x, positions):
    half = x.shape[-1] // 2
    inv = jnp.power(ROPE_THETA, -jnp.arange(half, dtype=jnp.float32) / half)
    ang = positions.astype(jnp.float32)[..., None] * inv
    cos = jnp.cos(ang)[:, :, None, :]
    sin = jnp.sin(ang)[:, :, None, :]
    xf = x.astype(jnp.float32)
    x1, x2 = xf[..., :half], xf[..., half:]
    out = jnp.concatenate([x1 * cos - x2 * sin, x2 * cos + x1 * sin], axis=-1)
    return out.astype(x.dtype)


def mla_attention(q, k, v):
    B, S, H, Dq = q.shape
    nqb = S // Q_BLOCK
    scale = Dq ** -0.5
    key_chunk = jnp.arange(S) // CHUNK
    qb = q.reshape(B, nqb, Q_BLOCK, H, Dq).swapaxes(0, 1)

    def block(args):
        qi, i = args
        s = jnp.einsum('bqhd,bkhd->bhqk', qi, k,
                       preferred_element_type=jnp.float32) * scale
        q_chunk = (i * Q_BLOCK + jnp.arange(Q_BLOCK)) // CHUNK
        mask = key_chunk[None, :] <= q_chunk[:, None]
        s = jnp.where(mask[None, None], s, NEG_INF)
        p = jax.nn.softmax(s, axis=-1).astype(v.dtype)
        return jnp.einsum('bhqk,bkhd->bqhd', p, v)

    o = lax.map(block, (qb, jnp.arange(nqb)))
    return o.swapaxes(0, 1).reshape(B, S, H * v.shape[-1])


def chunk_attention(q, k, v, rel_bias):
    B, S, H, D = q.shape
    nc = S // CHUNK
    pad = LEFT_CHUNKS * CHUNK
    scale = D ** -0.5
    k_pad = jnp.pad(k, ((0, 0), (pad, 0), (0, 0), (0, 0)))
    v_pad = jnp.pad(v, ((0, 0), (pad, 0), (0, 0), (0, 0)))
    qi = jnp.arange(CHUNK)
    kj = jnp.arange(BAND)
    rel = (pad + qi[:, None]) - kj[None, :]
    bias = rel_bias[:, jnp.clip(rel, -REL_CLIP, REL_CLIP) + REL_CLIP].astype(jnp.float32)
    qc = q.reshape(B, nc, CHUNK, H, D).swapaxes(0, 1)

    def one_chunk(args):
        qch, ci = args
        kb = lax.dynamic_slice_in_dim(k_pad, ci * CHUNK, BAND, axis=1)
        vb = lax.dynamic_slice_in_dim(v_pad, ci * CHUNK, BAND, axis=1)
        s = jnp.einsum('bqhd,bkhd->bhqk', qch, kb,
                       preferred_element_type=jnp.float32) * scale + bias[None]
        valid = kj >= (LEFT_CHUNKS - ci) * CHUNK
        s = jnp.where(valid[None, None, None, :], s, NEG_INF)
        p = jax.nn.softmax(s, axis=-1).astype(vb.dtype)
        return jnp.einsum('bhqk,bkhd->bqhd', p, vb)

    o = lax.map(one_chunk, (qc, jnp.arange(nc)))
    return o.swapaxes(0, 1).reshape(B, S, H * D)


def causal_dwconv(u, w, b):
    S = u.shape[1]
    up = jnp.pad(u, ((0, 0), (CONV_WIDTH - 1, 0), (0, 0)))
    out = up[:, 0:S] * w[0]
    for j in range(1, CONV_WIDTH):
        out = out + up[:, j:j + S] * w[j]
    return out + b


def setup_inputs(seed: int = 0) -> dict:
    key = jax.random.key(seed)
    ks = jax.random.split(key, 24)
    f32 = jnp.float32

    def nrm(k, shape, fan_in):
        return jax.random.normal(k, shape, f32) * (fan_in ** -0.5)

    def gain(k, shape):
        return 1.0 + 0.05 * jax.random.normal(k, shape, f32)

    L = DEPTH
    x = jax.random.normal(ks[0], (BATCH, SEQ, D_MODEL), f32)
    c = jax.random.normal(ks[1], (BATCH, D_MODEL), f32)
    offsets = jax.random.randint(ks[2], (BATCH, 1), 0, 4096, dtype=jnp.int32)
    positions = (offsets + jnp.arange(SEQ, dtype=jnp.int32)[None, :]).astype(jnp.int32)
    return {
        "x": x,
        "c": c,
        "positions": positions,
        "w_ada": nrm(ks[3], (L, D_MODEL, N_MOD * D_MODEL), D_MODEL),
        "b_ada": 0.02 * jax.random.normal(ks[4], (L, N_MOD * D_MODEL), f32),
        "g_attn_norm": gain(ks[5], (L, D_MODEL)),
        "w_in": nrm(ks[6], (L, D_MODEL, D_IN), D_MODEL),
        "g_q_latent": gain(ks[7], (L, MLA_Q_RANK)),
        "g_kv_latent": gain(ks[8], (L, MLA_KV_RANK)),
        "w_q_up": nrm(ks[9], (L, MLA_Q_RANK, MLA_HEADS * MLA_QK), MLA_Q_RANK),
        "w_kv_up": nrm(ks[10], (L, MLA_KV_RANK, MLA_HEADS * (MLA_NOPE + MLA_V)), MLA_KV_RANK),
        "g_mla_q": gain(ks[11], (L, MLA_QK)),
        "g_mla_k": gain(ks[12], (L, MLA_QK)),
        "g_ca_q": gain(ks[13], (L, CA_HEAD_DIM)),
        "g_ca_k": gain(ks[14], (L, CA_HEAD_DIM)),
        "rel_bias": 0.5 * jax.random.normal(ks[15], (L, CA_HEADS, 2 * REL_CLIP + 1), f32),
        "w_out": nrm(ks[16], (L, D_MIX, D_MODEL), D_MIX),
        "g_mlp_norm": gain(ks[17], (L, D_MODEL)),
        "w_up": nrm(ks[18], (L, D_MODEL, 2 * D_FF), D_MODEL),
        "conv_w": nrm(ks[19], (L, CONV_WIDTH, 2 * D_FF), CONV_WIDTH),
        "conv_b": 0.02 * jax.random.normal(ks[20], (L, 2 * D_FF), f32),
        "w_down": nrm(ks[21], (L, D_FF, D_MODEL), D_FF),
    }


def reference(x, c, positions, w_ada, b_ada, g_attn_norm, w_in, g_q_latent, g_kv_latent,
              w_q_up, w_kv_up, g_mla_q, g_mla_k, g_ca_q, g_ca_k, rel_bias, w_out,
              g_mlp_norm, w_up, conv_w, conv_b, w_down):
    B, S, _ = x.shape
    split_pts = [MLA_Q_RANK, MLA_Q_RANK + MLA_KV_RANK, MLA_Q_RANK + MLA_KV_RANK + MLA_ROPE]
    for l in range(DEPTH):
        mod = jnp.dot(jax.nn.silu(c), w_ada[l]) + b_ada[l]
        sh_a, sc_a, g_a, sh_m, sc_m, g_m = jnp.split(mod, N_MOD, axis=-1)

        h = modulate(rmsnorm(x, g_attn_norm[l]), sh_a, sc_a)
        proj = jnp.dot(h, w_in[l])
        q_lat, kv_lat, k_rope, ca_qkv = jnp.split(proj, split_pts, axis=-1)

        q = jnp.dot(rmsnorm(q_lat, g_q_latent[l]), w_q_up[l]).reshape(B, S, MLA_HEADS, MLA_QK)
        kv = jnp.dot(rmsnorm(kv_lat, g_kv_latent[l]), w_kv_up[l]).reshape(B, S, MLA_HEADS, MLA_NOPE + MLA_V)
        k_nope, v = kv[..., :MLA_NOPE], kv[..., MLA_NOPE:]
        k = jnp.concatenate(
            [k_nope, jnp.broadcast_to(k_rope[:, :, None, :], (B, S, MLA_HEADS, MLA_ROPE))], axis=-1)
        q = rmsnorm(q, g_mla_q[l])
        k = rmsnorm(k, g_mla_k[l])
        q = jnp.concatenate([q[..., :MLA_NOPE], rope(q[..., MLA_NOPE:], positions)], axis=-1)
        k = jnp.concatenate([k[..., :MLA_NOPE], rope(k[..., MLA_NOPE:], positions)], axis=-1)
        o_mla = mla_attention(q, k, v)

        ca = ca_qkv.reshape(B, S, 3, CA_HEADS, CA_HEAD_DIM)
        cq = rmsnorm(ca[:, :, 0], g_ca_q[l])
        ck = rmsnorm(ca[:, :, 1], g_ca_k[l])
        cv = ca[:, :, 2]
        o_ca = chunk_attention(cq, ck, cv, rel_bias[l])

        mixed = jnp.dot(jnp.concatenate([o_mla, o_ca], axis=-1), w_out[l])
        x = x + g_a[:, None, :] * mixed

        h = modulate(rmsnorm(x, g_mlp_norm[l]), sh_m, sc_m)
        u = causal_dwconv(jnp.dot(h, w_up[l]), conv_w[l], conv_b[l])
        gate, val = jnp.split(u, 2, axis=-1)
        x = x + g_m[:, None, :] * jnp.dot(jax.nn.silu(gate) * val, w_down[l])
    return x
```

```python
import math
from contextlib import ExitStack

import numpy as np
import concourse.bass as bass
import concourse.mybir as mybir
from concourse.bass_utils import run_bass_kernel_spmd

F32 = mybir.dt.float32
BF16 = mybir.dt.bfloat16
I32 = mybir.dt.int32
AF = mybir.ActivationFunctionType
ALU = mybir.AluOpType
AX = mybir.AxisListType

NCORES = 8
NSEQ = 2
S = 2048
D = 1024
NT = S // 128
D_IN = 1952
D_FF = 2816
NFF = D_FF // 128
EPS = 1e-6
TWO_PI = 2.0 * math.pi


class Op:
    __slots__ = ("eng", "fn", "r", "w", "chan", "deps", "signal", "sigval", "waitall")

    def __init__(self, eng, fn, r, w, chan, waitall):
        self.eng, self.fn, self.r, self.w, self.chan = eng, fn, tuple(r), tuple(w), chan
        self.deps = set()
        self.signal = False
        self.sigval = 0
        self.waitall = waitall


class Prog:
    def __init__(self, nc, es):
        self.nc = nc
        self.es = es
        self.ops = []
        self.last_w = {}
        self.readers = {}
        self.waitall_chans = set()

    tag = None

    def add(self, eng, fn, r=(), w=(), chan=None, waitall=False):
        if self.tag is not None and self.tag in SKIP:
            return None
        op = Op(eng, fn, r, w, chan, waitall)
        idx = len(self.ops)
        deps = set()
        for k in op.r:
            lw = self.last_w.get(k)
            if lw is not None:
                deps.add(lw)
            if isinstance(k, str) and k.startswith("bank"):
                for rd in self.readers.get(k, ()):
                    if self.ops[rd].eng != eng:
                        deps.add(rd)
        for k in op.w:
            lw = self.last_w.get(k)
            if lw is not None:
                deps.add(lw)
            for rd in self.readers.get(k, ()):
                deps.add(rd)
        deps.discard(idx)
        if chan is not None and waitall:
            deps = {d for d in deps if self.ops[d].chan != chan}
        op.deps = deps
        for k in op.w:
            self.last_w[k] = idx
            self.readers[k] = []
        for k in op.r:
            if k in op.w:
                continue
            self.readers.setdefault(k, []).append(idx)
        if chan is not None and waitall:
            self.waitall_chans.add(chan)
        self.ops.append(op)
        return idx

    def barrier(self):
        n = len(self.ops)
        last = {}
        for i, op in enumerate(self.ops):
            key = op.chan if op.chan is not None else ("E", op.eng)
            last[key] = i
        alld = set(last.values())
        for eng in ("pe", "act", "dve", "pool", "sp"):
            op = Op(eng, None, (), (), None, False)
            op.deps = set(alld)
            self.ops.append(op)

    def emit(self):
        nc = self.nc
        ops = self.ops
        engobj = {"pe": nc.tensor, "act": nc.scalar, "dve": nc.vector, "pool": nc.gpsimd, "sp": nc.sync}
        for op in ops:
            for d in op.deps:
                x = ops[d]
                if x.chan is None and x.eng == "pe" and op.eng == "pe" and op.chan is None:
                    continue
                x.signal = True
        sems = {}

        def sem(name):
            if name not in sems:
                sems[name] = self.es.enter_context(nc.semaphore("s_" + str(name)))
            return sems[name]

        cnt = {}
        chan_total = {}
        for op in ops:
            if op.fn is None:
                continue
            if op.chan is not None:
                c = ("C", op.chan)
                cnt[c] = cnt.get(c, 0) + 16
                op.sigval = cnt[c]
                op.signal = True
                chan_total[op.chan] = cnt[c]
            elif op.signal:
                c = ("E", op.eng)
                cnt[c] = cnt.get(c, 0) + 1
                op.sigval = cnt[c]
        known = {e: {} for e in engobj}
        nwaits = 0
        for op in ops:
            need = {}
            for d in op.deps:
                x = ops[d]
                if x.fn is None:
                    continue
                if x.chan is not None:
                    key = ("C", x.chan)
                    val = chan_total[x.chan] if x.chan in self.waitall_chans else x.sigval
                else:
                    if x.eng == "pe" and op.eng == "pe" and op.chan is None:
                        continue
                    key = ("E", x.eng)
                    val = x.sigval
                if val > need.get(key, 0):
                    need[key] = val
            e = engobj[op.eng]
            kn = known[op.eng]
            for key, val in need.items():
                if kn.get(key, 0) >= val:
                    continue
                e.wait_ge(sem(key), val)
                kn[key] = val
                nwaits += 1
            if op.fn is None:
                continue
            inst = op.fn(e)
            if op.chan is not None:
                inst.then_inc(sem(("C", op.chan)), 16)
            elif op.signal:
                inst.then_inc(sem(("E", op.eng)), 1)
        for chan, tot in chan_total.items():
            if known["sp"].get(("C", chan), 0) < tot:
                nc.sync.wait_ge(sem(("C", chan)), tot)
        self.stats = dict(n_ops=len(ops), n_waits=nwaits, n_sems=len(sems))


class Banks:
    def __init__(self, banks):
        self.banks = banks
        self.ptr = 0
        self.held = set()

    def get(self, hold=False):
        for _ in range(16):
            b = self.ptr
            self.ptr = (self.ptr + 1) % len(self.banks)
            if b not in self.held:
                if hold:
                    self.held.add(b)
                return b
        raise RuntimeError("no free PSUM bank")

    def release(self, b):
        self.held.discard(b)


import os
SKIP = set(os.environ.get('KSKIP', '').split(','))


def build_program(debug=None, stop=None):
    nc = bass.Bass("TRN2", target_bir_lowering=False)

    def din(name, shape, dt=F32):
        return nc.dram_tensor(name, list(shape), dt, kind="ExternalInput").ap()

    def dscr(name, shape, dt):
        return nc.dram_tensor(name, list(shape), dt, kind="Internal").ap()

    x_d = din("x", [NSEQ, S, D])
    cT_d = din("cT", [128, 8, NSEQ])
    pos_d = din("posl", [128, NSEQ * NT], I32)
    wada_d = din("w_ada", [D, 6 * D])
    bada_d = din("b_ada2", [NSEQ, 6 * D])
    win_d = din("w_in", [D, D_IN])
    wq_d = din("w_q_up", [256, 768])
    wkv_d = din("w_kv_up", [128, 1024])
    wout_d = din("w_out", [D, D])
    wup_d = din("w_up", [D, 2 * D_FF])
    wdn_d = din("w_down", [D_FF, D])
    gfeat_d = din("gfeat", [128, 19])
    convp_d = din("convp", [128, 4, 2 * NFF])
    grow_d = din("grow", [128, 320])
    bias34_d = din("bias34", [128, 8, 2, 128])
    bfar_d = din("bfar", [128, 8])
    invf_d = din("invf", [128, 16])
    out_d = nc.dram_tensor("out", [NSEQ, S, D], F32, kind="ExternalOutput").ap()

    mod_d = dscr("mod_scr", [NSEQ, 6 * D], F32)
    qT_d = dscr("qT_scr", [NSEQ, 8, 96, S], BF16)
    kT_d = dscr("kT_scr", [NSEQ, 8, 96, S], BF16)
    cqT_d = dscr("cqT_scr", [NSEQ, 8, 64, S], BF16)
    ckT_d = dscr("ckT_scr", [NSEQ, 8, 64, S], BF16)
    V_d = dscr("V_scr", [NSEQ, S, 8 * 65], BF16)
    cV_d = dscr("cV_scr", [NSEQ, S, 8 * 65], BF16)
    otn_d = dscr("otn_scr", [NSEQ, D, S], BF16)

    with ExitStack() as es:
        P = Prog(nc, es)

        def sb(stack, name, shape, dt):
            return stack.enter_context(nc.sbuf_tensor("sb_" + name, list(shape), dt))

        banks_f = [es.enter_context(nc.psum_tensor("bank%d" % i, [128, 512], F32)) for i in range(8)]
        banks_b = [b[:].bitcast(BF16) for b in banks_f]
        BK = Banks(banks_f)

        def bk(b):
            return "bank%d" % b

        ident = sb(es, "ident", [128, 128], BF16)
        identf = sb(es, "identf", [128, 128], F32)
        sel64 = sb(es, "sel64", [128, 64], F32)
        gfeat = sb(es, "gfeat", [128, 19], F32)
        grow = sb(es, "grow", [128, 320], F32)
        AB = sb(es, "AB", [128, NSEQ, 4, 8], F32)
        sa = es.enter_context(ExitStack())
        cs_all = sb(sa, "cs_all", [128, NSEQ * NT, 32], F32)
        junk = sb(sa, "junk", [128, 1024], BF16)

        P.add("pool", lambda e: e.memset(identf[:], 1.0), w=["identf"])
        P.add("pool", lambda e: e.affine_select(out=identf[:], in_=identf[:], pattern=[[-1, 128]],
                                                compare_op=ALU.is_equal, fill=0.0, base=0, channel_multiplier=1),
              r=["identf"], w=["identf"])
        P.add("pool", lambda e: e.tensor_copy(out=ident[:], in_=identf[:]), r=["identf"], w=["ident"])
        P.add("pool", lambda e: e.memset(sel64[:], 0.0), w=["sel64"])
        P.add("pool", lambda e: e.memset(sel64[64:65, :], 1.0), r=["sel64"], w=["sel64"])
        P.add("sp", lambda e: e.dma_start(out=gfeat[:], in_=gfeat_d), w=["gfeat"], chan="small", waitall=True)
        P.add("sp", lambda e: e.dma_start(out=grow[:], in_=grow_d), w=["grow"], chan="small", waitall=True)
        s0 = es.enter_context(ExitStack())
        cT = sb(s0, "cT", [128, 8, NSEQ], F32)
        bada = sb(s0, "bada", [NSEQ, 6 * D], F32)
        posi = sb(s0, "posi", [128, NSEQ * NT], I32)
        invf = sb(s0, "invf", [128, 16], F32)
        P.add("sp", lambda e: e.dma_start(out=cT[:], in_=cT_d), w=["cT"], chan="small", waitall=True)
        P.add("sp", lambda e: e.dma_start(out=bada[:], in_=bada_d), w=["bada"], chan="small", waitall=True)
        P.add("sp", lambda e: e.dma_start(out=posi[:], in_=pos_d), w=["posi"], chan="small", waitall=True)
        P.add("sp", lambda e: e.dma_start(out=invf[:], in_=invf_d), w=["invf"], chan="small", waitall=True)
        P.add("dve", lambda e: e.tensor_scalar(out=grow[:, 96:192], in0=grow[:, 96:192], scalar1=math.sqrt(96.0),
                                               scalar2=None, op0=ALU.mult), r=["grow"], w=["grow"])
        P.add("dve", lambda e: e.tensor_scalar(out=grow[:, 256:320], in0=grow[:, 256:320], scalar1=8.0,
                                               scalar2=None, op0=ALU.mult), r=["grow"], w=["grow"])

        if True:
            scb = sb(s0, "scb", [128, 8, NSEQ], BF16)
            wab = [sb(s0, "wab%d" % i, [128, 8, 512], BF16) for i in range(2)]
            modsb = sb(s0, "modsb", [NSEQ, 6 * D], F32)
            P.add("act", lambda e: e.activation(out=scb[:], in_=cT[:], func=AF.Silu), r=["cT"], w=["scb"])
            for nb in range(12):
                sl = nb % 2
                P.add("pool", (lambda nb, sl: lambda e: e.dma_start(
                    out=wab[sl][:], in_=wada_d[:, nb * 512:(nb + 1) * 512].rearrange("(j p) n -> p j n", p=128)))(nb, sl),
                    w=["wab%d" % sl], chan="wab%d" % sl)
                b = BK.get()
                for k in range(8):
                    P.add("pe", (lambda b, sl, k: lambda e: e.matmul(
                        banks_f[b][0:NSEQ, 0:512], lhsT=scb[:, k, :], rhs=wab[sl][:, k, :], start=(k == 0), stop=(k == 7)))(b, sl, k),
                        r=["scb", "wab%d" % sl], w=[bk(b)])
                P.add("dve", (lambda b, nb: lambda e: e.tensor_tensor(
                    out=modsb[:, nb * 512:(nb + 1) * 512], in0=banks_f[b][0:NSEQ, 0:512],
                    in1=bada[:, nb * 512:(nb + 1) * 512], op=ALU.add))(b, nb),
                    r=[bk(b), "bada"], w=["modsb"])
            P.add("sp", lambda e: e.dma_start(out=mod_d, in_=modsb[:]), r=["modsb"], w=["mod_d"], chan="modst")
            P.tag = "modT"
            modT = sb(s0, "modT", [128, NSEQ, 4, 8], F32)
            bT = BK.get()

            def modtr(qi, c, j):
                col = (qi * 8 + j) * NSEQ
                P.add("pe", lambda e: e.matmul(banks_f[bT][:, col:col + NSEQ], lhsT=modsb[0:NSEQ, c * D + j * 128:c * D + (j + 1) * 128],
                                               rhs=identf[0:NSEQ, 0:NSEQ], start=True, stop=True),
                      r=["modsb", "identf"], w=[bk(bT)])
            for qi, c in enumerate((0, 1, 3, 4)):
                for j in range(8):
                    modtr(qi, c, j)
            P.add("dve", lambda e: e.tensor_copy(out=modT[:].rearrange("p s k j -> p k j s"),
                                                 in_=banks_f[bT][:, 0:32 * NSEQ].rearrange("p (k j s) -> p k j s", k=4, j=8)),
                  r=[bk(bT)], w=["modT"])
            for s in range(NSEQ):
                for (dst, srcq, gcol) in ((0, 1, 0), (2, 3, 8)):
                    P.add("dve", (lambda s, dst, srcq, gcol: lambda e: e.scalar_tensor_tensor(
                        out=AB[:, s, dst, :], in0=modT[:, s, srcq, :], scalar=1.0, in1=gfeat[:, gcol:gcol + 8],
                        op0=ALU.add, op1=ALU.mult))(s, dst, srcq, gcol),
                        r=["modT", "gfeat"], w=["AB"])
                    P.add("dve", (lambda s, dst: lambda e: e.tensor_scalar(
                        out=AB[:, s, dst, :], in0=AB[:, s, dst, :], scalar1=32.0, scalar2=None, op0=ALU.mult))(s, dst),
                        r=["AB"], w=["AB"])
                    P.add("dve", (lambda s, dst, srcq: lambda e: e.tensor_copy(
                        out=AB[:, s, dst + 1, :], in_=modT[:, s, srcq - 1, :]))(s, dst, srcq),
                        r=["modT"], w=["AB"])
            P.tag = "rot"
            posf = sb(s0, "posf", [128, NSEQ * NT], F32)
            ang = sb(s0, "ang", [128, NSEQ * NT, 32], F32)
            kf = sb(s0, "kf", [128, NSEQ * NT, 32], F32)
            ki = sb(s0, "ki", [128, NSEQ * NT, 32], I32)
            mk = sb(s0, "mk", [128, NSEQ * NT, 32], F32)
            P.add("dve", lambda e: e.tensor_copy(out=posf[:], in_=posi[:]), r=["posi"], w=["posf"])
            NTT = NSEQ * NT
            P.add("dve", lambda e: e.tensor_tensor(out=ang[:, :, 16:32], in0=posf[:].unsqueeze(2).to_broadcast([128, NTT, 16]),
                                                   in1=invf[:].unsqueeze(1).to_broadcast([128, NTT, 16]), op=ALU.mult),
                  r=["posf", "invf"], w=["ang"])
            P.add("dve", lambda e: e.tensor_scalar(out=ang[:, :, 0:16], in0=ang[:, :, 16:32], scalar1=math.pi / 2.0,
                                                   scalar2=None, op0=ALU.add), r=["ang"], w=["ang"])
            P.add("dve", lambda e: e.tensor_scalar(out=kf[:], in0=ang[:], scalar1=1.0 / TWO_PI, scalar2=None, op0=ALU.mult),
                  r=["ang"], w=["kf"])
            P.add("dve", lambda e: e.tensor_copy(out=ki[:], in_=kf[:]), r=["kf"], w=["ki"])
            P.add("dve", lambda e: e.tensor_copy(out=kf[:], in_=ki[:]), r=["ki"], w=["kf"])
            P.add("dve", lambda e: e.scalar_tensor_tensor(out=ang[:], in0=kf[:], scalar=-TWO_PI, in1=ang[:],
                                                          op0=ALU.mult, op1=ALU.add), r=["kf", "ang"], w=["ang"])
            P.add("dve", lambda e: e.tensor_scalar(out=mk[:], in0=ang[:], scalar1=math.pi, scalar2=-TWO_PI,
                                                   op0=ALU.is_gt, op1=ALU.mult), r=["ang"], w=["mk"])
            P.add("dve", lambda e: e.tensor_tensor(out=ang[:], in0=ang[:], in1=mk[:], op=ALU.add), r=["ang", "mk"], w=["ang"])
            P.add("dve", lambda e: e.tensor_scalar(out=mk[:], in0=ang[:], scalar1=-math.pi, scalar2=TWO_PI,
                                                   op0=ALU.is_lt, op1=ALU.mult), r=["ang"], w=["mk"])
            P.add("dve", lambda e: e.tensor_tensor(out=ang[:], in0=ang[:], in1=mk[:], op=ALU.add), r=["ang", "mk"], w=["ang"])
            P.add("dve", lambda e: e.tensor_scalar(out=ang[:], in0=ang[:], scalar1=math.pi, scalar2=-math.pi,
                                                   op0=ALU.min, op1=ALU.max), r=["ang"], w=["ang"])
            P.add("act", lambda e: e.activation(out=cs_all[:], in_=ang[:], func=AF.Sin), r=["ang"], w=["cs_all"])
            P.tag = None
            P.barrier()
            s0.close()

        if True:
            P.tag = "wlA"
            win = sb(sa, "win", [128, 8, D_IN], BF16)
            wq = sb(sa, "wq", [128, 2, 768], BF16)
            wkv = sb(sa, "wkv", [128, 1024], BF16)
            wstage = sb(sa, "wstage", [128, 2, 768], F32)
            wstage2 = sb(sa, "wstage2", [128, 1024], F32)
            for (c0, c1) in ((0, 512), (512, 1024), (1024, 1536), (1536, 1952)):
                P.add("pool", (lambda c0, c1: lambda e: e.dma_start(out=win[:, :, c0:c1], in_=win_d[:, c0:c1].rearrange("(j p) n -> p j n", p=128)))(c0, c1),
                      w=["win"], chan="win", waitall=True)
            P.tag = "wlA2"
            P.add("sp", lambda e: e.dma_start(out=wstage[:], in_=wq_d.rearrange("(j p) n -> p j n", p=128)),
                  w=["wstage"], chan="small3", waitall=True)
            P.add("sp", lambda e: e.dma_start(out=wstage2[:], in_=wkv_d), w=["wstage2"], chan="small3", waitall=True)
            for j in range(2):
                P.add("dve", (lambda j: lambda e: e.tensor_scalar(
                    out=wq[:, j, :], in0=wstage[:, j, :], scalar1=gfeat[:, 16 + j:17 + j], scalar2=16.0,
                    op0=ALU.mult, op1=ALU.mult))(j), r=["wstage", "gfeat"], w=["wq"])
            P.add("dve", lambda e: e.tensor_scalar(out=wkv[:], in0=wstage2[:], scalar1=gfeat[:, 18:19],
                                                   scalar2=math.sqrt(128.0), op0=ALU.mult, op1=ALU.mult),
                  r=["wstage2", "gfeat"], w=["wkv"])

            P.tag = "wlA3"
            xt = [sb(sa, "xt%d" % i, [128, D], F32) for i in range(2)]
            xn = [sb(sa, "xn%d" % i, [128, D], BF16) for i in range(2)]
            hT = [sb(sa, "hT%d" % i, [128, 8, 128], BF16) for i in range(2)]
            st = sb(sa, "stA", [128, 2, 40], F32)
            lat = [sb(sa, "lat%d" % i, [128, 384], BF16) for i in range(2)]
            latT = [sb(sa, "latT%d" % i, [128, 3, 128], BF16) for i in range(2)]
            kr = [sb(sa, "kr%d" % i, [128, 4, 32], F32) for i in range(2)]
            qn = [sb(sa, "qn%d" % i, [128, 8, 96], F32) for i in range(2)]
            qsq = [sb(sa, "qsq%d" % i, [128, 8, 96], F32) for i in range(2)]
            qb = [sb(sa, "qb%d" % i, [128, 8, 96], BF16) for i in range(2)]
            kn = [sb(sa, "kn%d" % i, [128, 8, 64], F32) for i in range(2)]
            ksq = [sb(sa, "ksq%d" % i, [128, 8, 64], F32) for i in range(2)]
            kb = [sb(sa, "kb%d" % i, [128, 8, 96], BF16) for i in range(2)]
            rt = [sb(sa, "rt%d" % i, [128, 8, 32], F32) for i in range(2)]
            cqn = [sb(sa, "cqn%d" % i, [128, 8, 64], F32) for i in range(2)]
            cqs = [sb(sa, "cqs%d" % i, [128, 8, 64], F32) for i in range(2)]
            cqb = [sb(sa, "cqb%d" % i, [128, 8, 64], BF16) for i in range(2)]
            ckn = [sb(sa, "ckn%d" % i, [128, 8, 64], F32) for i in range(2)]
            cks = [sb(sa, "cks%d" % i, [128, 8, 64], F32) for i in range(2)]
            ckb = [sb(sa, "ckb%d" % i, [128, 8, 64], BF16) for i in range(2)]
            qT_st = [sb(sa, "qTst%d" % i, [96, 8, 512], BF16) for i in range(2)]
            kT_st = [sb(sa, "kTst%d" % i, [96, 8, 512], BF16) for i in range(2)]
            cqT_st = [sb(sa, "cqTst%d" % i, [64, 8, 512], BF16) for i in range(2)]
            ckT_st = [sb(sa, "ckTst%d" % i, [64, 8, 512], BF16) for i in range(2)]
            V_st = [sb(sa, "Vst%d" % i, [128, 4, 8, 65], BF16) for i in range(2)]
            cV_st = [sb(sa, "cVst%d" % i, [128, 4, 8, 65], BF16) for i in range(2)]
            for i in range(2):
                P.add("pool", (lambda i: lambda e: e.memset(V_st[i][:, :, :, 64:65], 1.0))(i), w=["Vst%d" % i])
                P.add("pool", (lambda i: lambda e: e.memset(cV_st[i][:, :, :, 64:65], 1.0))(i), w=["cVst%d" % i])

            P.tag = None

            def rstd_from_ssq(sl, col, n, tag):
                kst = "st%d_%s" % (sl, tag)
                P.add("act", lambda e: e.activation(out=st[:, sl, col:col + 1], in_=st[:, sl, col:col + 1], func=AF.Sqrt,
                                                    bias=float(n * EPS), scale=1.0), r=[kst], w=[kst])
                P.add("dve", lambda e: e.reciprocal(out=st[:, sl, col:col + 1], in_=st[:, sl, col:col + 1]), r=[kst], w=[kst])
                return kst

            def rstd_vec(sl, c0, nh, n, tag):
                kst = "st%d_%s" % (sl, tag)
                P.add("act", lambda e: e.activation(out=st[:, sl, c0:c0 + nh], in_=st[:, sl, c0:c0 + nh], func=AF.Sqrt,
                                                    bias=float(n * EPS), scale=1.0), r=[kst], w=[kst])
                P.add("dve", lambda e: e.reciprocal(out=st[:, sl, c0:c0 + nh], in_=st[:, sl, c0:c0 + nh]), r=[kst], w=[kst])
                return kst

            def rope(eng, src3, dst3, cs, keys_r, keys_w, tmp, nh, ktmp):
                cosb = cs[:, 0:16].unsqueeze(1).to_broadcast([128, nh, 16])
                sinb = cs[:, 16:32].unsqueeze(1).to_broadcast([128, nh, 16])
                P.add(eng, lambda e: e.tensor_tensor(out=tmp[:, :, 0:16], in0=src3[:, :, 16:32], in1=sinb, op=ALU.mult),
                      r=keys_r, w=[ktmp])
                P.add(eng, lambda e: e.tensor_tensor(out=tmp[:, :, 16:32], in0=src3[:, :, 0:16], in1=sinb, op=ALU.mult),
                      r=keys_r + [ktmp], w=[ktmp])
                P.add(eng, lambda e: e.tensor_tensor(out=src3[:, :, 0:16], in0=src3[:, :, 0:16], in1=cosb, op=ALU.mult),
                      r=keys_r + [ktmp], w=keys_r[:1])
                P.add(eng, lambda e: e.tensor_tensor(out=src3[:, :, 16:32], in0=src3[:, :, 16:32], in1=cosb, op=ALU.mult),
                      r=keys_r + [ktmp], w=keys_r[:1])
                P.add(eng, lambda e: e.tensor_tensor(out=dst3[:, :, 0:16], in0=src3[:, :, 0:16], in1=tmp[:, :, 0:16], op=ALU.subtract),
                      r=keys_r + [ktmp], w=keys_w)
                P.add(eng, lambda e: e.tensor_tensor(out=dst3[:, :, 16:32], in0=src3[:, :, 16:32], in1=tmp[:, :, 16:32], op=ALU.add),
                      r=keys_r + [ktmp], w=keys_w)

            def prep_tile(g):
                s, tt = divmod(g, NT)
                jb, i4 = divmod(tt, 4)
                sl = g % 2
                bs = (g // 4) % 2
                S_ = str(sl)
                kxt, kxn, khT = "xt" + S_, "xn" + S_, "hT" + S_
                P.add("sp", lambda e: e.dma_start(out=xt[sl][:], in_=x_d[s, tt * 128:(tt + 1) * 128, :]), w=[kxt], chan="xt" + S_)
                P.add("act", lambda e: e.activation(out=junk[:], in_=xt[sl][:], func=AF.Square, accum_out=st[:, sl, 0:1]),
                      r=[kxt], w=["st%s_x" % S_])
                kst = rstd_from_ssq(sl, 0, D, "x")
                P.add("act", lambda e: e.activation(out=xn[sl][:], in_=xt[sl][:], func=AF.Copy, scale=st[:, sl, 0:1]),
                      r=[kxt, kst], w=[kxn])
                b0 = BK.get()
                for c in range(8):
                    P.add("pe", (lambda c: lambda e: e.transpose(out=banks_b[b0][:, c * 128:(c + 1) * 128],
                                                                  in_=xn[sl][:, c * 128:(c + 1) * 128], identity=ident[:]))(c),
                          r=[kxn, "ident"], w=[bk(b0)])
                tp3 = banks_b[b0][:, 0:1024].rearrange("p (c t) -> p c t", c=8)
                P.add("dve", lambda e: e.tensor_tensor(out=hT[sl][:], in0=tp3, in1=AB[:, s, 0, :].unsqueeze(2).to_broadcast([128, 8, 128]),
                                                       op=ALU.mult), r=[bk(b0), "AB"], w=[khT])
                P.add("pool", lambda e: e.tensor_tensor(out=hT[sl][:], in0=hT[sl][:], in1=AB[:, s, 1, :].unsqueeze(2).to_broadcast([128, 8, 128]),
                                                        op=ALU.add), r=[khT, "AB"], w=[khT])
                pb = [BK.get(hold=True) for _ in range(4)]
                cols = [(0, 416), (416, 928), (928, 1440), (1440, 1952)]
                for bi, (c0, c1) in enumerate(cols):
                    for k in range(8):
                        P.add("pe", (lambda bi, c0, c1, k: lambda e: e.matmul(
                            banks_f[pb[bi]][:, 0:c1 - c0], lhsT=hT[sl][:, k, :], rhs=win[:, k, c0:c1],
                            start=(k == 0), stop=(k == 7)))(bi, c0, c1, k),
                            r=[khT, "win"], w=[bk(pb[bi])])
                pl, pq, pk, pv = [banks_f[b] for b in pb]
                kl, kq, kk_, kv_ = [bk(b) for b in pb]
                P.add("act", lambda e: e.activation(out=junk[:, 0:256], in_=pl[:, 0:256], func=AF.Square, accum_out=st[:, sl, 1:2]),
                      r=[kl], w=["st%s_ql" % S_])
                P.add("act", lambda e: e.activation(out=junk[:, 256:384], in_=pl[:, 256:384], func=AF.Square, accum_out=st[:, sl, 2:3]),
                      r=[kl], w=["st%s_kvl" % S_])
                k1 = rstd_from_ssq(sl, 1, 256, "ql")
                k2 = rstd_from_ssq(sl, 2, 128, "kvl")
                klat = "lat" + S_
                P.add("dve", lambda e: e.tensor_scalar(out=lat[sl][:, 0:256], in0=pl[:, 0:256], scalar1=st[:, sl, 1:2], scalar2=None,
                                                       op0=ALU.mult), r=[kl, k1], w=[klat + "a"])
                P.add("dve", lambda e: e.tensor_scalar(out=lat[sl][:, 256:384], in0=pl[:, 256:384], scalar1=st[:, sl, 2:3], scalar2=None,
                                                       op0=ALU.mult), r=[kl, k2], w=[klat + "b"])
                kkr = "kr" + S_
                P.add("dve", lambda e: e.tensor_tensor(out=kr[sl][:, 0, :], in0=pl[:, 384:416], in1=grow[:, 160:192], op=ALU.mult),
                      r=[kl, "grow"], w=[kkr])
                P.add("act", lambda e: e.activation(out=junk[:, 512:544], in_=pl[:, 384:416], func=AF.Square, accum_out=st[:, sl, 3:4]),
                      r=[kl], w=["st%s_kr" % S_])
                BK.release(pb[0])
                b1 = BK.get()
                for c in range(3):
                    P.add("pe", (lambda c: lambda e: e.transpose(out=banks_b[b1][:, c * 128:(c + 1) * 128],
                                                                  in_=lat[sl][:, c * 128:(c + 1) * 128], identity=ident[:]))(c),
                          r=[klat + "a", klat + "b", "ident"], w=[bk(b1)])
                klT = "latT" + S_
                P.add("act", lambda e: e.activation(out=latT[sl][:].rearrange("p c t -> p (c t)"), in_=banks_b[b1][:, 0:384], func=AF.Copy),
                      r=[bk(b1)], w=[klT])
                bq0, bq1 = BK.get(hold=True), BK.get(hold=True)
                for (bb, c0, c1) in ((bq0, 0, 480), (bq1, 480, 768)):
                    for k in range(2):
                        P.add("pe", (lambda bb, c0, c1, k: lambda e: e.matmul(
                            banks_f[bb][:, 0:c1 - c0], lhsT=latT[sl][:, k, :], rhs=wq[:, k, c0:c1], start=(k == 0), stop=(k == 1)))(bb, c0, c1, k),
                            r=[klT, "wq"], w=[bk(bb)])
                bkv0, bkv1 = BK.get(hold=True), BK.get(hold=True)
                for (bb, c0) in ((bkv0, 0), (bkv1, 512)):
                    P.add("pe", (lambda bb, c0: lambda e: e.matmul(
                        banks_f[bb][:, 0:512], lhsT=latT[sl][:, 2, :], rhs=wkv[:, c0:c0 + 512], start=True, stop=True))(bb, c0),
                        r=[klT, "wkv"], w=[bk(bb)])
                cs = cs_all[:, g, :]
                kqn, kqs, kqb = "qn" + S_, "qsq" + S_, "qb" + S_
                q0 = banks_f[bq0][:, 0:480].rearrange("p (h d) -> p h d", h=5)
                q1 = banks_f[bq1][:, 0:288].rearrange("p (h d) -> p h d", h=3)
                P.add("act", lambda e: e.activation(out=qn[sl][:, 0:5, :], in_=q0, func=AF.Copy), r=[bk(bq0)], w=[kqn + "a"])
                P.add("act", lambda e: e.activation(out=qn[sl][:, 5:8, :], in_=q1, func=AF.Copy), r=[bk(bq1)], w=[kqn + "b"])
                BK.release(bq0)
                BK.release(bq1)
                P.add("pool", lambda e: e.tensor_tensor(out=qsq[sl][:], in0=qn[sl][:], in1=qn[sl][:], op=ALU.mult),
                      r=[kqn + "a", kqn + "b"], w=[kqs])
                P.add("dve", lambda e: e.tensor_reduce(out=st[:, sl, 8:16], in_=qsq[sl][:], axis=AX.X, op=ALU.add),
                      r=[kqs], w=["st%s_q" % S_])
                k3 = rstd_vec(sl, 8, 8, 96, "q")
                P.add("dve", lambda e: e.tensor_tensor(out=qn[sl][:], in0=qn[sl][:], in1=st[:, sl, 8:16].unsqueeze(2).to_broadcast([128, 8, 96]),
                                                       op=ALU.mult), r=[kqn + "a", kqn + "b", k3], w=[kqn])
                P.add("pool", lambda e: e.tensor_tensor(out=qn[sl][:], in0=qn[sl][:], in1=grow[:, 0:96].unsqueeze(1).to_broadcast([128, 8, 96]),
                                                        op=ALU.mult), r=[kqn, "grow"], w=[kqn])
                P.add("act", lambda e: e.activation(out=qb[sl][:, :, 0:64], in_=qn[sl][:, :, 0:64], func=AF.Copy), r=[kqn], w=[kqb + "n"])
                rope("pool", qn[sl][:, :, 64:96], qb[sl][:, :, 64:96], cs, [kqn, "cs_all"], [kqb + "r"], rt[sl], 8, "rt" + S_)
                kkn, kks, kkb = "kn" + S_, "ksq" + S_, "kb" + S_
                kv0 = banks_f[bkv0][:, 0:512].rearrange("p (h d) -> p h d", h=4)
                kv1 = banks_f[bkv1][:, 0:512].rearrange("p (h d) -> p h d", h=4)
                P.add("act", lambda e: e.activation(out=kn[sl][:, 0:4, :], in_=kv0[:, :, 0:64], func=AF.Copy), r=[bk(bkv0)], w=[kkn + "a"])
                P.add("act", lambda e: e.activation(out=kn[sl][:, 4:8, :], in_=kv1[:, :, 0:64], func=AF.Copy), r=[bk(bkv1)], w=[kkn + "b"])
                P.add("act", lambda e: e.activation(out=V_st[bs][:, i4, 0:4, 0:64], in_=kv0[:, :, 64:128], func=AF.Copy),
                      r=[bk(bkv0)], w=["Vst%d" % bs])
                P.add("act", lambda e: e.activation(out=V_st[bs][:, i4, 4:8, 0:64], in_=kv1[:, :, 64:128], func=AF.Copy),
                      r=[bk(bkv1)], w=["Vst%d" % bs])
                BK.release(bkv0)
                BK.release(bkv1)
                P.add("pool", lambda e: e.tensor_tensor(out=ksq[sl][:], in0=kn[sl][:], in1=kn[sl][:], op=ALU.mult),
                      r=[kkn + "a", kkn + "b"], w=[kks])
                P.add("dve", lambda e: e.tensor_reduce(out=st[:, sl, 16:24], in_=ksq[sl][:], axis=AX.X, op=ALU.add),
                      r=[kks], w=["st%s_k" % S_])
                P.add("dve", lambda e: e.tensor_scalar(out=st[:, sl, 16:24], in0=st[:, sl, 16:24], scalar1=st[:, sl, 3:4], scalar2=None,
                                                       op0=ALU.add), r=["st%s_k" % S_, "st%s_kr" % S_], w=["st%s_k" % S_])
                k4 = rstd_vec(sl, 16, 8, 96, "k")
                P.add("dve", lambda e: e.tensor_tensor(out=kn[sl][:], in0=kn[sl][:], in1=st[:, sl, 16:24].unsqueeze(2).to_broadcast([128, 8, 64]),
                                                       op=ALU.mult), r=[kkn + "a", kkn + "b", k4], w=[kkn])
                P.add("pool", lambda e: e.tensor_tensor(out=kb[sl][:, :, 0:64], in0=kn[sl][:], in1=grow[:, 96:160].unsqueeze(1).to_broadcast([128, 8, 64]),
                                                        op=ALU.mult), r=[kkn, "grow"], w=[kkb + "n"])
                rope("dve", kr[sl][:, 0:1, :], kr[sl][:, 1:2, :], cs, [kkr, "cs_all"], [kkr + "o"], kr[sl][:, 2:3, :], 1, kkr + "t")
                P.add("dve", lambda e: e.tensor_tensor(out=kb[sl][:, :, 64:96], in0=kr[sl][:, 1:2, :].to_broadcast([128, 8, 32]),
                                                       in1=st[:, sl, 16:24].unsqueeze(2).to_broadcast([128, 8, 32]), op=ALU.mult),
                      r=[kkr + "o", k4], w=[kkb + "r"])
                for (src, keyb, dn, dsq, db, col, gc0, tag) in (
                        (pq, kq, cqn, cqs, cqb, 24, 192, "cq"), (pk, kk_, ckn, cks, ckb, 32, 256, "ck")):
                    kdn, kds, kdb = tag + "n" + S_, tag + "s" + S_, tag + "b" + S_
                    P.add("act", (lambda src, dn: lambda e: e.activation(out=dn[sl][:].rearrange("p h d -> p (h d)"), in_=src[:, 0:512], func=AF.Copy))(src, dn),
                          r=[keyb], w=[kdn])
                    P.add("pool", (lambda dn, dsq: lambda e: e.tensor_tensor(out=dsq[sl][:], in0=dn[sl][:], in1=dn[sl][:], op=ALU.mult))(dn, dsq),
                          r=[kdn], w=[kds])
                    P.add("dve", (lambda dsq, col: lambda e: e.tensor_reduce(out=st[:, sl, col:col + 8], in_=dsq[sl][:], axis=AX.X, op=ALU.add))(dsq, col),
                          r=[kds], w=["st%s_%s" % (S_, tag)])
                    k5 = rstd_vec(sl, col, 8, 64, tag)
                    P.add("dve", (lambda dn, col: lambda e: e.tensor_tensor(out=dn[sl][:], in0=dn[sl][:],
                                                                           in1=st[:, sl, col:col + 8].unsqueeze(2).to_broadcast([128, 8, 64]), op=ALU.mult))(dn, col),
                          r=[kdn, k5], w=[kdn])
                    P.add("pool", (lambda dn, db, gc0: lambda e: e.tensor_tensor(out=db[sl][:], in0=dn[sl][:],
                                                                                in1=grow[:, gc0:gc0 + 64].unsqueeze(1).to_broadcast([128, 8, 64]), op=ALU.mult))(dn, db, gc0),
                          r=[kdn, "grow"], w=[kdb])
                P.add("act", lambda e: e.activation(out=cV_st[bs][:, i4, :, 0:64], in_=pv[:, 0:512].rearrange("p (h d) -> p h d", h=8), func=AF.Copy),
                      r=[kv_], w=["cVst%d" % bs])
                BK.release(pb[1])
                BK.release(pb[2])
                BK.release(pb[3])
                for (srcb, keys, dst, kdst, dd) in ((qb, [kqb + "n", kqb + "r"], qT_st, "qTst%d" % bs, 96),
                                                    (kb, [kkb + "n", kkb + "r"], kT_st, "kTst%d" % bs, 96),
                                                    (cqb, ["cqb" + S_], cqT_st, "cqTst%d" % bs, 64),
                                                    (ckb, ["ckb" + S_], ckT_st, "ckTst%d" % bs, 64)):
                    bt = BK.get()
                    for h in range(8):
                        P.add("pe", (lambda srcb, bt, h, dd: lambda e: e.transpose(
                            out=banks_b[bt][0:dd, h * 128:(h + 1) * 128], in_=srcb[sl][:, h, :], identity=ident[:]))(srcb, bt, h, dd),
                            r=keys + ["ident"], w=[bk(bt)])
                    P.add("act", (lambda bt, dst, dd: lambda e: e.activation(
                        out=dst[bs][:, :, i4 * 128:(i4 + 1) * 128], in_=banks_b[bt][0:dd, 0:1024].rearrange("p (h t) -> p h t", h=8), func=AF.Copy))(bt, dst, dd),
                        r=[bk(bt)], w=[kdst])
                if i4 == 3:
                    t0 = jb * 512
                    P.add("sp", lambda e: e.dma_start(out=qT_d[s, :, :, t0:t0 + 512].rearrange("h d t -> d h t"), in_=qT_st[bs][:]),
                          r=["qTst%d" % bs], w=["qT_d%d" % s], chan="stq%d" % bs)
                    P.add("sp", lambda e: e.dma_start(out=kT_d[s, :, :, t0:t0 + 512].rearrange("h d t -> d h t"), in_=kT_st[bs][:]),
                          r=["kTst%d" % bs], w=["kT_d%d" % s], chan="stk%d" % bs)
                    P.add("sp", lambda e: e.dma_start(out=cqT_d[s, :, :, t0:t0 + 512].rearrange("h d t -> d h t"), in_=cqT_st[bs][:]),
                          r=["cqTst%d" % bs], w=["cqT_d%d" % s], chan="stcq%d" % bs)
                    P.add("sp", lambda e: e.dma_start(out=ckT_d[s, :, :, t0:t0 + 512].rearrange("h d t -> d h t"), in_=ckT_st[bs][:]),
                          r=["ckTst%d" % bs], w=["ckT_d%d" % s], chan="stck%d" % bs)
                    P.add("sp", lambda e: e.dma_start(out=V_d[s, t0:t0 + 512, :].rearrange("(k p) c -> p k c", p=128),
                                                      in_=V_st[bs][:].rearrange("p k h c -> p k (h c)")),
                          r=["Vst%d" % bs], w=["V_d%d" % s], chan="stv%d" % bs)
                    P.add("sp", lambda e: e.dma_start(out=cV_d[s, t0:t0 + 512, :].rearrange("(k p) c -> p k c", p=128),
                                                      in_=cV_st[bs][:].rearrange("p k h c -> p k (h c)")),
                          r=["cVst%d" % bs], w=["cV_d%d" % s], chan="stcv%d" % bs)

            for g in range(NSEQ * NT if stop not in ("setup",) else 0):
                prep_tile(g)
            P.barrier()
            sa.close()

        with ExitStack() as sbk:
            kT_all = sb(sbk, "kT_all", [96, 8, S], BF16)
            V_all = sb(sbk, "V_all", [128, NT, 8 * 65], BF16)
            ckT_all = sb(sbk, "ckT_all", [64, 8, S], BF16)
            cV_all = sb(sbk, "cV_all", [128, NT, 8 * 65], BF16)
            qT_b = [sb(sbk, "qT_b%d" % i, [96, 8, 512], BF16) for i in range(2)]
            cqT_b = [sb(sbk, "cqT_b%d" % i, [64, 8, 512], BF16) for i in range(2)]
            otn = [sb(sbk, "otn%d" % i, [64, 16, 512], BF16) for i in range(1)]
            Ef = sb(sbk, "Ef", [128, 8, 2, 128], F32)
            bfar = sb(sbk, "bfar", [128, 8], F32)
            NPT = 4
            Pt = [sb(sbk, "Pt%d" % i, [128, 512], BF16) for i in range(NPT)]
            PA = [sb(sbk, "PA%d" % i, [128, 384], BF16) for i in range(2)]
            PB = [sb(sbk, "PB%d" % i, [128, 256], F32) for i in range(2)]
            PBb = [sb(sbk, "PBb%d" % i, [128, 256], BF16) for i in range(2)]
            rz = [sb(sbk, "rz%d" % i, [128, 512], F32) for i in range(2)]
            bcs = [sb(sbk, "bcs%d" % i, [64, 512], F32) for i in range(2)]
            P.tag = "wlB"
            P.add("sp", lambda e: e.dma_start(out=Ef[:], in_=bias34_d), w=["Ef"], chan="small4", waitall=True)
            P.add("sp", lambda e: e.dma_start(out=bfar[:], in_=bfar_d), w=["bfar"], chan="small4", waitall=True)
            P.add("act", lambda e: e.activation(out=Ef[:], in_=Ef[:], func=AF.Exp), r=["Ef"], w=["Ef"])
            P.add("pool", lambda e: e.memset(Ef[64:128, :, 1, 0:64], 0.0), r=["Ef"], w=["Ef"])
            for i in range(2):
                P.add("pool", (lambda i: lambda e: e.memset(rz[i][:], 0.0))(i), w=["rz%d" % i])
            P.tag = None
            cnt = {"pt": 0, "pa": 0, "rz": 0}

            def normalize(bo, width, dst_ap, kdst):
                ri = cnt["rz"] % 2
                cnt["rz"] += 1
                P.add("dve", lambda e: e.reciprocal(out=rz[ri][64:65, 0:width], in_=banks_f[bo][64:65, 0:width]),
                      r=[bk(bo)], w=["rz%d" % ri])
                bb = BK.get()
                P.add("pe", lambda e: e.matmul(banks_f[bb][0:64, 0:width], lhsT=sel64[:, 0:64], rhs=rz[ri][:, 0:width], start=True, stop=True),
                      r=["rz%d" % ri, "sel64"], w=[bk(bb)])
                P.add("act", lambda e: e.activation(out=bcs[ri][:, 0:width], in_=banks_f[bb][0:64, 0:width], func=AF.Copy),
                      r=[bk(bb)], w=["bcs%d" % ri])
                P.add("dve", lambda e: e.tensor_tensor(out=dst_ap, in0=banks_f[bo][0:64, 0:width], in1=bcs[ri][:, 0:width], op=ALU.mult),
                      r=[bk(bo), "bcs%d" % ri], w=[kdst])
                BK.release(bo)

            def mla_head(s, j, h, qs, os_):
                kq = "qT_b%d" % qs
                nkt = 4 * j + 4
                bo = BK.get(hold=True)
                tiles = []
                for kt in range(nkt):
                    r_ = kt - 4 * j
                    c0 = 128 * r_ if r_ > 0 else 0
                    tiles.append((kt, c0, r_ >= 0))
                sbank = {}

                def emit_s(idx):
                    kt, c0, diag = tiles[idx]
                    b = BK.get()
                    sbank[idx] = b
                    P.add("pe", lambda e: e.matmul(banks_f[b][:, 0:512 - c0], lhsT=kT_all[:, h, kt * 128:(kt + 1) * 128],
                                                   rhs=qT_b[qs][:, h, c0:512], start=True, stop=True),
                          r=["kT_all", kq], w=[bk(b)])

                def emit_rest(idx):
                    kt, c0, diag = tiles[idx]
                    b = sbank[idx]
                    pi = cnt["pt"] % NPT
                    cnt["pt"] += 1
                    kp = "Pt%d" % pi
                    w_ = 512 - c0
                    P.add("act", lambda e: e.activation(out=Pt[pi][:, 0:w_], in_=banks_f[b][:, 0:w_], func=AF.Exp), r=[bk(b)], w=[kp])
                    if diag:
                        P.add("pool", lambda e: e.memset(Pt[pi][64:128, 0:64], 0.0), r=[kp], w=[kp])
                    P.add("pe", lambda e: e.matmul(banks_f[bo][0:65, c0:512], lhsT=V_all[:, kt, h * 65:(h + 1) * 65], rhs=Pt[pi][:, 0:w_],
                                                   start=(idx == 0), stop=(idx == nkt - 1)),
                          r=[kp, "V_all"], w=[bk(bo)])

                LOOK = 2
                for idx in range(min(LOOK, nkt)):
                    emit_s(idx)
                for idx in range(nkt):
                    emit_rest(idx)
                    if idx + LOOK < nkt:
                        emit_s(idx + LOOK)
                normalize(bo, 512, otn[os_][:, h, :], "otn%d" % os_)

            def ca_qtile(j, h, qs, bo, i):
                kq = "cqT_b%d" % qs
                gi = 4 * j + i
                tmin = max(0, 4 - gi)
                pai = cnt["pa"] % 2
                cnt["pa"] += 1
                ba = BK.get() if tmin <= 2 else None
                bb = BK.get()

                def s_mm(t):
                    ktile = gi - 4 + t
                    if t <= 2:
                        dst = banks_f[ba][:, t * 128:(t + 1) * 128]
                        kb_ = bk(ba)
                    else:
                        dst = banks_f[bb][:, (t - 3) * 128:(t - 2) * 128]
                        kb_ = bk(bb)
                    P.add("pe", lambda e: e.matmul(dst, lhsT=ckT_all[:, h, ktile * 128:(ktile + 1) * 128],
                                                   rhs=cqT_b[qs][:, h, i * 128:(i + 1) * 128], start=True, stop=True),
                          r=["ckT_all", kq], w=[kb_])

                def pv_mm(t):
                    ktile = gi - 4 + t
                    if t <= 2:
                        rhs = PA[pai][:, t * 128:(t + 1) * 128]
                        kr_ = "PA%d" % pai
                    else:
                        rhs = PBb[pai][:, (t - 3) * 128:(t - 2) * 128]
                        kr_ = "PBb%d" % pai
                    P.add("pe", lambda e: e.matmul(banks_f[bo][0:65, i * 128:(i + 1) * 128],
                                                   lhsT=cV_all[:, ktile, h * 65:(h + 1) * 65], rhs=rhs,
                                                   start=(t == tmin), stop=(t == 4)),
                          r=[kr_, "cV_all"], w=[bk(bo)])

                for t in range(tmin, 5):
                    s_mm(t)
                if ba is not None:
                    a0 = tmin * 128
                    P.add("act", lambda e: e.activation(out=PA[pai][:, a0:384], in_=banks_f[ba][:, a0:384], func=AF.Exp,
                                                        bias=bfar[:, h:h + 1], scale=1.0), r=[bk(ba), "bfar"], w=["PA%d" % pai])
                    if tmin == 0:
                        P.add("pool", lambda e: e.memset(PA[pai][0:64, 64:128], 0.0), r=["PA%d" % pai], w=["PA%d" % pai])
                b0 = 0 if tmin <= 3 else 128
                P.add("act", lambda e: e.activation(out=PB[pai][:, b0:256], in_=banks_f[bb][:, b0:256], func=AF.Exp), r=[bk(bb)], w=["PB%d" % pai])
                P.add("pool", lambda e: e.tensor_tensor(out=PBb[pai][:, b0:256], in0=PB[pai][:, b0:256],
                                                        in1=Ef[:, h, :, :].rearrange("p t q -> p (t q)")[:, b0:256], op=ALU.mult),
                      r=["PB%d" % pai, "Ef"], w=["PBb%d" % pai])
                for t in range(tmin, 5):
                    pv_mm(t)

            def ca_head(s, j, h, qs, os_):
                bo = BK.get(hold=True)
                for i in range(4):
                    ca_qtile(j, h, qs, bo, i)
                normalize(bo, 512, otn[os_][:, 8 + h, :], "otn%d" % os_)

            def load_seq(s):
                for hh in range(2):
                    P.add("sp", lambda e: e.dma_start(out=kT_all[:, 4 * hh:4 * hh + 4, :], in_=kT_d[s, 4 * hh:4 * hh + 4].rearrange("h d t -> d h t")),
                          r=["kT_d%d" % s], w=["kT_all"], chan="ldk%d" % hh)
                    P.add("sp", lambda e: e.dma_start(out=ckT_all[:, 4 * hh:4 * hh + 4, :], in_=ckT_d[s, 4 * hh:4 * hh + 4].rearrange("h d t -> d h t")),
                          r=["ckT_d%d" % s], w=["ckT_all"], chan="ldck%d" % hh)
                for q4 in range(4):
                    P.add("sp", lambda e: e.dma_start(out=V_all[:, 4 * q4:4 * q4 + 4, :], in_=V_d[s, 512 * q4:512 * q4 + 512, :].rearrange("(k p) c -> p k c", p=128)),
                          r=["V_d%d" % s], w=["V_all"], chan="ldv%d" % q4)
                    P.add("sp", lambda e: e.dma_start(out=cV_all[:, 4 * q4:4 * q4 + 4, :], in_=cV_d[s, 512 * q4:512 * q4 + 512, :].rearrange("(k p) c -> p k c", p=128)),
                          r=["cV_d%d" % s], w=["cV_all"], chan="ldcv%d" % q4)

            def load_seq_part(fn, *a):
                fn(*a)

            def attn_block(s, j, qs):
                t0 = 512 * j
                P.add("sp", lambda e: e.dma_start(out=qT_b[qs][:], in_=qT_d[s, :, :, t0:t0 + 512].rearrange("h d t -> d h t")),
                      r=["qT_d%d" % s], w=["qT_b%d" % qs], chan="ldq%d" % qs)
                P.add("sp", lambda e: e.dma_start(out=cqT_b[qs][:], in_=cqT_d[s, :, :, t0:t0 + 512].rearrange("h d t -> d h t")),
                      r=["cqT_d%d" % s], w=["cqT_b%d" % qs], chan="ldcq%d" % qs)
                for h in range(8):
                    mla_head(s, j, h, qs, 0)
                    ca_head(s, j, h, qs, 0)
                P.add("sp", lambda e: e.dma_start(out=otn_d[s, :, t0:t0 + 512].rearrange("(h d) t -> d h t", d=64), in_=otn[0][:]),
                      r=["otn0"], w=["otn_d%d_%d" % (s, j)], chan="stotn")

            def load_seq_safe(s):
                def ldk(hh):
                    P.add("sp", lambda e: e.dma_start(out=kT_all[:, 4 * hh:4 * hh + 4, :], in_=kT_d[s, 4 * hh:4 * hh + 4].rearrange("h d t -> d h t")),
                          r=["kT_d%d" % s], w=["kT_all"], chan="ldk%d" % hh)
                    P.add("sp", lambda e: e.dma_start(out=ckT_all[:, 4 * hh:4 * hh + 4, :], in_=ckT_d[s, 4 * hh:4 * hh + 4].rearrange("h d t -> d h t")),
                          r=["ckT_d%d" % s], w=["ckT_all"], chan="ldck%d" % hh)

                def ldv(q4):
                    P.add("sp", lambda e: e.dma_start(out=V_all[:, 4 * q4:4 * q4 + 4, :], in_=V_d[s, 512 * q4:512 * q4 + 512, :].rearrange("(k p) c -> p k c", p=128)),
                          r=["V_d%d" % s], w=["V_all"], chan="ldv%d" % q4)
                    P.add("sp", lambda e: e.dma_start(out=cV_all[:, 4 * q4:4 * q4 + 4, :], in_=cV_d[s, 512 * q4:512 * q4 + 512, :].rearrange("(k p) c -> p k c", p=128)),
                          r=["cV_d%d" % s], w=["cV_all"], chan="ldcv%d" % q4)
                for hh in range(2):
                    ldk(hh)
                for q4 in range(4):
                    ldv(q4)

            blk = 0
            for s in range(NSEQ if stop not in ("setup", "A") else 0):
                load_seq_safe(s)
                for j in range(4):
                    attn_block(s, j, blk % 2)
                    blk += 1
            P.barrier()

        with ExitStack() as sc:
            TB = 256
            NTB = TB // 128
            wdn = sb(sc, "wdn", [128, NFF, D], BF16)
            wout = sb(sc, "wout", [128, 8, D], BF16)
            wup = sb(sc, "wup", [128, 8, 2 * D_FF], BF16)
            convp = sb(sc, "convp", [128, 4, 2 * NFF], F32)
            gab = sb(sc, "gab", [128, D], F32)
            gmb = sb(sc, "gmb", [128, D], F32)
            otb = sb(sc, "otb", [128, 8, TB], BF16)
            x1 = [sb(sc, "x1_%d" % i, [128, D], F32) for i in range(NTB)]
            xn2 = sb(sc, "xn2", [128, D], BF16)
            hT2 = sb(sc, "hT2", [128, 8, TB], BF16)
            gT = sb(sc, "gT", [128, NFF, TB], BF16)
            NUB = 2
            ug = [sb(sc, "ug%d" % i, [128, TB + 2], F32) for i in range(NUB)]
            uv = [sb(sc, "uv%d" % i, [128, TB + 2], F32) for i in range(NUB)]
            cg = [sb(sc, "cg%d" % i, [128, TB], F32) for i in range(NUB)]
            cv = [sb(sc, "cv%d" % i, [128, TB], F32) for i in range(NUB)]
            halo = sb(sc, "halo", [128, 2 * NFF, 2], F32)
            stC = sb(sc, "stC", [128, 4], F32)
            otile = sb(sc, "otile", [128, D], F32)
            P.tag = "wlC"
            P.add("sp", lambda e: e.dma_start(out=convp[:], in_=convp_d), w=["convp"], chan="small5", waitall=True)

            def ldw(k):
                P.add("pool", lambda e: e.dma_start(out=wup[:, :, k * 512:(k + 1) * 512],
                                                    in_=wup_d[:, k * 512:(k + 1) * 512].rearrange("(j p) n -> p j n", p=128)),
                      w=["wup"], chan="wup", waitall=True)

            def ldwo(k):
                P.add("pool", lambda e: e.dma_start(out=wout[:, :, k * 512:(k + 1) * 512],
                                                    in_=wout_d[:, k * 512:(k + 1) * 512].rearrange("(j p) n -> p j n", p=128)),
                      w=["wout"], chan="wout", waitall=True)

            def ldwd(k, jg):
                P.add("pool", lambda e: e.dma_start(out=wdn[:, 11 * jg:11 * jg + 11, k * 512:(k + 1) * 512],
                                                    in_=wdn_d[1408 * jg:1408 * jg + 1408, k * 512:(k + 1) * 512].rearrange("(j p) n -> p j n", p=128)),
                      w=["wdn"], chan="wdn", waitall=True)
            for k in range(2):
                ldwo(k)
            for k in range(11):
                ldw(k)
            for k in range(2):
                for jg in range(2):
                    ldwd(k, jg)
            P.tag = None
            ucnt = {"u": 0}
            HALO_KEYS = ["halo%d" % ch for ch in range(2 * NFF)]

            def seq_start(s):
                P.add("sp", lambda e: e.dma_start(out=gab[:], in_=mod_d[s, 2 * D:3 * D].partition_broadcast(128)), r=["mod_d"], w=["gab"], chan="ldga")
                P.add("sp", lambda e: e.dma_start(out=gmb[:], in_=mod_d[s, 5 * D:6 * D].partition_broadcast(128)), r=["mod_d"], w=["gmb"], chan="ldgm")
                P.add("pool", lambda e: e.memset(halo[:], 0.0), r=HALO_KEYS, w=HALO_KEYS)

            def outproj_tile(s, t0, it):
                tk = t0 + it * 128
                kx1 = "x1_%d" % it
                kxs = [kx1 + "h0", kx1 + "h512"]
                P.add("sp", lambda e: e.dma_start(out=x1[it][:], in_=x_d[s, tk:tk + 128, :]), w=kxs, chan="ldx%d" % it)
                bo0, bo1 = BK.get(hold=True), BK.get(hold=True)

                def half(bb, n0):
                    for c in range(8):
                        P.add("pe", (lambda c: lambda e: e.matmul(banks_f[bb][:, 0:512], lhsT=otb[:, c, it * 128:(it + 1) * 128],
                                                                  rhs=wout[:, c, n0:n0 + 512], start=(c == 0), stop=(c == 7)))(c),
                              r=["otb", "wout"], w=[bk(bb)])
                    P.add("dve", lambda e: e.tensor_tensor(out=otile[:, n0:n0 + 512], in0=banks_f[bb][:, 0:512],
                                                           in1=gab[:, n0:n0 + 512], op=ALU.mult),
                          r=[bk(bb), "gab"], w=["otileh%d" % n0])
                    P.add("pool", lambda e: e.tensor_tensor(out=x1[it][:, n0:n0 + 512], in0=x1[it][:, n0:n0 + 512],
                                                            in1=otile[:, n0:n0 + 512], op=ALU.add),
                          r=[kx1 + "h%d" % n0, "otileh%d" % n0], w=[kx1 + "h%d" % n0])
                    BK.release(bb)
                half(bo0, 0)
                half(bo1, 512)
                P.add("act", lambda e: e.activation(out=xn2[:], in_=x1[it][:], func=AF.Square, accum_out=stC[:, 0:1]),
                      r=kxs, w=["stC", "xn2"])
                P.add("act", lambda e: e.activation(out=stC[:, 0:1], in_=stC[:, 0:1], func=AF.Sqrt, bias=float(D * EPS), scale=1.0),
                      r=["stC"], w=["stC"])
                P.add("dve", lambda e: e.reciprocal(out=stC[:, 0:1], in_=stC[:, 0:1]), r=["stC"], w=["stC"])
                P.add("act", lambda e: e.activation(out=xn2[:], in_=x1[it][:], func=AF.Copy, scale=stC[:, 0:1]),
                      r=kxs + ["stC"], w=["xn2"])
                bt = BK.get()
                for c in range(8):
                    P.add("pe", (lambda c: lambda e: e.transpose(out=banks_b[bt][:, c * 128:(c + 1) * 128],
                                                                 in_=xn2[:, c * 128:(c + 1) * 128], identity=ident[:]))(c),
                          r=["xn2", "ident"], w=[bk(bt)])
                tp3 = banks_b[bt][:, 0:1024].rearrange("p (c t) -> p c t", c=8)
                kh = "hT2_%d" % it
                P.add("dve", lambda e: e.tensor_tensor(out=hT2[:, :, it * 128:(it + 1) * 128], in0=tp3,
                                                       in1=AB[:, s, 2, :].unsqueeze(2).to_broadcast([128, 8, 128]), op=ALU.mult),
                      r=[bk(bt), "AB"], w=[kh])
                P.add("pool", lambda e: e.tensor_tensor(out=hT2[:, :, it * 128:(it + 1) * 128], in0=hT2[:, :, it * 128:(it + 1) * 128],
                                                        in1=AB[:, s, 3, :].unsqueeze(2).to_broadcast([128, 8, 128]), op=ALU.add),
                      r=[kh, "AB"], w=[kh])

            def ffn_chunk(f, khs):
                ui = ucnt["u"] % NUB
                ucnt["u"] += 1
                bu = BK.get()

                def up_mm(half, ch):
                    for k in range(8):
                        P.add("pe", (lambda k: lambda e: e.matmul(banks_f[bu][:, half * TB:(half + 1) * TB],
                                                                  lhsT=wup[:, k, ch * 128:(ch + 1) * 128], rhs=hT2[:, k, :],
                                                                  start=(k == 0), stop=(k == 7)))(k),
                              r=khs + ["wup"], w=[bk(bu)])
                up_mm(0, f)
                up_mm(1, NFF + f)

                def conv(half, ch, ub, cb, kub, kcb, e1):
                    P.add("act", lambda e: e.activation(out=ub[ui][:, 2:TB + 2], in_=banks_f[bu][:, half * TB:(half + 1) * TB], func=AF.Copy),
                          r=[bk(bu)], w=[kub])
                    P.add("pool", lambda e: e.tensor_copy(out=ub[ui][:, 0:2], in_=halo[:, ch, :]), r=["halo%d" % ch], w=[kub + "h"])
                    P.add("act", lambda e: e.activation(out=cb[ui][:], in_=banks_f[bu][:, half * TB:(half + 1) * TB], func=AF.Identity,
                                                        scale=convp[:, 2, ch:ch + 1], bias=convp[:, 3, ch:ch + 1]),
                          r=[bk(bu), "convp"], w=[kcb])
                    P.add(e1, lambda e: e.scalar_tensor_tensor(out=cb[ui][:], in0=ub[ui][:, 1:TB + 1], scalar=convp[:, 1, ch:ch + 1],
                                                               in1=cb[ui][:], op0=ALU.mult, op1=ALU.add),
                          r=[kub, kub + "h", kcb, "convp"], w=[kcb])
                    P.add(e1, lambda e: e.scalar_tensor_tensor(out=cb[ui][:], in0=ub[ui][:, 0:TB], scalar=convp[:, 0, ch:ch + 1],
                                                               in1=cb[ui][:], op0=ALU.mult, op1=ALU.add),
                          r=[kub, kub + "h", kcb, "convp"], w=[kcb])
                    P.add("pool", lambda e: e.tensor_copy(out=halo[:, ch, :], in_=ub[ui][:, TB:TB + 2]), r=[kub], w=["halo%d" % ch])
                kug, kuv, kcg, kcv = "ug%d" % ui, "uv%d" % ui, "cg%d" % ui, "cv%d" % ui
                conv(0, f, ug, cg, kug, kcg, "dve")
                conv(1, NFF + f, uv, cv, kuv, kcv, "dve")
                P.add("act", lambda e: e.activation(out=cg[ui][:], in_=cg[ui][:], func=AF.Silu), r=[kcg], w=[kcg])
                P.add("dve", lambda e: e.tensor_tensor(out=gT[:, f, :], in0=cg[ui][:], in1=cv[ui][:], op=ALU.mult),
                      r=[kcg, kcv], w=["gT%d" % f])

            def down_tile(s, t0, it, kgs):
                tk = t0 + it * 128
                bd0, bd1 = BK.get(hold=True), BK.get(hold=True)

                def half(bb, n0):
                    for f in range(NFF):
                        P.add("pe", (lambda f: lambda e: e.matmul(banks_f[bb][:, 0:512], lhsT=gT[:, f, it * 128:(it + 1) * 128],
                                                                  rhs=wdn[:, f, n0:n0 + 512], start=(f == 0), stop=(f == NFF - 1)))(f),
                              r=kgs + ["wdn"], w=[bk(bb)])
                    P.add("dve", lambda e: e.tensor_tensor(out=otile[:, n0:n0 + 512], in0=banks_f[bb][:, 0:512],
                                                           in1=gmb[:, n0:n0 + 512], op=ALU.mult),
                          r=[bk(bb), "gmb"], w=["otileh%d" % n0])
                    P.add("pool", lambda e: e.tensor_tensor(out=otile[:, n0:n0 + 512], in0=otile[:, n0:n0 + 512],
                                                            in1=x1[it][:, n0:n0 + 512], op=ALU.add),
                          r=["otileh%d" % n0, "x1_%dh%d" % (it, n0)], w=["otileh%d" % n0])
                    BK.release(bb)
                half(bd0, 0)
                half(bd1, 512)
                P.add("sp", lambda e: e.dma_start(out=out_d[s, tk:tk + 128, :], in_=otile[:]),
                      r=["otileh0", "otileh512"], w=["out_d"], chan="stout")

            def ffn_block(s, tb):
                t0 = tb * TB
                jblk = t0 // 512
                P.add("sp", lambda e: e.dma_start(out=otb[:], in_=otn_d[s, :, t0:t0 + TB].rearrange("(c p) t -> p c t", p=128)),
                      r=["otn_d%d_%d" % (s, jblk)], w=["otb"], chan="ldot")
                for it in range(NTB):
                    outproj_tile(s, t0, it)
                khs = ["hT2_%d" % it for it in range(NTB)]
                for f in range(NFF):
                    ffn_chunk(f, khs)
                kgs = ["gT%d" % f for f in range(NFF)]
                for it in range(NTB):
                    down_tile(s, t0, it, kgs)

            for s in range(NSEQ if stop not in ("setup", "A", "B") else 0):
                seq_start(s)
                for tb in range(S // TB):
                    ffn_block(s, tb)
            P.emit()
    return nc, P.stats


_CACHE = {}


def _feat_major(v):
    v = np.asarray(v, np.float32)
    return np.ascontiguousarray(v.reshape(-1, 128).T)


def _prepare(x, c, positions, w_ada, b_ada, g_attn_norm, w_in, g_q_latent, g_kv_latent, w_q_up, w_kv_up,
             g_mla_q, g_mla_k, g_ca_q, g_ca_k, rel_bias, w_out, g_mlp_norm, w_up, conv_w, conv_b, w_down):
    f = lambda a: np.ascontiguousarray(np.asarray(a))
    x = f(x); c = f(c); positions = f(positions)
    if "nc" not in _CACHE:
        _CACHE["nc"], _CACHE["stats"] = build_program(stop=_CACHE.get("stop"))
    nc = _CACHE["nc"]
    gfeat = np.concatenate([_feat_major(g_attn_norm[0]), _feat_major(g_mlp_norm[0]), _feat_major(g_q_latent[0]),
                            _feat_major(g_kv_latent[0])], axis=1).astype(np.float32)
    convp = np.zeros((128, 4, 2 * NFF), np.float32)
    for t in range(3):
        convp[:, t, :] = _feat_major(conv_w[0, t])
    convp[:, 3, :] = _feat_major(conv_b[0])
    grow = np.concatenate([np.asarray(g_mla_q[0]), np.asarray(g_mla_k[0]), np.asarray(g_ca_q[0]), np.asarray(g_ca_k[0])]).astype(np.float32)
    grow = np.ascontiguousarray(np.broadcast_to(grow[None, :], (128, 320)))
    rb = np.asarray(rel_bias[0], np.float32)
    kj = np.arange(128)[:, None]
    qi = np.arange(128)[None, :]
    bias34 = np.zeros((128, 8, 2, 128), np.float32)
    for ti, t in enumerate((3, 4)):
        idx = np.clip(128 * (4 - t) + qi - kj, -128, 128) + 128
        bias34[:, :, ti, :] = np.transpose(rb[:, idx], (1, 0, 2))
    bfar = np.ascontiguousarray(np.broadcast_to(rb[:, 256][None, :], (128, 8))).astype(np.float32)
    half = 16
    invf = np.power(np.float32(10000.0), -np.arange(half, dtype=np.float32) / np.float32(half)).astype(np.float32)
    invf = np.ascontiguousarray(np.broadcast_to(invf[None, :], (128, 16)))
    shared = {
        "w_ada": f(w_ada[0]), "w_in": f(w_in[0]), "w_q_up": f(w_q_up[0]), "w_kv_up": f(w_kv_up[0]), "w_out": f(w_out[0]),
        "w_up": f(w_up[0]), "w_down": f(w_down[0]), "gfeat": gfeat, "convp": convp, "grow": grow, "bias34": bias34,
        "bfar": bfar, "invf": invf,
    }
    in_maps = []
    for i in range(NCORES):
        b0 = NSEQ * i
        m = dict(shared)
        m["x"] = f(x[b0:b0 + NSEQ])
        m["cT"] = np.ascontiguousarray(c[b0:b0 + NSEQ].reshape(NSEQ, 8, 128).transpose(2, 1, 0)).astype(np.float32)
        pl = positions[b0:b0 + NSEQ].reshape(NSEQ, NT, 128).transpose(2, 0, 1).reshape(128, NSEQ * NT)
        m["posl"] = np.ascontiguousarray(pl).astype(np.int32)
        m["b_ada2"] = np.ascontiguousarray(np.broadcast_to(np.asarray(b_ada[0], np.float32)[None, :], (NSEQ, 6 * D)))
        in_maps.append(m)
    return nc, in_maps


def kernel(**inputs):
    nc, in_maps = _prepare(**inputs)
    res = run_bass_kernel_spmd(nc, in_maps, core_ids=list(range(NCORES)))
    out = np.concatenate([np.asarray(r["out"]) for r in res.results], axis=0)
    return out.astype(np.float32)
```

```python
import math
from contextlib import ExitStack

import numpy as np
import concourse.bass as bass
import concourse.mybir as mybir
from concourse.bass_utils import run_bass_kernel_spmd

F32 = mybir.dt.float32
BF16 = mybir.dt.bfloat16
I32 = mybir.dt.int32
AF = mybir.ActivationFunctionType
ALU = mybir.AluOpType
AX = mybir.AxisListType

NCORES = 8
NSEQ = 2
S = 2048
D = 1024
NT = S // 128
D_IN = 1952
D_FF = 2816
NFF = D_FF // 128
EPS = 1e-6
TWO_PI = 2.0 * math.pi


class Op:
    __slots__ = ("eng", "fn", "r", "w", "chan", "deps", "signal", "sigval", "waitall")

    def __init__(self, eng, fn, r, w, chan, waitall):
        self.eng, self.fn, self.r, self.w, self.chan = eng, fn, tuple(r), tuple(w), chan
        self.deps = set()
        self.signal = False
        self.sigval = 0
        self.waitall = waitall


class Prog:
    def __init__(self, nc, es):
        self.nc = nc
        self.es = es
        self.ops = []
        self.last_w = {}
        self.readers = {}
        self.waitall_chans = set()

    tag = None

    def add(self, eng, fn, r=(), w=(), chan=None, waitall=False):
        if self.tag is not None and self.tag in SKIP:
            return None
        op = Op(eng, fn, r, w, chan, waitall)
        idx = len(self.ops)
        deps = set()
        for k in op.r:
            lw = self.last_w.get(k)
            if lw is not None:
                deps.add(lw)
            if isinstance(k, str) and k.startswith("bank"):
                for rd in self.readers.get(k, ()):
                    if self.ops[rd].eng != eng:
                        deps.add(rd)
        for k in op.w:
            lw = self.last_w.get(k)
            if lw is not None:
                deps.add(lw)
            for rd in self.readers.get(k, ()):
                deps.add(rd)
        deps.discard(idx)
        if chan is not None and waitall:
            deps = {d for d in deps if self.ops[d].chan != chan}
        op.deps = deps
        for k in op.w:
            self.last_w[k] = idx
            self.readers[k] = []
        for k in op.r:
            if k in op.w:
                continue
            self.readers.setdefault(k, []).append(idx)
        if chan is not None and waitall:
            self.waitall_chans.add(chan)
        self.ops.append(op)
        return idx

    def barrier(self):
        n = len(self.ops)
        last = {}
        for i, op in enumerate(self.ops):
            key = op.chan if op.chan is not None else ("E", op.eng)
            last[key] = i
        alld = set(last.values())
        for eng in ("pe", "act", "dve", "pool", "sp"):
            op = Op(eng, None, (), (), None, False)
            op.deps = set(alld)
            self.ops.append(op)

    def emit(self):
        nc = self.nc
        ops = self.ops
        engobj = {"pe": nc.tensor, "act": nc.scalar, "dve": nc.vector, "pool": nc.gpsimd, "sp": nc.sync}
        for op in ops:
            for d in op.deps:
                x = ops[d]
                if x.chan is None and x.eng == "pe" and op.eng == "pe" and op.chan is None:
                    continue
                x.signal = True
        sems = {}

        def sem(name):
            if name not in sems:
                sems[name] = self.es.enter_context(nc.semaphore("s_" + str(name)))
            return sems[name]

        cnt = {}
        chan_total = {}
        for op in ops:
            if op.fn is None:
                continue
            if op.chan is not None:
                c = ("C", op.chan)
                cnt[c] = cnt.get(c, 0) + 16
                op.sigval = cnt[c]
                op.signal = True
                chan_total[op.chan] = cnt[c]
            elif op.signal:
                c = ("E", op.eng)
                cnt[c] = cnt.get(c, 0) + 1
                op.sigval = cnt[c]
        known = {e: {} for e in engobj}
        vcs = [None] * len(ops)
        ecount = {}
        nwaits = 0

        def merge(dst, src):
            for k_, v_ in src.items():
                if dst.get(k_, 0) < v_:
                    dst[k_] = v_

        for i, op in enumerate(ops):
            need = []
            for d in op.deps:
                x = ops[d]
                if x.fn is None:
                    if vcs[d] is not None:
                        need.append((None, 0, d))
                    continue
                if x.chan is not None:
                    key = ("C", x.chan)
                    val = chan_total[x.chan] if x.chan in self.waitall_chans else x.sigval
                else:
                    if x.eng == "pe" and op.eng == "pe" and op.chan is None:
                        continue
                    key = ("E", x.eng)
                    val = x.sigval
                need.append((key, val, d))
            need.sort(key=lambda t: -t[2])
            e = engobj[op.eng]
            kn = known[op.eng]
            for key, val, d in need:
                if key is None:
                    continue
                if kn.get(key, 0) >= val:
                    continue
                e.wait_ge(sem(key), val)
                kn[key] = val
                nwaits += 1
                if vcs[d] is not None and not (ops[d].chan in self.waitall_chans):
                    merge(kn, vcs[d])
            vc = dict(kn)
            if op.fn is not None:
                inst = op.fn(e)
                if op.chan is not None:
                    inst.then_inc(sem(("C", op.chan)), 16)
                    if op.chan not in self.waitall_chans:
                        vc[("C", op.chan)] = max(vc.get(("C", op.chan), 0), op.sigval)
                else:
                    if op.signal:
                        inst.then_inc(sem(("E", op.eng)), 1)
                        ecount[op.eng] = op.sigval
                    if op.eng != "pe" or True:
                        vc[("E", op.eng)] = max(vc.get(("E", op.eng), 0), ecount.get(op.eng, 0))
            vcs[i] = vc
        for chan, tot in chan_total.items():
            if known["sp"].get(("C", chan), 0) < tot:
                nc.sync.wait_ge(sem(("C", chan)), tot)
        self.stats = dict(n_ops=len(ops), n_waits=nwaits, n_sems=len(sems))


class Banks:
    def __init__(self, banks):
        self.banks = banks
        self.ptr = 0
        self.held = set()

    def get(self, hold=False):
        for _ in range(16):
            b = self.ptr
            self.ptr = (self.ptr + 1) % len(self.banks)
            if b not in self.held:
                if hold:
                    self.held.add(b)
                return b
        raise RuntimeError("no free PSUM bank")

    def release(self, b):
        self.held.discard(b)


import os
SKIP = set(os.environ.get('KSKIP', '').split(','))


def build_program(debug=None, stop=None):
    nc = bass.Bass("TRN2", target_bir_lowering=False)

    def din(name, shape, dt=F32):
        return nc.dram_tensor(name, list(shape), dt, kind="ExternalInput").ap()

    def dscr(name, shape, dt):
        return nc.dram_tensor(name, list(shape), dt, kind="Internal").ap()

    x_d = din("x", [NSEQ, S, D])
    cT_d = din("cT", [128, 8, NSEQ])
    pos_d = din("posl", [128, NSEQ * NT], I32)
    wada_d = din("w_ada", [D, 6 * D])
    bada_d = din("b_ada2", [NSEQ, 6 * D])
    win_d = din("w_in", [D, D_IN])
    wq_d = din("w_q_up", [256, 768])
    wkv_d = din("w_kv_up", [128, 1024])
    wout_d = din("w_out", [D, D])
    wup_d = din("w_up", [D, 2 * D_FF])
    wdn_d = din("w_down", [D_FF, D])
    gfeat_d = din("gfeat", [128, 19])
    convp_d = din("convp", [128, 4, 2 * NFF])
    grow_d = din("grow", [128, 320])
    bias34_d = din("bias34", [128, 8, 2, 128])
    bfar_d = din("bfar", [128, 8])
    invf_d = din("invf", [128, 16])
    out_d = nc.dram_tensor("out", [NSEQ, S, D], F32, kind="ExternalOutput").ap()

    mod_d = dscr("mod_scr", [NSEQ, 6 * D], F32)
    qT_d = dscr("qT_scr", [NSEQ, 8, 96, S], BF16)
    kT_d = dscr("kT_scr", [NSEQ, 8, 96, S], BF16)
    cqT_d = dscr("cqT_scr", [NSEQ, 8, 64, S], BF16)
    ckT_d = dscr("ckT_scr", [NSEQ, 8, 64, S], BF16)
    V_d = dscr("V_scr", [NSEQ, S, 8 * 65], BF16)
    cV_d = dscr("cV_scr", [NSEQ, S, 8 * 65], BF16)
    otn_d = dscr("otn_scr", [NSEQ, D, S], BF16)

    with ExitStack() as es:
        P = Prog(nc, es)

        def sb(stack, name, shape, dt):
            return stack.enter_context(nc.sbuf_tensor("sb_" + name, list(shape), dt))

        banks_f = [es.enter_context(nc.psum_tensor("bank%d" % i, [128, 512], F32)) for i in range(8)]
        banks_b = [b[:].bitcast(BF16) for b in banks_f]
        BK = Banks(banks_f)

        def bk(b):
            return "bank%d" % b

        ident = sb(es, "ident", [128, 128], BF16)
        identf = sb(es, "identf", [128, 128], F32)
        sel64 = sb(es, "sel64", [128, 64], F32)
        gfeat = sb(es, "gfeat", [128, 19], F32)
        grow = sb(es, "grow", [128, 320], F32)
        AB = sb(es, "AB", [128, NSEQ, 4, 8], F32)
        sa = es.enter_context(ExitStack())
        cs_all = sb(sa, "cs_all", [128, NSEQ * NT, 32], F32)
        junk = sb(sa, "junk", [128, 1024], BF16)

        P.add("pool", lambda e: e.memset(identf[:], 1.0), w=["identf"])
        P.add("pool", lambda e: e.affine_select(out=identf[:], in_=identf[:], pattern=[[-1, 128]],
                                                compare_op=ALU.is_equal, fill=0.0, base=0, channel_multiplier=1),
              r=["identf"], w=["identf"])
        P.add("pool", lambda e: e.tensor_copy(out=ident[:], in_=identf[:]), r=["identf"], w=["ident"])
        P.add("pool", lambda e: e.memset(sel64[:], 0.0), w=["sel64"])
        P.add("pool", lambda e: e.memset(sel64[64:65, :], 1.0), r=["sel64"], w=["sel64"])
        P.add("sp", lambda e: e.dma_start(out=gfeat[:], in_=gfeat_d), w=["gfeat"], chan="small", waitall=True)
        P.add("sp", lambda e: e.dma_start(out=grow[:], in_=grow_d), w=["grow"], chan="small", waitall=True)
        s0 = es.enter_context(ExitStack())
        cT = sb(s0, "cT", [128, 8, NSEQ], F32)
        bada = sb(s0, "bada", [NSEQ, 6 * D], F32)
        posi = sb(s0, "posi", [128, NSEQ * NT], I32)
        invf = sb(s0, "invf", [128, 16], F32)
        P.add("sp", lambda e: e.dma_start(out=cT[:], in_=cT_d), w=["cT"], chan="small", waitall=True)
        P.add("sp", lambda e: e.dma_start(out=bada[:], in_=bada_d), w=["bada"], chan="small", waitall=True)
        P.add("sp", lambda e: e.dma_start(out=posi[:], in_=pos_d), w=["posi"], chan="small", waitall=True)
        P.add("sp", lambda e: e.dma_start(out=invf[:], in_=invf_d), w=["invf"], chan="small", waitall=True)
        P.add("dve", lambda e: e.tensor_scalar(out=grow[:, 96:192], in0=grow[:, 96:192], scalar1=math.sqrt(96.0),
                                               scalar2=None, op0=ALU.mult), r=["grow"], w=["grow"])
        P.add("dve", lambda e: e.tensor_scalar(out=grow[:, 256:320], in0=grow[:, 256:320], scalar1=8.0,
                                               scalar2=None, op0=ALU.mult), r=["grow"], w=["grow"])

        if True:
            scb = sb(s0, "scb", [128, 8, NSEQ], BF16)
            wab = [sb(s0, "wab%d" % i, [128, 8, 512], BF16) for i in range(2)]
            modsb = sb(s0, "modsb", [NSEQ, 6 * D], F32)
            P.add("act", lambda e: e.activation(out=scb[:], in_=cT[:], func=AF.Silu), r=["cT"], w=["scb"])
            for nb in range(12):
                sl = nb % 2
                P.add("pool", (lambda nb, sl: lambda e: e.dma_start(
                    out=wab[sl][:], in_=wada_d[:, nb * 512:(nb + 1) * 512].rearrange("(j p) n -> p j n", p=128)))(nb, sl),
                    w=["wab%d" % sl], chan="wab%d" % sl)
                b = BK.get()
                for k in range(8):
                    P.add("pe", (lambda b, sl, k: lambda e: e.matmul(
                        banks_f[b][0:NSEQ, 0:512], lhsT=scb[:, k, :], rhs=wab[sl][:, k, :], start=(k == 0), stop=(k == 7)))(b, sl, k),
                        r=["scb", "wab%d" % sl], w=[bk(b)])
                P.add("dve", (lambda b, nb: lambda e: e.tensor_tensor(
                    out=modsb[:, nb * 512:(nb + 1) * 512], in0=banks_f[b][0:NSEQ, 0:512],
                    in1=bada[:, nb * 512:(nb + 1) * 512], op=ALU.add))(b, nb),
                    r=[bk(b), "bada"], w=["modsb"])
            P.add("sp", lambda e: e.dma_start(out=mod_d, in_=modsb[:]), r=["modsb"], w=["mod_d"], chan="modst")
            P.tag = "modT"
            modT = sb(s0, "modT", [128, NSEQ, 4, 8], F32)
            bT = BK.get()

            def modtr(qi, c, j):
                col = (qi * 8 + j) * NSEQ
                P.add("pe", lambda e: e.matmul(banks_f[bT][:, col:col + NSEQ], lhsT=modsb[0:NSEQ, c * D + j * 128:c * D + (j + 1) * 128],
                                               rhs=identf[0:NSEQ, 0:NSEQ], start=True, stop=True),
                      r=["modsb", "identf"], w=[bk(bT)])
            for qi, c in enumerate((0, 1, 3, 4)):
                for j in range(8):
                    modtr(qi, c, j)
            P.add("dve", lambda e: e.tensor_copy(out=modT[:].rearrange("p s k j -> p k j s"),
                                                 in_=banks_f[bT][:, 0:32 * NSEQ].rearrange("p (k j s) -> p k j s", k=4, j=8)),
                  r=[bk(bT)], w=["modT"])
            for s in range(NSEQ):
                for (dst, srcq, gcol) in ((0, 1, 0), (2, 3, 8)):
                    P.add("dve", (lambda s, dst, srcq, gcol: lambda e: e.scalar_tensor_tensor(
                        out=AB[:, s, dst, :], in0=modT[:, s, srcq, :], scalar=1.0, in1=gfeat[:, gcol:gcol + 8],
                        op0=ALU.add, op1=ALU.mult))(s, dst, srcq, gcol),
                        r=["modT", "gfeat"], w=["AB"])
                    P.add("dve", (lambda s, dst: lambda e: e.tensor_scalar(
                        out=AB[:, s, dst, :], in0=AB[:, s, dst, :], scalar1=32.0, scalar2=None, op0=ALU.mult))(s, dst),
                        r=["AB"], w=["AB"])
                    P.add("dve", (lambda s, dst, srcq: lambda e: e.tensor_copy(
                        out=AB[:, s, dst + 1, :], in_=modT[:, s, srcq - 1, :]))(s, dst, srcq),
                        r=["modT"], w=["AB"])
            P.tag = "rot"
            posf = sb(s0, "posf", [128, NSEQ * NT], F32)
            ang = sb(s0, "ang", [128, NSEQ * NT, 32], F32)
            kf = sb(s0, "kf", [128, NSEQ * NT, 32], F32)
            ki = sb(s0, "ki", [128, NSEQ * NT, 32], I32)
            mk = sb(s0, "mk", [128, NSEQ * NT, 32], F32)
            P.add("dve", lambda e: e.tensor_copy(out=posf[:], in_=posi[:]), r=["posi"], w=["posf"])
            NTT = NSEQ * NT
            P.add("dve", lambda e: e.tensor_tensor(out=ang[:, :, 16:32], in0=posf[:].unsqueeze(2).to_broadcast([128, NTT, 16]),
                                                   in1=invf[:].unsqueeze(1).to_broadcast([128, NTT, 16]), op=ALU.mult),
                  r=["posf", "invf"], w=["ang"])
            P.add("dve", lambda e: e.tensor_scalar(out=ang[:, :, 0:16], in0=ang[:, :, 16:32], scalar1=math.pi / 2.0,
                                                   scalar2=None, op0=ALU.add), r=["ang"], w=["ang"])
            P.add("dve", lambda e: e.tensor_scalar(out=kf[:], in0=ang[:], scalar1=1.0 / TWO_PI, scalar2=None, op0=ALU.mult),
                  r=["ang"], w=["kf"])
            P.add("dve", lambda e: e.tensor_copy(out=ki[:], in_=kf[:]), r=["kf"], w=["ki"])
            P.add("dve", lambda e: e.tensor_copy(out=kf[:], in_=ki[:]), r=["ki"], w=["kf"])
            P.add("dve", lambda e: e.scalar_tensor_tensor(out=ang[:], in0=kf[:], scalar=-TWO_PI, in1=ang[:],
                                                          op0=ALU.mult, op1=ALU.add), r=["kf", "ang"], w=["ang"])
            P.add("dve", lambda e: e.tensor_scalar(out=mk[:], in0=ang[:], scalar1=math.pi, scalar2=-TWO_PI,
                                                   op0=ALU.is_gt, op1=ALU.mult), r=["ang"], w=["mk"])
            P.add("dve", lambda e: e.tensor_tensor(out=ang[:], in0=ang[:], in1=mk[:], op=ALU.add), r=["ang", "mk"], w=["ang"])
            P.add("dve", lambda e: e.tensor_scalar(out=mk[:], in0=ang[:], scalar1=-math.pi, scalar2=TWO_PI,
                                                   op0=ALU.is_lt, op1=ALU.mult), r=["ang"], w=["mk"])
            P.add("dve", lambda e: e.tensor_tensor(out=ang[:], in0=ang[:], in1=mk[:], op=ALU.add), r=["ang", "mk"], w=["ang"])
            P.add("dve", lambda e: e.tensor_scalar(out=ang[:], in0=ang[:], scalar1=math.pi, scalar2=-math.pi,
                                                   op0=ALU.min, op1=ALU.max), r=["ang"], w=["ang"])
            P.add("act", lambda e: e.activation(out=cs_all[:], in_=ang[:], func=AF.Sin), r=["ang"], w=["cs_all"])
            P.tag = None
            P.barrier()
            s0.close()

        if True:
            P.tag = "wlA"
            win = sb(sa, "win", [128, 8, D_IN], BF16)
            wq = sb(sa, "wq", [128, 2, 768], BF16)
            wkv = sb(sa, "wkv", [128, 1024], BF16)
            wstage = sb(sa, "wstage", [128, 2, 768], F32)
            wstage2 = sb(sa, "wstage2", [128, 1024], F32)
            for (c0, c1) in ((0, 512), (512, 1024), (1024, 1536), (1536, 1952)):
                P.add("pool", (lambda c0, c1: lambda e: e.dma_start(out=win[:, :, c0:c1], in_=win_d[:, c0:c1].rearrange("(j p) n -> p j n", p=128)))(c0, c1),
                      w=["win"], chan="win", waitall=True)
            P.tag = "wlA2"
            P.add("sp", lambda e: e.dma_start(out=wstage[:], in_=wq_d.rearrange("(j p) n -> p j n", p=128)),
                  w=["wstage"], chan="small3", waitall=True)
            P.add("sp", lambda e: e.dma_start(out=wstage2[:], in_=wkv_d), w=["wstage2"], chan="small3", waitall=True)
            for j in range(2):
                P.add("dve", (lambda j: lambda e: e.tensor_scalar(
                    out=wq[:, j, :], in0=wstage[:, j, :], scalar1=gfeat[:, 16 + j:17 + j], scalar2=16.0,
                    op0=ALU.mult, op1=ALU.mult))(j), r=["wstage", "gfeat"], w=["wq"])
            P.add("dve", lambda e: e.tensor_scalar(out=wkv[:], in0=wstage2[:], scalar1=gfeat[:, 18:19],
                                                   scalar2=math.sqrt(128.0), op0=ALU.mult, op1=ALU.mult),
                  r=["wstage2", "gfeat"], w=["wkv"])

            P.tag = "wlA3"
            xt = [sb(sa, "xt%d" % i, [128, D], F32) for i in range(2)]
            xn = [sb(sa, "xn%d" % i, [128, D], BF16) for i in range(2)]
            hT = [sb(sa, "hT%d" % i, [128, 8, 128], BF16) for i in range(2)]
            st = sb(sa, "stA", [128, 2, 40], F32)
            lat = [sb(sa, "lat%d" % i, [128, 384], BF16) for i in range(2)]
            latT = [sb(sa, "latT%d" % i, [128, 3, 128], BF16) for i in range(2)]
            kr = [sb(sa, "kr%d" % i, [128, 4, 32], F32) for i in range(2)]
            qn = [sb(sa, "qn%d" % i, [128, 8, 96], F32) for i in range(2)]
            qsq = [sb(sa, "qsq%d" % i, [128, 8, 96], F32) for i in range(2)]
            qb = [sb(sa, "qb%d" % i, [128, 8, 96], BF16) for i in range(2)]
            kn = [sb(sa, "kn%d" % i, [128, 8, 64], F32) for i in range(2)]
            ksq = [sb(sa, "ksq%d" % i, [128, 8, 64], F32) for i in range(2)]
            kb = [sb(sa, "kb%d" % i, [128, 8, 96], BF16) for i in range(2)]
            rt = [sb(sa, "rt%d" % i, [128, 8, 32], F32) for i in range(2)]
            cqn = [sb(sa, "cqn%d" % i, [128, 8, 64], F32) for i in range(2)]
            cqs = [sb(sa, "cqs%d" % i, [128, 8, 64], F32) for i in range(2)]
            cqb = [sb(sa, "cqb%d" % i, [128, 8, 64], BF16) for i in range(2)]
            ckn = [sb(sa, "ckn%d" % i, [128, 8, 64], F32) for i in range(2)]
            cks = [sb(sa, "cks%d" % i, [128, 8, 64], F32) for i in range(2)]
            ckb = [sb(sa, "ckb%d" % i, [128, 8, 64], BF16) for i in range(2)]
            qT_st = [sb(sa, "qTst%d" % i, [96, 8, 512], BF16) for i in range(2)]
            kT_st = [sb(sa, "kTst%d" % i, [96, 8, 512], BF16) for i in range(2)]
            cqT_st = [sb(sa, "cqTst%d" % i, [64, 8, 512], BF16) for i in range(2)]
            ckT_st = [sb(sa, "ckTst%d" % i, [64, 8, 512], BF16) for i in range(2)]
            V_st = [sb(sa, "Vst%d" % i, [128, 4, 8, 65], BF16) for i in range(2)]
            cV_st = [sb(sa, "cVst%d" % i, [128, 4, 8, 65], BF16) for i in range(2)]
            for i in range(2):
                P.add("pool", (lambda i: lambda e: e.memset(V_st[i][:, :, :, 64:65], 1.0))(i), w=["Vst%d" % i])
                P.add("pool", (lambda i: lambda e: e.memset(cV_st[i][:, :, :, 64:65], 1.0))(i), w=["cVst%d" % i])

            P.tag = None

            def rstd_from_ssq(sl, col, n, tag):
                kst = "st%d_%s" % (sl, tag)
                P.add("act", lambda e: e.activation(out=st[:, sl, col:col + 1], in_=st[:, sl, col:col + 1], func=AF.Sqrt,
                                                    bias=float(n * EPS), scale=1.0), r=[kst], w=[kst])
                P.add("dve", lambda e: e.reciprocal(out=st[:, sl, col:col + 1], in_=st[:, sl, col:col + 1]), r=[kst], w=[kst])
                return kst

            def rstd_vec(sl, c0, nh, n, tag):
                kst = "st%d_%s" % (sl, tag)
                P.add("act", lambda e: e.activation(out=st[:, sl, c0:c0 + nh], in_=st[:, sl, c0:c0 + nh], func=AF.Sqrt,
                                                    bias=float(n * EPS), scale=1.0), r=[kst], w=[kst])
                P.add("dve", lambda e: e.reciprocal(out=st[:, sl, c0:c0 + nh], in_=st[:, sl, c0:c0 + nh]), r=[kst], w=[kst])
                return kst

            def rope(eng, src3, dst3, cs, keys_r, keys_w, tmp, nh, ktmp):
                cosb = cs[:, 0:16].unsqueeze(1).to_broadcast([128, nh, 16])
                sinb = cs[:, 16:32].unsqueeze(1).to_broadcast([128, nh, 16])
                P.add(eng, lambda e: e.tensor_tensor(out=tmp[:, :, 0:16], in0=src3[:, :, 16:32], in1=sinb, op=ALU.mult),
                      r=keys_r, w=[ktmp])
                P.add(eng, lambda e: e.tensor_tensor(out=tmp[:, :, 16:32], in0=src3[:, :, 0:16], in1=sinb, op=ALU.mult),
                      r=keys_r + [ktmp], w=[ktmp])
                P.add(eng, lambda e: e.tensor_tensor(out=src3[:, :, 0:16], in0=src3[:, :, 0:16], in1=cosb, op=ALU.mult),
                      r=keys_r + [ktmp], w=keys_r[:1])
                P.add(eng, lambda e: e.tensor_tensor(out=src3[:, :, 16:32], in0=src3[:, :, 16:32], in1=cosb, op=ALU.mult),
                      r=keys_r + [ktmp], w=keys_r[:1])
                P.add(eng, lambda e: e.tensor_tensor(out=dst3[:, :, 0:16], in0=src3[:, :, 0:16], in1=tmp[:, :, 0:16], op=ALU.subtract),
                      r=keys_r + [ktmp], w=keys_w)
                P.add(eng, lambda e: e.tensor_tensor(out=dst3[:, :, 16:32], in0=src3[:, :, 16:32], in1=tmp[:, :, 16:32], op=ALU.add),
                      r=keys_r + [ktmp], w=keys_w)

            def prep_tile(g):
                s, tt = divmod(g, NT)
                jb, i4 = divmod(tt, 4)
                sl = g % 2
                bs = (g // 4) % 2
                S_ = str(sl)
                kxt, kxn, khT = "xt" + S_, "xn" + S_, "hT" + S_
                P.add("sp", lambda e: e.dma_start(out=xt[sl][:], in_=x_d[s, tt * 128:(tt + 1) * 128, :]), w=[kxt], chan="xt" + S_)
                P.add("act", lambda e: e.activation(out=junk[:], in_=xt[sl][:], func=AF.Square, accum_out=st[:, sl, 0:1]),
                      r=[kxt], w=["st%s_x" % S_])
                kst = rstd_from_ssq(sl, 0, D, "x")
                P.add("act", lambda e: e.activation(out=xn[sl][:], in_=xt[sl][:], func=AF.Copy, scale=st[:, sl, 0:1]),
                      r=[kxt, kst], w=[kxn])
                b0 = BK.get()
                for c in range(8):
                    P.add("pe", (lambda c: lambda e: e.transpose(out=banks_b[b0][:, c * 128:(c + 1) * 128],
                                                                  in_=xn[sl][:, c * 128:(c + 1) * 128], identity=ident[:]))(c),
                          r=[kxn, "ident"], w=[bk(b0)])
                tp3 = banks_b[b0][:, 0:1024].rearrange("p (c t) -> p c t", c=8)
                P.add("dve", lambda e: e.tensor_tensor(out=hT[sl][:], in0=tp3, in1=AB[:, s, 0, :].unsqueeze(2).to_broadcast([128, 8, 128]),
                                                       op=ALU.mult), r=[bk(b0), "AB"], w=[khT])
                P.add("pool", lambda e: e.tensor_tensor(out=hT[sl][:], in0=hT[sl][:], in1=AB[:, s, 1, :].unsqueeze(2).to_broadcast([128, 8, 128]),
                                                        op=ALU.add), r=[khT, "AB"], w=[khT])
                pb = [BK.get(hold=True) for _ in range(4)]
                cols = [(0, 416), (416, 928), (928, 1440), (1440, 1952)]
                for bi, (c0, c1) in enumerate(cols):
                    for k in range(8):
                        P.add("pe", (lambda bi, c0, c1, k: lambda e: e.matmul(
                            banks_f[pb[bi]][:, 0:c1 - c0], lhsT=hT[sl][:, k, :], rhs=win[:, k, c0:c1],
                            start=(k == 0), stop=(k == 7)))(bi, c0, c1, k),
                            r=[khT, "win"], w=[bk(pb[bi])])
                pl, pq, pk, pv = [banks_f[b] for b in pb]
                kl, kq, kk_, kv_ = [bk(b) for b in pb]
                P.add("act", lambda e: e.activation(out=junk[:, 0:256], in_=pl[:, 0:256], func=AF.Square, accum_out=st[:, sl, 1:2]),
                      r=[kl], w=["st%s_ql" % S_])
                P.add("act", lambda e: e.activation(out=junk[:, 256:384], in_=pl[:, 256:384], func=AF.Square, accum_out=st[:, sl, 2:3]),
                      r=[kl], w=["st%s_kvl" % S_])
                k1 = rstd_from_ssq(sl, 1, 256, "ql")
                k2 = rstd_from_ssq(sl, 2, 128, "kvl")
                klat = "lat" + S_
                P.add("dve", lambda e: e.tensor_scalar(out=lat[sl][:, 0:256], in0=pl[:, 0:256], scalar1=st[:, sl, 1:2], scalar2=None,
                                                       op0=ALU.mult), r=[kl, k1], w=[klat + "a"])
                P.add("dve", lambda e: e.tensor_scalar(out=lat[sl][:, 256:384], in0=pl[:, 256:384], scalar1=st[:, sl, 2:3], scalar2=None,
                                                       op0=ALU.mult), r=[kl, k2], w=[klat + "b"])
                kkr = "kr" + S_
                P.add("dve", lambda e: e.tensor_tensor(out=kr[sl][:, 0, :], in0=pl[:, 384:416], in1=grow[:, 160:192], op=ALU.mult),
                      r=[kl, "grow"], w=[kkr])
                P.add("act", lambda e: e.activation(out=junk[:, 512:544], in_=pl[:, 384:416], func=AF.Square, accum_out=st[:, sl, 3:4]),
                      r=[kl], w=["st%s_kr" % S_])
                BK.release(pb[0])
                b1 = BK.get()
                for c in range(3):
                    P.add("pe", (lambda c: lambda e: e.transpose(out=banks_b[b1][:, c * 128:(c + 1) * 128],
                                                                  in_=lat[sl][:, c * 128:(c + 1) * 128], identity=ident[:]))(c),
                          r=[klat + "a", klat + "b", "ident"], w=[bk(b1)])
                klT = "latT" + S_
                P.add("act", lambda e: e.activation(out=latT[sl][:].rearrange("p c t -> p (c t)"), in_=banks_b[b1][:, 0:384], func=AF.Copy),
                      r=[bk(b1)], w=[klT])
                bq0, bq1 = BK.get(hold=True), BK.get(hold=True)
                for (bb, c0, c1) in ((bq0, 0, 480), (bq1, 480, 768)):
                    for k in range(2):
                        P.add("pe", (lambda bb, c0, c1, k: lambda e: e.matmul(
                            banks_f[bb][:, 0:c1 - c0], lhsT=latT[sl][:, k, :], rhs=wq[:, k, c0:c1], start=(k == 0), stop=(k == 1)))(bb, c0, c1, k),
                            r=[klT, "wq"], w=[bk(bb)])
                bkv0, bkv1 = BK.get(hold=True), BK.get(hold=True)
                for (bb, c0) in ((bkv0, 0), (bkv1, 512)):
                    P.add("pe", (lambda bb, c0: lambda e: e.matmul(
                        banks_f[bb][:, 0:512], lhsT=latT[sl][:, 2, :], rhs=wkv[:, c0:c0 + 512], start=True, stop=True))(bb, c0),
                        r=[klT, "wkv"], w=[bk(bb)])
                cs = cs_all[:, g, :]
                kqn, kqs, kqb = "qn" + S_, "qsq" + S_, "qb" + S_
                q0 = banks_f[bq0][:, 0:480].rearrange("p (h d) -> p h d", h=5)
                q1 = banks_f[bq1][:, 0:288].rearrange("p (h d) -> p h d", h=3)
                P.add("act", lambda e: e.activation(out=qn[sl][:, 0:5, :], in_=q0, func=AF.Copy), r=[bk(bq0)], w=[kqn + "a"])
                P.add("act", lambda e: e.activation(out=qn[sl][:, 5:8, :], in_=q1, func=AF.Copy), r=[bk(bq1)], w=[kqn + "b"])
                BK.release(bq0)
                BK.release(bq1)
                P.add("pool", lambda e: e.tensor_tensor(out=qsq[sl][:], in0=qn[sl][:], in1=qn[sl][:], op=ALU.mult),
                      r=[kqn + "a", kqn + "b"], w=[kqs])
                P.add("dve", lambda e: e.tensor_reduce(out=st[:, sl, 8:16], in_=qsq[sl][:], axis=AX.X, op=ALU.add),
                      r=[kqs], w=["st%s_q" % S_])
                k3 = rstd_vec(sl, 8, 8, 96, "q")
                P.add("dve", lambda e: e.tensor_tensor(out=qn[sl][:], in0=qn[sl][:], in1=st[:, sl, 8:16].unsqueeze(2).to_broadcast([128, 8, 96]),
                                                       op=ALU.mult), r=[kqn + "a", kqn + "b", k3], w=[kqn])
                P.add("pool", lambda e: e.tensor_tensor(out=qn[sl][:], in0=qn[sl][:], in1=grow[:, 0:96].unsqueeze(1).to_broadcast([128, 8, 96]),
                                                        op=ALU.mult), r=[kqn, "grow"], w=[kqn])
                P.add("act", lambda e: e.activation(out=qb[sl][:, :, 0:64], in_=qn[sl][:, :, 0:64], func=AF.Copy), r=[kqn], w=[kqb + "n"])
                rope("pool", qn[sl][:, :, 64:96], qb[sl][:, :, 64:96], cs, [kqn, "cs_all"], [kqb + "r"], rt[sl], 8, "rt" + S_)
                kkn, kks, kkb = "kn" + S_, "ksq" + S_, "kb" + S_
                kv0 = banks_f[bkv0][:, 0:512].rearrange("p (h d) -> p h d", h=4)
                kv1 = banks_f[bkv1][:, 0:512].rearrange("p (h d) -> p h d", h=4)
                P.add("act", lambda e: e.activation(out=kn[sl][:, 0:4, :], in_=kv0[:, :, 0:64], func=AF.Copy), r=[bk(bkv0)], w=[kkn + "a"])
                P.add("act", lambda e: e.activation(out=kn[sl][:, 4:8, :], in_=kv1[:, :, 0:64], func=AF.Copy), r=[bk(bkv1)], w=[kkn + "b"])
                P.add("act", lambda e: e.activation(out=V_st[bs][:, i4, 0:4, 0:64], in_=kv0[:, :, 64:128], func=AF.Copy),
                      r=[bk(bkv0)], w=["Vst%d" % bs])
                P.add("act", lambda e: e.activation(out=V_st[bs][:, i4, 4:8, 0:64], in_=kv1[:, :, 64:128], func=AF.Copy),
                      r=[bk(bkv1)], w=["Vst%d" % bs])
                BK.release(bkv0)
                BK.release(bkv1)
                P.add("pool", lambda e: e.tensor_tensor(out=ksq[sl][:], in0=kn[sl][:], in1=kn[sl][:], op=ALU.mult),
                      r=[kkn + "a", kkn + "b"], w=[kks])
                P.add("dve", lambda e: e.tensor_reduce(out=st[:, sl, 16:24], in_=ksq[sl][:], axis=AX.X, op=ALU.add),
                      r=[kks], w=["st%s_k" % S_])
                P.add("dve", lambda e: e.tensor_scalar(out=st[:, sl, 16:24], in0=st[:, sl, 16:24], scalar1=st[:, sl, 3:4], scalar2=None,
                                                       op0=ALU.add), r=["st%s_k" % S_, "st%s_kr" % S_], w=["st%s_k" % S_])
                k4 = rstd_vec(sl, 16, 8, 96, "k")
                P.add("dve", lambda e: e.tensor_tensor(out=kn[sl][:], in0=kn[sl][:], in1=st[:, sl, 16:24].unsqueeze(2).to_broadcast([128, 8, 64]),
                                                       op=ALU.mult), r=[kkn + "a", kkn + "b", k4], w=[kkn])
                P.add("pool", lambda e: e.tensor_tensor(out=kb[sl][:, :, 0:64], in0=kn[sl][:], in1=grow[:, 96:160].unsqueeze(1).to_broadcast([128, 8, 64]),
                                                        op=ALU.mult), r=[kkn, "grow"], w=[kkb + "n"])
                rope("dve", kr[sl][:, 0:1, :], kr[sl][:, 1:2, :], cs, [kkr, "cs_all"], [kkr + "o"], kr[sl][:, 2:3, :], 1, kkr + "t")
                P.add("dve", lambda e: e.tensor_tensor(out=kb[sl][:, :, 64:96], in0=kr[sl][:, 1:2, :].to_broadcast([128, 8, 32]),
                                                       in1=st[:, sl, 16:24].unsqueeze(2).to_broadcast([128, 8, 32]), op=ALU.mult),
                      r=[kkr + "o", k4], w=[kkb + "r"])
                for (src, keyb, dn, dsq, db, col, gc0, tag) in (
                        (pq, kq, cqn, cqs, cqb, 24, 192, "cq"), (pk, kk_, ckn, cks, ckb, 32, 256, "ck")):
                    kdn, kds, kdb = tag + "n" + S_, tag + "s" + S_, tag + "b" + S_
                    P.add("act", (lambda src, dn: lambda e: e.activation(out=dn[sl][:].rearrange("p h d -> p (h d)"), in_=src[:, 0:512], func=AF.Copy))(src, dn),
                          r=[keyb], w=[kdn])
                    P.add("pool", (lambda dn, dsq: lambda e: e.tensor_tensor(out=dsq[sl][:], in0=dn[sl][:], in1=dn[sl][:], op=ALU.mult))(dn, dsq),
                          r=[kdn], w=[kds])
                    P.add("dve", (lambda dsq, col: lambda e: e.tensor_reduce(out=st[:, sl, col:col + 8], in_=dsq[sl][:], axis=AX.X, op=ALU.add))(dsq, col),
                          r=[kds], w=["st%s_%s" % (S_, tag)])
                    k5 = rstd_vec(sl, col, 8, 64, tag)
                    P.add("dve", (lambda dn, col: lambda e: e.tensor_tensor(out=dn[sl][:], in0=dn[sl][:],
                                                                           in1=st[:, sl, col:col + 8].unsqueeze(2).to_broadcast([128, 8, 64]), op=ALU.mult))(dn, col),
                          r=[kdn, k5], w=[kdn])
                    P.add("pool", (lambda dn, db, gc0: lambda e: e.tensor_tensor(out=db[sl][:], in0=dn[sl][:],
                                                                                in1=grow[:, gc0:gc0 + 64].unsqueeze(1).to_broadcast([128, 8, 64]), op=ALU.mult))(dn, db, gc0),
                          r=[kdn, "grow"], w=[kdb])
                P.add("act", lambda e: e.activation(out=cV_st[bs][:, i4, :, 0:64], in_=pv[:, 0:512].rearrange("p (h d) -> p h d", h=8), func=AF.Copy),
                      r=[kv_], w=["cVst%d" % bs])
                BK.release(pb[1])
                BK.release(pb[2])
                BK.release(pb[3])
                for (srcb, keys, dst, kdst, dd) in ((qb, [kqb + "n", kqb + "r"], qT_st, "qTst%d" % bs, 96),
                                                    (kb, [kkb + "n", kkb + "r"], kT_st, "kTst%d" % bs, 96),
                                                    (cqb, ["cqb" + S_], cqT_st, "cqTst%d" % bs, 64),
                                                    (ckb, ["ckb" + S_], ckT_st, "ckTst%d" % bs, 64)):
                    bt = BK.get()
                    for h in range(8):
                        P.add("pe", (lambda srcb, bt, h, dd: lambda e: e.transpose(
                            out=banks_b[bt][0:dd, h * 128:(h + 1) * 128], in_=srcb[sl][:, h, :], identity=ident[:]))(srcb, bt, h, dd),
                            r=keys + ["ident"], w=[bk(bt)])
                    P.add("act", (lambda bt, dst, dd: lambda e: e.activation(
                        out=dst[bs][:, :, i4 * 128:(i4 + 1) * 128], in_=banks_b[bt][0:dd, 0:1024].rearrange("p (h t) -> p h t", h=8), func=AF.Copy))(bt, dst, dd),
                        r=[bk(bt)], w=[kdst])
                if i4 == 3:
                    t0 = jb * 512
                    P.add("sp", lambda e: e.dma_start(out=qT_d[s, :, :, t0:t0 + 512].rearrange("h d t -> d h t"), in_=qT_st[bs][:]),
                          r=["qTst%d" % bs], w=["qT_d%d" % s], chan="stq%d" % bs)
                    P.add("sp", lambda e: e.dma_start(out=kT_d[s, :, :, t0:t0 + 512].rearrange("h d t -> d h t"), in_=kT_st[bs][:]),
                          r=["kTst%d" % bs], w=["kT_d%d" % s], chan="stk%d" % bs)
                    P.add("sp", lambda e: e.dma_start(out=cqT_d[s, :, :, t0:t0 + 512].rearrange("h d t -> d h t"), in_=cqT_st[bs][:]),
                          r=["cqTst%d" % bs], w=["cqT_d%d" % s], chan="stcq%d" % bs)
                    P.add("sp", lambda e: e.dma_start(out=ckT_d[s, :, :, t0:t0 + 512].rearrange("h d t -> d h t"), in_=ckT_st[bs][:]),
                          r=["ckTst%d" % bs], w=["ckT_d%d" % s], chan="stck%d" % bs)
                    P.add("sp", lambda e: e.dma_start(out=V_d[s, t0:t0 + 512, :].rearrange("(k p) c -> p k c", p=128),
                                                      in_=V_st[bs][:].rearrange("p k h c -> p k (h c)")),
                          r=["Vst%d" % bs], w=["V_d%d" % s], chan="stv%d" % bs)
                    P.add("sp", lambda e: e.dma_start(out=cV_d[s, t0:t0 + 512, :].rearrange("(k p) c -> p k c", p=128),
                                                      in_=cV_st[bs][:].rearrange("p k h c -> p k (h c)")),
                          r=["cVst%d" % bs], w=["cV_d%d" % s], chan="stcv%d" % bs)

            for g in range(NSEQ * NT if stop not in ("setup",) else 0):
                prep_tile(g)
            P.barrier()
            sa.close()

        with ExitStack() as sbk:
            kT_all = sb(sbk, "kT_all", [96, 8, S], BF16)
            V_all = sb(sbk, "V_all", [128, NT, 8 * 65], BF16)
            ckT_all = sb(sbk, "ckT_all", [64, 8, S], BF16)
            cV_all = sb(sbk, "cV_all", [128, NT, 8 * 65], BF16)
            qT_b = [sb(sbk, "qT_b%d" % i, [96, 8, 512], BF16) for i in range(2)]
            cqT_b = [sb(sbk, "cqT_b%d" % i, [64, 8, 512], BF16) for i in range(2)]
            otn = [sb(sbk, "otn%d" % i, [64, 16, 512], BF16) for i in range(1)]
            Ef = sb(sbk, "Ef", [128, 8, 2, 128], F32)
            bfar = sb(sbk, "bfar", [128, 8], F32)
            NPT = 4
            Pt = [sb(sbk, "Pt%d" % i, [128, 512], BF16) for i in range(NPT)]
            PA = [sb(sbk, "PA%d" % i, [128, 384], BF16) for i in range(2)]
            PB = [sb(sbk, "PB%d" % i, [128, 256], F32) for i in range(2)]
            PBb = [sb(sbk, "PBb%d" % i, [128, 256], BF16) for i in range(2)]
            NRZ = 4
            ots = [sb(sbk, "ots%d" % i, [128, 512], F32) for i in range(NRZ)]
            rzb = [sb(sbk, "rzb%d" % i, [128, 2, 512], BF16) for i in range(NRZ)]
            sel64b = sb(sbk, "sel64b", [128, 64], BF16)
            P.tag = "wlB"
            P.add("sp", lambda e: e.dma_start(out=Ef[:], in_=bias34_d), w=["Ef"], chan="small4", waitall=True)
            P.add("sp", lambda e: e.dma_start(out=bfar[:], in_=bfar_d), w=["bfar"], chan="small4", waitall=True)
            P.add("act", lambda e: e.activation(out=Ef[:], in_=Ef[:], func=AF.Exp), r=["Ef"], w=["Ef"])
            P.add("pool", lambda e: e.memset(Ef[64:128, :, 1, 0:64], 0.0), r=["Ef"], w=["Ef"])
            for i in range(NRZ):
                P.add("pool", (lambda i: lambda e: e.memset(rzb[i][:], 0.0))(i), w=["rzb%d" % i])
            P.add("pool", lambda e: e.tensor_copy(out=sel64b[:], in_=sel64[:]), r=["sel64"], w=["sel64b"])
            P.tag = None
            cnt = {"pt": 0, "pa": 0, "rz": 0}

            def normalize_head(bo, width):
                ri = cnt["rz"] % NRZ
                cnt["rz"] += 1
                P.add("dve", lambda e: e.tensor_copy(out=ots[ri][0:65, 0:width], in_=banks_f[bo][0:65, 0:width]),
                      r=[bk(bo)], w=["ots%d" % ri])
                BK.release(bo)
                P.add("act", lambda e: e.activation(out=ots[ri][64:65, 0:width], in_=ots[ri][64:65, 0:width], func=AF.Ln),
                      r=["ots%d" % ri], w=["ots%d" % ri])
                P.add("act", lambda e: e.activation(out=ots[ri][64:65, 0:width], in_=ots[ri][64:65, 0:width], func=AF.Exp, scale=-1.0),
                      r=["ots%d" % ri], w=["ots%d" % ri])
                P.add("dve", lambda e: e.tensor_copy(out=rzb[ri][64:65, 0, 0:width], in_=ots[ri][64:65, 0:width]),
                      r=["ots%d" % ri], w=["rzb%d" % ri])
                P.add("dve", lambda e: e.tensor_tensor(out=rzb[ri][64:65, 1, 0:width], in0=ots[ri][64:65, 0:width],
                                                       in1=rzb[ri][64:65, 0, 0:width], op=ALU.subtract),
                      r=["ots%d" % ri, "rzb%d" % ri], w=["rzb%d" % ri])
                return ri

            def normalize_tail(ri, width, dst_ap, kdst):
                bb = BK.get(hold=True)
                for pl_ in range(2):
                    P.add("pe", (lambda pl_: lambda e: e.matmul(banks_f[bb][0:64, 0:width], lhsT=sel64b[:, 0:64], rhs=rzb[ri][:, pl_, 0:width],
                                                                start=(pl_ == 0), stop=(pl_ == 1)))(pl_),
                          r=["rzb%d" % ri, "sel64b"], w=[bk(bb)])
                P.add("dve", lambda e: e.tensor_tensor(out=dst_ap, in0=banks_f[bb][0:64, 0:width], in1=ots[ri][0:64, 0:width], op=ALU.mult),
                      r=[bk(bb), "ots%d" % ri], w=[kdst])
                BK.release(bb)

            def mla_gen(s, j, h, qs):
                kq = "qT_b%d" % qs
                nkt = 4 * j + 4
                bo = BK.get(hold=True)
                tiles = []
                for kt in range(nkt):
                    r_ = kt - 4 * j
                    c0 = 128 * r_ if r_ > 0 else 0
                    tiles.append((kt, c0, r_ >= 0))
                sbank = {}

                def emit_s(idx):
                    kt, c0, diag = tiles[idx]
                    b = BK.get(hold=True)
                    sbank[idx] = b
                    P.add("pe", lambda e: e.matmul(banks_f[b][:, 0:512 - c0], lhsT=kT_all[:, h, kt * 128:(kt + 1) * 128],
                                                   rhs=qT_b[qs][:, h, c0:512], start=True, stop=True),
                          r=["kT_all", kq], w=[bk(b)])

                def emit_rest(idx):
                    kt, c0, diag = tiles[idx]
                    b = sbank[idx]
                    pi = cnt["pt"] % NPT
                    cnt["pt"] += 1
                    kp = "Pt%d" % pi
                    w_ = 512 - c0
                    P.add("act", lambda e: e.activation(out=Pt[pi][:, 0:w_], in_=banks_f[b][:, 0:w_], func=AF.Exp), r=[bk(b)], w=[kp])
                    BK.release(b)
                    if diag:
                        P.add("pool", lambda e: e.memset(Pt[pi][64:128, 0:64], 0.0), r=[kp], w=[kp])
                    P.add("pe", lambda e: e.matmul(banks_f[bo][0:65, c0:512], lhsT=V_all[:, kt, h * 65:(h + 1) * 65], rhs=Pt[pi][:, 0:w_],
                                                   start=(idx == 0), stop=(idx == nkt - 1)),
                          r=[kp, "V_all"], w=[bk(bo)])

                LOOK = 2
                for idx in range(min(LOOK, nkt)):
                    emit_s(idx)
                yield
                for idx in range(nkt):
                    emit_rest(idx)
                    if idx + LOOK < nkt:
                        emit_s(idx + LOOK)
                    yield
                ri = normalize_head(bo, 512)
                pend_m.append((ri, 512, otn[0][:, h, :], "otn0h%d" % h))
                yield

            def ca_qtile(j, h, qs, bo, i):
                kq = "cqT_b%d" % qs
                gi = 4 * j + i
                tmin = max(0, 4 - gi)
                pai = cnt["pa"] % 2
                cnt["pa"] += 1
                ba = BK.get(hold=True) if tmin <= 2 else None
                bb = BK.get(hold=True)

                def s_mm(t):
                    ktile = gi - 4 + t
                    if t <= 2:
                        dst = banks_f[ba][:, t * 128:(t + 1) * 128]
                        kb_ = bk(ba)
                    else:
                        dst = banks_f[bb][:, (t - 3) * 128:(t - 2) * 128]
                        kb_ = bk(bb)
                    P.add("pe", lambda e: e.matmul(dst, lhsT=ckT_all[:, h, ktile * 128:(ktile + 1) * 128],
                                                   rhs=cqT_b[qs][:, h, i * 128:(i + 1) * 128], start=True, stop=True),
                          r=["ckT_all", kq], w=[kb_])

                def pv_mm(t):
                    ktile = gi - 4 + t
                    if t <= 2:
                        rhs = PA[pai][:, t * 128:(t + 1) * 128]
                        kr_ = "PA%d" % pai
                    else:
                        rhs = PBb[pai][:, (t - 3) * 128:(t - 2) * 128]
                        kr_ = "PBb%d" % pai
                    P.add("pe", lambda e: e.matmul(banks_f[bo][0:65, i * 128:(i + 1) * 128],
                                                   lhsT=cV_all[:, ktile, h * 65:(h + 1) * 65], rhs=rhs,
                                                   start=(t == tmin), stop=(t == 4)),
                          r=[kr_, "cV_all"], w=[bk(bo)])

                for t in range(tmin, 5):
                    s_mm(t)
                yield
                if ba is not None:
                    a0 = tmin * 128
                    P.add("act", lambda e: e.activation(out=PA[pai][:, a0:384], in_=banks_f[ba][:, a0:384], func=AF.Exp,
                                                        bias=bfar[:, h:h + 1], scale=1.0), r=[bk(ba), "bfar"], w=["PA%d" % pai])
                    BK.release(ba)
                    if tmin == 0:
                        P.add("pool", lambda e: e.memset(PA[pai][0:64, 64:128], 0.0), r=["PA%d" % pai], w=["PA%d" % pai])
                b0 = 0 if tmin <= 3 else 128
                P.add("act", lambda e: e.activation(out=PB[pai][:, b0:256], in_=banks_f[bb][:, b0:256], func=AF.Exp), r=[bk(bb)], w=["PB%d" % pai])
                BK.release(bb)
                P.add("pool", lambda e: e.tensor_tensor(out=PBb[pai][:, b0:256], in0=PB[pai][:, b0:256],
                                                        in1=Ef[:, h, :, :].rearrange("p t q -> p (t q)")[:, b0:256], op=ALU.mult),
                      r=["PB%d" % pai, "Ef"], w=["PBb%d" % pai])
                yield
                for t in range(tmin, 5):
                    pv_mm(t)
                yield

            def ca_gen(s, j, h, qs):
                bo = BK.get(hold=True)
                for i in range(4):
                    yield from ca_qtile(j, h, qs, bo, i)
                ri = normalize_head(bo, 512)
                pend_c.append((ri, 512, otn[0][:, 8 + h, :], "otn0h%d" % (8 + h)))
                yield

            pend_m, pend_c = [], []

            def stream(gens, pend, delay):
                for g in gens:
                    n = 0
                    old = list(pend)
                    del pend[:]
                    for _ in g:
                        n += 1
                        yield
                        if n == delay and old:
                            for t in old:
                                normalize_tail(*t)
                            old = []
                            yield
                    if old:
                        for t in old:
                            normalize_tail(*t)
                        yield
                for t in pend:
                    normalize_tail(*t)
                del pend[:]
                yield

            def interleave(ga, gb, ra, rb):
                alive_a = alive_b = True
                while alive_a or alive_b:
                    for _ in range(ra):
                        if alive_a:
                            try:
                                next(ga)
                            except StopIteration:
                                alive_a = False
                    for _ in range(rb):
                        if alive_b:
                            try:
                                next(gb)
                            except StopIteration:
                                alive_b = False

            def load_seq(s):
                for hh in range(2):
                    P.add("sp", lambda e: e.dma_start(out=kT_all[:, 4 * hh:4 * hh + 4, :], in_=kT_d[s, 4 * hh:4 * hh + 4].rearrange("h d t -> d h t")),
                          r=["kT_d%d" % s], w=["kT_all"], chan="ldk%d" % hh)
                    P.add("sp", lambda e: e.dma_start(out=ckT_all[:, 4 * hh:4 * hh + 4, :], in_=ckT_d[s, 4 * hh:4 * hh + 4].rearrange("h d t -> d h t")),
                          r=["ckT_d%d" % s], w=["ckT_all"], chan="ldck%d" % hh)
                for q4 in range(4):
                    P.add("sp", lambda e: e.dma_start(out=V_all[:, 4 * q4:4 * q4 + 4, :], in_=V_d[s, 512 * q4:512 * q4 + 512, :].rearrange("(k p) c -> p k c", p=128)),
                          r=["V_d%d" % s], w=["V_all"], chan="ldv%d" % q4)
                    P.add("sp", lambda e: e.dma_start(out=cV_all[:, 4 * q4:4 * q4 + 4, :], in_=cV_d[s, 512 * q4:512 * q4 + 512, :].rearrange("(k p) c -> p k c", p=128)),
                          r=["cV_d%d" % s], w=["cV_all"], chan="ldcv%d" % q4)

            def load_seq_part(fn, *a):
                fn(*a)

            def load_q(s, j, qs):
                t0 = 512 * j
                P.add("sp", lambda e: e.dma_start(out=qT_b[qs][:], in_=qT_d[s, :, :, t0:t0 + 512].rearrange("h d t -> d h t")),
                      r=["qT_d%d" % s], w=["qT_b%d" % qs], chan="ldq%d" % qs)
                P.add("sp", lambda e: e.dma_start(out=cqT_b[qs][:], in_=cqT_d[s, :, :, t0:t0 + 512].rearrange("h d t -> d h t")),
                      r=["cqT_d%d" % s], w=["cqT_b%d" % qs], chan="ldcq%d" % qs)

            def attn_block(s, j, qs):
                t0 = 512 * j
                nb_ = s * 4 + j + 1
                if nb_ < NSEQ * 4:
                    load_q(nb_ // 4, nb_ % 4, 1 - qs)
                gm = stream([mla_gen(s, j, h, qs) for h in range(8)], pend_m, 3)
                gc = stream([ca_gen(s, j, h, qs) for h in range(8)], pend_c, 4)
                ra, rb = {0: (1, 2), 1: (3, 4), 2: (1, 1), 3: (3, 2)}[j]
                interleave(gm, gc, ra, rb)
                P.add("sp", lambda e: e.dma_start(out=otn_d[s, :, t0:t0 + 512].rearrange("(h d) t -> d h t", d=64), in_=otn[0][:]),
                      r=["otn0h%d" % hh for hh in range(16)], w=["otn_d%d_%d" % (s, j)], chan="stotn")

            def load_seq_safe(s):
                def ldk(hh):
                    P.add("sp", lambda e: e.dma_start(out=kT_all[:, 4 * hh:4 * hh + 4, :], in_=kT_d[s, 4 * hh:4 * hh + 4].rearrange("h d t -> d h t")),
                          r=["kT_d%d" % s], w=["kT_all"], chan="ldk%d" % hh)
                    P.add("sp", lambda e: e.dma_start(out=ckT_all[:, 4 * hh:4 * hh + 4, :], in_=ckT_d[s, 4 * hh:4 * hh + 4].rearrange("h d t -> d h t")),
                          r=["ckT_d%d" % s], w=["ckT_all"], chan="ldck%d" % hh)

                def ldv(q4):
                    P.add("sp", lambda e: e.dma_start(out=V_all[:, 4 * q4:4 * q4 + 4, :], in_=V_d[s, 512 * q4:512 * q4 + 512, :].rearrange("(k p) c -> p k c", p=128)),
                          r=["V_d%d" % s], w=["V_all"], chan="ldv%d" % q4)
                    P.add("sp", lambda e: e.dma_start(out=cV_all[:, 4 * q4:4 * q4 + 4, :], in_=cV_d[s, 512 * q4:512 * q4 + 512, :].rearrange("(k p) c -> p k c", p=128)),
                          r=["cV_d%d" % s], w=["cV_all"], chan="ldcv%d" % q4)
                for hh in range(2):
                    ldk(hh)
                for q4 in range(4):
                    ldv(q4)

            blk = 0
            if stop not in ("setup", "A"):
                load_q(0, 0, 0)
            for s in range(NSEQ if stop not in ("setup", "A") else 0):
                load_seq_safe(s)
                for j in range(4):
                    attn_block(s, j, blk % 2)
                    blk += 1
            P.barrier()

        with ExitStack() as sc:
            TB = 256
            NTB = TB // 128
            wdn = sb(sc, "wdn", [128, NFF, D], BF16)
            wout = sb(sc, "wout", [128, 8, D], BF16)
            wup = sb(sc, "wup", [128, 8, 2 * D_FF], BF16)
            convp = sb(sc, "convp", [128, 4, 2 * NFF], F32)
            gab = sb(sc, "gab", [128, D], F32)
            gmb = sb(sc, "gmb", [128, D], F32)
            otb = sb(sc, "otb", [128, 8, TB], BF16)
            x1 = [sb(sc, "x1_%d" % i, [128, D], F32) for i in range(NTB)]
            xn2 = sb(sc, "xn2", [128, D], BF16)
            hT2 = sb(sc, "hT2", [128, 8, TB], BF16)
            gT = sb(sc, "gT", [128, NFF, TB], BF16)
            NUB = 2
            ug = [sb(sc, "ug%d" % i, [128, TB + 2], F32) for i in range(NUB)]
            uv = [sb(sc, "uv%d" % i, [128, TB + 2], F32) for i in range(NUB)]
            cg = [sb(sc, "cg%d" % i, [128, TB], F32) for i in range(NUB)]
            cv = [sb(sc, "cv%d" % i, [128, TB], F32) for i in range(NUB)]
            halo = sb(sc, "halo", [128, 2 * NFF, 2], F32)
            stC = sb(sc, "stC", [128, 4], F32)
            otile = sb(sc, "otile", [128, D], F32)
            P.tag = "wlC"
            P.add("sp", lambda e: e.dma_start(out=convp[:], in_=convp_d), w=["convp"], chan="small5", waitall=True)

            def ldw(k):
                P.add("pool", lambda e: e.dma_start(out=wup[:, :, k * 512:(k + 1) * 512],
                                                    in_=wup_d[:, k * 512:(k + 1) * 512].rearrange("(j p) n -> p j n", p=128)),
                      w=["wup"], chan="wup", waitall=True)

            def ldwo(k):
                P.add("pool", lambda e: e.dma_start(out=wout[:, :, k * 512:(k + 1) * 512],
                                                    in_=wout_d[:, k * 512:(k + 1) * 512].rearrange("(j p) n -> p j n", p=128)),
                      w=["wout"], chan="wout", waitall=True)

            def ldwd(k, jg):
                P.add("pool", lambda e: e.dma_start(out=wdn[:, 11 * jg:11 * jg + 11, k * 512:(k + 1) * 512],
                                                    in_=wdn_d[1408 * jg:1408 * jg + 1408, k * 512:(k + 1) * 512].rearrange("(j p) n -> p j n", p=128)),
                      w=["wdn"], chan="wdn", waitall=True)
            for k in range(2):
                ldwo(k)
            for k in range(11):
                ldw(k)
            for k in range(2):
                for jg in range(2):
                    ldwd(k, jg)
            P.tag = None
            ucnt = {"u": 0}
            HALO_KEYS = ["halo%d" % ch for ch in range(2 * NFF)]

            def seq_start(s):
                P.add("sp", lambda e: e.dma_start(out=gab[:], in_=mod_d[s, 2 * D:3 * D].partition_broadcast(128)), r=["mod_d"], w=["gab"], chan="ldga")
                P.add("sp", lambda e: e.dma_start(out=gmb[:], in_=mod_d[s, 5 * D:6 * D].partition_broadcast(128)), r=["mod_d"], w=["gmb"], chan="ldgm")
                P.add("pool", lambda e: e.memset(halo[:], 0.0), r=HALO_KEYS, w=HALO_KEYS)

            def outproj_tile(s, t0, it):
                tk = t0 + it * 128
                kx1 = "x1_%d" % it
                kxs = [kx1 + "h0", kx1 + "h512"]
                P.add("sp", lambda e: e.dma_start(out=x1[it][:], in_=x_d[s, tk:tk + 128, :]), w=kxs, chan="ldx%d" % it)
                bo0, bo1 = BK.get(hold=True), BK.get(hold=True)

                def half(bb, n0):
                    for c in range(8):
                        P.add("pe", (lambda c: lambda e: e.matmul(banks_f[bb][:, 0:512], lhsT=otb[:, c, it * 128:(it + 1) * 128],
                                                                  rhs=wout[:, c, n0:n0 + 512], start=(c == 0), stop=(c == 7)))(c),
                              r=["otb", "wout"], w=[bk(bb)])
                    P.add("dve", lambda e: e.tensor_tensor(out=otile[:, n0:n0 + 512], in0=banks_f[bb][:, 0:512],
                                                           in1=gab[:, n0:n0 + 512], op=ALU.mult),
                          r=[bk(bb), "gab"], w=["otileh%d" % n0])
                    P.add("pool", lambda e: e.tensor_tensor(out=x1[it][:, n0:n0 + 512], in0=x1[it][:, n0:n0 + 512],
                                                            in1=otile[:, n0:n0 + 512], op=ALU.add),
                          r=[kx1 + "h%d" % n0, "otileh%d" % n0], w=[kx1 + "h%d" % n0])
                    BK.release(bb)
                half(bo0, 0)
                half(bo1, 512)
                P.add("act", lambda e: e.activation(out=xn2[:], in_=x1[it][:], func=AF.Square, accum_out=stC[:, 0:1]),
                      r=kxs, w=["stC", "xn2"])
                P.add("act", lambda e: e.activation(out=stC[:, 0:1], in_=stC[:, 0:1], func=AF.Sqrt, bias=float(D * EPS), scale=1.0),
                      r=["stC"], w=["stC"])
                P.add("dve", lambda e: e.reciprocal(out=stC[:, 0:1], in_=stC[:, 0:1]), r=["stC"], w=["stC"])
                P.add("act", lambda e: e.activation(out=xn2[:], in_=x1[it][:], func=AF.Copy, scale=stC[:, 0:1]),
                      r=kxs + ["stC"], w=["xn2"])
                bt = BK.get()
                for c in range(8):
                    P.add("pe", (lambda c: lambda e: e.transpose(out=banks_b[bt][:, c * 128:(c + 1) * 128],
                                                                 in_=xn2[:, c * 128:(c + 1) * 128], identity=ident[:]))(c),
                          r=["xn2", "ident"], w=[bk(bt)])
                tp3 = banks_b[bt][:, 0:1024].rearrange("p (c t) -> p c t", c=8)
                kh = "hT2_%d" % it
                P.add("dve", lambda e: e.tensor_tensor(out=hT2[:, :, it * 128:(it + 1) * 128], in0=tp3,
                                                       in1=AB[:, s, 2, :].unsqueeze(2).to_broadcast([128, 8, 128]), op=ALU.mult),
                      r=[bk(bt), "AB"], w=[kh])
                P.add("pool", lambda e: e.tensor_tensor(out=hT2[:, :, it * 128:(it + 1) * 128], in0=hT2[:, :, it * 128:(it + 1) * 128],
                                                        in1=AB[:, s, 3, :].unsqueeze(2).to_broadcast([128, 8, 128]), op=ALU.add),
                      r=[kh, "AB"], w=[kh])

            def ffn_chunk(f, khs):
                ui = ucnt["u"] % NUB
                ucnt["u"] += 1
                bu = BK.get()

                def up_mm(half, ch):
                    for k in range(8):
                        P.add("pe", (lambda k: lambda e: e.matmul(banks_f[bu][:, half * TB:(half + 1) * TB],
                                                                  lhsT=wup[:, k, ch * 128:(ch + 1) * 128], rhs=hT2[:, k, :],
                                                                  start=(k == 0), stop=(k == 7)))(k),
                              r=khs + ["wup"], w=[bk(bu)])
                up_mm(0, f)
                up_mm(1, NFF + f)

                def conv(half, ch, ub, cb, kub, kcb, e1):
                    P.add("act", lambda e: e.activation(out=ub[ui][:, 2:TB + 2], in_=banks_f[bu][:, half * TB:(half + 1) * TB], func=AF.Copy),
                          r=[bk(bu)], w=[kub])
                    P.add("pool", lambda e: e.tensor_copy(out=ub[ui][:, 0:2], in_=halo[:, ch, :]), r=["halo%d" % ch], w=[kub + "h"])
                    P.add("act", lambda e: e.activation(out=cb[ui][:], in_=banks_f[bu][:, half * TB:(half + 1) * TB], func=AF.Identity,
                                                        scale=convp[:, 2, ch:ch + 1], bias=convp[:, 3, ch:ch + 1]),
                          r=[bk(bu), "convp"], w=[kcb])
                    P.add(e1, lambda e: e.scalar_tensor_tensor(out=cb[ui][:], in0=ub[ui][:, 1:TB + 1], scalar=convp[:, 1, ch:ch + 1],
                                                               in1=cb[ui][:], op0=ALU.mult, op1=ALU.add),
                          r=[kub, kub + "h", kcb, "convp"], w=[kcb])
                    P.add(e1, lambda e: e.scalar_tensor_tensor(out=cb[ui][:], in0=ub[ui][:, 0:TB], scalar=convp[:, 0, ch:ch + 1],
                                                               in1=cb[ui][:], op0=ALU.mult, op1=ALU.add),
                          r=[kub, kub + "h", kcb, "convp"], w=[kcb])
                    P.add("pool", lambda e: e.tensor_copy(out=halo[:, ch, :], in_=ub[ui][:, TB:TB + 2]), r=[kub], w=["halo%d" % ch])
                kug, kuv, kcg, kcv = "ug%d" % ui, "uv%d" % ui, "cg%d" % ui, "cv%d" % ui
                conv(0, f, ug, cg, kug, kcg, "dve")
                conv(1, NFF + f, uv, cv, kuv, kcv, "dve")
                P.add("act", lambda e: e.activation(out=cg[ui][:], in_=cg[ui][:], func=AF.Silu), r=[kcg], w=[kcg])
                P.add("dve", lambda e: e.tensor_tensor(out=gT[:, f, :], in0=cg[ui][:], in1=cv[ui][:], op=ALU.mult),
                      r=[kcg, kcv], w=["gT%d" % f])

            def down_tile(s, t0, it, kgs):
                tk = t0 + it * 128
                bd0, bd1 = BK.get(hold=True), BK.get(hold=True)

                def half(bb, n0):
                    for f in range(NFF):
                        P.add("pe", (lambda f: lambda e: e.matmul(banks_f[bb][:, 0:512], lhsT=gT[:, f, it * 128:(it + 1) * 128],
                                                                  rhs=wdn[:, f, n0:n0 + 512], start=(f == 0), stop=(f == NFF - 1)))(f),
                              r=kgs + ["wdn"], w=[bk(bb)])
                    P.add("dve", lambda e: e.tensor_tensor(out=otile[:, n0:n0 + 512], in0=banks_f[bb][:, 0:512],
                                                           in1=gmb[:, n0:n0 + 512], op=ALU.mult),
                          r=[bk(bb), "gmb"], w=["otileh%d" % n0])
                    P.add("pool", lambda e: e.tensor_tensor(out=otile[:, n0:n0 + 512], in0=otile[:, n0:n0 + 512],
                                                            in1=x1[it][:, n0:n0 + 512], op=ALU.add),
                          r=["otileh%d" % n0, "x1_%dh%d" % (it, n0)], w=["otileh%d" % n0])
                    BK.release(bb)
                half(bd0, 0)
                half(bd1, 512)
                P.add("sp", lambda e: e.dma_start(out=out_d[s, tk:tk + 128, :], in_=otile[:]),
                      r=["otileh0", "otileh512"], w=["out_d"], chan="stout")

            def ffn_block(s, tb):
                t0 = tb * TB
                jblk = t0 // 512
                P.add("sp", lambda e: e.dma_start(out=otb[:], in_=otn_d[s, :, t0:t0 + TB].rearrange("(c p) t -> p c t", p=128)),
                      r=["otn_d%d_%d" % (s, jblk)], w=["otb"], chan="ldot")
                for it in range(NTB):
                    outproj_tile(s, t0, it)
                khs = ["hT2_%d" % it for it in range(NTB)]
                for f in range(NFF):
                    ffn_chunk(f, khs)
                kgs = ["gT%d" % f for f in range(NFF)]
                for it in range(NTB):
                    down_tile(s, t0, it, kgs)

            for s in range(NSEQ if stop not in ("setup", "A", "B") else 0):
                seq_start(s)
                for tb in range(S // TB):
                    ffn_block(s, tb)
            P.emit()
    return nc, P.stats


_CACHE = {}


def _feat_major(v):
    v = np.asarray(v, np.float32)
    return np.ascontiguousarray(v.reshape(-1, 128).T)


def _prepare(x, c, positions, w_ada, b_ada, g_attn_norm, w_in, g_q_latent, g_kv_latent, w_q_up, w_kv_up,
             g_mla_q, g_mla_k, g_ca_q, g_ca_k, rel_bias, w_out, g_mlp_norm, w_up, conv_w, conv_b, w_down):
    f = lambda a: np.ascontiguousarray(np.asarray(a))
    x = f(x); c = f(c); positions = f(positions)
    if "nc" not in _CACHE:
        _CACHE["nc"], _CACHE["stats"] = build_program(stop=_CACHE.get("stop"))
    nc = _CACHE["nc"]
    gfeat = np.concatenate([_feat_major(g_attn_norm[0]), _feat_major(g_mlp_norm[0]), _feat_major(g_q_latent[0]),
                            _feat_major(g_kv_latent[0])], axis=1).astype(np.float32)
    convp = np.zeros((128, 4, 2 * NFF), np.float32)
    for t in range(3):
        convp[:, t, :] = _feat_major(conv_w[0, t])
    convp[:, 3, :] = _feat_major(conv_b[0])
    grow = np.concatenate([np.asarray(g_mla_q[0]), np.asarray(g_mla_k[0]), np.asarray(g_ca_q[0]), np.asarray(g_ca_k[0])]).astype(np.float32)
    grow = np.ascontiguousarray(np.broadcast_to(grow[None, :], (128, 320)))
    rb = np.asarray(rel_bias[0], np.float32)
    kj = np.arange(128)[:, None]
    qi = np.arange(128)[None, :]
    bias34 = np.zeros((128, 8, 2, 128), np.float32)
    for ti, t in enumerate((3, 4)):
        idx = np.clip(128 * (4 - t) + qi - kj, -128, 128) + 128
        bias34[:, :, ti, :] = np.transpose(rb[:, idx], (1, 0, 2))
    bfar = np.ascontiguousarray(np.broadcast_to(rb[:, 256][None, :], (128, 8))).astype(np.float32)
    half = 16
    invf = np.power(np.float32(10000.0), -np.arange(half, dtype=np.float32) / np.float32(half)).astype(np.float32)
    invf = np.ascontiguousarray(np.broadcast_to(invf[None, :], (128, 16)))
    shared = {
        "w_ada": f(w_ada[0]), "w_in": f(w_in[0]), "w_q_up": f(w_q_up[0]), "w_kv_up": f(w_kv_up[0]), "w_out": f(w_out[0]),
        "w_up": f(w_up[0]), "w_down": f(w_down[0]), "gfeat": gfeat, "convp": convp, "grow": grow, "bias34": bias34,
        "bfar": bfar, "invf": invf,
    }
    in_maps = []
    for i in range(NCORES):
        b0 = NSEQ * i
        m = dict(shared)
        m["x"] = f(x[b0:b0 + NSEQ])
        m["cT"] = np.ascontiguousarray(c[b0:b0 + NSEQ].reshape(NSEQ, 8, 128).transpose(2, 1, 0)).astype(np.float32)
        pl = positions[b0:b0 + NSEQ].reshape(NSEQ, NT, 128).transpose(2, 0, 1).reshape(128, NSEQ * NT)
        m["posl"] = np.ascontiguousarray(pl).astype(np.int32)
        m["b_ada2"] = np.ascontiguousarray(np.broadcast_to(np.asarray(b_ada[0], np.float32)[None, :], (NSEQ, 6 * D)))
        in_maps.append(m)
    return nc, in_maps


def kernel(**inputs):
    nc, in_maps = _prepare(**inputs)
    res = run_bass_kernel_spmd(nc, in_maps, core_ids=list(range(NCORES)))
    out = np.concatenate([np.asarray(r["out"]) for r in res.results], axis=0)
    return out.astype(np.float32)
```

```python
import math
from contextlib import ExitStack

import numpy as np
import concourse.bass as bass
import concourse.mybir as mybir
from concourse.bass_utils import run_bass_kernel_spmd

F32 = mybir.dt.float32
BF16 = mybir.dt.bfloat16
I32 = mybir.dt.int32
AF = mybir.ActivationFunctionType
ALU = mybir.AluOpType
AX = mybir.AxisListType

NCORES = 8
NSEQ = 2
S = 2048
D = 1024
NT = S // 128
D_IN = 1952
D_FF = 2816
NFF = D_FF // 128
EPS = 1e-6
TWO_PI = 2.0 * math.pi


class Op:
    __slots__ = ("eng", "fn", "r", "w", "chan", "deps", "signal", "sigval", "waitall")

    def __init__(self, eng, fn, r, w, chan, waitall):
        self.eng, self.fn, self.r, self.w, self.chan = eng, fn, tuple(r), tuple(w), chan
        self.deps = set()
        self.signal = False
        self.sigval = 0
        self.waitall = waitall


class Prog:
    def __init__(self, nc, es):
        self.nc = nc
        self.es = es
        self.ops = []
        self.last_w = {}
        self.readers = {}
        self.waitall_chans = set()

    tag = None

    def add(self, eng, fn, r=(), w=(), chan=None, waitall=False):
        if self.tag is not None and self.tag in SKIP:
            return None
        op = Op(eng, fn, r, w, chan, waitall)
        idx = len(self.ops)
        deps = set()
        for k in op.r:
            lw = self.last_w.get(k)
            if lw is not None:
                deps.add(lw)
            if isinstance(k, str) and k.startswith("bank"):
                for rd in self.readers.get(k, ()):
                    if self.ops[rd].eng != eng:
                        deps.add(rd)
        for k in op.w:
            lw = self.last_w.get(k)
            if lw is not None:
                deps.add(lw)
            for rd in self.readers.get(k, ()):
                deps.add(rd)
        deps.discard(idx)
        if chan is not None and waitall:
            deps = {d for d in deps if self.ops[d].chan != chan}
        op.deps = deps
        for k in op.w:
            self.last_w[k] = idx
            self.readers[k] = []
        for k in op.r:
            if k in op.w:
                continue
            self.readers.setdefault(k, []).append(idx)
        if chan is not None and waitall:
            self.waitall_chans.add(chan)
        self.ops.append(op)
        return idx

    def barrier(self):
        n = len(self.ops)
        last = {}
        for i, op in enumerate(self.ops):
            key = op.chan if op.chan is not None else ("E", op.eng)
            last[key] = i
        alld = set(last.values())
        for eng in ("pe", "act", "dve", "pool", "sp"):
            op = Op(eng, None, (), (), None, False)
            op.deps = set(alld)
            self.ops.append(op)

    def emit(self):
        nc = self.nc
        ops = self.ops
        engobj = {"pe": nc.tensor, "act": nc.scalar, "dve": nc.vector, "pool": nc.gpsimd, "sp": nc.sync}
        for op in ops:
            for d in op.deps:
                x = ops[d]
                if x.chan is None and x.eng == "pe" and op.eng == "pe" and op.chan is None:
                    continue
                x.signal = True
        sems = {}

        def sem(name):
            if name not in sems:
                sems[name] = self.es.enter_context(nc.semaphore("s_" + str(name)))
            return sems[name]

        cnt = {}
        chan_total = {}
        for op in ops:
            if op.fn is None:
                continue
            if op.chan is not None:
                c = ("C", op.chan)
                cnt[c] = cnt.get(c, 0) + 16
                op.sigval = cnt[c]
                op.signal = True
                chan_total[op.chan] = cnt[c]
            elif op.signal:
                c = ("E", op.eng)
                cnt[c] = cnt.get(c, 0) + 1
                op.sigval = cnt[c]
        known = {e: {} for e in engobj}
        vcs = [None] * len(ops)
        ecount = {}
        nwaits = 0

        def merge(dst, src):
            for k_, v_ in src.items():
                if dst.get(k_, 0) < v_:
                    dst[k_] = v_

        for i, op in enumerate(ops):
            need = []
            for d in op.deps:
                x = ops[d]
                if x.fn is None:
                    if vcs[d] is not None:
                        need.append((None, 0, d))
                    continue
                if x.chan is not None:
                    key = ("C", x.chan)
                    val = chan_total[x.chan] if x.chan in self.waitall_chans else x.sigval
                else:
                    if x.eng == "pe" and op.eng == "pe" and op.chan is None:
                        continue
                    key = ("E", x.eng)
                    val = x.sigval
                need.append((key, val, d))
            need.sort(key=lambda t: -t[2])
            e = engobj[op.eng]
            kn = known[op.eng]
            for key, val, d in need:
                if key is None:
                    continue
                if kn.get(key, 0) >= val:
                    continue
                e.wait_ge(sem(key), val)
                kn[key] = val
                nwaits += 1
                if vcs[d] is not None and not (ops[d].chan in self.waitall_chans):
                    merge(kn, vcs[d])
            vc = dict(kn)
            if op.fn is not None:
                inst = op.fn(e)
                if op.chan is not None:
                    inst.then_inc(sem(("C", op.chan)), 16)
                    if op.chan not in self.waitall_chans:
                        vc[("C", op.chan)] = max(vc.get(("C", op.chan), 0), op.sigval)
                else:
                    if op.signal:
                        inst.then_inc(sem(("E", op.eng)), 1)
                        ecount[op.eng] = op.sigval
                    if op.eng != "pe" or True:
                        vc[("E", op.eng)] = max(vc.get(("E", op.eng), 0), ecount.get(op.eng, 0))
            vcs[i] = vc
        for chan, tot in chan_total.items():
            if known["sp"].get(("C", chan), 0) < tot:
                nc.sync.wait_ge(sem(("C", chan)), tot)
        self.stats = dict(n_ops=len(ops), n_waits=nwaits, n_sems=len(sems))


class Banks:
    def __init__(self, banks):
        self.banks = banks
        self.ptr = 0
        self.held = set()

    def get(self, hold=False):
        for _ in range(16):
            b = self.ptr
            self.ptr = (self.ptr + 1) % len(self.banks)
            if b not in self.held:
                if hold:
                    self.held.add(b)
                return b
        raise RuntimeError("no free PSUM bank")

    def release(self, b):
        self.held.discard(b)


import os
SKIP = set(os.environ.get('KSKIP', '').split(','))


def build_program(debug=None, stop=None):
    nc = bass.Bass("TRN2", target_bir_lowering=False)

    def din(name, shape, dt=F32):
        return nc.dram_tensor(name, list(shape), dt, kind="ExternalInput").ap()

    def dscr(name, shape, dt):
        return nc.dram_tensor(name, list(shape), dt, kind="Internal").ap()

    x_d = din("x", [NSEQ, S, D])
    cT_d = din("cT", [128, 8, NSEQ])
    pos_d = din("posl", [128, NSEQ * NT], I32)
    wada_d = din("w_ada", [D, 6 * D])
    bada_d = din("b_ada2", [NSEQ, 6 * D])
    win_d = din("w_in", [D, D_IN])
    wq_d = din("w_q_up", [256, 768])
    wkv_d = din("w_kv_up", [128, 1024])
    wout_d = din("w_out", [D, D])
    wup_d = din("w_up", [D, 2 * D_FF])
    wdn_d = din("w_down", [D_FF, D])
    gfeat_d = din("gfeat", [128, 19])
    convp_d = din("convp", [128, 4, 2 * NFF])
    grow_d = din("grow", [128, 320])
    bias34_d = din("bias34", [128, 8, 2, 128])
    bfar_d = din("bfar", [128, 8])
    invf_d = din("invf", [128, 16])
    out_d = nc.dram_tensor("out", [NSEQ, S, D], F32, kind="ExternalOutput").ap()

    mod_d = dscr("mod_scr", [NSEQ, 6 * D], F32)
    qT_d = dscr("qT_scr", [NSEQ, 8, 96, S], BF16)
    kT_d = dscr("kT_scr", [NSEQ, 8, 96, S], BF16)
    cqT_d = dscr("cqT_scr", [NSEQ, 8, 64, S], BF16)
    ckT_d = dscr("ckT_scr", [NSEQ, 8, 64, S], BF16)
    V_d = dscr("V_scr", [NSEQ, S, 8 * 65], BF16)
    cV_d = dscr("cV_scr", [NSEQ, S, 8 * 65], BF16)
    otn_d = dscr("otn_scr", [NSEQ, D, S], BF16)

    with ExitStack() as es:
        P = Prog(nc, es)

        def sb(stack, name, shape, dt):
            return stack.enter_context(nc.sbuf_tensor("sb_" + name, list(shape), dt))

        banks_f = [es.enter_context(nc.psum_tensor("bank%d" % i, [128, 512], F32)) for i in range(8)]
        banks_b = [b[:].bitcast(BF16) for b in banks_f]
        BK = Banks(banks_f)

        def bk(b):
            return "bank%d" % b

        ident = sb(es, "ident", [128, 128], BF16)
        identf = sb(es, "identf", [128, 128], F32)
        sel64 = sb(es, "sel64", [128, 64], F32)
        gfeat = sb(es, "gfeat", [128, 19], F32)
        grow = sb(es, "grow", [128, 320], F32)
        AB = sb(es, "AB", [128, NSEQ, 4, 8], F32)
        sa = es.enter_context(ExitStack())
        cs_all = sb(sa, "cs_all", [128, NSEQ * NT, 32], F32)
        junk = sb(sa, "junk", [128, 1024], BF16)

        P.add("pool", lambda e: e.memset(identf[:], 1.0), w=["identf"])
        P.add("pool", lambda e: e.affine_select(out=identf[:], in_=identf[:], pattern=[[-1, 128]],
                                                compare_op=ALU.is_equal, fill=0.0, base=0, channel_multiplier=1),
              r=["identf"], w=["identf"])
        P.add("pool", lambda e: e.tensor_copy(out=ident[:], in_=identf[:]), r=["identf"], w=["ident"])
        P.add("pool", lambda e: e.memset(sel64[:], 0.0), w=["sel64"])
        P.add("pool", lambda e: e.memset(sel64[64:65, :], 1.0), r=["sel64"], w=["sel64"])
        P.add("sp", lambda e: e.dma_start(out=gfeat[:], in_=gfeat_d), w=["gfeat"], chan="small", waitall=True)
        P.add("sp", lambda e: e.dma_start(out=grow[:], in_=grow_d), w=["grow"], chan="small", waitall=True)
        s0 = es.enter_context(ExitStack())
        cT = sb(s0, "cT", [128, 8, NSEQ], F32)
        bada = sb(s0, "bada", [NSEQ, 6 * D], F32)
        posi = sb(s0, "posi", [128, NSEQ * NT], I32)
        invf = sb(s0, "invf", [128, 16], F32)
        P.add("sp", lambda e: e.dma_start(out=cT[:], in_=cT_d), w=["cT"], chan="small", waitall=True)
        P.add("sp", lambda e: e.dma_start(out=bada[:], in_=bada_d), w=["bada"], chan="small", waitall=True)
        P.add("sp", lambda e: e.dma_start(out=posi[:], in_=pos_d), w=["posi"], chan="small", waitall=True)
        P.add("sp", lambda e: e.dma_start(out=invf[:], in_=invf_d), w=["invf"], chan="small", waitall=True)
        P.add("dve", lambda e: e.tensor_scalar(out=grow[:, 96:192], in0=grow[:, 96:192], scalar1=math.sqrt(96.0),
                                               scalar2=None, op0=ALU.mult), r=["grow"], w=["grow"])
        P.add("dve", lambda e: e.tensor_scalar(out=grow[:, 256:320], in0=grow[:, 256:320], scalar1=8.0,
                                               scalar2=None, op0=ALU.mult), r=["grow"], w=["grow"])

        if True:
            scb = sb(s0, "scb", [128, 8, NSEQ], BF16)
            wab = [sb(s0, "wab%d" % i, [128, 8, 512], BF16) for i in range(2)]
            modsb = sb(s0, "modsb", [NSEQ, 6 * D], F32)
            P.add("act", lambda e: e.activation(out=scb[:], in_=cT[:], func=AF.Silu), r=["cT"], w=["scb"])
            for nb in range(12):
                sl = nb % 2
                P.add("pool", (lambda nb, sl: lambda e: e.dma_start(
                    out=wab[sl][:], in_=wada_d[:, nb * 512:(nb + 1) * 512].rearrange("(j p) n -> p j n", p=128)))(nb, sl),
                    w=["wab%d" % sl], chan="wab%d" % sl)
                b = BK.get()
                for k in range(8):
                    P.add("pe", (lambda b, sl, k: lambda e: e.matmul(
                        banks_f[b][0:NSEQ, 0:512], lhsT=scb[:, k, :], rhs=wab[sl][:, k, :], start=(k == 0), stop=(k == 7)))(b, sl, k),
                        r=["scb", "wab%d" % sl], w=[bk(b)])
                P.add("dve", (lambda b, nb: lambda e: e.tensor_tensor(
                    out=modsb[:, nb * 512:(nb + 1) * 512], in0=banks_f[b][0:NSEQ, 0:512],
                    in1=bada[:, nb * 512:(nb + 1) * 512], op=ALU.add))(b, nb),
                    r=[bk(b), "bada"], w=["modsb"])
            P.add("sp", lambda e: e.dma_start(out=mod_d, in_=modsb[:]), r=["modsb"], w=["mod_d"], chan="modst")
            P.tag = "modT"
            modT = sb(s0, "modT", [128, NSEQ, 4, 8], F32)
            bT = BK.get()

            def modtr(qi, c, j):
                col = (qi * 8 + j) * NSEQ
                P.add("pe", lambda e: e.matmul(banks_f[bT][:, col:col + NSEQ], lhsT=modsb[0:NSEQ, c * D + j * 128:c * D + (j + 1) * 128],
                                               rhs=identf[0:NSEQ, 0:NSEQ], start=True, stop=True),
                      r=["modsb", "identf"], w=[bk(bT)])
            for qi, c in enumerate((0, 1, 3, 4)):
                for j in range(8):
                    modtr(qi, c, j)
            P.add("dve", lambda e: e.tensor_copy(out=modT[:].rearrange("p s k j -> p k j s"),
                                                 in_=banks_f[bT][:, 0:32 * NSEQ].rearrange("p (k j s) -> p k j s", k=4, j=8)),
                  r=[bk(bT)], w=["modT"])
            for s in range(NSEQ):
                for (dst, srcq, gcol) in ((0, 1, 0), (2, 3, 8)):
                    P.add("dve", (lambda s, dst, srcq, gcol: lambda e: e.scalar_tensor_tensor(
                        out=AB[:, s, dst, :], in0=modT[:, s, srcq, :], scalar=1.0, in1=gfeat[:, gcol:gcol + 8],
                        op0=ALU.add, op1=ALU.mult))(s, dst, srcq, gcol),
                        r=["modT", "gfeat"], w=["AB"])
                    P.add("dve", (lambda s, dst: lambda e: e.tensor_scalar(
                        out=AB[:, s, dst, :], in0=AB[:, s, dst, :], scalar1=32.0, scalar2=None, op0=ALU.mult))(s, dst),
                        r=["AB"], w=["AB"])
                    P.add("dve", (lambda s, dst, srcq: lambda e: e.tensor_copy(
                        out=AB[:, s, dst + 1, :], in_=modT[:, s, srcq - 1, :]))(s, dst, srcq),
                        r=["modT"], w=["AB"])
            P.tag = "rot"
            posf = sb(s0, "posf", [128, NSEQ * NT], F32)
            ang = sb(s0, "ang", [128, NSEQ * NT, 32], F32)
            kf = sb(s0, "kf", [128, NSEQ * NT, 32], F32)
            ki = sb(s0, "ki", [128, NSEQ * NT, 32], I32)
            mk = sb(s0, "mk", [128, NSEQ * NT, 32], F32)
            P.add("dve", lambda e: e.tensor_copy(out=posf[:], in_=posi[:]), r=["posi"], w=["posf"])
            NTT = NSEQ * NT
            P.add("dve", lambda e: e.tensor_tensor(out=ang[:, :, 16:32], in0=posf[:].unsqueeze(2).to_broadcast([128, NTT, 16]),
                                                   in1=invf[:].unsqueeze(1).to_broadcast([128, NTT, 16]), op=ALU.mult),
                  r=["posf", "invf"], w=["ang"])
            P.add("dve", lambda e: e.tensor_scalar(out=ang[:, :, 0:16], in0=ang[:, :, 16:32], scalar1=math.pi / 2.0,
                                                   scalar2=None, op0=ALU.add), r=["ang"], w=["ang"])
            P.add("dve", lambda e: e.tensor_scalar(out=kf[:], in0=ang[:], scalar1=1.0 / TWO_PI, scalar2=None, op0=ALU.mult),
                  r=["ang"], w=["kf"])
            P.add("dve", lambda e: e.tensor_copy(out=ki[:], in_=kf[:]), r=["kf"], w=["ki"])
            P.add("dve", lambda e: e.tensor_copy(out=kf[:], in_=ki[:]), r=["ki"], w=["kf"])
            P.add("dve", lambda e: e.scalar_tensor_tensor(out=ang[:], in0=kf[:], scalar=-TWO_PI, in1=ang[:],
                                                          op0=ALU.mult, op1=ALU.add), r=["kf", "ang"], w=["ang"])
            P.add("dve", lambda e: e.tensor_scalar(out=mk[:], in0=ang[:], scalar1=math.pi, scalar2=-TWO_PI,
                                                   op0=ALU.is_gt, op1=ALU.mult), r=["ang"], w=["mk"])
            P.add("dve", lambda e: e.tensor_tensor(out=ang[:], in0=ang[:], in1=mk[:], op=ALU.add), r=["ang", "mk"], w=["ang"])
            P.add("dve", lambda e: e.tensor_scalar(out=mk[:], in0=ang[:], scalar1=-math.pi, scalar2=TWO_PI,
                                                   op0=ALU.is_lt, op1=ALU.mult), r=["ang"], w=["mk"])
            P.add("dve", lambda e: e.tensor_tensor(out=ang[:], in0=ang[:], in1=mk[:], op=ALU.add), r=["ang", "mk"], w=["ang"])
            P.add("dve", lambda e: e.tensor_scalar(out=ang[:], in0=ang[:], scalar1=math.pi, scalar2=-math.pi,
                                                   op0=ALU.min, op1=ALU.max), r=["ang"], w=["ang"])
            P.add("act", lambda e: e.activation(out=cs_all[:], in_=ang[:], func=AF.Sin), r=["ang"], w=["cs_all"])
            P.tag = None
            P.barrier()
            s0.close()

        if True:
            P.tag = "wlA"
            win = sb(sa, "win", [128, 8, D_IN], BF16)
            wq = sb(sa, "wq", [128, 2, 768], BF16)
            wkv = sb(sa, "wkv", [128, 1024], BF16)
            wstage = sb(sa, "wstage", [128, 2, 768], F32)
            wstage2 = sb(sa, "wstage2", [128, 1024], F32)
            for (c0, c1) in ((0, 512), (512, 1024), (1024, 1536), (1536, 1952)):
                P.add("pool", (lambda c0, c1: lambda e: e.dma_start(out=win[:, :, c0:c1], in_=win_d[:, c0:c1].rearrange("(j p) n -> p j n", p=128)))(c0, c1),
                      w=["win"], chan="win", waitall=True)
            P.tag = "wlA2"
            P.add("sp", lambda e: e.dma_start(out=wstage[:], in_=wq_d.rearrange("(j p) n -> p j n", p=128)),
                  w=["wstage"], chan="small3", waitall=True)
            P.add("sp", lambda e: e.dma_start(out=wstage2[:], in_=wkv_d), w=["wstage2"], chan="small3", waitall=True)
            for j in range(2):
                P.add("dve", (lambda j: lambda e: e.tensor_scalar(
                    out=wq[:, j, :], in0=wstage[:, j, :], scalar1=gfeat[:, 16 + j:17 + j], scalar2=16.0,
                    op0=ALU.mult, op1=ALU.mult))(j), r=["wstage", "gfeat"], w=["wq"])
            P.add("dve", lambda e: e.tensor_scalar(out=wkv[:], in0=wstage2[:], scalar1=gfeat[:, 18:19],
                                                   scalar2=math.sqrt(128.0), op0=ALU.mult, op1=ALU.mult),
                  r=["wstage2", "gfeat"], w=["wkv"])

            P.tag = "wlA3"
            xt = [sb(sa, "xt%d" % i, [128, D], F32) for i in range(2)]
            xn = [sb(sa, "xn%d" % i, [128, D], BF16) for i in range(2)]
            hT = [sb(sa, "hT%d" % i, [128, 8, 128], BF16) for i in range(2)]
            st = sb(sa, "stA", [128, 2, 40], F32)
            lat = [sb(sa, "lat%d" % i, [128, 384], BF16) for i in range(2)]
            latT = [sb(sa, "latT%d" % i, [128, 3, 128], BF16) for i in range(2)]
            kr = [sb(sa, "kr%d" % i, [128, 4, 32], F32) for i in range(2)]
            qn = [sb(sa, "qn%d" % i, [128, 8, 96], F32) for i in range(2)]
            qsq = [sb(sa, "qsq%d" % i, [128, 8, 96], F32) for i in range(2)]
            qb = [sb(sa, "qb%d" % i, [128, 8, 96], BF16) for i in range(2)]
            kn = [sb(sa, "kn%d" % i, [128, 8, 64], F32) for i in range(2)]
            ksq = [sb(sa, "ksq%d" % i, [128, 8, 64], F32) for i in range(2)]
            kb = [sb(sa, "kb%d" % i, [128, 8, 96], BF16) for i in range(2)]
            rt = [sb(sa, "rt%d" % i, [128, 8, 32], F32) for i in range(2)]
            cqn = [sb(sa, "cqn%d" % i, [128, 8, 64], F32) for i in range(2)]
            cqs = [sb(sa, "cqs%d" % i, [128, 8, 64], F32) for i in range(2)]
            cqb = [sb(sa, "cqb%d" % i, [128, 8, 64], BF16) for i in range(2)]
            ckn = [sb(sa, "ckn%d" % i, [128, 8, 64], F32) for i in range(2)]
            cks = [sb(sa, "cks%d" % i, [128, 8, 64], F32) for i in range(2)]
            ckb = [sb(sa, "ckb%d" % i, [128, 8, 64], BF16) for i in range(2)]
            qT_st = [sb(sa, "qTst%d" % i, [96, 8, 512], BF16) for i in range(2)]
            kT_st = [sb(sa, "kTst%d" % i, [96, 8, 512], BF16) for i in range(2)]
            cqT_st = [sb(sa, "cqTst%d" % i, [64, 8, 512], BF16) for i in range(2)]
            ckT_st = [sb(sa, "ckTst%d" % i, [64, 8, 512], BF16) for i in range(2)]
            V_st = [sb(sa, "Vst%d" % i, [128, 4, 8, 65], BF16) for i in range(2)]
            cV_st = [sb(sa, "cVst%d" % i, [128, 4, 8, 65], BF16) for i in range(2)]
            for i in range(2):
                P.add("pool", (lambda i: lambda e: e.memset(V_st[i][:, :, :, 64:65], 1.0))(i), w=["Vst%d" % i])
                P.add("pool", (lambda i: lambda e: e.memset(cV_st[i][:, :, :, 64:65], 1.0))(i), w=["cVst%d" % i])

            P.tag = None

            def rstd_from_ssq(sl, col, n, tag):
                kst = "st%d_%s" % (sl, tag)
                P.add("act", lambda e: e.activation(out=st[:, sl, col:col + 1], in_=st[:, sl, col:col + 1], func=AF.Sqrt,
                                                    bias=float(n * EPS), scale=1.0), r=[kst], w=[kst])
                P.add("dve", lambda e: e.reciprocal(out=st[:, sl, col:col + 1], in_=st[:, sl, col:col + 1]), r=[kst], w=[kst])
                return kst

            def rstd_vec(sl, c0, nh, n, tag):
                kst = "st%d_%s" % (sl, tag)
                P.add("act", lambda e: e.activation(out=st[:, sl, c0:c0 + nh], in_=st[:, sl, c0:c0 + nh], func=AF.Sqrt,
                                                    bias=float(n * EPS), scale=1.0), r=[kst], w=[kst])
                P.add("dve", lambda e: e.reciprocal(out=st[:, sl, c0:c0 + nh], in_=st[:, sl, c0:c0 + nh]), r=[kst], w=[kst])
                return kst

            def rope(eng, src3, dst3, cs, keys_r, keys_w, tmp, nh, ktmp):
                cosb = cs[:, 0:16].unsqueeze(1).to_broadcast([128, nh, 16])
                sinb = cs[:, 16:32].unsqueeze(1).to_broadcast([128, nh, 16])
                P.add(eng, lambda e: e.tensor_tensor(out=tmp[:, :, 0:16], in0=src3[:, :, 16:32], in1=sinb, op=ALU.mult),
                      r=keys_r, w=[ktmp])
                P.add(eng, lambda e: e.tensor_tensor(out=tmp[:, :, 16:32], in0=src3[:, :, 0:16], in1=sinb, op=ALU.mult),
                      r=keys_r + [ktmp], w=[ktmp])
                P.add(eng, lambda e: e.tensor_tensor(out=src3[:, :, 0:16], in0=src3[:, :, 0:16], in1=cosb, op=ALU.mult),
                      r=keys_r + [ktmp], w=keys_r[:1])
                P.add(eng, lambda e: e.tensor_tensor(out=src3[:, :, 16:32], in0=src3[:, :, 16:32], in1=cosb, op=ALU.mult),
                      r=keys_r + [ktmp], w=keys_r[:1])
                P.add(eng, lambda e: e.tensor_tensor(out=dst3[:, :, 0:16], in0=src3[:, :, 0:16], in1=tmp[:, :, 0:16], op=ALU.subtract),
                      r=keys_r + [ktmp], w=keys_w)
                P.add(eng, lambda e: e.tensor_tensor(out=dst3[:, :, 16:32], in0=src3[:, :, 16:32], in1=tmp[:, :, 16:32], op=ALU.add),
                      r=keys_r + [ktmp], w=keys_w)

            def prep_tile(g):
                s, tt = divmod(g, NT)
                jb, i4 = divmod(tt, 4)
                sl = g % 2
                bs = (g // 4) % 2
                S_ = str(sl)
                kxt, kxn, khT = "xt" + S_, "xn" + S_, "hT" + S_
                P.add("sp", lambda e: e.dma_start(out=xt[sl][:], in_=x_d[s, tt * 128:(tt + 1) * 128, :]), w=[kxt], chan="xt" + S_)
                P.add("act", lambda e: e.activation(out=junk[:], in_=xt[sl][:], func=AF.Square, accum_out=st[:, sl, 0:1]),
                      r=[kxt], w=["st%s_x" % S_])
                kst = rstd_from_ssq(sl, 0, D, "x")
                P.add("act", lambda e: e.activation(out=xn[sl][:], in_=xt[sl][:], func=AF.Copy, scale=st[:, sl, 0:1]),
                      r=[kxt, kst], w=[kxn])
                b0 = BK.get()
                for c in range(8):
                    P.add("pe", (lambda c: lambda e: e.transpose(out=banks_b[b0][:, c * 128:(c + 1) * 128],
                                                                  in_=xn[sl][:, c * 128:(c + 1) * 128], identity=ident[:]))(c),
                          r=[kxn, "ident"], w=[bk(b0)])
                tp3 = banks_b[b0][:, 0:1024].rearrange("p (c t) -> p c t", c=8)
                P.add("dve", lambda e: e.tensor_tensor(out=hT[sl][:], in0=tp3, in1=AB[:, s, 0, :].unsqueeze(2).to_broadcast([128, 8, 128]),
                                                       op=ALU.mult), r=[bk(b0), "AB"], w=[khT])
                P.add("pool", lambda e: e.tensor_tensor(out=hT[sl][:], in0=hT[sl][:], in1=AB[:, s, 1, :].unsqueeze(2).to_broadcast([128, 8, 128]),
                                                        op=ALU.add), r=[khT, "AB"], w=[khT])
                pb = [BK.get(hold=True) for _ in range(4)]
                cols = [(0, 416), (416, 928), (928, 1440), (1440, 1952)]
                for bi, (c0, c1) in enumerate(cols):
                    for k in range(8):
                        P.add("pe", (lambda bi, c0, c1, k: lambda e: e.matmul(
                            banks_f[pb[bi]][:, 0:c1 - c0], lhsT=hT[sl][:, k, :], rhs=win[:, k, c0:c1],
                            start=(k == 0), stop=(k == 7)))(bi, c0, c1, k),
                            r=[khT, "win"], w=[bk(pb[bi])])
                pl, pq, pk, pv = [banks_f[b] for b in pb]
                kl, kq, kk_, kv_ = [bk(b) for b in pb]
                P.add("act", lambda e: e.activation(out=junk[:, 0:256], in_=pl[:, 0:256], func=AF.Square, accum_out=st[:, sl, 1:2]),
                      r=[kl], w=["st%s_ql" % S_])
                P.add("act", lambda e: e.activation(out=junk[:, 256:384], in_=pl[:, 256:384], func=AF.Square, accum_out=st[:, sl, 2:3]),
                      r=[kl], w=["st%s_kvl" % S_])
                k1 = rstd_from_ssq(sl, 1, 256, "ql")
                k2 = rstd_from_ssq(sl, 2, 128, "kvl")
                klat = "lat" + S_
                P.add("dve", lambda e: e.tensor_scalar(out=lat[sl][:, 0:256], in0=pl[:, 0:256], scalar1=st[:, sl, 1:2], scalar2=None,
                                                       op0=ALU.mult), r=[kl, k1], w=[klat + "a"])
                P.add("dve", lambda e: e.tensor_scalar(out=lat[sl][:, 256:384], in0=pl[:, 256:384], scalar1=st[:, sl, 2:3], scalar2=None,
                                                       op0=ALU.mult), r=[kl, k2], w=[klat + "b"])
                kkr = "kr" + S_
                P.add("dve", lambda e: e.tensor_tensor(out=kr[sl][:, 0, :], in0=pl[:, 384:416], in1=grow[:, 160:192], op=ALU.mult),
                      r=[kl, "grow"], w=[kkr])
                P.add("act", lambda e: e.activation(out=junk[:, 512:544], in_=pl[:, 384:416], func=AF.Square, accum_out=st[:, sl, 3:4]),
                      r=[kl], w=["st%s_kr" % S_])
                BK.release(pb[0])
                b1 = BK.get()
                for c in range(3):
                    P.add("pe", (lambda c: lambda e: e.transpose(out=banks_b[b1][:, c * 128:(c + 1) * 128],
                                                                  in_=lat[sl][:, c * 128:(c + 1) * 128], identity=ident[:]))(c),
                          r=[klat + "a", klat + "b", "ident"], w=[bk(b1)])
                klT = "latT" + S_
                P.add("act", lambda e: e.activation(out=latT[sl][:].rearrange("p c t -> p (c t)"), in_=banks_b[b1][:, 0:384], func=AF.Copy),
                      r=[bk(b1)], w=[klT])
                bq0, bq1 = BK.get(hold=True), BK.get(hold=True)
                for (bb, c0, c1) in ((bq0, 0, 480), (bq1, 480, 768)):
                    for k in range(2):
                        P.add("pe", (lambda bb, c0, c1, k: lambda e: e.matmul(
                            banks_f[bb][:, 0:c1 - c0], lhsT=latT[sl][:, k, :], rhs=wq[:, k, c0:c1], start=(k == 0), stop=(k == 1)))(bb, c0, c1, k),
                            r=[klT, "wq"], w=[bk(bb)])
                bkv0, bkv1 = BK.get(hold=True), BK.get(hold=True)
                for (bb, c0) in ((bkv0, 0), (bkv1, 512)):
                    P.add("pe", (lambda bb, c0: lambda e: e.matmul(
                        banks_f[bb][:, 0:512], lhsT=latT[sl][:, 2, :], rhs=wkv[:, c0:c0 + 512], start=True, stop=True))(bb, c0),
                        r=[klT, "wkv"], w=[bk(bb)])
                cs = cs_all[:, g, :]
                kqn, kqs, kqb = "qn" + S_, "qsq" + S_, "qb" + S_
                q0 = banks_f[bq0][:, 0:480].rearrange("p (h d) -> p h d", h=5)
                q1 = banks_f[bq1][:, 0:288].rearrange("p (h d) -> p h d", h=3)
                P.add("act", lambda e: e.activation(out=qn[sl][:, 0:5, :], in_=q0, func=AF.Copy), r=[bk(bq0)], w=[kqn + "a"])
                P.add("act", lambda e: e.activation(out=qn[sl][:, 5:8, :], in_=q1, func=AF.Copy), r=[bk(bq1)], w=[kqn + "b"])
                BK.release(bq0)
                BK.release(bq1)
                P.add("pool", lambda e: e.tensor_tensor(out=qsq[sl][:], in0=qn[sl][:], in1=qn[sl][:], op=ALU.mult),
                      r=[kqn + "a", kqn + "b"], w=[kqs])
                P.add("dve", lambda e: e.tensor_reduce(out=st[:, sl, 8:16], in_=qsq[sl][:], axis=AX.X, op=ALU.add),
                      r=[kqs], w=["st%s_q" % S_])
                k3 = rstd_vec(sl, 8, 8, 96, "q")
                P.add("dve", lambda e: e.tensor_tensor(out=qn[sl][:], in0=qn[sl][:], in1=st[:, sl, 8:16].unsqueeze(2).to_broadcast([128, 8, 96]),
                                                       op=ALU.mult), r=[kqn + "a", kqn + "b", k3], w=[kqn])
                P.add("pool", lambda e: e.tensor_tensor(out=qn[sl][:], in0=qn[sl][:], in1=grow[:, 0:96].unsqueeze(1).to_broadcast([128, 8, 96]),
                                                        op=ALU.mult), r=[kqn, "grow"], w=[kqn])
                P.add("act", lambda e: e.activation(out=qb[sl][:, :, 0:64], in_=qn[sl][:, :, 0:64], func=AF.Copy), r=[kqn], w=[kqb + "n"])
                rope("pool", qn[sl][:, :, 64:96], qb[sl][:, :, 64:96], cs, [kqn, "cs_all"], [kqb + "r"], rt[sl], 8, "rt" + S_)
                kkn, kks, kkb = "kn" + S_, "ksq" + S_, "kb" + S_
                kv0 = banks_f[bkv0][:, 0:512].rearrange("p (h d) -> p h d", h=4)
                kv1 = banks_f[bkv1][:, 0:512].rearrange("p (h d) -> p h d", h=4)
                P.add("act", lambda e: e.activation(out=kn[sl][:, 0:4, :], in_=kv0[:, :, 0:64], func=AF.Copy), r=[bk(bkv0)], w=[kkn + "a"])
                P.add("act", lambda e: e.activation(out=kn[sl][:, 4:8, :], in_=kv1[:, :, 0:64], func=AF.Copy), r=[bk(bkv1)], w=[kkn + "b"])
                P.add("act", lambda e: e.activation(out=V_st[bs][:, i4, 0:4, 0:64], in_=kv0[:, :, 64:128], func=AF.Copy),
                      r=[bk(bkv0)], w=["Vst%d" % bs])
                P.add("act", lambda e: e.activation(out=V_st[bs][:, i4, 4:8, 0:64], in_=kv1[:, :, 64:128], func=AF.Copy),
                      r=[bk(bkv1)], w=["Vst%d" % bs])
                BK.release(bkv0)
                BK.release(bkv1)
                P.add("pool", lambda e: e.tensor_tensor(out=ksq[sl][:], in0=kn[sl][:], in1=kn[sl][:], op=ALU.mult),
                      r=[kkn + "a", kkn + "b"], w=[kks])
                P.add("dve", lambda e: e.tensor_reduce(out=st[:, sl, 16:24], in_=ksq[sl][:], axis=AX.X, op=ALU.add),
                      r=[kks], w=["st%s_k" % S_])
                P.add("dve", lambda e: e.tensor_scalar(out=st[:, sl, 16:24], in0=st[:, sl, 16:24], scalar1=st[:, sl, 3:4], scalar2=None,
                                                       op0=ALU.add), r=["st%s_k" % S_, "st%s_kr" % S_], w=["st%s_k" % S_])
                k4 = rstd_vec(sl, 16, 8, 96, "k")
                P.add("dve", lambda e: e.tensor_tensor(out=kn[sl][:], in0=kn[sl][:], in1=st[:, sl, 16:24].unsqueeze(2).to_broadcast([128, 8, 64]),
                                                       op=ALU.mult), r=[kkn + "a", kkn + "b", k4], w=[kkn])
                P.add("pool", lambda e: e.tensor_tensor(out=kb[sl][:, :, 0:64], in0=kn[sl][:], in1=grow[:, 96:160].unsqueeze(1).to_broadcast([128, 8, 64]),
                                                        op=ALU.mult), r=[kkn, "grow"], w=[kkb + "n"])
                rope("dve", kr[sl][:, 0:1, :], kr[sl][:, 1:2, :], cs, [kkr, "cs_all"], [kkr + "o"], kr[sl][:, 2:3, :], 1, kkr + "t")
                P.add("dve", lambda e: e.tensor_tensor(out=kb[sl][:, :, 64:96], in0=kr[sl][:, 1:2, :].to_broadcast([128, 8, 32]),
                                                       in1=st[:, sl, 16:24].unsqueeze(2).to_broadcast([128, 8, 32]), op=ALU.mult),
                      r=[kkr + "o", k4], w=[kkb + "r"])
                for (src, keyb, dn, dsq, db, col, gc0, tag) in (
                        (pq, kq, cqn, cqs, cqb, 24, 192, "cq"), (pk, kk_, ckn, cks, ckb, 32, 256, "ck")):
                    kdn, kds, kdb = tag + "n" + S_, tag + "s" + S_, tag + "b" + S_
                    P.add("act", (lambda src, dn: lambda e: e.activation(out=dn[sl][:].rearrange("p h d -> p (h d)"), in_=src[:, 0:512], func=AF.Copy))(src, dn),
                          r=[keyb], w=[kdn])
                    P.add("pool", (lambda dn, dsq: lambda e: e.tensor_tensor(out=dsq[sl][:], in0=dn[sl][:], in1=dn[sl][:], op=ALU.mult))(dn, dsq),
                          r=[kdn], w=[kds])
                    P.add("dve", (lambda dsq, col: lambda e: e.tensor_reduce(out=st[:, sl, col:col + 8], in_=dsq[sl][:], axis=AX.X, op=ALU.add))(dsq, col),
                          r=[kds], w=["st%s_%s" % (S_, tag)])
                    k5 = rstd_vec(sl, col, 8, 64, tag)
                    P.add("dve", (lambda dn, col: lambda e: e.tensor_tensor(out=dn[sl][:], in0=dn[sl][:],
                                                                           in1=st[:, sl, col:col + 8].unsqueeze(2).to_broadcast([128, 8, 64]), op=ALU.mult))(dn, col),
                          r=[kdn, k5], w=[kdn])
                    P.add("pool", (lambda dn, db, gc0: lambda e: e.tensor_tensor(out=db[sl][:], in0=dn[sl][:],
                                                                                in1=grow[:, gc0:gc0 + 64].unsqueeze(1).to_broadcast([128, 8, 64]), op=ALU.mult))(dn, db, gc0),
                          r=[kdn, "grow"], w=[kdb])
                P.add("act", lambda e: e.activation(out=cV_st[bs][:, i4, :, 0:64], in_=pv[:, 0:512].rearrange("p (h d) -> p h d", h=8), func=AF.Copy),
                      r=[kv_], w=["cVst%d" % bs])
                BK.release(pb[1])
                BK.release(pb[2])
                BK.release(pb[3])
                for (srcb, keys, dst, kdst, dd) in ((qb, [kqb + "n", kqb + "r"], qT_st, "qTst%d" % bs, 96),
                                                    (kb, [kkb + "n", kkb + "r"], kT_st, "kTst%d" % bs, 96),
                                                    (cqb, ["cqb" + S_], cqT_st, "cqTst%d" % bs, 64),
                                                    (ckb, ["ckb" + S_], ckT_st, "ckTst%d" % bs, 64)):
                    bt = BK.get()
                    for h in range(8):
                        P.add("pe", (lambda srcb, bt, h, dd: lambda e: e.transpose(
                            out=banks_b[bt][0:dd, h * 128:(h + 1) * 128], in_=srcb[sl][:, h, :], identity=ident[:]))(srcb, bt, h, dd),
                            r=keys + ["ident"], w=[bk(bt)])
                    P.add("act", (lambda bt, dst, dd: lambda e: e.activation(
                        out=dst[bs][:, :, i4 * 128:(i4 + 1) * 128], in_=banks_b[bt][0:dd, 0:1024].rearrange("p (h t) -> p h t", h=8), func=AF.Copy))(bt, dst, dd),
                        r=[bk(bt)], w=[kdst])
                if i4 == 3:
                    t0 = jb * 512
                    P.add("sp", lambda e: e.dma_start(out=qT_d[s, :, :, t0:t0 + 512].rearrange("h d t -> d h t"), in_=qT_st[bs][:]),
                          r=["qTst%d" % bs], w=["qT_d%d" % s], chan="stq%d" % bs)
                    P.add("sp", lambda e: e.dma_start(out=kT_d[s, :, :, t0:t0 + 512].rearrange("h d t -> d h t"), in_=kT_st[bs][:]),
                          r=["kTst%d" % bs], w=["kT_d%d" % s], chan="stk%d" % bs)
                    P.add("sp", lambda e: e.dma_start(out=cqT_d[s, :, :, t0:t0 + 512].rearrange("h d t -> d h t"), in_=cqT_st[bs][:]),
                          r=["cqTst%d" % bs], w=["cqT_d%d" % s], chan="stcq%d" % bs)
                    P.add("sp", lambda e: e.dma_start(out=ckT_d[s, :, :, t0:t0 + 512].rearrange("h d t -> d h t"), in_=ckT_st[bs][:]),
                          r=["ckTst%d" % bs], w=["ckT_d%d" % s], chan="stck%d" % bs)
                    P.add("sp", lambda e: e.dma_start(out=V_d[s, t0:t0 + 512, :].rearrange("(k p) c -> p k c", p=128),
                                                      in_=V_st[bs][:].rearrange("p k h c -> p k (h c)")),
                          r=["Vst%d" % bs], w=["V_d%d" % s], chan="stv%d" % bs)
                    P.add("sp", lambda e: e.dma_start(out=cV_d[s, t0:t0 + 512, :].rearrange("(k p) c -> p k c", p=128),
                                                      in_=cV_st[bs][:].rearrange("p k h c -> p k (h c)")),
                          r=["cVst%d" % bs], w=["cV_d%d" % s], chan="stcv%d" % bs)

            for g in range(NSEQ * NT if stop not in ("setup",) else 0):
                prep_tile(g)
            P.barrier()
            sa.close()

        with ExitStack() as sbk:
            kT_all = sb(sbk, "kT_all", [96, 8, S], BF16)
            V_all = sb(sbk, "V_all", [128, NT, 8 * 65], BF16)
            ckT_all = sb(sbk, "ckT_all", [64, 8, S], BF16)
            cV_all = sb(sbk, "cV_all", [128, NT, 8 * 65], BF16)
            qT_b = [sb(sbk, "qT_b%d" % i, [96, 8, 512], BF16) for i in range(2)]
            cqT_b = [sb(sbk, "cqT_b%d" % i, [64, 8, 512], BF16) for i in range(2)]
            otn = [sb(sbk, "otn%d" % i, [64, 16, 512], BF16) for i in range(1)]
            Ef = sb(sbk, "Ef", [128, 8, 2, 128], F32)
            bfar = sb(sbk, "bfar", [128, 8], F32)
            NPT = 4
            Pt = [sb(sbk, "Pt%d" % i, [128, 512], BF16) for i in range(NPT)]
            PA = [sb(sbk, "PA%d" % i, [128, 384], BF16) for i in range(2)]
            PB = [sb(sbk, "PB%d" % i, [128, 256], F32) for i in range(2)]
            PBb = [sb(sbk, "PBb%d" % i, [128, 256], BF16) for i in range(2)]
            NRZ = 4
            ots = [sb(sbk, "ots%d" % i, [128, 512], F32) for i in range(NRZ)]
            rzb = [sb(sbk, "rzb%d" % i, [128, 2, 512], BF16) for i in range(NRZ)]
            sel64b = sb(sbk, "sel64b", [128, 64], BF16)
            P.tag = "wlB"
            P.add("sp", lambda e: e.dma_start(out=Ef[:], in_=bias34_d), w=["Ef"], chan="small4", waitall=True)
            P.add("sp", lambda e: e.dma_start(out=bfar[:], in_=bfar_d), w=["bfar"], chan="small4", waitall=True)
            P.add("act", lambda e: e.activation(out=Ef[:], in_=Ef[:], func=AF.Exp), r=["Ef"], w=["Ef"])
            P.add("pool", lambda e: e.memset(Ef[64:128, :, 1, 0:64], 0.0), r=["Ef"], w=["Ef"])
            for i in range(NRZ):
                P.add("pool", (lambda i: lambda e: e.memset(rzb[i][:], 0.0))(i), w=["rzb%d" % i])
            P.add("pool", lambda e: e.tensor_copy(out=sel64b[:], in_=sel64[:]), r=["sel64"], w=["sel64b"])
            P.tag = None
            cnt = {"pt": 0, "pa": 0, "rz": 0}

            def normalize_head(bo, width):
                ri = cnt["rz"] % NRZ
                cnt["rz"] += 1
                P.add("dve", lambda e: e.tensor_copy(out=ots[ri][0:65, 0:width], in_=banks_f[bo][0:65, 0:width]),
                      r=[bk(bo)], w=["ots%d" % ri])
                BK.release(bo)
                P.add("act", lambda e: e.activation(out=ots[ri][64:65, 0:width], in_=ots[ri][64:65, 0:width], func=AF.Ln),
                      r=["ots%d" % ri], w=["ots%d" % ri])
                P.add("act", lambda e: e.activation(out=ots[ri][64:65, 0:width], in_=ots[ri][64:65, 0:width], func=AF.Exp, scale=-1.0),
                      r=["ots%d" % ri], w=["ots%d" % ri])
                P.add("dve", lambda e: e.tensor_copy(out=rzb[ri][64:65, 0, 0:width], in_=ots[ri][64:65, 0:width]),
                      r=["ots%d" % ri], w=["rzb%d" % ri])
                P.add("dve", lambda e: e.tensor_tensor(out=rzb[ri][64:65, 1, 0:width], in0=ots[ri][64:65, 0:width],
                                                       in1=rzb[ri][64:65, 0, 0:width], op=ALU.subtract),
                      r=["ots%d" % ri, "rzb%d" % ri], w=["rzb%d" % ri])
                return ri

            def normalize_tail(ri, width, dst_ap, kdst):
                bb = BK.get(hold=True)
                for pl_ in range(2):
                    P.add("pe", (lambda pl_: lambda e: e.matmul(banks_f[bb][0:64, 0:width], lhsT=sel64b[:, 0:64], rhs=rzb[ri][:, pl_, 0:width],
                                                                start=(pl_ == 0), stop=(pl_ == 1)))(pl_),
                          r=["rzb%d" % ri, "sel64b"], w=[bk(bb)])
                P.add("dve", lambda e: e.tensor_tensor(out=dst_ap, in0=banks_f[bb][0:64, 0:width], in1=ots[ri][0:64, 0:width], op=ALU.mult),
                      r=[bk(bb), "ots%d" % ri], w=[kdst])
                BK.release(bb)

            def mla_gen(s, j, h, qs):
                kq = "qT_b%d" % qs
                nkt = 4 * j + 4
                bo = BK.get(hold=True)
                tiles = []
                for kt in range(nkt):
                    r_ = kt - 4 * j
                    c0 = 128 * r_ if r_ > 0 else 0
                    tiles.append((kt, c0, r_ >= 0))
                sbank = {}

                def emit_s(idx):
                    kt, c0, diag = tiles[idx]
                    b = BK.get(hold=True)
                    sbank[idx] = b
                    P.add("pe", lambda e: e.matmul(banks_f[b][:, 0:512 - c0], lhsT=kT_all[:, h, kt * 128:(kt + 1) * 128],
                                                   rhs=qT_b[qs][:, h, c0:512], start=True, stop=True),
                          r=["kT_all", kq], w=[bk(b)])

                def emit_rest(idx):
                    kt, c0, diag = tiles[idx]
                    b = sbank[idx]
                    pi = cnt["pt"] % NPT
                    cnt["pt"] += 1
                    kp = "Pt%d" % pi
                    w_ = 512 - c0
                    P.add("act", lambda e: e.activation(out=Pt[pi][:, 0:w_], in_=banks_f[b][:, 0:w_], func=AF.Exp), r=[bk(b)], w=[kp])
                    BK.release(b)
                    if diag:
                        P.add("pool", lambda e: e.memset(Pt[pi][64:128, 0:64], 0.0), r=[kp], w=[kp])
                    P.add("pe", lambda e: e.matmul(banks_f[bo][0:65, c0:512], lhsT=V_all[:, kt, h * 65:(h + 1) * 65], rhs=Pt[pi][:, 0:w_],
                                                   start=(idx == 0), stop=(idx == nkt - 1)),
                          r=[kp, "V_all"], w=[bk(bo)])

                LOOK = 2
                for idx in range(min(LOOK, nkt)):
                    emit_s(idx)
                yield
                for idx in range(nkt):
                    emit_rest(idx)
                    if idx + LOOK < nkt:
                        emit_s(idx + LOOK)
                    yield
                ri = normalize_head(bo, 512)
                pend_m.append((ri, 512, otn[0][:, h, :], "otn0h%d" % h))
                yield

            def ca_qtile(j, h, qs, bo, i):
                kq = "cqT_b%d" % qs
                gi = 4 * j + i
                tmin = max(0, 4 - gi)
                pai = cnt["pa"] % 2
                cnt["pa"] += 1
                ba = BK.get(hold=True) if tmin <= 2 else None
                bb = BK.get(hold=True)

                def s_mm(t):
                    ktile = gi - 4 + t
                    if t <= 2:
                        dst = banks_f[ba][:, t * 128:(t + 1) * 128]
                        kb_ = bk(ba)
                    else:
                        dst = banks_f[bb][:, (t - 3) * 128:(t - 2) * 128]
                        kb_ = bk(bb)
                    P.add("pe", lambda e: e.matmul(dst, lhsT=ckT_all[:, h, ktile * 128:(ktile + 1) * 128],
                                                   rhs=cqT_b[qs][:, h, i * 128:(i + 1) * 128], start=True, stop=True),
                          r=["ckT_all", kq], w=[kb_])

                def pv_mm(t):
                    ktile = gi - 4 + t
                    if t <= 2:
                        rhs = PA[pai][:, t * 128:(t + 1) * 128]
                        kr_ = "PA%d" % pai
                    else:
                        rhs = PBb[pai][:, (t - 3) * 128:(t - 2) * 128]
                        kr_ = "PBb%d" % pai
                    P.add("pe", lambda e: e.matmul(banks_f[bo][0:65, i * 128:(i + 1) * 128],
                                                   lhsT=cV_all[:, ktile, h * 65:(h + 1) * 65], rhs=rhs,
                                                   start=(t == tmin), stop=(t == 4)),
                          r=[kr_, "cV_all"], w=[bk(bo)])

                for t in range(tmin, 5):
                    s_mm(t)
                yield
                if ba is not None:
                    a0 = tmin * 128
                    P.add("act", lambda e: e.activation(out=PA[pai][:, a0:384], in_=banks_f[ba][:, a0:384], func=AF.Exp,
                                                        bias=bfar[:, h:h + 1], scale=1.0), r=[bk(ba), "bfar"], w=["PA%d" % pai])
                    BK.release(ba)
                    if tmin == 0:
                        P.add("pool", lambda e: e.memset(PA[pai][0:64, 64:128], 0.0), r=["PA%d" % pai], w=["PA%d" % pai])
                b0 = 0 if tmin <= 3 else 128
                P.add("act", lambda e: e.activation(out=PB[pai][:, b0:256], in_=banks_f[bb][:, b0:256], func=AF.Exp), r=[bk(bb)], w=["PB%d" % pai])
                BK.release(bb)
                P.add("pool", lambda e: e.tensor_tensor(out=PBb[pai][:, b0:256], in0=PB[pai][:, b0:256],
                                                        in1=Ef[:, h, :, :].rearrange("p t q -> p (t q)")[:, b0:256], op=ALU.mult),
                      r=["PB%d" % pai, "Ef"], w=["PBb%d" % pai])
                yield
                for t in range(tmin, 5):
                    pv_mm(t)
                yield

            def ca_gen(s, j, h, qs):
                bo = BK.get(hold=True)
                for i in range(4):
                    yield from ca_qtile(j, h, qs, bo, i)
                ri = normalize_head(bo, 512)
                pend_c.append((ri, 512, otn[0][:, 8 + h, :], "otn0h%d" % (8 + h)))
                yield

            pend_m, pend_c = [], []

            def stream(gens, pend, delay):
                for g in gens:
                    n = 0
                    old = list(pend)
                    del pend[:]
                    for _ in g:
                        n += 1
                        yield
                        if n == delay and old:
                            for t in old:
                                normalize_tail(*t)
                            old = []
                            yield
                    if old:
                        for t in old:
                            normalize_tail(*t)
                        yield
                for t in pend:
                    normalize_tail(*t)
                del pend[:]
                yield

            def interleave(ga, gb, ra, rb):
                alive_a = alive_b = True
                while alive_a or alive_b:
                    for _ in range(ra):
                        if alive_a:
                            try:
                                next(ga)
                            except StopIteration:
                                alive_a = False
                    for _ in range(rb):
                        if alive_b:
                            try:
                                next(gb)
                            except StopIteration:
                                alive_b = False

            def load_seq(s):
                for hh in range(2):
                    P.add("sp", lambda e: e.dma_start(out=kT_all[:, 4 * hh:4 * hh + 4, :], in_=kT_d[s, 4 * hh:4 * hh + 4].rearrange("h d t -> d h t")),
                          r=["kT_d%d" % s], w=["kT_all"], chan="ldk%d" % hh)
                    P.add("sp", lambda e: e.dma_start(out=ckT_all[:, 4 * hh:4 * hh + 4, :], in_=ckT_d[s, 4 * hh:4 * hh + 4].rearrange("h d t -> d h t")),
                          r=["ckT_d%d" % s], w=["ckT_all"], chan="ldck%d" % hh)
                for q4 in range(4):
                    P.add("sp", lambda e: e.dma_start(out=V_all[:, 4 * q4:4 * q4 + 4, :], in_=V_d[s, 512 * q4:512 * q4 + 512, :].rearrange("(k p) c -> p k c", p=128)),
                          r=["V_d%d" % s], w=["V_all"], chan="ldv%d" % q4)
                    P.add("sp", lambda e: e.dma_start(out=cV_all[:, 4 * q4:4 * q4 + 4, :], in_=cV_d[s, 512 * q4:512 * q4 + 512, :].rearrange("(k p) c -> p k c", p=128)),
                          r=["cV_d%d" % s], w=["cV_all"], chan="ldcv%d" % q4)

            def load_seq_part(fn, *a):
                fn(*a)

            def load_q(s, j, qs):
                t0 = 512 * j
                P.add("sp", lambda e: e.dma_start(out=qT_b[qs][:], in_=qT_d[s, :, :, t0:t0 + 512].rearrange("h d t -> d h t")),
                      r=["qT_d%d" % s], w=["qT_b%d" % qs], chan="ldq%d" % qs)
                P.add("sp", lambda e: e.dma_start(out=cqT_b[qs][:], in_=cqT_d[s, :, :, t0:t0 + 512].rearrange("h d t -> d h t")),
                      r=["cqT_d%d" % s], w=["cqT_b%d" % qs], chan="ldcq%d" % qs)

            def attn_block(s, j, qs):
                t0 = 512 * j
                nb_ = s * 4 + j + 1
                if nb_ < NSEQ * 4:
                    load_q(nb_ // 4, nb_ % 4, 1 - qs)
                gm = stream([mla_gen(s, j, h, qs) for h in range(8)], pend_m, 3)
                gc = stream([ca_gen(s, j, h, qs) for h in range(8)], pend_c, 4)
                ra, rb = {0: (1, 2), 1: (3, 4), 2: (1, 1), 3: (3, 2)}[j]
                interleave(gm, gc, ra, rb)
                P.add("sp", lambda e: e.dma_start(out=otn_d[s, :, t0:t0 + 512].rearrange("(h d) t -> d h t", d=64), in_=otn[0][:]),
                      r=["otn0h%d" % hh for hh in range(16)], w=["otn_d%d_%d" % (s, j)], chan="stotn")

            def load_seq_safe(s):
                def ldk(hh):
                    P.add("sp", lambda e: e.dma_start(out=kT_all[:, 4 * hh:4 * hh + 4, :], in_=kT_d[s, 4 * hh:4 * hh + 4].rearrange("h d t -> d h t")),
                          r=["kT_d%d" % s], w=["kT_all"], chan="ldk%d" % hh)
                    P.add("sp", lambda e: e.dma_start(out=ckT_all[:, 4 * hh:4 * hh + 4, :], in_=ckT_d[s, 4 * hh:4 * hh + 4].rearrange("h d t -> d h t")),
                          r=["ckT_d%d" % s], w=["ckT_all"], chan="ldck%d" % hh)

                def ldv(q4):
                    P.add("sp", lambda e: e.dma_start(out=V_all[:, 4 * q4:4 * q4 + 4, :], in_=V_d[s, 512 * q4:512 * q4 + 512, :].rearrange("(k p) c -> p k c", p=128)),
                          r=["V_d%d" % s], w=["V_all"], chan="ldv%d" % q4)
                    P.add("sp", lambda e: e.dma_start(out=cV_all[:, 4 * q4:4 * q4 + 4, :], in_=cV_d[s, 512 * q4:512 * q4 + 512, :].rearrange("(k p) c -> p k c", p=128)),
                          r=["cV_d%d" % s], w=["cV_all"], chan="ldcv%d" % q4)
                for hh in range(2):
                    ldk(hh)
                for q4 in range(4):
                    ldv(q4)

            blk = 0
            if stop not in ("setup", "A"):
                load_q(0, 0, 0)
            for s in range(NSEQ if stop not in ("setup", "A") else 0):
                load_seq_safe(s)
                for j in range(4):
                    attn_block(s, j, blk % 2)
                    blk += 1
            P.barrier()

        with ExitStack() as sc:
            TB = 256
            NTB = TB // 128
            wdn = sb(sc, "wdn", [128, NFF, D], BF16)
            wout = sb(sc, "wout", [128, 8, D], BF16)
            wup = sb(sc, "wup", [128, 8, 2 * D_FF], BF16)
            convp = sb(sc, "convp", [128, 4, 2 * NFF], F32)
            gab = sb(sc, "gab", [128, D], F32)
            gmb = sb(sc, "gmb", [128, D], F32)
            otb = sb(sc, "otb", [128, 8, TB], BF16)
            x1 = [sb(sc, "x1_%d" % i, [128, D], F32) for i in range(NTB)]
            xn2 = sb(sc, "xn2", [128, D], BF16)
            hT2 = sb(sc, "hT2", [128, 8, TB + 2], BF16)
            gT = sb(sc, "gT", [128, NFF, TB], BF16)
            NUB = 3
            cg = [sb(sc, "cg%d" % i, [128, TB], F32) for i in range(NUB)]
            cv = [sb(sc, "cv%d" % i, [128, TB], F32) for i in range(NUB)]
            stC = sb(sc, "stC", [128, 4], F32)
            otile = sb(sc, "otile", [128, D], F32)
            P.tag = "wlC"
            P.add("sp", lambda e: e.dma_start(out=convp[:], in_=convp_d), w=["convp"], chan="small5", waitall=True)

            def ldw(k):
                P.add("pool", lambda e: e.dma_start(out=wup[:, :, k * 512:(k + 1) * 512],
                                                    in_=wup_d[:, k * 512:(k + 1) * 512].rearrange("(j p) n -> p j n", p=128)),
                      w=["wup"], chan="wup", waitall=True)

            def ldwo(k):
                P.add("pool", lambda e: e.dma_start(out=wout[:, :, k * 512:(k + 1) * 512],
                                                    in_=wout_d[:, k * 512:(k + 1) * 512].rearrange("(j p) n -> p j n", p=128)),
                      w=["wout"], chan="wout", waitall=True)

            def ldwd(k, jg):
                P.add("pool", lambda e: e.dma_start(out=wdn[:, 11 * jg:11 * jg + 11, k * 512:(k + 1) * 512],
                                                    in_=wdn_d[1408 * jg:1408 * jg + 1408, k * 512:(k + 1) * 512].rearrange("(j p) n -> p j n", p=128)),
                      w=["wdn"], chan="wdn", waitall=True)
            for k in range(2):
                ldwo(k)
            for k in range(11):
                ldw(k)
            for k in range(2):
                for jg in range(2):
                    ldwd(k, jg)
            P.tag = None
            ucnt = {"u": 0}
            HALO_KEYS = ["halo%d" % ch for ch in range(2 * NFF)]

            def seq_start(s):
                P.add("sp", lambda e: e.dma_start(out=gab[:], in_=mod_d[s, 2 * D:3 * D].partition_broadcast(128)), r=["mod_d"], w=["gab"], chan="ldga")
                P.add("sp", lambda e: e.dma_start(out=gmb[:], in_=mod_d[s, 5 * D:6 * D].partition_broadcast(128)), r=["mod_d"], w=["gmb"], chan="ldgm")

            def outproj_tile(s, t0, it):
                tk = t0 + it * 128
                kx1 = "x1_%d" % it
                kxs = [kx1 + "h0", kx1 + "h512"]
                P.add("sp", lambda e: e.dma_start(out=x1[it][:], in_=x_d[s, tk:tk + 128, :]), w=kxs, chan="ldx%d" % it)
                bo0, bo1 = BK.get(hold=True), BK.get(hold=True)

                def half(bb, n0):
                    for c in range(8):
                        P.add("pe", (lambda c: lambda e: e.matmul(banks_f[bb][:, 0:512], lhsT=otb[:, c, it * 128:(it + 1) * 128],
                                                                  rhs=wout[:, c, n0:n0 + 512], start=(c == 0), stop=(c == 7)))(c),
                              r=["otb", "wout"], w=[bk(bb)])
                    P.add("dve", lambda e: e.tensor_tensor(out=otile[:, n0:n0 + 512], in0=banks_f[bb][:, 0:512],
                                                           in1=gab[:, n0:n0 + 512], op=ALU.mult),
                          r=[bk(bb), "gab"], w=["otileh%d" % n0])
                    P.add("pool", lambda e: e.tensor_tensor(out=x1[it][:, n0:n0 + 512], in0=x1[it][:, n0:n0 + 512],
                                                            in1=otile[:, n0:n0 + 512], op=ALU.add),
                          r=[kx1 + "h%d" % n0, "otileh%d" % n0], w=[kx1 + "h%d" % n0])
                    BK.release(bb)
                half(bo0, 0)
                half(bo1, 512)
                P.add("act", lambda e: e.activation(out=xn2[:], in_=x1[it][:], func=AF.Square, accum_out=stC[:, 0:1]),
                      r=kxs, w=["stC", "xn2"])
                P.add("act", lambda e: e.activation(out=stC[:, 0:1], in_=stC[:, 0:1], func=AF.Sqrt, bias=float(D * EPS), scale=1.0),
                      r=["stC"], w=["stC"])
                P.add("dve", lambda e: e.reciprocal(out=stC[:, 0:1], in_=stC[:, 0:1]), r=["stC"], w=["stC"])
                P.add("act", lambda e: e.activation(out=xn2[:], in_=x1[it][:], func=AF.Copy, scale=stC[:, 0:1]),
                      r=kxs + ["stC"], w=["xn2"])
                bt = BK.get()
                for c in range(8):
                    P.add("pe", (lambda c: lambda e: e.transpose(out=banks_b[bt][:, c * 128:(c + 1) * 128],
                                                                 in_=xn2[:, c * 128:(c + 1) * 128], identity=ident[:]))(c),
                          r=["xn2", "ident"], w=[bk(bt)])
                tp3 = banks_b[bt][:, 0:1024].rearrange("p (c t) -> p c t", c=8)
                kh = "hT2_%d" % it
                P.add("dve", lambda e: e.tensor_tensor(out=hT2[:, :, 2 + it * 128:2 + (it + 1) * 128], in0=tp3,
                                                       in1=AB[:, s, 2, :].unsqueeze(2).to_broadcast([128, 8, 128]), op=ALU.mult),
                      r=[bk(bt), "AB"], w=[kh])
                P.add("pool", lambda e: e.tensor_tensor(out=hT2[:, :, 2 + it * 128:2 + (it + 1) * 128], in0=hT2[:, :, 2 + it * 128:2 + (it + 1) * 128],
                                                        in1=AB[:, s, 3, :].unsqueeze(2).to_broadcast([128, 8, 128]), op=ALU.add),
                      r=[kh, "AB"], w=[kh])

            def ffn_up(f, khs):
                ui = ucnt["u"] % NUB
                ucnt["u"] += 1
                bg, bv = BK.get(hold=True), BK.get(hold=True)

                def up_mm(bb, ch):
                    for k in range(8):
                        P.add("pe", (lambda k: lambda e: e.matmul(banks_f[bb][:, 0:TB + 2],
                                                                  lhsT=wup[:, k, ch * 128:(ch + 1) * 128], rhs=hT2[:, k, :],
                                                                  start=(k == 0), stop=(k == 7)))(k),
                              r=khs + ["hT2_halo", "wup"], w=[bk(bb)])
                up_mm(bg, f)
                up_mm(bv, NFF + f)
                kcg, kcv = "cg%d" % ui, "cv%d" % ui

                def tap2(bb, ch, cb, kcb):
                    P.add("act", lambda e: e.activation(out=cb[ui][:], in_=banks_f[bb][:, 2:TB + 2], func=AF.Identity,
                                                        scale=convp[:, 2, ch:ch + 1], bias=convp[:, 3, ch:ch + 1]),
                          r=[bk(bb), "convp"], w=[kcb])

                def tap(bb, ch, cb, kcb, j):
                    P.add("dve", lambda e: e.scalar_tensor_tensor(out=cb[ui][:], in0=banks_f[bb][:, j:TB + j], scalar=convp[:, j, ch:ch + 1],
                                                                  in1=cb[ui][:], op0=ALU.mult, op1=ALU.add),
                          r=[bk(bb), kcb, "convp"], w=[kcb])
                tap2(bg, f, cg, kcg)
                tap2(bv, NFF + f, cv, kcv)
                tap(bg, f, cg, kcg, 1)
                tap(bv, NFF + f, cv, kcv, 1)
                tap(bg, f, cg, kcg, 0)
                tap(bv, NFF + f, cv, kcv, 0)
                BK.release(bg)
                BK.release(bv)
                return (f, ui)

            def ffn_gate(f, ui):
                kcg, kcv = "cg%d" % ui, "cv%d" % ui
                P.add("act", lambda e: e.activation(out=cg[ui][:], in_=cg[ui][:], func=AF.Silu), r=[kcg], w=[kcg])
                P.add("pool", lambda e: e.tensor_tensor(out=gT[:, f, :], in0=cg[ui][:], in1=cv[ui][:], op=ALU.mult),
                      r=[kcg, kcv], w=["gT%d" % f])

            def down_tile(s, t0, it, kgs):
                tk = t0 + it * 128
                bd0, bd1 = BK.get(hold=True), BK.get(hold=True)

                def half(bb, n0):
                    for f in range(NFF):
                        P.add("pe", (lambda f: lambda e: e.matmul(banks_f[bb][:, 0:512], lhsT=gT[:, f, it * 128:(it + 1) * 128],
                                                                  rhs=wdn[:, f, n0:n0 + 512], start=(f == 0), stop=(f == NFF - 1)))(f),
                              r=kgs + ["wdn"], w=[bk(bb)])
                    P.add("dve", lambda e: e.tensor_tensor(out=otile[:, n0:n0 + 512], in0=banks_f[bb][:, 0:512],
                                                           in1=gmb[:, n0:n0 + 512], op=ALU.mult),
                          r=[bk(bb), "gmb"], w=["otileh%d" % n0])
                    P.add("pool", lambda e: e.tensor_tensor(out=otile[:, n0:n0 + 512], in0=otile[:, n0:n0 + 512],
                                                            in1=x1[it][:, n0:n0 + 512], op=ALU.add),
                          r=["otileh%d" % n0, "x1_%dh%d" % (it, n0)], w=["otileh%d" % n0])
                    BK.release(bb)
                half(bd0, 0)
                half(bd1, 512)
                P.add("sp", lambda e: e.dma_start(out=out_d[s, tk:tk + 128, :], in_=otile[:]),
                      r=["otileh0", "otileh512"], w=["out_d"], chan="stout")

            def ffn_block(s, tb):
                t0 = tb * TB
                jblk = t0 // 512
                P.add("sp", lambda e: e.dma_start(out=otb[:], in_=otn_d[s, :, t0:t0 + TB].rearrange("(c p) t -> p c t", p=128)),
                      r=["otn_d%d_%d" % (s, jblk)], w=["otb"], chan="ldot")
                if tb == 0:
                    P.add("pool", lambda e: e.memset(hT2[:, :, 0:2], 0.0), r=["hT2_%d" % (NTB - 1)], w=["hT2_halo"])
                else:
                    P.add("pool", lambda e: e.tensor_copy(out=hT2[:, :, 0:2], in_=hT2[:, :, TB:TB + 2]), r=["hT2_%d" % (NTB - 1)], w=["hT2_halo"])
                for it in range(NTB):
                    outproj_tile(s, t0, it)
                khs = ["hT2_%d" % it for it in range(NTB)]
                prev = None
                for f in range(NFF):
                    cur = ffn_up(f, khs)
                    if prev is not None:
                        ffn_gate(*prev)
                    prev = cur
                ffn_gate(*prev)
                kgs = ["gT%d" % f for f in range(NFF)]
                for it in range(NTB):
                    down_tile(s, t0, it, kgs)

            for s in range(NSEQ if stop not in ("setup", "A", "B") else 0):
                seq_start(s)
                for tb in range(S // TB):
                    ffn_block(s, tb)
            P.emit()
    return nc, P.stats


_CACHE = {}


def _feat_major(v):
    v = np.asarray(v, np.float32)
    return np.ascontiguousarray(v.reshape(-1, 128).T)


def _prepare(x, c, positions, w_ada, b_ada, g_attn_norm, w_in, g_q_latent, g_kv_latent, w_q_up, w_kv_up,
             g_mla_q, g_mla_k, g_ca_q, g_ca_k, rel_bias, w_out, g_mlp_norm, w_up, conv_w, conv_b, w_down):
    f = lambda a: np.ascontiguousarray(np.asarray(a))
    x = f(x); c = f(c); positions = f(positions)
    if "nc" not in _CACHE:
        _CACHE["nc"], _CACHE["stats"] = build_program(stop=_CACHE.get("stop"))
    nc = _CACHE["nc"]
    gfeat = np.concatenate([_feat_major(g_attn_norm[0]), _feat_major(g_mlp_norm[0]), _feat_major(g_q_latent[0]),
                            _feat_major(g_kv_latent[0])], axis=1).astype(np.float32)
    convp = np.zeros((128, 4, 2 * NFF), np.float32)
    for t in range(3):
        convp[:, t, :] = _feat_major(conv_w[0, t])
    convp[:, 3, :] = _feat_major(conv_b[0])
    grow = np.concatenate([np.asarray(g_mla_q[0]), np.asarray(g_mla_k[0]), np.asarray(g_ca_q[0]), np.asarray(g_ca_k[0])]).astype(np.float32)
    grow = np.ascontiguousarray(np.broadcast_to(grow[None, :], (128, 320)))
    rb = np.asarray(rel_bias[0], np.float32)
    kj = np.arange(128)[:, None]
    qi = np.arange(128)[None, :]
    bias34 = np.zeros((128, 8, 2, 128), np.float32)
    for ti, t in enumerate((3, 4)):
        idx = np.clip(128 * (4 - t) + qi - kj, -128, 128) + 128
        bias34[:, :, ti, :] = np.transpose(rb[:, idx], (1, 0, 2))
    bfar = np.ascontiguousarray(np.broadcast_to(rb[:, 256][None, :], (128, 8))).astype(np.float32)
    half = 16
    invf = np.power(np.float32(10000.0), -np.arange(half, dtype=np.float32) / np.float32(half)).astype(np.float32)
    invf = np.ascontiguousarray(np.broadcast_to(invf[None, :], (128, 16)))
    shared = {
        "w_ada": f(w_ada[0]), "w_in": f(w_in[0]), "w_q_up": f(w_q_up[0]), "w_kv_up": f(w_kv_up[0]), "w_out": f(w_out[0]),
        "w_up": f(w_up[0]), "w_down": f(w_down[0]), "gfeat": gfeat, "convp": convp, "grow": grow, "bias34": bias34,
        "bfar": bfar, "invf": invf,
    }
    in_maps = []
    for i in range(NCORES):
        b0 = NSEQ * i
        m = dict(shared)
        m["x"] = f(x[b0:b0 + NSEQ])
        m["cT"] = np.ascontiguousarray(c[b0:b0 + NSEQ].reshape(NSEQ, 8, 128).transpose(2, 1, 0)).astype(np.float32)
        pl = positions[b0:b0 + NSEQ].reshape(NSEQ, NT, 128).transpose(2, 0, 1).reshape(128, NSEQ * NT)
        m["posl"] = np.ascontiguousarray(pl).astype(np.int32)
        m["b_ada2"] = np.ascontiguousarray(np.broadcast_to(np.asarray(b_ada[0], np.float32)[None, :], (NSEQ, 6 * D)))
        in_maps.append(m)
    return nc, in_maps


def kernel(**inputs):
    nc, in_maps = _prepare(**inputs)
    res = run_bass_kernel_spmd(nc, in_maps, core_ids=list(range(NCORES)))
    out = np.concatenate([np.asarray(r["out"]) for r in res.results], axis=0)
    return out.astype(np.float32)
```

```python
import math
from contextlib import ExitStack

import numpy as np
import concourse.bass as bass
import concourse.mybir as mybir
from concourse.bass_utils import run_bass_kernel_spmd

F32 = mybir.dt.float32
BF16 = mybir.dt.bfloat16
I32 = mybir.dt.int32
AF = mybir.ActivationFunctionType
ALU = mybir.AluOpType
AX = mybir.AxisListType

NCORES = 8
NSEQ = 2
S = 2048
D = 1024
NT = S // 128
D_IN = 1952
D_FF = 2816
NFF = D_FF // 128
EPS = 1e-6
TWO_PI = 2.0 * math.pi


class Op:
    __slots__ = ("eng", "fn", "r", "w", "chan", "deps", "signal", "sigval", "waitall")

    def __init__(self, eng, fn, r, w, chan, waitall):
        self.eng, self.fn, self.r, self.w, self.chan = eng, fn, tuple(r), tuple(w), chan
        self.deps = set()
        self.signal = False
        self.sigval = 0
        self.waitall = waitall


class Prog:
    def __init__(self, nc, es):
        self.nc = nc
        self.es = es
        self.ops = []
        self.last_w = {}
        self.readers = {}
        self.waitall_chans = set()

    tag = None

    def add(self, eng, fn, r=(), w=(), chan=None, waitall=False):
        if self.tag is not None and self.tag in SKIP:
            return None
        op = Op(eng, fn, r, w, chan, waitall)
        idx = len(self.ops)
        deps = set()
        for k in op.r:
            lw = self.last_w.get(k)
            if lw is not None:
                deps.add(lw)
            if isinstance(k, str) and k.startswith("bank"):
                for rd in self.readers.get(k, ()):
                    if self.ops[rd].eng != eng:
                        deps.add(rd)
        for k in op.w:
            lw = self.last_w.get(k)
            if lw is not None:
                deps.add(lw)
            for rd in self.readers.get(k, ()):
                deps.add(rd)
        deps.discard(idx)
        if chan is not None and waitall:
            deps = {d for d in deps if self.ops[d].chan != chan}
        op.deps = deps
        for k in op.w:
            self.last_w[k] = idx
            self.readers[k] = []
        for k in op.r:
            if k in op.w:
                continue
            self.readers.setdefault(k, []).append(idx)
        if chan is not None and waitall:
            self.waitall_chans.add(chan)
        self.ops.append(op)
        return idx

    def barrier(self):
        n = len(self.ops)
        last = {}
        for i, op in enumerate(self.ops):
            key = op.chan if op.chan is not None else ("E", op.eng)
            last[key] = i
        alld = set(last.values())
        for eng in ("pe", "act", "dve", "pool", "sp"):
            op = Op(eng, None, (), (), None, False)
            op.deps = set(alld)
            self.ops.append(op)

    def emit(self):
        nc = self.nc
        ops = self.ops
        engobj = {"pe": nc.tensor, "act": nc.scalar, "dve": nc.vector, "pool": nc.gpsimd, "sp": nc.sync}
        for op in ops:
            for d in op.deps:
                x = ops[d]
                if x.chan is None and x.eng == "pe" and op.eng == "pe" and op.chan is None:
                    continue
                x.signal = True
        sems = {}

        def sem(name):
            if name not in sems:
                sems[name] = self.es.enter_context(nc.semaphore("s_" + str(name)))
            return sems[name]

        cnt = {}
        chan_total = {}
        for op in ops:
            if op.fn is None:
                continue
            if op.chan is not None:
                c = ("C", op.chan)
                cnt[c] = cnt.get(c, 0) + 16
                op.sigval = cnt[c]
                op.signal = True
                chan_total[op.chan] = cnt[c]
            elif op.signal:
                c = ("E", op.eng)
                cnt[c] = cnt.get(c, 0) + 1
                op.sigval = cnt[c]
        known = {e: {} for e in engobj}
        vcs = [None] * len(ops)
        ecount = {}
        nwaits = 0

        def merge(dst, src):
            for k_, v_ in src.items():
                if dst.get(k_, 0) < v_:
                    dst[k_] = v_

        for i, op in enumerate(ops):
            need = []
            for d in op.deps:
                x = ops[d]
                if x.fn is None:
                    if vcs[d] is not None:
                        need.append((None, 0, d))
                    continue
                if x.chan is not None:
                    key = ("C", x.chan)
                    val = chan_total[x.chan] if x.chan in self.waitall_chans else x.sigval
                else:
                    if x.eng == "pe" and op.eng == "pe" and op.chan is None:
                        continue
                    key = ("E", x.eng)
                    val = x.sigval
                need.append((key, val, d))
            need.sort(key=lambda t: -t[2])
            e = engobj[op.eng]
            kn = known[op.eng]
            for key, val, d in need:
                if key is None:
                    continue
                if kn.get(key, 0) >= val:
                    continue
                e.wait_ge(sem(key), val)
                kn[key] = val
                nwaits += 1
                if vcs[d] is not None and not (ops[d].chan in self.waitall_chans):
                    merge(kn, vcs[d])
            vc = dict(kn)
            if op.fn is not None:
                inst = op.fn(e)
                if op.chan is not None:
                    inst.then_inc(sem(("C", op.chan)), 16)
                    if op.chan not in self.waitall_chans:
                        vc[("C", op.chan)] = max(vc.get(("C", op.chan), 0), op.sigval)
                else:
                    if op.signal:
                        inst.then_inc(sem(("E", op.eng)), 1)
                        ecount[op.eng] = op.sigval
                    if op.eng != "pe" or True:
                        vc[("E", op.eng)] = max(vc.get(("E", op.eng), 0), ecount.get(op.eng, 0))
            vcs[i] = vc
        for chan, tot in chan_total.items():
            if known["sp"].get(("C", chan), 0) < tot:
                nc.sync.wait_ge(sem(("C", chan)), tot)
        self.stats = dict(n_ops=len(ops), n_waits=nwaits, n_sems=len(sems))


class Banks:
    def __init__(self, banks):
        self.banks = banks
        self.ptr = 0
        self.held = set()

    def get(self, hold=False):
        for _ in range(16):
            b = self.ptr
            self.ptr = (self.ptr + 1) % len(self.banks)
            if b not in self.held:
                if hold:
                    self.held.add(b)
                return b
        raise RuntimeError("no free PSUM bank")

    def release(self, b):
        self.held.discard(b)


import os
SKIP = set(os.environ.get('KSKIP', '').split(','))


def build_program(debug=None, stop=None):
    nc = bass.Bass("TRN2", target_bir_lowering=False)

    def din(name, shape, dt=F32):
        return nc.dram_tensor(name, list(shape), dt, kind="ExternalInput").ap()

    def dscr(name, shape, dt):
        return nc.dram_tensor(name, list(shape), dt, kind="Internal").ap()

    x_d = din("x", [NSEQ, S, D])
    cT_d = din("cT", [128, 8, NSEQ])
    pos_d = din("posl", [128, NSEQ * NT], I32)
    wada_d = din("w_ada", [D, 6 * D])
    bada_d = din("b_ada2", [NSEQ, 6 * D])
    win_d = din("w_in", [D, D_IN])
    wq_d = din("w_q_up", [256, 768])
    wkv_d = din("w_kv_up", [128, 1024])
    wout_d = din("w_out", [D, D])
    wup_d = din("w_up", [D, 2 * D_FF])
    wdn_d = din("w_down", [D_FF, D])
    gfeat_d = din("gfeat", [128, 19])
    convp_d = din("convp", [128, 4, 2 * NFF])
    grow_d = din("grow", [128, 320])
    bias34_d = din("bias34", [128, 8, 2, 128])
    bfar_d = din("bfar", [128, 8])
    invf_d = din("invf", [128, 16])
    out_d = nc.dram_tensor("out", [NSEQ, S, D], F32, kind="ExternalOutput").ap()

    mod_d = dscr("mod_scr", [NSEQ, 6 * D], F32)
    qT_d = dscr("qT_scr", [NSEQ, 8, 96, S], BF16)
    kT_d = dscr("kT_scr", [NSEQ, 8, 96, S], BF16)
    cqT_d = dscr("cqT_scr", [NSEQ, 8, 64, S], BF16)
    ckT_d = dscr("ckT_scr", [NSEQ, 8, 64, S], BF16)
    V_d = dscr("V_scr", [NSEQ, S, 8 * 65], BF16)
    cV_d = dscr("cV_scr", [NSEQ, S, 8 * 65], BF16)
    otn_d = dscr("otn_scr", [NSEQ, D, S], BF16)

    with ExitStack() as es:
        P = Prog(nc, es)

        def sb(stack, name, shape, dt):
            return stack.enter_context(nc.sbuf_tensor("sb_" + name, list(shape), dt))

        banks_f = [es.enter_context(nc.psum_tensor("bank%d" % i, [128, 512], F32)) for i in range(8)]
        banks_b = [b[:].bitcast(BF16) for b in banks_f]
        BK = Banks(banks_f)

        def bk(b):
            return "bank%d" % b

        ident = sb(es, "ident", [128, 128], BF16)
        identf = sb(es, "identf", [128, 128], F32)
        sel64 = sb(es, "sel64", [128, 64], F32)
        gfeat = sb(es, "gfeat", [128, 19], F32)
        grow = sb(es, "grow", [128, 320], F32)
        AB = sb(es, "AB", [128, NSEQ, 4, 8], F32)
        sa = es.enter_context(ExitStack())
        cs_all = sb(sa, "cs_all", [128, NSEQ * NT, 32], F32)
        junk = sb(sa, "junk", [128, 1024], BF16)

        P.add("pool", lambda e: e.memset(identf[:], 1.0), w=["identf"])
        P.add("pool", lambda e: e.affine_select(out=identf[:], in_=identf[:], pattern=[[-1, 128]],
                                                compare_op=ALU.is_equal, fill=0.0, base=0, channel_multiplier=1),
              r=["identf"], w=["identf"])
        P.add("pool", lambda e: e.tensor_copy(out=ident[:], in_=identf[:]), r=["identf"], w=["ident"])
        P.add("pool", lambda e: e.memset(sel64[:], 0.0), w=["sel64"])
        P.add("pool", lambda e: e.memset(sel64[64:65, :], 1.0), r=["sel64"], w=["sel64"])
        P.add("sp", lambda e: e.dma_start(out=gfeat[:], in_=gfeat_d), w=["gfeat"], chan="small", waitall=True)
        P.add("sp", lambda e: e.dma_start(out=grow[:], in_=grow_d), w=["grow"], chan="small", waitall=True)
        s0 = es.enter_context(ExitStack())
        cT = sb(s0, "cT", [128, 8, NSEQ], F32)
        bada = sb(s0, "bada", [NSEQ, 6 * D], F32)
        posi = sb(s0, "posi", [128, NSEQ * NT], I32)
        invf = sb(s0, "invf", [128, 16], F32)
        P.add("sp", lambda e: e.dma_start(out=cT[:], in_=cT_d), w=["cT"], chan="small", waitall=True)
        P.add("sp", lambda e: e.dma_start(out=bada[:], in_=bada_d), w=["bada"], chan="small", waitall=True)
        P.add("sp", lambda e: e.dma_start(out=posi[:], in_=pos_d), w=["posi"], chan="small", waitall=True)
        P.add("sp", lambda e: e.dma_start(out=invf[:], in_=invf_d), w=["invf"], chan="small", waitall=True)
        P.add("dve", lambda e: e.tensor_scalar(out=grow[:, 96:192], in0=grow[:, 96:192], scalar1=math.sqrt(96.0),
                                               scalar2=None, op0=ALU.mult), r=["grow"], w=["grow"])
        P.add("dve", lambda e: e.tensor_scalar(out=grow[:, 256:320], in0=grow[:, 256:320], scalar1=8.0,
                                               scalar2=None, op0=ALU.mult), r=["grow"], w=["grow"])

        if True:
            scb = sb(s0, "scb", [128, 8, NSEQ], BF16)
            wab = [sb(s0, "wab%d" % i, [128, 8, 512], BF16) for i in range(2)]
            modsb = sb(s0, "modsb", [NSEQ, 6 * D], F32)
            P.add("act", lambda e: e.activation(out=scb[:], in_=cT[:], func=AF.Silu), r=["cT"], w=["scb"])
            for nb in range(12):
                sl = nb % 2
                P.add("pool", (lambda nb, sl: lambda e: e.dma_start(
                    out=wab[sl][:], in_=wada_d[:, nb * 512:(nb + 1) * 512].rearrange("(j p) n -> p j n", p=128)))(nb, sl),
                    w=["wab%d" % sl], chan="wab%d" % sl)
                b = BK.get()
                for k in range(8):
                    P.add("pe", (lambda b, sl, k: lambda e: e.matmul(
                        banks_f[b][0:NSEQ, 0:512], lhsT=scb[:, k, :], rhs=wab[sl][:, k, :], start=(k == 0), stop=(k == 7)))(b, sl, k),
                        r=["scb", "wab%d" % sl], w=[bk(b)])
                P.add("dve", (lambda b, nb: lambda e: e.tensor_tensor(
                    out=modsb[:, nb * 512:(nb + 1) * 512], in0=banks_f[b][0:NSEQ, 0:512],
                    in1=bada[:, nb * 512:(nb + 1) * 512], op=ALU.add))(b, nb),
                    r=[bk(b), "bada"], w=["modsb"])
            P.add("sp", lambda e: e.dma_start(out=mod_d, in_=modsb[:]), r=["modsb"], w=["mod_d"], chan="modst")
            P.tag = "modT"
            modT = sb(s0, "modT", [128, NSEQ, 4, 8], F32)
            bT = BK.get()

            def modtr(qi, c, j):
                col = (qi * 8 + j) * NSEQ
                P.add("pe", lambda e: e.matmul(banks_f[bT][:, col:col + NSEQ], lhsT=modsb[0:NSEQ, c * D + j * 128:c * D + (j + 1) * 128],
                                               rhs=identf[0:NSEQ, 0:NSEQ], start=True, stop=True),
                      r=["modsb", "identf"], w=[bk(bT)])
            for qi, c in enumerate((0, 1, 3, 4)):
                for j in range(8):
                    modtr(qi, c, j)
            P.add("dve", lambda e: e.tensor_copy(out=modT[:].rearrange("p s k j -> p k j s"),
                                                 in_=banks_f[bT][:, 0:32 * NSEQ].rearrange("p (k j s) -> p k j s", k=4, j=8)),
                  r=[bk(bT)], w=["modT"])
            for s in range(NSEQ):
                for (dst, srcq, gcol) in ((0, 1, 0), (2, 3, 8)):
                    P.add("dve", (lambda s, dst, srcq, gcol: lambda e: e.scalar_tensor_tensor(
                        out=AB[:, s, dst, :], in0=modT[:, s, srcq, :], scalar=1.0, in1=gfeat[:, gcol:gcol + 8],
                        op0=ALU.add, op1=ALU.mult))(s, dst, srcq, gcol),
                        r=["modT", "gfeat"], w=["AB"])
                    P.add("dve", (lambda s, dst: lambda e: e.tensor_scalar(
                        out=AB[:, s, dst, :], in0=AB[:, s, dst, :], scalar1=32.0, scalar2=None, op0=ALU.mult))(s, dst),
                        r=["AB"], w=["AB"])
                    P.add("dve", (lambda s, dst, srcq: lambda e: e.tensor_copy(
                        out=AB[:, s, dst + 1, :], in_=modT[:, s, srcq - 1, :]))(s, dst, srcq),
                        r=["modT"], w=["AB"])
            P.tag = "rot"
            posf = sb(s0, "posf", [128, NSEQ * NT], F32)
            ang = sb(s0, "ang", [128, NSEQ * NT, 32], F32)
            kf = sb(s0, "kf", [128, NSEQ * NT, 32], F32)
            ki = sb(s0, "ki", [128, NSEQ * NT, 32], I32)
            mk = sb(s0, "mk", [128, NSEQ * NT, 32], F32)
            P.add("dve", lambda e: e.tensor_copy(out=posf[:], in_=posi[:]), r=["posi"], w=["posf"])
            NTT = NSEQ * NT
            P.add("dve", lambda e: e.tensor_tensor(out=ang[:, :, 16:32], in0=posf[:].unsqueeze(2).to_broadcast([128, NTT, 16]),
                                                   in1=invf[:].unsqueeze(1).to_broadcast([128, NTT, 16]), op=ALU.mult),
                  r=["posf", "invf"], w=["ang"])
            P.add("dve", lambda e: e.tensor_scalar(out=ang[:, :, 0:16], in0=ang[:, :, 16:32], scalar1=math.pi / 2.0,
                                                   scalar2=None, op0=ALU.add), r=["ang"], w=["ang"])
            P.add("dve", lambda e: e.tensor_scalar(out=kf[:], in0=ang[:], scalar1=1.0 / TWO_PI, scalar2=None, op0=ALU.mult),
                  r=["ang"], w=["kf"])
            P.add("dve", lambda e: e.tensor_copy(out=ki[:], in_=kf[:]), r=["kf"], w=["ki"])
            P.add("dve", lambda e: e.tensor_copy(out=kf[:], in_=ki[:]), r=["ki"], w=["kf"])
            P.add("dve", lambda e: e.scalar_tensor_tensor(out=ang[:], in0=kf[:], scalar=-TWO_PI, in1=ang[:],
                                                          op0=ALU.mult, op1=ALU.add), r=["kf", "ang"], w=["ang"])
            P.add("dve", lambda e: e.tensor_scalar(out=mk[:], in0=ang[:], scalar1=math.pi, scalar2=-TWO_PI,
                                                   op0=ALU.is_gt, op1=ALU.mult), r=["ang"], w=["mk"])
            P.add("dve", lambda e: e.tensor_tensor(out=ang[:], in0=ang[:], in1=mk[:], op=ALU.add), r=["ang", "mk"], w=["ang"])
            P.add("dve", lambda e: e.tensor_scalar(out=mk[:], in0=ang[:], scalar1=-math.pi, scalar2=TWO_PI,
                                                   op0=ALU.is_lt, op1=ALU.mult), r=["ang"], w=["mk"])
            P.add("dve", lambda e: e.tensor_tensor(out=ang[:], in0=ang[:], in1=mk[:], op=ALU.add), r=["ang", "mk"], w=["ang"])
            P.add("dve", lambda e: e.tensor_scalar(out=ang[:], in0=ang[:], scalar1=math.pi, scalar2=-math.pi,
                                                   op0=ALU.min, op1=ALU.max), r=["ang"], w=["ang"])
            P.add("act", lambda e: e.activation(out=cs_all[:], in_=ang[:], func=AF.Sin), r=["ang"], w=["cs_all"])
            P.tag = None
            P.barrier()
            s0.close()

        if True:
            P.tag = "wlA"
            win = sb(sa, "win", [128, 8, D_IN], BF16)
            wq = sb(sa, "wq", [128, 2, 768], BF16)
            wkv = sb(sa, "wkv", [128, 1024], BF16)
            wstage = sb(sa, "wstage", [128, 2, 768], F32)
            wstage2 = sb(sa, "wstage2", [128, 1024], F32)
            WIN_KEYS = []
            for pi_, (c0, c1) in enumerate(((0, 512), (512, 1024), (1024, 1536), (1536, 1952))):
                P.add("pool", (lambda c0, c1: lambda e: e.dma_start(out=win[:, :, c0:c1], in_=win_d[:, c0:c1].rearrange("(j p) n -> p j n", p=128)))(c0, c1),
                      r=(["win_p%d" % (pi_ - 2)] if pi_ >= 2 else []), w=["win_p%d" % pi_], chan="win_c%d" % (pi_ % 2))
                WIN_KEYS.append("win_p%d" % pi_)
            P.tag = "wlA2"
            P.add("sp", lambda e: e.dma_start(out=wstage[:], in_=wq_d.rearrange("(j p) n -> p j n", p=128)),
                  w=["wstage"], chan="small3", waitall=True)
            P.add("sp", lambda e: e.dma_start(out=wstage2[:], in_=wkv_d), w=["wstage2"], chan="small3", waitall=True)
            for j in range(2):
                P.add("dve", (lambda j: lambda e: e.tensor_scalar(
                    out=wq[:, j, :], in0=wstage[:, j, :], scalar1=gfeat[:, 16 + j:17 + j], scalar2=16.0,
                    op0=ALU.mult, op1=ALU.mult))(j), r=["wstage", "gfeat"], w=["wq"])
            P.add("dve", lambda e: e.tensor_scalar(out=wkv[:], in0=wstage2[:], scalar1=gfeat[:, 18:19],
                                                   scalar2=math.sqrt(128.0), op0=ALU.mult, op1=ALU.mult),
                  r=["wstage2", "gfeat"], w=["wkv"])

            P.tag = "wlA3"
            xt = [sb(sa, "xt%d" % i, [128, D], F32) for i in range(2)]
            xn = [sb(sa, "xn%d" % i, [128, D], BF16) for i in range(2)]
            hT = [sb(sa, "hT%d" % i, [128, 8, 128], BF16) for i in range(2)]
            st = sb(sa, "stA", [128, 2, 40], F32)
            lat = [sb(sa, "lat%d" % i, [128, 384], BF16) for i in range(2)]
            latT = [sb(sa, "latT%d" % i, [128, 3, 128], BF16) for i in range(2)]
            kr = [sb(sa, "kr%d" % i, [128, 4, 32], F32) for i in range(2)]
            qn = [sb(sa, "qn%d" % i, [128, 8, 96], F32) for i in range(2)]
            qsq = [sb(sa, "qsq%d" % i, [128, 8, 96], F32) for i in range(2)]
            qb = [sb(sa, "qb%d" % i, [128, 8, 96], BF16) for i in range(2)]
            kn = [sb(sa, "kn%d" % i, [128, 8, 64], F32) for i in range(2)]
            ksq = [sb(sa, "ksq%d" % i, [128, 8, 64], F32) for i in range(2)]
            kb = [sb(sa, "kb%d" % i, [128, 8, 96], BF16) for i in range(2)]
            rt = [sb(sa, "rt%d" % i, [128, 8, 32], F32) for i in range(2)]
            cqn = [sb(sa, "cqn%d" % i, [128, 8, 64], F32) for i in range(2)]
            cqs = [sb(sa, "cqs%d" % i, [128, 8, 64], F32) for i in range(2)]
            cqb = [sb(sa, "cqb%d" % i, [128, 8, 64], BF16) for i in range(2)]
            ckn = [sb(sa, "ckn%d" % i, [128, 8, 64], F32) for i in range(2)]
            cks = [sb(sa, "cks%d" % i, [128, 8, 64], F32) for i in range(2)]
            ckb = [sb(sa, "ckb%d" % i, [128, 8, 64], BF16) for i in range(2)]
            qT_st = [sb(sa, "qTst%d" % i, [96, 8, 512], BF16) for i in range(2)]
            kT_st = [sb(sa, "kTst%d" % i, [96, 8, 512], BF16) for i in range(2)]
            cqT_st = [sb(sa, "cqTst%d" % i, [64, 8, 512], BF16) for i in range(2)]
            ckT_st = [sb(sa, "ckTst%d" % i, [64, 8, 512], BF16) for i in range(2)]
            V_st = [sb(sa, "Vst%d" % i, [128, 4, 8, 65], BF16) for i in range(2)]
            cV_st = [sb(sa, "cVst%d" % i, [128, 4, 8, 65], BF16) for i in range(2)]
            for i in range(2):
                P.add("pool", (lambda i: lambda e: e.memset(V_st[i][:, :, :, 64:65], 1.0))(i), w=["Vst%d" % i])
                P.add("pool", (lambda i: lambda e: e.memset(cV_st[i][:, :, :, 64:65], 1.0))(i), w=["cVst%d" % i])

            P.tag = None

            def rstd_from_ssq(sl, col, n, tag):
                kst = "st%d_%s" % (sl, tag)
                P.add("act", lambda e: e.activation(out=st[:, sl, col:col + 1], in_=st[:, sl, col:col + 1], func=AF.Sqrt,
                                                    bias=float(n * EPS), scale=1.0), r=[kst], w=[kst])
                P.add("dve", lambda e: e.reciprocal(out=st[:, sl, col:col + 1], in_=st[:, sl, col:col + 1]), r=[kst], w=[kst])
                return kst

            def rstd_vec(sl, c0, nh, n, tag):
                kst = "st%d_%s" % (sl, tag)
                P.add("act", lambda e: e.activation(out=st[:, sl, c0:c0 + nh], in_=st[:, sl, c0:c0 + nh], func=AF.Sqrt,
                                                    bias=float(n * EPS), scale=1.0), r=[kst], w=[kst])
                P.add("dve", lambda e: e.reciprocal(out=st[:, sl, c0:c0 + nh], in_=st[:, sl, c0:c0 + nh]), r=[kst], w=[kst])
                return kst

            def rope(eng, src3, dst3, cs, keys_r, keys_w, tmp, nh, ktmp):
                cosb = cs[:, 0:16].unsqueeze(1).to_broadcast([128, nh, 16])
                sinb = cs[:, 16:32].unsqueeze(1).to_broadcast([128, nh, 16])
                P.add(eng, lambda e: e.tensor_tensor(out=tmp[:, :, 0:16], in0=src3[:, :, 16:32], in1=sinb, op=ALU.mult),
                      r=keys_r, w=[ktmp])
                P.add(eng, lambda e: e.tensor_tensor(out=tmp[:, :, 16:32], in0=src3[:, :, 0:16], in1=sinb, op=ALU.mult),
                      r=keys_r + [ktmp], w=[ktmp])
                P.add(eng, lambda e: e.tensor_tensor(out=src3[:, :, 0:16], in0=src3[:, :, 0:16], in1=cosb, op=ALU.mult),
                      r=keys_r + [ktmp], w=keys_r[:1])
                P.add(eng, lambda e: e.tensor_tensor(out=src3[:, :, 16:32], in0=src3[:, :, 16:32], in1=cosb, op=ALU.mult),
                      r=keys_r + [ktmp], w=keys_r[:1])
                P.add(eng, lambda e: e.tensor_tensor(out=dst3[:, :, 0:16], in0=src3[:, :, 0:16], in1=tmp[:, :, 0:16], op=ALU.subtract),
                      r=keys_r + [ktmp], w=keys_w)
                P.add(eng, lambda e: e.tensor_tensor(out=dst3[:, :, 16:32], in0=src3[:, :, 16:32], in1=tmp[:, :, 16:32], op=ALU.add),
                      r=keys_r + [ktmp], w=keys_w)

            def prep_tile(g):
                s, tt = divmod(g, NT)
                jb, i4 = divmod(tt, 4)
                sl = g % 2
                bs = (g // 4) % 2
                S_ = str(sl)
                kxt, kxn, khT = "xt" + S_, "xn" + S_, "hT" + S_
                P.add("sp", lambda e: e.dma_start(out=xt[sl][:], in_=x_d[s, tt * 128:(tt + 1) * 128, :]), w=[kxt], chan="xt" + S_)
                P.add("act", lambda e: e.activation(out=junk[:], in_=xt[sl][:], func=AF.Square, accum_out=st[:, sl, 0:1]),
                      r=[kxt], w=["st%s_x" % S_])
                kst = rstd_from_ssq(sl, 0, D, "x")
                P.add("act", lambda e: e.activation(out=xn[sl][:], in_=xt[sl][:], func=AF.Copy, scale=st[:, sl, 0:1]),
                      r=[kxt, kst], w=[kxn])
                b0 = BK.get()
                for c in range(8):
                    P.add("pe", (lambda c: lambda e: e.transpose(out=banks_b[b0][:, c * 128:(c + 1) * 128],
                                                                  in_=xn[sl][:, c * 128:(c + 1) * 128], identity=ident[:]))(c),
                          r=[kxn, "ident"], w=[bk(b0)])
                tp3 = banks_b[b0][:, 0:1024].rearrange("p (c t) -> p c t", c=8)
                P.add("dve", lambda e: e.tensor_tensor(out=hT[sl][:], in0=tp3, in1=AB[:, s, 0, :].unsqueeze(2).to_broadcast([128, 8, 128]),
                                                       op=ALU.mult), r=[bk(b0), "AB"], w=[khT])
                P.add("pool", lambda e: e.tensor_tensor(out=hT[sl][:], in0=hT[sl][:], in1=AB[:, s, 1, :].unsqueeze(2).to_broadcast([128, 8, 128]),
                                                        op=ALU.add), r=[khT, "AB"], w=[khT])
                pb = [BK.get(hold=True) for _ in range(4)]
                cols = [(0, 416), (416, 928), (928, 1440), (1440, 1952)]
                for bi, (c0, c1) in enumerate(cols):
                    for k in range(8):
                        P.add("pe", (lambda bi, c0, c1, k: lambda e: e.matmul(
                            banks_f[pb[bi]][:, 0:c1 - c0], lhsT=hT[sl][:, k, :], rhs=win[:, k, c0:c1],
                            start=(k == 0), stop=(k == 7)))(bi, c0, c1, k),
                            r=[khT] + WIN_KEYS, w=[bk(pb[bi])])
                pl, pq, pk, pv = [banks_f[b] for b in pb]
                kl, kq, kk_, kv_ = [bk(b) for b in pb]
                P.add("act", lambda e: e.activation(out=junk[:, 0:256], in_=pl[:, 0:256], func=AF.Square, accum_out=st[:, sl, 1:2]),
                      r=[kl], w=["st%s_ql" % S_])
                P.add("act", lambda e: e.activation(out=junk[:, 256:384], in_=pl[:, 256:384], func=AF.Square, accum_out=st[:, sl, 2:3]),
                      r=[kl], w=["st%s_kvl" % S_])
                k1 = rstd_from_ssq(sl, 1, 256, "ql")
                k2 = rstd_from_ssq(sl, 2, 128, "kvl")
                klat = "lat" + S_
                P.add("dve", lambda e: e.tensor_scalar(out=lat[sl][:, 0:256], in0=pl[:, 0:256], scalar1=st[:, sl, 1:2], scalar2=None,
                                                       op0=ALU.mult), r=[kl, k1], w=[klat + "a"])
                P.add("dve", lambda e: e.tensor_scalar(out=lat[sl][:, 256:384], in0=pl[:, 256:384], scalar1=st[:, sl, 2:3], scalar2=None,
                                                       op0=ALU.mult), r=[kl, k2], w=[klat + "b"])
                kkr = "kr" + S_
                P.add("dve", lambda e: e.tensor_tensor(out=kr[sl][:, 0, :], in0=pl[:, 384:416], in1=grow[:, 160:192], op=ALU.mult),
                      r=[kl, "grow"], w=[kkr])
                P.add("act", lambda e: e.activation(out=junk[:, 512:544], in_=pl[:, 384:416], func=AF.Square, accum_out=st[:, sl, 3:4]),
                      r=[kl], w=["st%s_kr" % S_])
                BK.release(pb[0])
                b1 = BK.get()
                for c in range(3):
                    P.add("pe", (lambda c: lambda e: e.transpose(out=banks_b[b1][:, c * 128:(c + 1) * 128],
                                                                  in_=lat[sl][:, c * 128:(c + 1) * 128], identity=ident[:]))(c),
                          r=[klat + "a", klat + "b", "ident"], w=[bk(b1)])
                klT = "latT" + S_
                P.add("act", lambda e: e.activation(out=latT[sl][:].rearrange("p c t -> p (c t)"), in_=banks_b[b1][:, 0:384], func=AF.Copy),
                      r=[bk(b1)], w=[klT])
                bq0, bq1 = BK.get(hold=True), BK.get(hold=True)
                for (bb, c0, c1) in ((bq0, 0, 480), (bq1, 480, 768)):
                    for k in range(2):
                        P.add("pe", (lambda bb, c0, c1, k: lambda e: e.matmul(
                            banks_f[bb][:, 0:c1 - c0], lhsT=latT[sl][:, k, :], rhs=wq[:, k, c0:c1], start=(k == 0), stop=(k == 1)))(bb, c0, c1, k),
                            r=[klT, "wq"], w=[bk(bb)])
                bkv0, bkv1 = BK.get(hold=True), BK.get(hold=True)
                for (bb, c0) in ((bkv0, 0), (bkv1, 512)):
                    P.add("pe", (lambda bb, c0: lambda e: e.matmul(
                        banks_f[bb][:, 0:512], lhsT=latT[sl][:, 2, :], rhs=wkv[:, c0:c0 + 512], start=True, stop=True))(bb, c0),
                        r=[klT, "wkv"], w=[bk(bb)])
                cs = cs_all[:, g, :]
                kqn, kqs, kqb = "qn" + S_, "qsq" + S_, "qb" + S_
                q0 = banks_f[bq0][:, 0:480].rearrange("p (h d) -> p h d", h=5)
                q1 = banks_f[bq1][:, 0:288].rearrange("p (h d) -> p h d", h=3)
                P.add("act", lambda e: e.activation(out=qn[sl][:, 0:5, :], in_=q0, func=AF.Copy), r=[bk(bq0)], w=[kqn + "a"])
                P.add("act", lambda e: e.activation(out=qn[sl][:, 5:8, :], in_=q1, func=AF.Copy), r=[bk(bq1)], w=[kqn + "b"])
                BK.release(bq0)
                BK.release(bq1)
                P.add("pool", lambda e: e.tensor_tensor(out=qsq[sl][:], in0=qn[sl][:], in1=qn[sl][:], op=ALU.mult),
                      r=[kqn + "a", kqn + "b"], w=[kqs])
                P.add("dve", lambda e: e.tensor_reduce(out=st[:, sl, 8:16], in_=qsq[sl][:], axis=AX.X, op=ALU.add),
                      r=[kqs], w=["st%s_q" % S_])
                k3 = rstd_vec(sl, 8, 8, 96, "q")
                P.add("dve", lambda e: e.tensor_tensor(out=qn[sl][:], in0=qn[sl][:], in1=st[:, sl, 8:16].unsqueeze(2).to_broadcast([128, 8, 96]),
                                                       op=ALU.mult), r=[kqn + "a", kqn + "b", k3], w=[kqn])
                P.add("pool", lambda e: e.tensor_tensor(out=qn[sl][:], in0=qn[sl][:], in1=grow[:, 0:96].unsqueeze(1).to_broadcast([128, 8, 96]),
                                                        op=ALU.mult), r=[kqn, "grow"], w=[kqn])
                P.add("act", lambda e: e.activation(out=qb[sl][:, :, 0:64], in_=qn[sl][:, :, 0:64], func=AF.Copy), r=[kqn], w=[kqb + "n"])
                rope("pool", qn[sl][:, :, 64:96], qb[sl][:, :, 64:96], cs, [kqn, "cs_all"], [kqb + "r"], rt[sl], 8, "rt" + S_)
                kkn, kks, kkb = "kn" + S_, "ksq" + S_, "kb" + S_
                kv0 = banks_f[bkv0][:, 0:512].rearrange("p (h d) -> p h d", h=4)
                kv1 = banks_f[bkv1][:, 0:512].rearrange("p (h d) -> p h d", h=4)
                P.add("act", lambda e: e.activation(out=kn[sl][:, 0:4, :], in_=kv0[:, :, 0:64], func=AF.Copy), r=[bk(bkv0)], w=[kkn + "a"])
                P.add("act", lambda e: e.activation(out=kn[sl][:, 4:8, :], in_=kv1[:, :, 0:64], func=AF.Copy), r=[bk(bkv1)], w=[kkn + "b"])
                P.add("act", lambda e: e.activation(out=V_st[bs][:, i4, 0:4, 0:64], in_=kv0[:, :, 64:128], func=AF.Copy),
                      r=[bk(bkv0)], w=["Vst%d" % bs])
                P.add("act", lambda e: e.activation(out=V_st[bs][:, i4, 4:8, 0:64], in_=kv1[:, :, 64:128], func=AF.Copy),
                      r=[bk(bkv1)], w=["Vst%d" % bs])
                BK.release(bkv0)
                BK.release(bkv1)
                P.add("pool", lambda e: e.tensor_tensor(out=ksq[sl][:], in0=kn[sl][:], in1=kn[sl][:], op=ALU.mult),
                      r=[kkn + "a", kkn + "b"], w=[kks])
                P.add("dve", lambda e: e.tensor_reduce(out=st[:, sl, 16:24], in_=ksq[sl][:], axis=AX.X, op=ALU.add),
                      r=[kks], w=["st%s_k" % S_])
                P.add("dve", lambda e: e.tensor_scalar(out=st[:, sl, 16:24], in0=st[:, sl, 16:24], scalar1=st[:, sl, 3:4], scalar2=None,
                                                       op0=ALU.add), r=["st%s_k" % S_, "st%s_kr" % S_], w=["st%s_k" % S_])
                k4 = rstd_vec(sl, 16, 8, 96, "k")
                P.add("dve", lambda e: e.tensor_tensor(out=kn[sl][:], in0=kn[sl][:], in1=st[:, sl, 16:24].unsqueeze(2).to_broadcast([128, 8, 64]),
                                                       op=ALU.mult), r=[kkn + "a", kkn + "b", k4], w=[kkn])
                P.add("pool", lambda e: e.tensor_tensor(out=kb[sl][:, :, 0:64], in0=kn[sl][:], in1=grow[:, 96:160].unsqueeze(1).to_broadcast([128, 8, 64]),
                                                        op=ALU.mult), r=[kkn, "grow"], w=[kkb + "n"])
                rope("dve", kr[sl][:, 0:1, :], kr[sl][:, 1:2, :], cs, [kkr, "cs_all"], [kkr + "o"], kr[sl][:, 2:3, :], 1, kkr + "t")
                P.add("dve", lambda e: e.tensor_tensor(out=kb[sl][:, :, 64:96], in0=kr[sl][:, 1:2, :].to_broadcast([128, 8, 32]),
                                                       in1=st[:, sl, 16:24].unsqueeze(2).to_broadcast([128, 8, 32]), op=ALU.mult),
                      r=[kkr + "o", k4], w=[kkb + "r"])
                for (src, keyb, dn, dsq, db, col, gc0, tag) in (
                        (pq, kq, cqn, cqs, cqb, 24, 192, "cq"), (pk, kk_, ckn, cks, ckb, 32, 256, "ck")):
                    kdn, kds, kdb = tag + "n" + S_, tag + "s" + S_, tag + "b" + S_
                    P.add("act", (lambda src, dn: lambda e: e.activation(out=dn[sl][:].rearrange("p h d -> p (h d)"), in_=src[:, 0:512], func=AF.Copy))(src, dn),
                          r=[keyb], w=[kdn])
                    P.add("pool", (lambda dn, dsq: lambda e: e.tensor_tensor(out=dsq[sl][:], in0=dn[sl][:], in1=dn[sl][:], op=ALU.mult))(dn, dsq),
                          r=[kdn], w=[kds])
                    P.add("dve", (lambda dsq, col: lambda e: e.tensor_reduce(out=st[:, sl, col:col + 8], in_=dsq[sl][:], axis=AX.X, op=ALU.add))(dsq, col),
                          r=[kds], w=["st%s_%s" % (S_, tag)])
                    k5 = rstd_vec(sl, col, 8, 64, tag)
                    P.add("dve", (lambda dn, col: lambda e: e.tensor_tensor(out=dn[sl][:], in0=dn[sl][:],
                                                                           in1=st[:, sl, col:col + 8].unsqueeze(2).to_broadcast([128, 8, 64]), op=ALU.mult))(dn, col),
                          r=[kdn, k5], w=[kdn])
                    P.add("pool", (lambda dn, db, gc0: lambda e: e.tensor_tensor(out=db[sl][:], in0=dn[sl][:],
                                                                                in1=grow[:, gc0:gc0 + 64].unsqueeze(1).to_broadcast([128, 8, 64]), op=ALU.mult))(dn, db, gc0),
                          r=[kdn, "grow"], w=[kdb])
                P.add("act", lambda e: e.activation(out=cV_st[bs][:, i4, :, 0:64], in_=pv[:, 0:512].rearrange("p (h d) -> p h d", h=8), func=AF.Copy),
                      r=[kv_], w=["cVst%d" % bs])
                BK.release(pb[1])
                BK.release(pb[2])
                BK.release(pb[3])
                for (srcb, keys, dst, kdst, dd) in ((qb, [kqb + "n", kqb + "r"], qT_st, "qTst%d" % bs, 96),
                                                    (kb, [kkb + "n", kkb + "r"], kT_st, "kTst%d" % bs, 96),
                                                    (cqb, ["cqb" + S_], cqT_st, "cqTst%d" % bs, 64),
                                                    (ckb, ["ckb" + S_], ckT_st, "ckTst%d" % bs, 64)):
                    bt = BK.get()
                    for h in range(8):
                        P.add("pe", (lambda srcb, bt, h, dd: lambda e: e.transpose(
                            out=banks_b[bt][0:dd, h * 128:(h + 1) * 128], in_=srcb[sl][:, h, :], identity=ident[:]))(srcb, bt, h, dd),
                            r=keys + ["ident"], w=[bk(bt)])
                    P.add("act", (lambda bt, dst, dd: lambda e: e.activation(
                        out=dst[bs][:, :, i4 * 128:(i4 + 1) * 128], in_=banks_b[bt][0:dd, 0:1024].rearrange("p (h t) -> p h t", h=8), func=AF.Copy))(bt, dst, dd),
                        r=[bk(bt)], w=[kdst])
                if i4 == 3:
                    t0 = jb * 512
                    P.add("sp", lambda e: e.dma_start(out=qT_d[s, :, :, t0:t0 + 512].rearrange("h d t -> d h t"), in_=qT_st[bs][:]),
                          r=["qTst%d" % bs], w=["qT_d%d" % s], chan="stq%d" % bs)
                    P.add("sp", lambda e: e.dma_start(out=kT_d[s, :, :, t0:t0 + 512].rearrange("h d t -> d h t"), in_=kT_st[bs][:]),
                          r=["kTst%d" % bs], w=["kT_d%d" % s], chan="stk%d" % bs)
                    P.add("sp", lambda e: e.dma_start(out=cqT_d[s, :, :, t0:t0 + 512].rearrange("h d t -> d h t"), in_=cqT_st[bs][:]),
                          r=["cqTst%d" % bs], w=["cqT_d%d" % s], chan="stcq%d" % bs)
                    P.add("sp", lambda e: e.dma_start(out=ckT_d[s, :, :, t0:t0 + 512].rearrange("h d t -> d h t"), in_=ckT_st[bs][:]),
                          r=["ckTst%d" % bs], w=["ckT_d%d" % s], chan="stck%d" % bs)
                    P.add("sp", lambda e: e.dma_start(out=V_d[s, t0:t0 + 512, :].rearrange("(k p) c -> p k c", p=128),
                                                      in_=V_st[bs][:].rearrange("p k h c -> p k (h c)")),
                          r=["Vst%d" % bs], w=["V_d%d" % s], chan="stv%d" % bs)
                    P.add("sp", lambda e: e.dma_start(out=cV_d[s, t0:t0 + 512, :].rearrange("(k p) c -> p k c", p=128),
                                                      in_=cV_st[bs][:].rearrange("p k h c -> p k (h c)")),
                          r=["cVst%d" % bs], w=["cV_d%d" % s], chan="stcv%d" % bs)

            for g in range(NSEQ * NT if stop not in ("setup",) else 0):
                prep_tile(g)
            P.barrier()
            sa.close()

        with ExitStack() as sbk:
            kT_all = sb(sbk, "kT_all", [96, 8, S], BF16)
            V_all = sb(sbk, "V_all", [128, NT, 8 * 65], BF16)
            ckT_all = sb(sbk, "ckT_all", [64, 8, S], BF16)
            cV_all = sb(sbk, "cV_all", [128, NT, 8 * 65], BF16)
            qT_b = [sb(sbk, "qT_b%d" % i, [96, 8, 512], BF16) for i in range(2)]
            cqT_b = [sb(sbk, "cqT_b%d" % i, [64, 8, 512], BF16) for i in range(2)]
            otn = [sb(sbk, "otn%d" % i, [64, 16, 512], BF16) for i in range(1)]
            Ef = sb(sbk, "Ef", [128, 8, 2, 128], F32)
            bfar = sb(sbk, "bfar", [128, 8], F32)
            NPT = 4
            Pt = [sb(sbk, "Pt%d" % i, [128, 512], BF16) for i in range(NPT)]
            PA = [sb(sbk, "PA%d" % i, [128, 384], BF16) for i in range(2)]
            PB = [sb(sbk, "PB%d" % i, [128, 256], F32) for i in range(2)]
            PBb = [sb(sbk, "PBb%d" % i, [128, 256], BF16) for i in range(2)]
            NRZ = 4
            ots = [sb(sbk, "ots%d" % i, [128, 512], F32) for i in range(NRZ)]
            rzb = [sb(sbk, "rzb%d" % i, [128, 2, 512], BF16) for i in range(NRZ)]
            sel64b = sb(sbk, "sel64b", [128, 64], BF16)
            P.tag = "wlB"
            P.add("sp", lambda e: e.dma_start(out=Ef[:], in_=bias34_d), w=["Ef"], chan="small4", waitall=True)
            P.add("sp", lambda e: e.dma_start(out=bfar[:], in_=bfar_d), w=["bfar"], chan="small4", waitall=True)
            P.add("act", lambda e: e.activation(out=Ef[:], in_=Ef[:], func=AF.Exp), r=["Ef"], w=["Ef"])
            P.add("pool", lambda e: e.memset(Ef[64:128, :, 1, 0:64], 0.0), r=["Ef"], w=["Ef"])
            for i in range(NRZ):
                P.add("pool", (lambda i: lambda e: e.memset(rzb[i][:], 0.0))(i), w=["rzb%d" % i])
            P.add("pool", lambda e: e.tensor_copy(out=sel64b[:], in_=sel64[:]), r=["sel64"], w=["sel64b"])
            P.tag = None
            cnt = {"pt": 0, "pa": 0, "rz": 0}

            def normalize_head(bo, width):
                ri = cnt["rz"] % NRZ
                cnt["rz"] += 1
                P.add("dve", lambda e: e.tensor_copy(out=ots[ri][0:65, 0:width], in_=banks_f[bo][0:65, 0:width]),
                      r=[bk(bo)], w=["ots%d" % ri])
                BK.release(bo)
                P.add("act", lambda e: e.activation(out=ots[ri][64:65, 0:width], in_=ots[ri][64:65, 0:width], func=AF.Ln),
                      r=["ots%d" % ri], w=["ots%d" % ri])
                P.add("act", lambda e: e.activation(out=ots[ri][64:65, 0:width], in_=ots[ri][64:65, 0:width], func=AF.Exp, scale=-1.0),
                      r=["ots%d" % ri], w=["ots%d" % ri])
                P.add("dve", lambda e: e.tensor_copy(out=rzb[ri][64:65, 0, 0:width], in_=ots[ri][64:65, 0:width]),
                      r=["ots%d" % ri], w=["rzb%d" % ri])
                P.add("dve", lambda e: e.tensor_tensor(out=rzb[ri][64:65, 1, 0:width], in0=ots[ri][64:65, 0:width],
                                                       in1=rzb[ri][64:65, 0, 0:width], op=ALU.subtract),
                      r=["ots%d" % ri, "rzb%d" % ri], w=["rzb%d" % ri])
                return ri

            def normalize_tail(ri, width, dst_ap, kdst):
                bb = BK.get(hold=True)
                for pl_ in range(2):
                    P.add("pe", (lambda pl_: lambda e: e.matmul(banks_f[bb][0:64, 0:width], lhsT=sel64b[:, 0:64], rhs=rzb[ri][:, pl_, 0:width],
                                                                start=(pl_ == 0), stop=(pl_ == 1)))(pl_),
                          r=["rzb%d" % ri, "sel64b"], w=[bk(bb)])
                P.add("dve", lambda e: e.tensor_tensor(out=dst_ap, in0=banks_f[bb][0:64, 0:width], in1=ots[ri][0:64, 0:width], op=ALU.mult),
                      r=[bk(bb), "ots%d" % ri], w=[kdst])
                BK.release(bb)

            def mla_gen(s, j, h, qs):
                kq = "qT_b%d" % qs
                nkt = 4 * j + 4
                bo = BK.get(hold=True)
                tiles = []
                for kt in range(nkt):
                    r_ = kt - 4 * j
                    c0 = 128 * r_ if r_ > 0 else 0
                    tiles.append((kt, c0, r_ >= 0))
                sbank = {}

                def emit_s(idx):
                    kt, c0, diag = tiles[idx]
                    b = BK.get(hold=True)
                    sbank[idx] = b
                    P.add("pe", lambda e: e.matmul(banks_f[b][:, 0:512 - c0], lhsT=kT_all[:, h, kt * 128:(kt + 1) * 128],
                                                   rhs=qT_b[qs][:, h, c0:512], start=True, stop=True),
                          r=["kT_all", kq], w=[bk(b)])

                def emit_rest(idx):
                    kt, c0, diag = tiles[idx]
                    b = sbank[idx]
                    pi = cnt["pt"] % NPT
                    cnt["pt"] += 1
                    kp = "Pt%d" % pi
                    w_ = 512 - c0
                    P.add("act", lambda e: e.activation(out=Pt[pi][:, 0:w_], in_=banks_f[b][:, 0:w_], func=AF.Exp), r=[bk(b)], w=[kp])
                    BK.release(b)
                    if diag:
                        P.add("pool", lambda e: e.memset(Pt[pi][64:128, 0:64], 0.0), r=[kp], w=[kp])
                    P.add("pe", lambda e: e.matmul(banks_f[bo][0:65, c0:512], lhsT=V_all[:, kt, h * 65:(h + 1) * 65], rhs=Pt[pi][:, 0:w_],
                                                   start=(idx == 0), stop=(idx == nkt - 1)),
                          r=[kp, "V_all"], w=[bk(bo)])

                LOOK = 2
                for idx in range(min(LOOK, nkt)):
                    emit_s(idx)
                yield
                for idx in range(nkt):
                    emit_rest(idx)
                    if idx + LOOK < nkt:
                        emit_s(idx + LOOK)
                    yield
                ri = normalize_head(bo, 512)
                pend_m.append((ri, 512, otn[0][:, h, :], "otn0h%d" % h))
                yield

            def ca_qtile(j, h, qs, bo, i):
                kq = "cqT_b%d" % qs
                gi = 4 * j + i
                tmin = max(0, 4 - gi)
                pai = cnt["pa"] % 2
                cnt["pa"] += 1
                ba = BK.get(hold=True) if tmin <= 2 else None
                bb = BK.get(hold=True)

                def s_mm(t):
                    ktile = gi - 4 + t
                    if t <= 2:
                        dst = banks_f[ba][:, t * 128:(t + 1) * 128]
                        kb_ = bk(ba)
                    else:
                        dst = banks_f[bb][:, (t - 3) * 128:(t - 2) * 128]
                        kb_ = bk(bb)
                    P.add("pe", lambda e: e.matmul(dst, lhsT=ckT_all[:, h, ktile * 128:(ktile + 1) * 128],
                                                   rhs=cqT_b[qs][:, h, i * 128:(i + 1) * 128], start=True, stop=True),
                          r=["ckT_all", kq], w=[kb_])

                def pv_mm(t):
                    ktile = gi - 4 + t
                    if t <= 2:
                        rhs = PA[pai][:, t * 128:(t + 1) * 128]
                        kr_ = "PA%d" % pai
                    else:
                        rhs = PBb[pai][:, (t - 3) * 128:(t - 2) * 128]
                        kr_ = "PBb%d" % pai
                    P.add("pe", lambda e: e.matmul(banks_f[bo][0:65, i * 128:(i + 1) * 128],
                                                   lhsT=cV_all[:, ktile, h * 65:(h + 1) * 65], rhs=rhs,
                                                   start=(t == tmin), stop=(t == 4)),
                          r=[kr_, "cV_all"], w=[bk(bo)])

                for t in range(tmin, 5):
                    s_mm(t)
                yield
                if ba is not None:
                    a0 = tmin * 128
                    P.add("act", lambda e: e.activation(out=PA[pai][:, a0:384], in_=banks_f[ba][:, a0:384], func=AF.Exp,
                                                        bias=bfar[:, h:h + 1], scale=1.0), r=[bk(ba), "bfar"], w=["PA%d" % pai])
                    BK.release(ba)
                    if tmin == 0:
                        P.add("pool", lambda e: e.memset(PA[pai][0:64, 64:128], 0.0), r=["PA%d" % pai], w=["PA%d" % pai])
                b0 = 0 if tmin <= 3 else 128
                P.add("act", lambda e: e.activation(out=PB[pai][:, b0:256], in_=banks_f[bb][:, b0:256], func=AF.Exp), r=[bk(bb)], w=["PB%d" % pai])
                BK.release(bb)
                P.add("pool", lambda e: e.tensor_tensor(out=PBb[pai][:, b0:256], in0=PB[pai][:, b0:256],
                                                        in1=Ef[:, h, :, :].rearrange("p t q -> p (t q)")[:, b0:256], op=ALU.mult),
                      r=["PB%d" % pai, "Ef"], w=["PBb%d" % pai])
                yield
                for t in range(tmin, 5):
                    pv_mm(t)
                yield

            def ca_gen(s, j, h, qs):
                bo = BK.get(hold=True)
                for i in range(4):
                    yield from ca_qtile(j, h, qs, bo, i)
                ri = normalize_head(bo, 512)
                pend_c.append((ri, 512, otn[0][:, 8 + h, :], "otn0h%d" % (8 + h)))
                yield

            pend_m, pend_c = [], []

            def stream(gens, pend, delay):
                for g in gens:
                    n = 0
                    old = list(pend)
                    del pend[:]
                    for _ in g:
                        n += 1
                        yield
                        if n == delay and old:
                            for t in old:
                                normalize_tail(*t)
                            old = []
                            yield
                    if old:
                        for t in old:
                            normalize_tail(*t)
                        yield
                for t in pend:
                    normalize_tail(*t)
                del pend[:]
                yield

            def interleave(ga, gb, ra, rb):
                alive_a = alive_b = True
                while alive_a or alive_b:
                    for _ in range(ra):
                        if alive_a:
                            try:
                                next(ga)
                            except StopIteration:
                                alive_a = False
                    for _ in range(rb):
                        if alive_b:
                            try:
                                next(gb)
                            except StopIteration:
                                alive_b = False

            def load_seq(s):
                for hh in range(2):
                    P.add("sp", lambda e: e.dma_start(out=kT_all[:, 4 * hh:4 * hh + 4, :], in_=kT_d[s, 4 * hh:4 * hh + 4].rearrange("h d t -> d h t")),
                          r=["kT_d%d" % s], w=["kT_all"], chan="ldk%d" % hh)
                    P.add("sp", lambda e: e.dma_start(out=ckT_all[:, 4 * hh:4 * hh + 4, :], in_=ckT_d[s, 4 * hh:4 * hh + 4].rearrange("h d t -> d h t")),
                          r=["ckT_d%d" % s], w=["ckT_all"], chan="ldck%d" % hh)
                for q4 in range(4):
                    P.add("sp", lambda e: e.dma_start(out=V_all[:, 4 * q4:4 * q4 + 4, :], in_=V_d[s, 512 * q4:512 * q4 + 512, :].rearrange("(k p) c -> p k c", p=128)),
                          r=["V_d%d" % s], w=["V_all"], chan="ldv%d" % q4)
                    P.add("sp", lambda e: e.dma_start(out=cV_all[:, 4 * q4:4 * q4 + 4, :], in_=cV_d[s, 512 * q4:512 * q4 + 512, :].rearrange("(k p) c -> p k c", p=128)),
                          r=["cV_d%d" % s], w=["cV_all"], chan="ldcv%d" % q4)

            def load_seq_part(fn, *a):
                fn(*a)

            def load_q(s, j, qs):
                t0 = 512 * j
                P.add("sp", lambda e: e.dma_start(out=qT_b[qs][:], in_=qT_d[s, :, :, t0:t0 + 512].rearrange("h d t -> d h t")),
                      r=["qT_d%d" % s], w=["qT_b%d" % qs], chan="ldq%d" % qs)
                P.add("sp", lambda e: e.dma_start(out=cqT_b[qs][:], in_=cqT_d[s, :, :, t0:t0 + 512].rearrange("h d t -> d h t")),
                      r=["cqT_d%d" % s], w=["cqT_b%d" % qs], chan="ldcq%d" % qs)

            def attn_block(s, j, qs):
                t0 = 512 * j
                nb_ = s * 4 + j + 1
                if nb_ < NSEQ * 4:
                    load_q(nb_ // 4, nb_ % 4, 1 - qs)
                gm = stream([mla_gen(s, j, h, qs) for h in range(8)], pend_m, 3)
                gc = stream([ca_gen(s, j, h, qs) for h in range(8)], pend_c, 4)
                ra, rb = {0: (1, 2), 1: (3, 4), 2: (1, 1), 3: (3, 2)}[j]
                interleave(gm, gc, ra, rb)
                P.add("sp", lambda e: e.dma_start(out=otn_d[s, :, t0:t0 + 512].rearrange("(h d) t -> d h t", d=64), in_=otn[0][:]),
                      r=["otn0h%d" % hh for hh in range(16)], w=["otn_d%d_%d" % (s, j)], chan="stotn")

            def load_seq_safe(s):
                def ldk(hh):
                    P.add("sp", lambda e: e.dma_start(out=kT_all[:, 4 * hh:4 * hh + 4, :], in_=kT_d[s, 4 * hh:4 * hh + 4].rearrange("h d t -> d h t")),
                          r=["kT_d%d" % s], w=["kT_all"], chan="ldk%d" % hh)
                    P.add("sp", lambda e: e.dma_start(out=ckT_all[:, 4 * hh:4 * hh + 4, :], in_=ckT_d[s, 4 * hh:4 * hh + 4].rearrange("h d t -> d h t")),
                          r=["ckT_d%d" % s], w=["ckT_all"], chan="ldck%d" % hh)

                def ldv(q4):
                    P.add("sp", lambda e: e.dma_start(out=V_all[:, 4 * q4:4 * q4 + 4, :], in_=V_d[s, 512 * q4:512 * q4 + 512, :].rearrange("(k p) c -> p k c", p=128)),
                          r=["V_d%d" % s], w=["V_all"], chan="ldv%d" % q4)
                    P.add("sp", lambda e: e.dma_start(out=cV_all[:, 4 * q4:4 * q4 + 4, :], in_=cV_d[s, 512 * q4:512 * q4 + 512, :].rearrange("(k p) c -> p k c", p=128)),
                          r=["cV_d%d" % s], w=["cV_all"], chan="ldcv%d" % q4)
                for hh in range(2):
                    ldk(hh)
                for q4 in range(4):
                    ldv(q4)

            blk = 0
            if stop not in ("setup", "A"):
                load_q(0, 0, 0)
            for s in range(NSEQ if stop not in ("setup", "A") else 0):
                load_seq_safe(s)
                for j in range(4):
                    attn_block(s, j, blk % 2)
                    blk += 1
            P.barrier()

        with ExitStack() as sc:
            TB = 256
            NTB = TB // 128
            wdn = sb(sc, "wdn", [128, NFF, D], BF16)
            wout = sb(sc, "wout", [128, 8, D], BF16)
            wup = sb(sc, "wup", [128, 8, 2 * D_FF], BF16)
            convp = sb(sc, "convp", [128, 4, 2 * NFF], F32)
            gab = sb(sc, "gab", [128, D], F32)
            gmb = sb(sc, "gmb", [128, D], F32)
            otb = sb(sc, "otb", [128, 8, TB], BF16)
            x1 = [sb(sc, "x1_%d" % i, [128, D], F32) for i in range(NTB)]
            xn2 = sb(sc, "xn2", [128, D], BF16)
            hT2 = sb(sc, "hT2", [128, 8, TB + 2], BF16)
            gT = sb(sc, "gT", [128, NFF, TB], BF16)
            NUB = 3
            cg = [sb(sc, "cg%d" % i, [128, TB], F32) for i in range(NUB)]
            cv = [sb(sc, "cv%d" % i, [128, TB], F32) for i in range(NUB)]
            stC = sb(sc, "stC", [128, 4], F32)
            otile = sb(sc, "otile", [128, D], F32)
            P.tag = "wlC"
            P.add("sp", lambda e: e.dma_start(out=convp[:], in_=convp_d), w=["convp"], chan="small5", waitall=True)

            wl_cnt = {"n": 0}
            WUP_KEYS, WOUT_KEYS, WDN_KEYS = [], [], []

            def wl_piece(fn, keylist, name):
                n = wl_cnt["n"]
                wl_cnt["n"] += 1
                key = "%s_p%d" % (name, len(keylist))
                P.add("pool", fn, r=([wl_cnt["prev2"]] if n >= 2 else []), w=[key], chan="wlc%d" % (n % 2))
                wl_cnt["prev2"] = wl_cnt.get("prev1")
                wl_cnt["prev1"] = key
                keylist.append(key)

            def ldw(k):
                wl_piece(lambda e: e.dma_start(out=wup[:, :, k * 512:(k + 1) * 512],
                                               in_=wup_d[:, k * 512:(k + 1) * 512].rearrange("(j p) n -> p j n", p=128)), WUP_KEYS, "wup")

            def ldwo(k):
                wl_piece(lambda e: e.dma_start(out=wout[:, :, k * 512:(k + 1) * 512],
                                               in_=wout_d[:, k * 512:(k + 1) * 512].rearrange("(j p) n -> p j n", p=128)), WOUT_KEYS, "wout")

            def ldwd(k, jg):
                wl_piece(lambda e: e.dma_start(out=wdn[:, 11 * jg:11 * jg + 11, k * 512:(k + 1) * 512],
                                               in_=wdn_d[1408 * jg:1408 * jg + 1408, k * 512:(k + 1) * 512].rearrange("(j p) n -> p j n", p=128)), WDN_KEYS, "wdn")
            for k in range(2):
                ldwo(k)
            for k in range(11):
                ldw(k)
            for k in range(2):
                for jg in range(2):
                    ldwd(k, jg)
            P.tag = None
            ucnt = {"u": 0}
            HALO_KEYS = ["halo%d" % ch for ch in range(2 * NFF)]

            def seq_start(s):
                P.add("sp", lambda e: e.dma_start(out=gab[:], in_=mod_d[s, 2 * D:3 * D].partition_broadcast(128)), r=["mod_d"], w=["gab"], chan="ldga")
                P.add("sp", lambda e: e.dma_start(out=gmb[:], in_=mod_d[s, 5 * D:6 * D].partition_broadcast(128)), r=["mod_d"], w=["gmb"], chan="ldgm")

            def outproj_tile(s, t0, it):
                tk = t0 + it * 128
                kx1 = "x1_%d" % it
                kxs = [kx1 + "h0", kx1 + "h512"]
                P.add("sp", lambda e: e.dma_start(out=x1[it][:], in_=x_d[s, tk:tk + 128, :]), w=kxs, chan="ldx%d" % it)
                bo0, bo1 = BK.get(hold=True), BK.get(hold=True)

                def half(bb, n0):
                    for c in range(8):
                        P.add("pe", (lambda c: lambda e: e.matmul(banks_f[bb][:, 0:512], lhsT=otb[:, c, it * 128:(it + 1) * 128],
                                                                  rhs=wout[:, c, n0:n0 + 512], start=(c == 0), stop=(c == 7)))(c),
                              r=["otb"] + WOUT_KEYS, w=[bk(bb)])
                    P.add("dve", lambda e: e.tensor_tensor(out=otile[:, n0:n0 + 512], in0=banks_f[bb][:, 0:512],
                                                           in1=gab[:, n0:n0 + 512], op=ALU.mult),
                          r=[bk(bb), "gab"], w=["otileh%d" % n0])
                    P.add("pool", lambda e: e.tensor_tensor(out=x1[it][:, n0:n0 + 512], in0=x1[it][:, n0:n0 + 512],
                                                            in1=otile[:, n0:n0 + 512], op=ALU.add),
                          r=[kx1 + "h%d" % n0, "otileh%d" % n0], w=[kx1 + "h%d" % n0])
                    BK.release(bb)
                half(bo0, 0)
                half(bo1, 512)
                P.add("act", lambda e: e.activation(out=xn2[:], in_=x1[it][:], func=AF.Square, accum_out=stC[:, 0:1]),
                      r=kxs, w=["stC", "xn2"])
                P.add("act", lambda e: e.activation(out=stC[:, 0:1], in_=stC[:, 0:1], func=AF.Sqrt, bias=float(D * EPS), scale=1.0),
                      r=["stC"], w=["stC"])
                P.add("dve", lambda e: e.reciprocal(out=stC[:, 0:1], in_=stC[:, 0:1]), r=["stC"], w=["stC"])
                P.add("act", lambda e: e.activation(out=xn2[:], in_=x1[it][:], func=AF.Copy, scale=stC[:, 0:1]),
                      r=kxs + ["stC"], w=["xn2"])
                bt = BK.get()
                for c in range(8):
                    P.add("pe", (lambda c: lambda e: e.transpose(out=banks_b[bt][:, c * 128:(c + 1) * 128],
                                                                 in_=xn2[:, c * 128:(c + 1) * 128], identity=ident[:]))(c),
                          r=["xn2", "ident"], w=[bk(bt)])
                tp3 = banks_b[bt][:, 0:1024].rearrange("p (c t) -> p c t", c=8)
                kh = "hT2_%d" % it
                P.add("dve", lambda e: e.tensor_tensor(out=hT2[:, :, 2 + it * 128:2 + (it + 1) * 128], in0=tp3,
                                                       in1=AB[:, s, 2, :].unsqueeze(2).to_broadcast([128, 8, 128]), op=ALU.mult),
                      r=[bk(bt), "AB"], w=[kh])
                P.add("pool", lambda e: e.tensor_tensor(out=hT2[:, :, 2 + it * 128:2 + (it + 1) * 128], in0=hT2[:, :, 2 + it * 128:2 + (it + 1) * 128],
                                                        in1=AB[:, s, 3, :].unsqueeze(2).to_broadcast([128, 8, 128]), op=ALU.add),
                      r=[kh, "AB"], w=[kh])

            def ffn_up(f, khs):
                ui = ucnt["u"] % NUB
                ucnt["u"] += 1
                bg, bv = BK.get(hold=True), BK.get(hold=True)

                def up_mm(bb, ch):
                    for k in range(8):
                        P.add("pe", (lambda k: lambda e: e.matmul(banks_f[bb][:, 0:TB + 2],
                                                                  lhsT=wup[:, k, ch * 128:(ch + 1) * 128], rhs=hT2[:, k, :],
                                                                  start=(k == 0), stop=(k == 7)))(k),
                              r=khs + ["hT2_halo"] + WUP_KEYS, w=[bk(bb)])
                up_mm(bg, f)
                up_mm(bv, NFF + f)
                kcg, kcv = "cg%d" % ui, "cv%d" % ui

                def tap2(bb, ch, cb, kcb):
                    P.add("act", lambda e: e.activation(out=cb[ui][:], in_=banks_f[bb][:, 2:TB + 2], func=AF.Identity,
                                                        scale=convp[:, 2, ch:ch + 1], bias=convp[:, 3, ch:ch + 1]),
                          r=[bk(bb), "convp"], w=[kcb])

                def tap(bb, ch, cb, kcb, j):
                    P.add("dve", lambda e: e.scalar_tensor_tensor(out=cb[ui][:], in0=banks_f[bb][:, j:TB + j], scalar=convp[:, j, ch:ch + 1],
                                                                  in1=cb[ui][:], op0=ALU.mult, op1=ALU.add),
                          r=[bk(bb), kcb, "convp"], w=[kcb])
                tap2(bg, f, cg, kcg)
                tap2(bv, NFF + f, cv, kcv)
                tap(bg, f, cg, kcg, 1)
                tap(bv, NFF + f, cv, kcv, 1)
                tap(bg, f, cg, kcg, 0)
                tap(bv, NFF + f, cv, kcv, 0)
                BK.release(bg)
                BK.release(bv)
                return (f, ui)

            def ffn_gate(f, ui):
                kcg, kcv = "cg%d" % ui, "cv%d" % ui
                P.add("act", lambda e: e.activation(out=cg[ui][:], in_=cg[ui][:], func=AF.Silu), r=[kcg], w=[kcg])
                P.add("pool", lambda e: e.tensor_tensor(out=gT[:, f, :], in0=cg[ui][:], in1=cv[ui][:], op=ALU.mult),
                      r=[kcg, kcv], w=["gT%d" % f])

            def down_tile(s, t0, it, kgs):
                tk = t0 + it * 128
                bd0, bd1 = BK.get(hold=True), BK.get(hold=True)

                def half(bb, n0):
                    for f in range(NFF):
                        P.add("pe", (lambda f: lambda e: e.matmul(banks_f[bb][:, 0:512], lhsT=gT[:, f, it * 128:(it + 1) * 128],
                                                                  rhs=wdn[:, f, n0:n0 + 512], start=(f == 0), stop=(f == NFF - 1)))(f),
                              r=kgs + WDN_KEYS, w=[bk(bb)])
                    P.add("dve", lambda e: e.tensor_tensor(out=otile[:, n0:n0 + 512], in0=banks_f[bb][:, 0:512],
                                                           in1=gmb[:, n0:n0 + 512], op=ALU.mult),
                          r=[bk(bb), "gmb"], w=["otileh%d" % n0])
                    P.add("pool", lambda e: e.tensor_tensor(out=otile[:, n0:n0 + 512], in0=otile[:, n0:n0 + 512],
                                                            in1=x1[it][:, n0:n0 + 512], op=ALU.add),
                          r=["otileh%d" % n0, "x1_%dh%d" % (it, n0)], w=["otileh%d" % n0])
                    BK.release(bb)
                half(bd0, 0)
                half(bd1, 512)
                P.add("sp", lambda e: e.dma_start(out=out_d[s, tk:tk + 128, :], in_=otile[:]),
                      r=["otileh0", "otileh512"], w=["out_d"], chan="stout")

            def ffn_block(s, tb):
                t0 = tb * TB
                jblk = t0 // 512
                P.add("sp", lambda e: e.dma_start(out=otb[:], in_=otn_d[s, :, t0:t0 + TB].rearrange("(c p) t -> p c t", p=128)),
                      r=["otn_d%d_%d" % (s, jblk)], w=["otb"], chan="ldot")
                if tb == 0:
                    P.add("pool", lambda e: e.memset(hT2[:, :, 0:2], 0.0), r=["hT2_%d" % (NTB - 1)], w=["hT2_halo"])
                else:
                    P.add("pool", lambda e: e.tensor_copy(out=hT2[:, :, 0:2], in_=hT2[:, :, TB:TB + 2]), r=["hT2_%d" % (NTB - 1)], w=["hT2_halo"])
                for it in range(NTB):
                    outproj_tile(s, t0, it)
                khs = ["hT2_%d" % it for it in range(NTB)]
                prev = None
                for f in range(NFF):
                    cur = ffn_up(f, khs)
                    if prev is not None:
                        ffn_gate(*prev)
                    prev = cur
                ffn_gate(*prev)
                kgs = ["gT%d" % f for f in range(NFF)]
                for it in range(NTB):
                    down_tile(s, t0, it, kgs)

            for s in range(NSEQ if stop not in ("setup", "A", "B") else 0):
                seq_start(s)
                for tb in range(S // TB):
                    ffn_block(s, tb)
            P.emit()
    return nc, P.stats


_CACHE = {}


def _feat_major(v):
    v = np.asarray(v, np.float32)
    return np.ascontiguousarray(v.reshape(-1, 128).T)


def _prepare(x, c, positions, w_ada, b_ada, g_attn_norm, w_in, g_q_latent, g_kv_latent, w_q_up, w_kv_up,
             g_mla_q, g_mla_k, g_ca_q, g_ca_k, rel_bias, w_out, g_mlp_norm, w_up, conv_w, conv_b, w_down):
    f = lambda a: np.ascontiguousarray(np.asarray(a))
    x = f(x); c = f(c); positions = f(positions)
    if "nc" not in _CACHE:
        _CACHE["nc"], _CACHE["stats"] = build_program(stop=_CACHE.get("stop"))
    nc = _CACHE["nc"]
    gfeat = np.concatenate([_feat_major(g_attn_norm[0]), _feat_major(g_mlp_norm[0]), _feat_major(g_q_latent[0]),
                            _feat_major(g_kv_latent[0])], axis=1).astype(np.float32)
    convp = np.zeros((128, 4, 2 * NFF), np.float32)
    for t in range(3):
        convp[:, t, :] = _feat_major(conv_w[0, t])
    convp[:, 3, :] = _feat_major(conv_b[0])
    grow = np.concatenate([np.asarray(g_mla_q[0]), np.asarray(g_mla_k[0]), np.asarray(g_ca_q[0]), np.asarray(g_ca_k[0])]).astype(np.float32)
    grow = np.ascontiguousarray(np.broadcast_to(grow[None, :], (128, 320)))
    rb = np.asarray(rel_bias[0], np.float32)
    kj = np.arange(128)[:, None]
    qi = np.arange(128)[None, :]
    bias34 = np.zeros((128, 8, 2, 128), np.float32)
    for ti, t in enumerate((3, 4)):
        idx = np.clip(128 * (4 - t) + qi - kj, -128, 128) + 128
        bias34[:, :, ti, :] = np.transpose(rb[:, idx], (1, 0, 2))
    bfar = np.ascontiguousarray(np.broadcast_to(rb[:, 256][None, :], (128, 8))).astype(np.float32)
    half = 16
    invf = np.power(np.float32(10000.0), -np.arange(half, dtype=np.float32) / np.float32(half)).astype(np.float32)
    invf = np.ascontiguousarray(np.broadcast_to(invf[None, :], (128, 16)))
    shared = {
        "w_ada": f(w_ada[0]), "w_in": f(w_in[0]), "w_q_up": f(w_q_up[0]), "w_kv_up": f(w_kv_up[0]), "w_out": f(w_out[0]),
        "w_up": f(w_up[0]), "w_down": f(w_down[0]), "gfeat": gfeat, "convp": convp, "grow": grow, "bias34": bias34,
        "bfar": bfar, "invf": invf,
    }
    in_maps = []
    for i in range(NCORES):
        b0 = NSEQ * i
        m = dict(shared)
        m["x"] = f(x[b0:b0 + NSEQ])
        m["cT"] = np.ascontiguousarray(c[b0:b0 + NSEQ].reshape(NSEQ, 8, 128).transpose(2, 1, 0)).astype(np.float32)
        pl = positions[b0:b0 + NSEQ].reshape(NSEQ, NT, 128).transpose(2, 0, 1).reshape(128, NSEQ * NT)
        m["posl"] = np.ascontiguousarray(pl).astype(np.int32)
        m["b_ada2"] = np.ascontiguousarray(np.broadcast_to(np.asarray(b_ada[0], np.float32)[None, :], (NSEQ, 6 * D)))
        in_maps.append(m)
    return nc, in_maps


def kernel(**inputs):
    nc, in_maps = _prepare(**inputs)
    res = run_bass_kernel_spmd(nc, in_maps, core_ids=list(range(NCORES)))
    out = np.concatenate([np.asarray(r["out"]) for r in res.results], axis=0)
    return out.astype(np.float32)
```

```python
import math
from contextlib import ExitStack

import numpy as np
import concourse.bass as bass
import concourse.mybir as mybir
from concourse.bass_utils import run_bass_kernel_spmd

F32 = mybir.dt.float32
BF16 = mybir.dt.bfloat16
I32 = mybir.dt.int32
AF = mybir.ActivationFunctionType
ALU = mybir.AluOpType
AX = mybir.AxisListType

NCORES = 8
NSEQ = 2
S = 2048
D = 1024
NT = S // 128
D_IN = 1952
D_FF = 2816
NFF = D_FF // 128
EPS = 1e-6
TWO_PI = 2.0 * math.pi


class Op:
    __slots__ = ("eng", "fn", "r", "w", "chan", "deps", "signal", "sigval", "waitall")

    def __init__(self, eng, fn, r, w, chan, waitall):
        self.eng, self.fn, self.r, self.w, self.chan = eng, fn, tuple(r), tuple(w), chan
        self.deps = set()
        self.signal = False
        self.sigval = 0
        self.waitall = waitall


class Prog:
    def __init__(self, nc, es):
        self.nc = nc
        self.es = es
        self.ops = []
        self.last_w = {}
        self.readers = {}
        self.waitall_chans = set()

    tag = None

    def add(self, eng, fn, r=(), w=(), chan=None, waitall=False):
        if self.tag is not None and self.tag in SKIP:
            return None
        op = Op(eng, fn, r, w, chan, waitall)
        idx = len(self.ops)
        deps = set()
        for k in op.r:
            lw = self.last_w.get(k)
            if lw is not None:
                deps.add(lw)
            if isinstance(k, str) and k.startswith("bank"):
                for rd in self.readers.get(k, ()):
                    if self.ops[rd].eng != eng:
                        deps.add(rd)
        for k in op.w:
            lw = self.last_w.get(k)
            if lw is not None:
                deps.add(lw)
            for rd in self.readers.get(k, ()):
                deps.add(rd)
        deps.discard(idx)
        if chan is not None and waitall:
            deps = {d for d in deps if self.ops[d].chan != chan}
        op.deps = deps
        for k in op.w:
            self.last_w[k] = idx
            self.readers[k] = []
        for k in op.r:
            if k in op.w:
                continue
            self.readers.setdefault(k, []).append(idx)
        if chan is not None and waitall:
            self.waitall_chans.add(chan)
        self.ops.append(op)
        return idx

    def barrier(self):
        n = len(self.ops)
        last = {}
        for i, op in enumerate(self.ops):
            key = op.chan if op.chan is not None else ("E", op.eng)
            last[key] = i
        alld = set(last.values())
        for eng in ("pe", "act", "dve", "pool", "sp"):
            op = Op(eng, None, (), (), None, False)
            op.deps = set(alld)
            self.ops.append(op)

    def emit(self):
        nc = self.nc
        ops = self.ops
        engobj = {"pe": nc.tensor, "act": nc.scalar, "dve": nc.vector, "pool": nc.gpsimd, "sp": nc.sync}
        for op in ops:
            for d in op.deps:
                x = ops[d]
                if x.chan is None and x.eng == "pe" and op.eng == "pe" and op.chan is None:
                    continue
                x.signal = True
        sems = {}

        def sem(name):
            if name not in sems:
                sems[name] = self.es.enter_context(nc.semaphore("s_" + str(name)))
            return sems[name]

        cnt = {}
        chan_total = {}
        for op in ops:
            if op.fn is None:
                continue
            if op.chan is not None:
                c = ("C", op.chan)
                cnt[c] = cnt.get(c, 0) + 16
                op.sigval = cnt[c]
                op.signal = True
                chan_total[op.chan] = cnt[c]
            elif op.signal:
                c = ("E", op.eng)
                cnt[c] = cnt.get(c, 0) + 1
                op.sigval = cnt[c]
        known = {e: {} for e in engobj}
        vcs = [None] * len(ops)
        ecount = {}
        nwaits = 0

        def merge(dst, src):
            for k_, v_ in src.items():
                if dst.get(k_, 0) < v_:
                    dst[k_] = v_

        for i, op in enumerate(ops):
            need = []
            for d in op.deps:
                x = ops[d]
                if x.fn is None:
                    if vcs[d] is not None:
                        need.append((None, 0, d))
                    continue
                if x.chan is not None:
                    key = ("C", x.chan)
                    val = chan_total[x.chan] if x.chan in self.waitall_chans else x.sigval
                else:
                    if x.eng == "pe" and op.eng == "pe" and op.chan is None:
                        continue
                    key = ("E", x.eng)
                    val = x.sigval
                need.append((key, val, d))
            need.sort(key=lambda t: -t[2])
            e = engobj[op.eng]
            kn = known[op.eng]
            for key, val, d in need:
                if key is None:
                    continue
                if kn.get(key, 0) >= val:
                    continue
                e.wait_ge(sem(key), val)
                kn[key] = val
                nwaits += 1
                if vcs[d] is not None and not (ops[d].chan in self.waitall_chans):
                    merge(kn, vcs[d])
            vc = dict(kn)
            if op.fn is not None:
                inst = op.fn(e)
                if op.chan is not None:
                    inst.then_inc(sem(("C", op.chan)), 16)
                    if op.chan not in self.waitall_chans:
                        vc[("C", op.chan)] = max(vc.get(("C", op.chan), 0), op.sigval)
                else:
                    if op.signal:
                        inst.then_inc(sem(("E", op.eng)), 1)
                        ecount[op.eng] = op.sigval
                    if op.eng != "pe" or True:
                        vc[("E", op.eng)] = max(vc.get(("E", op.eng), 0), ecount.get(op.eng, 0))
            vcs[i] = vc
        for chan, tot in chan_total.items():
            if known["sp"].get(("C", chan), 0) < tot:
                nc.sync.wait_ge(sem(("C", chan)), tot)
        self.stats = dict(n_ops=len(ops), n_waits=nwaits, n_sems=len(sems))


class Banks:
    def __init__(self, banks):
        self.banks = banks
        self.ptr = 0
        self.held = set()

    def get(self, hold=False):
        for _ in range(16):
            b = self.ptr
            self.ptr = (self.ptr + 1) % len(self.banks)
            if b not in self.held:
                if hold:
                    self.held.add(b)
                return b
        raise RuntimeError("no free PSUM bank")

    def release(self, b):
        self.held.discard(b)


import os
SKIP = set(os.environ.get('KSKIP', '').split(','))


def build_program(debug=None, stop=None):
    nc = bass.Bass("TRN2", target_bir_lowering=False)

    def din(name, shape, dt=F32):
        return nc.dram_tensor(name, list(shape), dt, kind="ExternalInput").ap()

    def dscr(name, shape, dt):
        return nc.dram_tensor(name, list(shape), dt, kind="Internal").ap()

    x_d = din("x", [NSEQ, S, D])
    cT_d = din("cT", [128, 8, NSEQ])
    pos_d = din("posl", [128, NSEQ * NT], I32)
    wada_d = din("w_ada", [D, 6 * D])
    bada_d = din("b_ada2", [NSEQ, 6 * D])
    win_d = din("w_in", [D, D_IN])
    wq_d = din("w_q_up", [256, 768])
    wkv_d = din("w_kv_up", [128, 1024])
    wout_d = din("w_out", [D, D])
    wup_d = din("w_up", [D, 2 * D_FF])
    wdn_d = din("w_down", [D_FF, D])
    gfeat_d = din("gfeat", [128, 19])
    convp_d = din("convp", [128, 4, 2 * NFF])
    grow_d = din("grow", [128, 320])
    bias34_d = din("bias34", [128, 8, 2, 128])
    bfar_d = din("bfar", [128, 8])
    invf_d = din("invf", [128, 16])
    out_d = nc.dram_tensor("out", [NSEQ, S, D], F32, kind="ExternalOutput").ap()

    mod_d = dscr("mod_scr", [NSEQ, 6 * D], F32)
    qT_d = dscr("qT_scr", [NSEQ, 8, 96, S], BF16)
    kT_d = dscr("kT_scr", [NSEQ, 8, 96, S], BF16)
    cqT_d = dscr("cqT_scr", [NSEQ, 8, 64, S], BF16)
    ckT_d = dscr("ckT_scr", [NSEQ, 8, 64, S], BF16)
    V_d = dscr("V_scr", [NSEQ, S, 8 * 65], BF16)
    cV_d = dscr("cV_scr", [NSEQ, S, 8 * 65], BF16)
    otn_d = dscr("otn_scr", [NSEQ, D, S], BF16)

    with ExitStack() as es:
        P = Prog(nc, es)

        def sb(stack, name, shape, dt):
            return stack.enter_context(nc.sbuf_tensor("sb_" + name, list(shape), dt))

        banks_f = [es.enter_context(nc.psum_tensor("bank%d" % i, [128, 512], F32)) for i in range(8)]
        banks_b = [b[:].bitcast(BF16) for b in banks_f]
        BK = Banks(banks_f)

        def bk(b):
            return "bank%d" % b

        ident = sb(es, "ident", [128, 128], BF16)
        identf = sb(es, "identf", [128, 128], F32)
        sel64 = sb(es, "sel64", [128, 64], F32)
        gfeat = sb(es, "gfeat", [128, 19], F32)
        grow = sb(es, "grow", [128, 320], F32)
        AB = sb(es, "AB", [128, NSEQ, 4, 8], F32)
        sa = es.enter_context(ExitStack())
        cs_all = sb(sa, "cs_all", [128, NSEQ * NT, 32], F32)
        junk = sb(sa, "junk", [128, 1024], BF16)

        P.add("pool", lambda e: e.memset(identf[:], 1.0), w=["identf"])
        P.add("pool", lambda e: e.affine_select(out=identf[:], in_=identf[:], pattern=[[-1, 128]],
                                                compare_op=ALU.is_equal, fill=0.0, base=0, channel_multiplier=1),
              r=["identf"], w=["identf"])
        P.add("pool", lambda e: e.tensor_copy(out=ident[:], in_=identf[:]), r=["identf"], w=["ident"])
        P.add("pool", lambda e: e.memset(sel64[:], 0.0), w=["sel64"])
        P.add("pool", lambda e: e.memset(sel64[64:65, :], 1.0), r=["sel64"], w=["sel64"])
        P.add("sp", lambda e: e.dma_start(out=gfeat[:], in_=gfeat_d), w=["gfeat"], chan="small", waitall=True)
        P.add("sp", lambda e: e.dma_start(out=grow[:], in_=grow_d), w=["grow"], chan="small", waitall=True)
        s0 = es.enter_context(ExitStack())
        cT = sb(s0, "cT", [128, 8, NSEQ], F32)
        bada = sb(s0, "bada", [NSEQ, 6 * D], F32)
        posi = sb(s0, "posi", [128, NSEQ * NT], I32)
        invf = sb(s0, "invf", [128, 16], F32)
        P.add("sp", lambda e: e.dma_start(out=cT[:], in_=cT_d), w=["cT"], chan="small", waitall=True)
        P.add("sp", lambda e: e.dma_start(out=bada[:], in_=bada_d), w=["bada"], chan="small", waitall=True)
        P.add("sp", lambda e: e.dma_start(out=posi[:], in_=pos_d), w=["posi"], chan="small", waitall=True)
        P.add("sp", lambda e: e.dma_start(out=invf[:], in_=invf_d), w=["invf"], chan="small", waitall=True)
        P.add("dve", lambda e: e.tensor_scalar(out=grow[:, 96:192], in0=grow[:, 96:192], scalar1=math.sqrt(96.0),
                                               scalar2=None, op0=ALU.mult), r=["grow"], w=["grow"])
        P.add("dve", lambda e: e.tensor_scalar(out=grow[:, 256:320], in0=grow[:, 256:320], scalar1=8.0,
                                               scalar2=None, op0=ALU.mult), r=["grow"], w=["grow"])

        if True:
            scb = sb(s0, "scb", [128, 8, NSEQ], BF16)
            wab = [sb(s0, "wab%d" % i, [128, 8, 512], BF16) for i in range(2)]
            modsb = sb(s0, "modsb", [NSEQ, 6 * D], F32)
            P.add("act", lambda e: e.activation(out=scb[:], in_=cT[:], func=AF.Silu), r=["cT"], w=["scb"])
            for nb in range(12):
                sl = nb % 2
                P.add("pool", (lambda nb, sl: lambda e: e.dma_start(
                    out=wab[sl][:], in_=wada_d[:, nb * 512:(nb + 1) * 512].rearrange("(j p) n -> p j n", p=128)))(nb, sl),
                    w=["wab%d" % sl], chan="wab%d" % sl)
                b = BK.get()
                for k in range(8):
                    P.add("pe", (lambda b, sl, k: lambda e: e.matmul(
                        banks_f[b][0:NSEQ, 0:512], lhsT=scb[:, k, :], rhs=wab[sl][:, k, :], start=(k == 0), stop=(k == 7)))(b, sl, k),
                        r=["scb", "wab%d" % sl], w=[bk(b)])
                P.add("dve", (lambda b, nb: lambda e: e.tensor_tensor(
                    out=modsb[:, nb * 512:(nb + 1) * 512], in0=banks_f[b][0:NSEQ, 0:512],
                    in1=bada[:, nb * 512:(nb + 1) * 512], op=ALU.add))(b, nb),
                    r=[bk(b), "bada"], w=["modsb"])
            P.add("sp", lambda e: e.dma_start(out=mod_d, in_=modsb[:]), r=["modsb"], w=["mod_d"], chan="modst")
            P.tag = "modT"
            modT = sb(s0, "modT", [128, NSEQ, 4, 8], F32)
            bT = BK.get()

            def modtr(qi, c, j):
                col = (qi * 8 + j) * NSEQ
                P.add("pe", lambda e: e.matmul(banks_f[bT][:, col:col + NSEQ], lhsT=modsb[0:NSEQ, c * D + j * 128:c * D + (j + 1) * 128],
                                               rhs=identf[0:NSEQ, 0:NSEQ], start=True, stop=True),
                      r=["modsb", "identf"], w=[bk(bT)])
            for qi, c in enumerate((0, 1, 3, 4)):
                for j in range(8):
                    modtr(qi, c, j)
            P.add("dve", lambda e: e.tensor_copy(out=modT[:].rearrange("p s k j -> p k j s"),
                                                 in_=banks_f[bT][:, 0:32 * NSEQ].rearrange("p (k j s) -> p k j s", k=4, j=8)),
                  r=[bk(bT)], w=["modT"])
            for s in range(NSEQ):
                for (dst, srcq, gcol) in ((0, 1, 0), (2, 3, 8)):
                    P.add("dve", (lambda s, dst, srcq, gcol: lambda e: e.scalar_tensor_tensor(
                        out=AB[:, s, dst, :], in0=modT[:, s, srcq, :], scalar=1.0, in1=gfeat[:, gcol:gcol + 8],
                        op0=ALU.add, op1=ALU.mult))(s, dst, srcq, gcol),
                        r=["modT", "gfeat"], w=["AB"])
                    P.add("dve", (lambda s, dst: lambda e: e.tensor_scalar(
                        out=AB[:, s, dst, :], in0=AB[:, s, dst, :], scalar1=32.0, scalar2=None, op0=ALU.mult))(s, dst),
                        r=["AB"], w=["AB"])
                    P.add("dve", (lambda s, dst, srcq: lambda e: e.tensor_copy(
                        out=AB[:, s, dst + 1, :], in_=modT[:, s, srcq - 1, :]))(s, dst, srcq),
                        r=["modT"], w=["AB"])
            P.tag = "rot"
            posf = sb(s0, "posf", [128, NSEQ * NT], F32)
            ang = sb(s0, "ang", [128, NSEQ * NT, 32], F32)
            kf = sb(s0, "kf", [128, NSEQ * NT, 32], F32)
            ki = sb(s0, "ki", [128, NSEQ * NT, 32], I32)
            mk = sb(s0, "mk", [128, NSEQ * NT, 32], F32)
            P.add("dve", lambda e: e.tensor_copy(out=posf[:], in_=posi[:]), r=["posi"], w=["posf"])
            NTT = NSEQ * NT
            P.add("dve", lambda e: e.tensor_tensor(out=ang[:, :, 16:32], in0=posf[:].unsqueeze(2).to_broadcast([128, NTT, 16]),
                                                   in1=invf[:].unsqueeze(1).to_broadcast([128, NTT, 16]), op=ALU.mult),
                  r=["posf", "invf"], w=["ang"])
            P.add("dve", lambda e: e.tensor_scalar(out=ang[:, :, 0:16], in0=ang[:, :, 16:32], scalar1=math.pi / 2.0,
                                                   scalar2=None, op0=ALU.add), r=["ang"], w=["ang"])
            P.add("dve", lambda e: e.tensor_scalar(out=kf[:], in0=ang[:], scalar1=1.0 / TWO_PI, scalar2=None, op0=ALU.mult),
                  r=["ang"], w=["kf"])
            P.add("dve", lambda e: e.tensor_copy(out=ki[:], in_=kf[:]), r=["kf"], w=["ki"])
            P.add("dve", lambda e: e.tensor_copy(out=kf[:], in_=ki[:]), r=["ki"], w=["kf"])
            P.add("dve", lambda e: e.scalar_tensor_tensor(out=ang[:], in0=kf[:], scalar=-TWO_PI, in1=ang[:],
                                                          op0=ALU.mult, op1=ALU.add), r=["kf", "ang"], w=["ang"])
            P.add("dve", lambda e: e.tensor_scalar(out=mk[:], in0=ang[:], scalar1=math.pi, scalar2=-TWO_PI,
                                                   op0=ALU.is_gt, op1=ALU.mult), r=["ang"], w=["mk"])
            P.add("dve", lambda e: e.tensor_tensor(out=ang[:], in0=ang[:], in1=mk[:], op=ALU.add), r=["ang", "mk"], w=["ang"])
            P.add("dve", lambda e: e.tensor_scalar(out=mk[:], in0=ang[:], scalar1=-math.pi, scalar2=TWO_PI,
                                                   op0=ALU.is_lt, op1=ALU.mult), r=["ang"], w=["mk"])
            P.add("dve", lambda e: e.tensor_tensor(out=ang[:], in0=ang[:], in1=mk[:], op=ALU.add), r=["ang", "mk"], w=["ang"])
            P.add("dve", lambda e: e.tensor_scalar(out=ang[:], in0=ang[:], scalar1=math.pi, scalar2=-math.pi,
                                                   op0=ALU.min, op1=ALU.max), r=["ang"], w=["ang"])
            P.add("act", lambda e: e.activation(out=cs_all[:], in_=ang[:], func=AF.Sin), r=["ang"], w=["cs_all"])
            P.tag = None
            P.barrier()
            s0.close()

        if True:
            P.tag = "wlA"
            win = sb(sa, "win", [128, 8, D_IN], BF16)
            wq = sb(sa, "wq", [128, 2, 768], BF16)
            wkv = sb(sa, "wkv", [128, 1024], BF16)
            wstage = sb(sa, "wstage", [128, 2, 768], F32)
            wstage2 = sb(sa, "wstage2", [128, 1024], F32)
            WIN_KEYS = []
            for pi_, (c0, c1) in enumerate(((0, 512), (512, 1024), (1024, 1536), (1536, 1952))):
                P.add("pool", (lambda c0, c1: lambda e: e.dma_start(out=win[:, :, c0:c1], in_=win_d[:, c0:c1].rearrange("(j p) n -> p j n", p=128)))(c0, c1),
                      r=(["win_p%d" % (pi_ - 2)] if pi_ >= 2 else []), w=["win_p%d" % pi_], chan="win_c%d" % (pi_ % 2))
                WIN_KEYS.append("win_p%d" % pi_)
            P.tag = "wlA2"
            P.add("sp", lambda e: e.dma_start(out=wstage[:], in_=wq_d.rearrange("(j p) n -> p j n", p=128)),
                  w=["wstage"], chan="small3", waitall=True)
            P.add("sp", lambda e: e.dma_start(out=wstage2[:], in_=wkv_d), w=["wstage2"], chan="small3", waitall=True)
            for j in range(2):
                P.add("dve", (lambda j: lambda e: e.tensor_scalar(
                    out=wq[:, j, :], in0=wstage[:, j, :], scalar1=gfeat[:, 16 + j:17 + j], scalar2=16.0,
                    op0=ALU.mult, op1=ALU.mult))(j), r=["wstage", "gfeat"], w=["wq"])
            P.add("dve", lambda e: e.tensor_scalar(out=wkv[:], in0=wstage2[:], scalar1=gfeat[:, 18:19],
                                                   scalar2=math.sqrt(128.0), op0=ALU.mult, op1=ALU.mult),
                  r=["wstage2", "gfeat"], w=["wkv"])

            P.tag = "wlA3"
            xt = [sb(sa, "xt%d" % i, [128, D], F32) for i in range(2)]
            xn = [sb(sa, "xn%d" % i, [128, D], BF16) for i in range(2)]
            hT = [sb(sa, "hT%d" % i, [128, 8, 128], BF16) for i in range(2)]
            st = sb(sa, "stA", [128, 2, 40], F32)
            lat = [sb(sa, "lat%d" % i, [128, 384], BF16) for i in range(2)]
            latT = [sb(sa, "latT%d" % i, [128, 3, 128], BF16) for i in range(2)]
            kr = [sb(sa, "kr%d" % i, [128, 4, 32], F32) for i in range(2)]
            qn = [sb(sa, "qn%d" % i, [128, 8, 96], F32) for i in range(2)]
            qsq = [sb(sa, "qsq%d" % i, [128, 8, 96], F32) for i in range(2)]
            qb = [sb(sa, "qb%d" % i, [128, 8, 96], BF16) for i in range(2)]
            kn = [sb(sa, "kn%d" % i, [128, 8, 64], F32) for i in range(2)]
            ksq = [sb(sa, "ksq%d" % i, [128, 8, 64], F32) for i in range(2)]
            kb = [sb(sa, "kb%d" % i, [128, 8, 96], BF16) for i in range(2)]
            rt = [sb(sa, "rt%d" % i, [128, 8, 32], F32) for i in range(2)]
            cqn = [sb(sa, "cqn%d" % i, [128, 8, 64], F32) for i in range(2)]
            cqs = [sb(sa, "cqs%d" % i, [128, 8, 64], F32) for i in range(2)]
            cqb = [sb(sa, "cqb%d" % i, [128, 8, 64], BF16) for i in range(2)]
            ckn = [sb(sa, "ckn%d" % i, [128, 8, 64], F32) for i in range(2)]
            cks = [sb(sa, "cks%d" % i, [128, 8, 64], F32) for i in range(2)]
            ckb = [sb(sa, "ckb%d" % i, [128, 8, 64], BF16) for i in range(2)]
            qT_st = [sb(sa, "qTst%d" % i, [96, 8, 512], BF16) for i in range(2)]
            kT_st = [sb(sa, "kTst%d" % i, [96, 8, 512], BF16) for i in range(2)]
            cqT_st = [sb(sa, "cqTst%d" % i, [64, 8, 512], BF16) for i in range(2)]
            ckT_st = [sb(sa, "ckTst%d" % i, [64, 8, 512], BF16) for i in range(2)]
            V_st = [sb(sa, "Vst%d" % i, [128, 4, 8, 65], BF16) for i in range(2)]
            cV_st = [sb(sa, "cVst%d" % i, [128, 4, 8, 65], BF16) for i in range(2)]
            for i in range(2):
                P.add("pool", (lambda i: lambda e: e.memset(V_st[i][:, :, :, 64:65], 1.0))(i), w=["Vst%d" % i])
                P.add("pool", (lambda i: lambda e: e.memset(cV_st[i][:, :, :, 64:65], 1.0))(i), w=["cVst%d" % i])

            P.tag = None

            def rstd_from_ssq(sl, col, n, tag):
                kst = "st%d_%s" % (sl, tag)
                P.add("act", lambda e: e.activation(out=st[:, sl, col:col + 1], in_=st[:, sl, col:col + 1], func=AF.Sqrt,
                                                    bias=float(n * EPS), scale=1.0), r=[kst], w=[kst])
                P.add("dve", lambda e: e.reciprocal(out=st[:, sl, col:col + 1], in_=st[:, sl, col:col + 1]), r=[kst], w=[kst])
                return kst

            def rstd_vec(sl, c0, nh, n, tag):
                kst = "st%d_%s" % (sl, tag)
                P.add("act", lambda e: e.activation(out=st[:, sl, c0:c0 + nh], in_=st[:, sl, c0:c0 + nh], func=AF.Sqrt,
                                                    bias=float(n * EPS), scale=1.0), r=[kst], w=[kst])
                P.add("dve", lambda e: e.reciprocal(out=st[:, sl, c0:c0 + nh], in_=st[:, sl, c0:c0 + nh]), r=[kst], w=[kst])
                return kst

            def rope(eng, src3, dst3, cs, keys_r, keys_w, tmp, nh, ktmp):
                cosb = cs[:, 0:16].unsqueeze(1).to_broadcast([128, nh, 16])
                sinb = cs[:, 16:32].unsqueeze(1).to_broadcast([128, nh, 16])
                P.add(eng, lambda e: e.tensor_tensor(out=tmp[:, :, 0:16], in0=src3[:, :, 16:32], in1=sinb, op=ALU.mult),
                      r=keys_r, w=[ktmp])
                P.add(eng, lambda e: e.tensor_tensor(out=tmp[:, :, 16:32], in0=src3[:, :, 0:16], in1=sinb, op=ALU.mult),
                      r=keys_r + [ktmp], w=[ktmp])
                P.add(eng, lambda e: e.tensor_tensor(out=src3[:, :, 0:16], in0=src3[:, :, 0:16], in1=cosb, op=ALU.mult),
                      r=keys_r + [ktmp], w=keys_r[:1])
                P.add(eng, lambda e: e.tensor_tensor(out=src3[:, :, 16:32], in0=src3[:, :, 16:32], in1=cosb, op=ALU.mult),
                      r=keys_r + [ktmp], w=keys_r[:1])
                P.add(eng, lambda e: e.tensor_tensor(out=dst3[:, :, 0:16], in0=src3[:, :, 0:16], in1=tmp[:, :, 0:16], op=ALU.subtract),
                      r=keys_r + [ktmp], w=keys_w)
                P.add(eng, lambda e: e.tensor_tensor(out=dst3[:, :, 16:32], in0=src3[:, :, 16:32], in1=tmp[:, :, 16:32], op=ALU.add),
                      r=keys_r + [ktmp], w=keys_w)

            def prep_tile(g):
                s, tt = divmod(g, NT)
                jb, i4 = divmod(tt, 4)
                sl = g % 2
                bs = (g // 4) % 2
                S_ = str(sl)
                kxt, kxn, khT = "xt" + S_, "xn" + S_, "hT" + S_
                P.add("sp", lambda e: e.dma_start(out=xt[sl][:], in_=x_d[s, tt * 128:(tt + 1) * 128, :]), w=[kxt], chan="xt" + S_)
                P.add("act", lambda e: e.activation(out=junk[:], in_=xt[sl][:], func=AF.Square, accum_out=st[:, sl, 0:1]),
                      r=[kxt], w=["st%s_x" % S_])
                kst = rstd_from_ssq(sl, 0, D, "x")
                P.add("act", lambda e: e.activation(out=xn[sl][:], in_=xt[sl][:], func=AF.Copy, scale=st[:, sl, 0:1]),
                      r=[kxt, kst], w=[kxn])
                yield
                b0 = BK.get(hold=True)
                for c in range(8):
                    P.add("pe", (lambda c: lambda e: e.transpose(out=banks_b[b0][:, c * 128:(c + 1) * 128],
                                                                  in_=xn[sl][:, c * 128:(c + 1) * 128], identity=ident[:]))(c),
                          r=[kxn, "ident"], w=[bk(b0)])
                tp3 = banks_b[b0][:, 0:1024].rearrange("p (c t) -> p c t", c=8)
                P.add("dve", lambda e: e.tensor_tensor(out=hT[sl][:], in0=tp3, in1=AB[:, s, 0, :].unsqueeze(2).to_broadcast([128, 8, 128]),
                                                       op=ALU.mult), r=[bk(b0), "AB"], w=[khT])
                BK.release(b0)
                P.add("pool", lambda e: e.tensor_tensor(out=hT[sl][:], in0=hT[sl][:], in1=AB[:, s, 1, :].unsqueeze(2).to_broadcast([128, 8, 128]),
                                                        op=ALU.add), r=[khT, "AB"], w=[khT])
                cols = [(0, 416), (416, 928), (928, 1440), (1440, 1952)]

                def proj_group(bi):
                    bb_ = BK.get(hold=True)
                    c0, c1 = cols[bi]
                    for k in range(8):
                        P.add("pe", (lambda k: lambda e: e.matmul(
                            banks_f[bb_][:, 0:c1 - c0], lhsT=hT[sl][:, k, :], rhs=win[:, k, c0:c1],
                            start=(k == 0), stop=(k == 7)))(k),
                            r=[khT] + WIN_KEYS, w=[bk(bb_)])
                    return bb_
                yield
                pb0 = proj_group(0)
                pl = banks_f[pb0]
                kl = bk(pb0)
                yield
                P.add("act", lambda e: e.activation(out=junk[:, 0:256], in_=pl[:, 0:256], func=AF.Square, accum_out=st[:, sl, 1:2]),
                      r=[kl], w=["st%s_ql" % S_])
                P.add("act", lambda e: e.activation(out=junk[:, 256:384], in_=pl[:, 256:384], func=AF.Square, accum_out=st[:, sl, 2:3]),
                      r=[kl], w=["st%s_kvl" % S_])
                k1 = rstd_from_ssq(sl, 1, 256, "ql")
                k2 = rstd_from_ssq(sl, 2, 128, "kvl")
                klat = "lat" + S_
                P.add("dve", lambda e: e.tensor_scalar(out=lat[sl][:, 0:256], in0=pl[:, 0:256], scalar1=st[:, sl, 1:2], scalar2=None,
                                                       op0=ALU.mult), r=[kl, k1], w=[klat + "a"])
                P.add("dve", lambda e: e.tensor_scalar(out=lat[sl][:, 256:384], in0=pl[:, 256:384], scalar1=st[:, sl, 2:3], scalar2=None,
                                                       op0=ALU.mult), r=[kl, k2], w=[klat + "b"])
                kkr = "kr" + S_
                P.add("dve", lambda e: e.tensor_tensor(out=kr[sl][:, 0, :], in0=pl[:, 384:416], in1=grow[:, 160:192], op=ALU.mult),
                      r=[kl, "grow"], w=[kkr])
                P.add("act", lambda e: e.activation(out=junk[:, 512:544], in_=pl[:, 384:416], func=AF.Square, accum_out=st[:, sl, 3:4]),
                      r=[kl], w=["st%s_kr" % S_])
                BK.release(pb0)
                yield
                b1 = BK.get(hold=True)
                for c in range(3):
                    P.add("pe", (lambda c: lambda e: e.transpose(out=banks_b[b1][:, c * 128:(c + 1) * 128],
                                                                  in_=lat[sl][:, c * 128:(c + 1) * 128], identity=ident[:]))(c),
                          r=[klat + "a", klat + "b", "ident"], w=[bk(b1)])
                klT = "latT" + S_
                P.add("act", lambda e: e.activation(out=latT[sl][:].rearrange("p c t -> p (c t)"), in_=banks_b[b1][:, 0:384], func=AF.Copy),
                      r=[bk(b1)], w=[klT])
                BK.release(b1)
                yield
                bq0, bq1 = BK.get(hold=True), BK.get(hold=True)
                for (bb, c0, c1) in ((bq0, 0, 480), (bq1, 480, 768)):
                    for k in range(2):
                        P.add("pe", (lambda bb, c0, c1, k: lambda e: e.matmul(
                            banks_f[bb][:, 0:c1 - c0], lhsT=latT[sl][:, k, :], rhs=wq[:, k, c0:c1], start=(k == 0), stop=(k == 1)))(bb, c0, c1, k),
                            r=[klT, "wq"], w=[bk(bb)])
                bkv0, bkv1 = BK.get(hold=True), BK.get(hold=True)
                for (bb, c0) in ((bkv0, 0), (bkv1, 512)):
                    P.add("pe", (lambda bb, c0: lambda e: e.matmul(
                        banks_f[bb][:, 0:512], lhsT=latT[sl][:, 2, :], rhs=wkv[:, c0:c0 + 512], start=True, stop=True))(bb, c0),
                        r=[klT, "wkv"], w=[bk(bb)])
                yield
                cs = cs_all[:, g, :]
                kqn, kqs, kqb = "qn" + S_, "qsq" + S_, "qb" + S_
                q0 = banks_f[bq0][:, 0:480].rearrange("p (h d) -> p h d", h=5)
                q1 = banks_f[bq1][:, 0:288].rearrange("p (h d) -> p h d", h=3)
                P.add("act", lambda e: e.activation(out=qn[sl][:, 0:5, :], in_=q0, func=AF.Copy), r=[bk(bq0)], w=[kqn + "a"])
                P.add("act", lambda e: e.activation(out=qn[sl][:, 5:8, :], in_=q1, func=AF.Copy), r=[bk(bq1)], w=[kqn + "b"])
                BK.release(bq0)
                BK.release(bq1)
                yield
                P.add("pool", lambda e: e.tensor_tensor(out=qsq[sl][:], in0=qn[sl][:], in1=qn[sl][:], op=ALU.mult),
                      r=[kqn + "a", kqn + "b"], w=[kqs])
                P.add("dve", lambda e: e.tensor_reduce(out=st[:, sl, 8:16], in_=qsq[sl][:], axis=AX.X, op=ALU.add),
                      r=[kqs], w=["st%s_q" % S_])
                k3 = rstd_vec(sl, 8, 8, 96, "q")
                P.add("dve", lambda e: e.tensor_tensor(out=qn[sl][:], in0=qn[sl][:], in1=st[:, sl, 8:16].unsqueeze(2).to_broadcast([128, 8, 96]),
                                                       op=ALU.mult), r=[kqn + "a", kqn + "b", k3], w=[kqn])
                P.add("pool", lambda e: e.tensor_tensor(out=qn[sl][:], in0=qn[sl][:], in1=grow[:, 0:96].unsqueeze(1).to_broadcast([128, 8, 96]),
                                                        op=ALU.mult), r=[kqn, "grow"], w=[kqn])
                P.add("act", lambda e: e.activation(out=qb[sl][:, :, 0:64], in_=qn[sl][:, :, 0:64], func=AF.Copy), r=[kqn], w=[kqb + "n"])
                yield
                rope("pool", qn[sl][:, :, 64:96], qb[sl][:, :, 64:96], cs, [kqn, "cs_all"], [kqb + "r"], rt[sl], 8, "rt" + S_)
                kkn, kks, kkb = "kn" + S_, "ksq" + S_, "kb" + S_
                kv0 = banks_f[bkv0][:, 0:512].rearrange("p (h d) -> p h d", h=4)
                kv1 = banks_f[bkv1][:, 0:512].rearrange("p (h d) -> p h d", h=4)
                P.add("act", lambda e: e.activation(out=kn[sl][:, 0:4, :], in_=kv0[:, :, 0:64], func=AF.Copy), r=[bk(bkv0)], w=[kkn + "a"])
                P.add("act", lambda e: e.activation(out=kn[sl][:, 4:8, :], in_=kv1[:, :, 0:64], func=AF.Copy), r=[bk(bkv1)], w=[kkn + "b"])
                P.add("act", lambda e: e.activation(out=V_st[bs][:, i4, 0:4, 0:64], in_=kv0[:, :, 64:128], func=AF.Copy),
                      r=[bk(bkv0)], w=["Vst%d" % bs])
                P.add("act", lambda e: e.activation(out=V_st[bs][:, i4, 4:8, 0:64], in_=kv1[:, :, 64:128], func=AF.Copy),
                      r=[bk(bkv1)], w=["Vst%d" % bs])
                BK.release(bkv0)
                BK.release(bkv1)
                yield
                P.add("pool", lambda e: e.tensor_tensor(out=ksq[sl][:], in0=kn[sl][:], in1=kn[sl][:], op=ALU.mult),
                      r=[kkn + "a", kkn + "b"], w=[kks])
                P.add("dve", lambda e: e.tensor_reduce(out=st[:, sl, 16:24], in_=ksq[sl][:], axis=AX.X, op=ALU.add),
                      r=[kks], w=["st%s_k" % S_])
                P.add("dve", lambda e: e.tensor_scalar(out=st[:, sl, 16:24], in0=st[:, sl, 16:24], scalar1=st[:, sl, 3:4], scalar2=None,
                                                       op0=ALU.add), r=["st%s_k" % S_, "st%s_kr" % S_], w=["st%s_k" % S_])
                k4 = rstd_vec(sl, 16, 8, 96, "k")
                yield
                P.add("dve", lambda e: e.tensor_tensor(out=kn[sl][:], in0=kn[sl][:], in1=st[:, sl, 16:24].unsqueeze(2).to_broadcast([128, 8, 64]),
                                                       op=ALU.mult), r=[kkn + "a", kkn + "b", k4], w=[kkn])
                P.add("pool", lambda e: e.tensor_tensor(out=kb[sl][:, :, 0:64], in0=kn[sl][:], in1=grow[:, 96:160].unsqueeze(1).to_broadcast([128, 8, 64]),
                                                        op=ALU.mult), r=[kkn, "grow"], w=[kkb + "n"])
                rope("dve", kr[sl][:, 0:1, :], kr[sl][:, 1:2, :], cs, [kkr, "cs_all"], [kkr + "o"], kr[sl][:, 2:3, :], 1, kkr + "t")
                P.add("dve", lambda e: e.tensor_tensor(out=kb[sl][:, :, 64:96], in0=kr[sl][:, 1:2, :].to_broadcast([128, 8, 32]),
                                                       in1=st[:, sl, 16:24].unsqueeze(2).to_broadcast([128, 8, 32]), op=ALU.mult),
                      r=[kkr + "o", k4], w=[kkb + "r"])
                for (bi_, dn, dsq, db, col, gc0, tag) in (
                        (1, cqn, cqs, cqb, 24, 192, "cq"), (2, ckn, cks, ckb, 32, 256, "ck")):
                    kdn, kds, kdb = tag + "n" + S_, tag + "s" + S_, tag + "b" + S_
                    yield
                    bbx = proj_group(bi_)
                    P.add("act", (lambda bbx, dn: lambda e: e.activation(out=dn[sl][:].rearrange("p h d -> p (h d)"), in_=banks_f[bbx][:, 0:512], func=AF.Copy))(bbx, dn),
                          r=[bk(bbx)], w=[kdn])
                    BK.release(bbx)
                    yield
                    P.add("pool", (lambda dn, dsq: lambda e: e.tensor_tensor(out=dsq[sl][:], in0=dn[sl][:], in1=dn[sl][:], op=ALU.mult))(dn, dsq),
                          r=[kdn], w=[kds])
                    P.add("dve", (lambda dsq, col: lambda e: e.tensor_reduce(out=st[:, sl, col:col + 8], in_=dsq[sl][:], axis=AX.X, op=ALU.add))(dsq, col),
                          r=[kds], w=["st%s_%s" % (S_, tag)])
                    k5 = rstd_vec(sl, col, 8, 64, tag)
                    P.add("dve", (lambda dn, col: lambda e: e.tensor_tensor(out=dn[sl][:], in0=dn[sl][:],
                                                                           in1=st[:, sl, col:col + 8].unsqueeze(2).to_broadcast([128, 8, 64]), op=ALU.mult))(dn, col),
                          r=[kdn, k5], w=[kdn])
                    P.add("pool", (lambda dn, db, gc0: lambda e: e.tensor_tensor(out=db[sl][:], in0=dn[sl][:],
                                                                                in1=grow[:, gc0:gc0 + 64].unsqueeze(1).to_broadcast([128, 8, 64]), op=ALU.mult))(dn, db, gc0),
                          r=[kdn, "grow"], w=[kdb])
                yield
                bbv = proj_group(3)
                P.add("act", lambda e: e.activation(out=cV_st[bs][:, i4, :, 0:64], in_=banks_f[bbv][:, 0:512].rearrange("p (h d) -> p h d", h=8), func=AF.Copy),
                      r=[bk(bbv)], w=["cVst%d" % bs])
                BK.release(bbv)
                for (srcb, keys, dst, kdst, dd) in ((qb, [kqb + "n", kqb + "r"], qT_st, "qTst%d" % bs, 96),
                                                    (kb, [kkb + "n", kkb + "r"], kT_st, "kTst%d" % bs, 96),
                                                    (cqb, ["cqb" + S_], cqT_st, "cqTst%d" % bs, 64),
                                                    (ckb, ["ckb" + S_], ckT_st, "ckTst%d" % bs, 64)):
                    yield
                    bt = BK.get(hold=True)
                    for h in range(8):
                        P.add("pe", (lambda srcb, bt, h, dd: lambda e: e.transpose(
                            out=banks_b[bt][0:dd, h * 128:(h + 1) * 128], in_=srcb[sl][:, h, :], identity=ident[:]))(srcb, bt, h, dd),
                            r=keys + ["ident"], w=[bk(bt)])
                    P.add("act", (lambda bt, dst, dd: lambda e: e.activation(
                        out=dst[bs][:, :, i4 * 128:(i4 + 1) * 128], in_=banks_b[bt][0:dd, 0:1024].rearrange("p (h t) -> p h t", h=8), func=AF.Copy))(bt, dst, dd),
                        r=[bk(bt)], w=[kdst])
                    BK.release(bt)
                if i4 == 3:
                    t0 = jb * 512
                    P.add("sp", lambda e: e.dma_start(out=qT_d[s, :, :, t0:t0 + 512].rearrange("h d t -> d h t"), in_=qT_st[bs][:]),
                          r=["qTst%d" % bs], w=["qT_d%d" % s], chan="stq%d" % bs)
                    P.add("sp", lambda e: e.dma_start(out=kT_d[s, :, :, t0:t0 + 512].rearrange("h d t -> d h t"), in_=kT_st[bs][:]),
                          r=["kTst%d" % bs], w=["kT_d%d" % s], chan="stk%d" % bs)
                    P.add("sp", lambda e: e.dma_start(out=cqT_d[s, :, :, t0:t0 + 512].rearrange("h d t -> d h t"), in_=cqT_st[bs][:]),
                          r=["cqTst%d" % bs], w=["cqT_d%d" % s], chan="stcq%d" % bs)
                    P.add("sp", lambda e: e.dma_start(out=ckT_d[s, :, :, t0:t0 + 512].rearrange("h d t -> d h t"), in_=ckT_st[bs][:]),
                          r=["ckTst%d" % bs], w=["ckT_d%d" % s], chan="stck%d" % bs)
                    P.add("sp", lambda e: e.dma_start(out=V_d[s, t0:t0 + 512, :].rearrange("(k p) c -> p k c", p=128),
                                                      in_=V_st[bs][:].rearrange("p k h c -> p k (h c)")),
                          r=["Vst%d" % bs], w=["V_d%d" % s], chan="stv%d" % bs)
                    P.add("sp", lambda e: e.dma_start(out=cV_d[s, t0:t0 + 512, :].rearrange("(k p) c -> p k c", p=128),
                                                      in_=cV_st[bs][:].rearrange("p k h c -> p k (h c)")),
                          r=["cVst%d" % bs], w=["cV_d%d" % s], chan="stcv%d" % bs)

            ntiles = NSEQ * NT if stop not in ("setup",) else 0
            active = []
            nxt = 0
            INFL = 2
            while nxt < ntiles or active:
                while nxt < ntiles and len(active) < INFL:
                    active.append(prep_tile(nxt))
                    nxt += 1
                for gen in list(active):
                    try:
                        next(gen)
                    except StopIteration:
                        active.remove(gen)
            P.barrier()
            sa.close()

        with ExitStack() as sbk:
            kT_all = sb(sbk, "kT_all", [96, 8, S], BF16)
            V_all = sb(sbk, "V_all", [128, NT, 8 * 65], BF16)
            ckT_all = sb(sbk, "ckT_all", [64, 8, S], BF16)
            cV_all = sb(sbk, "cV_all", [128, NT, 8 * 65], BF16)
            qT_b = [sb(sbk, "qT_b%d" % i, [96, 8, 512], BF16) for i in range(2)]
            cqT_b = [sb(sbk, "cqT_b%d" % i, [64, 8, 512], BF16) for i in range(2)]
            otn = [sb(sbk, "otn%d" % i, [64, 16, 512], BF16) for i in range(1)]
            Ef = sb(sbk, "Ef", [128, 8, 2, 128], F32)
            bfar = sb(sbk, "bfar", [128, 8], F32)
            NPT = 4
            Pt = [sb(sbk, "Pt%d" % i, [128, 512], BF16) for i in range(NPT)]
            PA = [sb(sbk, "PA%d" % i, [128, 384], BF16) for i in range(2)]
            PB = [sb(sbk, "PB%d" % i, [128, 256], F32) for i in range(2)]
            PBb = [sb(sbk, "PBb%d" % i, [128, 256], BF16) for i in range(2)]
            NRZ = 4
            ots = [sb(sbk, "ots%d" % i, [128, 512], F32) for i in range(NRZ)]
            rzb = [sb(sbk, "rzb%d" % i, [128, 2, 512], BF16) for i in range(NRZ)]
            sel64b = sb(sbk, "sel64b", [128, 64], BF16)
            P.tag = "wlB"
            P.add("sp", lambda e: e.dma_start(out=Ef[:], in_=bias34_d), w=["Ef"], chan="small4", waitall=True)
            P.add("sp", lambda e: e.dma_start(out=bfar[:], in_=bfar_d), w=["bfar"], chan="small4", waitall=True)
            P.add("act", lambda e: e.activation(out=Ef[:], in_=Ef[:], func=AF.Exp), r=["Ef"], w=["Ef"])
            P.add("pool", lambda e: e.memset(Ef[64:128, :, 1, 0:64], 0.0), r=["Ef"], w=["Ef"])
            for i in range(NRZ):
                P.add("pool", (lambda i: lambda e: e.memset(rzb[i][:], 0.0))(i), w=["rzb%d" % i])
            P.add("pool", lambda e: e.tensor_copy(out=sel64b[:], in_=sel64[:]), r=["sel64"], w=["sel64b"])
            P.tag = None
            cnt = {"pt": 0, "pa": 0, "rz": 0}

            def normalize_head(bo, width):
                ri = cnt["rz"] % NRZ
                cnt["rz"] += 1
                P.add("dve", lambda e: e.tensor_copy(out=ots[ri][0:65, 0:width], in_=banks_f[bo][0:65, 0:width]),
                      r=[bk(bo)], w=["ots%d" % ri])
                BK.release(bo)
                P.add("act", lambda e: e.activation(out=ots[ri][64:65, 0:width], in_=ots[ri][64:65, 0:width], func=AF.Ln),
                      r=["ots%d" % ri], w=["ots%d" % ri])
                P.add("act", lambda e: e.activation(out=ots[ri][64:65, 0:width], in_=ots[ri][64:65, 0:width], func=AF.Exp, scale=-1.0),
                      r=["ots%d" % ri], w=["ots%d" % ri])
                P.add("dve", lambda e: e.tensor_copy(out=rzb[ri][64:65, 0, 0:width], in_=ots[ri][64:65, 0:width]),
                      r=["ots%d" % ri], w=["rzb%d" % ri])
                P.add("dve", lambda e: e.tensor_tensor(out=rzb[ri][64:65, 1, 0:width], in0=ots[ri][64:65, 0:width],
                                                       in1=rzb[ri][64:65, 0, 0:width], op=ALU.subtract),
                      r=["ots%d" % ri, "rzb%d" % ri], w=["rzb%d" % ri])
                return ri

            def normalize_tail(ri, width, dst_ap, kdst):
                bb = BK.get(hold=True)
                for pl_ in range(2):
                    P.add("pe", (lambda pl_: lambda e: e.matmul(banks_f[bb][0:64, 0:width], lhsT=sel64b[:, 0:64], rhs=rzb[ri][:, pl_, 0:width],
                                                                start=(pl_ == 0), stop=(pl_ == 1)))(pl_),
                          r=["rzb%d" % ri, "sel64b"], w=[bk(bb)])
                P.add("dve", lambda e: e.tensor_tensor(out=dst_ap, in0=banks_f[bb][0:64, 0:width], in1=ots[ri][0:64, 0:width], op=ALU.mult),
                      r=[bk(bb), "ots%d" % ri], w=[kdst])
                BK.release(bb)

            def mla_gen(s, j, h, qs):
                kq = "qT_b%d" % qs
                nkt = 4 * j + 4
                bo = BK.get(hold=True)
                tiles = []
                for kt in range(nkt):
                    r_ = kt - 4 * j
                    c0 = 128 * r_ if r_ > 0 else 0
                    tiles.append((kt, c0, r_ >= 0))
                sbank = {}

                def emit_s(idx):
                    kt, c0, diag = tiles[idx]
                    b = BK.get(hold=True)
                    sbank[idx] = b
                    P.add("pe", lambda e: e.matmul(banks_f[b][:, 0:512 - c0], lhsT=kT_all[:, h, kt * 128:(kt + 1) * 128],
                                                   rhs=qT_b[qs][:, h, c0:512], start=True, stop=True),
                          r=["kT_all", kq], w=[bk(b)])

                def emit_rest(idx):
                    kt, c0, diag = tiles[idx]
                    b = sbank[idx]
                    pi = cnt["pt"] % NPT
                    cnt["pt"] += 1
                    kp = "Pt%d" % pi
                    w_ = 512 - c0
                    P.add("act", lambda e: e.activation(out=Pt[pi][:, 0:w_], in_=banks_f[b][:, 0:w_], func=AF.Exp), r=[bk(b)], w=[kp])
                    BK.release(b)
                    if diag:
                        P.add("pool", lambda e: e.memset(Pt[pi][64:128, 0:64], 0.0), r=[kp], w=[kp])
                    P.add("pe", lambda e: e.matmul(banks_f[bo][0:65, c0:512], lhsT=V_all[:, kt, h * 65:(h + 1) * 65], rhs=Pt[pi][:, 0:w_],
                                                   start=(idx == 0), stop=(idx == nkt - 1)),
                          r=[kp, "V_all"], w=[bk(bo)])

                LOOK = 2
                for idx in range(min(LOOK, nkt)):
                    emit_s(idx)
                yield
                for idx in range(nkt):
                    emit_rest(idx)
                    if idx + LOOK < nkt:
                        emit_s(idx + LOOK)
                    yield
                ri = normalize_head(bo, 512)
                pend_m.append((ri, 512, otn[0][:, h, :], "otn0h%d" % h))
                yield

            def ca_qtile(j, h, qs, bo, i):
                kq = "cqT_b%d" % qs
                gi = 4 * j + i
                tmin = max(0, 4 - gi)
                pai = cnt["pa"] % 2
                cnt["pa"] += 1
                ba = BK.get(hold=True) if tmin <= 2 else None
                bb = BK.get(hold=True)

                def s_mm(t):
                    ktile = gi - 4 + t
                    if t <= 2:
                        dst = banks_f[ba][:, t * 128:(t + 1) * 128]
                        kb_ = bk(ba)
                    else:
                        dst = banks_f[bb][:, (t - 3) * 128:(t - 2) * 128]
                        kb_ = bk(bb)
                    P.add("pe", lambda e: e.matmul(dst, lhsT=ckT_all[:, h, ktile * 128:(ktile + 1) * 128],
                                                   rhs=cqT_b[qs][:, h, i * 128:(i + 1) * 128], start=True, stop=True),
                          r=["ckT_all", kq], w=[kb_])

                def pv_mm(t):
                    ktile = gi - 4 + t
                    if t <= 2:
                        rhs = PA[pai][:, t * 128:(t + 1) * 128]
                        kr_ = "PA%d" % pai
                    else:
                        rhs = PBb[pai][:, (t - 3) * 128:(t - 2) * 128]
                        kr_ = "PBb%d" % pai
                    P.add("pe", lambda e: e.matmul(banks_f[bo][0:65, i * 128:(i + 1) * 128],
                                                   lhsT=cV_all[:, ktile, h * 65:(h + 1) * 65], rhs=rhs,
                                                   start=(t == tmin), stop=(t == 4)),
                          r=[kr_, "cV_all"], w=[bk(bo)])

                for t in range(tmin, 5):
                    s_mm(t)
                yield
                if ba is not None:
                    a0 = tmin * 128
                    P.add("act", lambda e: e.activation(out=PA[pai][:, a0:384], in_=banks_f[ba][:, a0:384], func=AF.Exp,
                                                        bias=bfar[:, h:h + 1], scale=1.0), r=[bk(ba), "bfar"], w=["PA%d" % pai])
                    BK.release(ba)
                    if tmin == 0:
                        P.add("pool", lambda e: e.memset(PA[pai][0:64, 64:128], 0.0), r=["PA%d" % pai], w=["PA%d" % pai])
                b0 = 0 if tmin <= 3 else 128
                P.add("act", lambda e: e.activation(out=PB[pai][:, b0:256], in_=banks_f[bb][:, b0:256], func=AF.Exp), r=[bk(bb)], w=["PB%d" % pai])
                BK.release(bb)
                P.add("pool", lambda e: e.tensor_tensor(out=PBb[pai][:, b0:256], in0=PB[pai][:, b0:256],
                                                        in1=Ef[:, h, :, :].rearrange("p t q -> p (t q)")[:, b0:256], op=ALU.mult),
                      r=["PB%d" % pai, "Ef"], w=["PBb%d" % pai])
                yield
                for t in range(tmin, 5):
                    pv_mm(t)
                yield

            def ca_gen(s, j, h, qs):
                bo = BK.get(hold=True)
                for i in range(4):
                    yield from ca_qtile(j, h, qs, bo, i)
                ri = normalize_head(bo, 512)
                pend_c.append((ri, 512, otn[0][:, 8 + h, :], "otn0h%d" % (8 + h)))
                yield

            pend_m, pend_c = [], []

            def stream(gens, pend, delay):
                for g in gens:
                    n = 0
                    old = list(pend)
                    del pend[:]
                    for _ in g:
                        n += 1
                        yield
                        if n == delay and old:
                            for t in old:
                                normalize_tail(*t)
                            old = []
                            yield
                    if old:
                        for t in old:
                            normalize_tail(*t)
                        yield
                for t in pend:
                    normalize_tail(*t)
                del pend[:]
                yield

            def interleave(ga, gb, ra, rb):
                alive_a = alive_b = True
                while alive_a or alive_b:
                    for _ in range(ra):
                        if alive_a:
                            try:
                                next(ga)
                            except StopIteration:
                                alive_a = False
                    for _ in range(rb):
                        if alive_b:
                            try:
                                next(gb)
                            except StopIteration:
                                alive_b = False

            def load_seq(s):
                for hh in range(2):
                    P.add("sp", lambda e: e.dma_start(out=kT_all[:, 4 * hh:4 * hh + 4, :], in_=kT_d[s, 4 * hh:4 * hh + 4].rearrange("h d t -> d h t")),
                          r=["kT_d%d" % s], w=["kT_all"], chan="ldk%d" % hh)
                    P.add("sp", lambda e: e.dma_start(out=ckT_all[:, 4 * hh:4 * hh + 4, :], in_=ckT_d[s, 4 * hh:4 * hh + 4].rearrange("h d t -> d h t")),
                          r=["ckT_d%d" % s], w=["ckT_all"], chan="ldck%d" % hh)
                for q4 in range(4):
                    P.add("sp", lambda e: e.dma_start(out=V_all[:, 4 * q4:4 * q4 + 4, :], in_=V_d[s, 512 * q4:512 * q4 + 512, :].rearrange("(k p) c -> p k c", p=128)),
                          r=["V_d%d" % s], w=["V_all"], chan="ldv%d" % q4)
                    P.add("sp", lambda e: e.dma_start(out=cV_all[:, 4 * q4:4 * q4 + 4, :], in_=cV_d[s, 512 * q4:512 * q4 + 512, :].rearrange("(k p) c -> p k c", p=128)),
                          r=["cV_d%d" % s], w=["cV_all"], chan="ldcv%d" % q4)

            def load_seq_part(fn, *a):
                fn(*a)

            def load_q(s, j, qs):
                t0 = 512 * j
                P.add("sp", lambda e: e.dma_start(out=qT_b[qs][:], in_=qT_d[s, :, :, t0:t0 + 512].rearrange("h d t -> d h t")),
                      r=["qT_d%d" % s], w=["qT_b%d" % qs], chan="ldq%d" % qs)
                P.add("sp", lambda e: e.dma_start(out=cqT_b[qs][:], in_=cqT_d[s, :, :, t0:t0 + 512].rearrange("h d t -> d h t")),
                      r=["cqT_d%d" % s], w=["cqT_b%d" % qs], chan="ldcq%d" % qs)

            def attn_block(s, j, qs):
                t0 = 512 * j
                nb_ = s * 4 + j + 1
                if nb_ < NSEQ * 4:
                    load_q(nb_ // 4, nb_ % 4, 1 - qs)
                gm = stream([mla_gen(s, j, h, qs) for h in range(8)], pend_m, 3)
                gc = stream([ca_gen(s, j, h, qs) for h in range(8)], pend_c, 4)
                ra, rb = {0: (1, 2), 1: (3, 4), 2: (1, 1), 3: (3, 2)}[j]
                interleave(gm, gc, ra, rb)
                P.add("sp", lambda e: e.dma_start(out=otn_d[s, :, t0:t0 + 512].rearrange("(h d) t -> d h t", d=64), in_=otn[0][:]),
                      r=["otn0h%d" % hh for hh in range(16)], w=["otn_d%d_%d" % (s, j)], chan="stotn")

            def load_seq_safe(s):
                def ldk(hh):
                    P.add("sp", lambda e: e.dma_start(out=kT_all[:, 4 * hh:4 * hh + 4, :], in_=kT_d[s, 4 * hh:4 * hh + 4].rearrange("h d t -> d h t")),
                          r=["kT_d%d" % s], w=["kT_all"], chan="ldk%d" % hh)
                    P.add("sp", lambda e: e.dma_start(out=ckT_all[:, 4 * hh:4 * hh + 4, :], in_=ckT_d[s, 4 * hh:4 * hh + 4].rearrange("h d t -> d h t")),
                          r=["ckT_d%d" % s], w=["ckT_all"], chan="ldck%d" % hh)

                def ldv(q4):
                    P.add("sp", lambda e: e.dma_start(out=V_all[:, 4 * q4:4 * q4 + 4, :], in_=V_d[s, 512 * q4:512 * q4 + 512, :].rearrange("(k p) c -> p k c", p=128)),
                          r=["V_d%d" % s], w=["V_all"], chan="ldv%d" % q4)
                    P.add("sp", lambda e: e.dma_start(out=cV_all[:, 4 * q4:4 * q4 + 4, :], in_=cV_d[s, 512 * q4:512 * q4 + 512, :].rearrange("(k p) c -> p k c", p=128)),
                          r=["cV_d%d" % s], w=["cV_all"], chan="ldcv%d" % q4)
                for hh in range(2):
                    ldk(hh)
                for q4 in range(4):
                    ldv(q4)

            blk = 0
            if stop not in ("setup", "A"):
                load_q(0, 0, 0)
            for s in range(NSEQ if stop not in ("setup", "A") else 0):
                load_seq_safe(s)
                for j in range(4):
                    attn_block(s, j, blk % 2)
                    blk += 1
            P.barrier()

        with ExitStack() as sc:
            TB = 256
            NTB = TB // 128
            wdn = sb(sc, "wdn", [128, NFF, D], BF16)
            wout = sb(sc, "wout", [128, 8, D], BF16)
            wup = sb(sc, "wup", [128, 8, 2 * D_FF], BF16)
            convp = sb(sc, "convp", [128, 4, 2 * NFF], F32)
            gab = sb(sc, "gab", [128, D], F32)
            gmb = sb(sc, "gmb", [128, D], F32)
            otb = sb(sc, "otb", [128, 8, TB], BF16)
            x1 = [sb(sc, "x1_%d" % i, [128, D], F32) for i in range(NTB)]
            xn2 = sb(sc, "xn2", [128, D], BF16)
            hT2 = sb(sc, "hT2", [128, 8, TB + 2], BF16)
            gT = sb(sc, "gT", [128, NFF, TB], BF16)
            NUB = 3
            cg = [sb(sc, "cg%d" % i, [128, TB], F32) for i in range(NUB)]
            cv = [sb(sc, "cv%d" % i, [128, TB], F32) for i in range(NUB)]
            stC = sb(sc, "stC", [128, 4], F32)
            otile = sb(sc, "otile", [128, D], F32)
            P.tag = "wlC"
            P.add("sp", lambda e: e.dma_start(out=convp[:], in_=convp_d), w=["convp"], chan="small5", waitall=True)

            wl_cnt = {"n": 0}
            WUP_KEYS, WOUT_KEYS, WDN_KEYS = [], [], []

            def wl_piece(fn, keylist, name):
                n = wl_cnt["n"]
                wl_cnt["n"] += 1
                key = "%s_p%d" % (name, len(keylist))
                P.add("pool", fn, r=([wl_cnt["prev2"]] if n >= 2 else []), w=[key], chan="wlc%d" % (n % 2))
                wl_cnt["prev2"] = wl_cnt.get("prev1")
                wl_cnt["prev1"] = key
                keylist.append(key)

            def ldw(k):
                wl_piece(lambda e: e.dma_start(out=wup[:, :, k * 512:(k + 1) * 512],
                                               in_=wup_d[:, k * 512:(k + 1) * 512].rearrange("(j p) n -> p j n", p=128)), WUP_KEYS, "wup")

            def ldwo(k):
                wl_piece(lambda e: e.dma_start(out=wout[:, :, k * 512:(k + 1) * 512],
                                               in_=wout_d[:, k * 512:(k + 1) * 512].rearrange("(j p) n -> p j n", p=128)), WOUT_KEYS, "wout")

            def ldwd(k, jg):
                wl_piece(lambda e: e.dma_start(out=wdn[:, 11 * jg:11 * jg + 11, k * 512:(k + 1) * 512],
                                               in_=wdn_d[1408 * jg:1408 * jg + 1408, k * 512:(k + 1) * 512].rearrange("(j p) n -> p j n", p=128)), WDN_KEYS, "wdn")
            for k in range(2):
                ldwo(k)
            for k in range(11):
                ldw(k)
            for k in range(2):
                for jg in range(2):
                    ldwd(k, jg)
            P.tag = None
            ucnt = {"u": 0}
            HALO_KEYS = ["halo%d" % ch for ch in range(2 * NFF)]

            def seq_start(s):
                P.add("sp", lambda e: e.dma_start(out=gab[:], in_=mod_d[s, 2 * D:3 * D].partition_broadcast(128)), r=["mod_d"], w=["gab"], chan="ldga")
                P.add("sp", lambda e: e.dma_start(out=gmb[:], in_=mod_d[s, 5 * D:6 * D].partition_broadcast(128)), r=["mod_d"], w=["gmb"], chan="ldgm")

            def outproj_tile(s, t0, it):
                tk = t0 + it * 128
                kx1 = "x1_%d" % it
                kxs = [kx1 + "h0", kx1 + "h512"]
                P.add("sp", lambda e: e.dma_start(out=x1[it][:], in_=x_d[s, tk:tk + 128, :]), w=kxs, chan="ldx%d" % it)
                bo0, bo1 = BK.get(hold=True), BK.get(hold=True)

                def half(bb, n0):
                    for c in range(8):
                        P.add("pe", (lambda c: lambda e: e.matmul(banks_f[bb][:, 0:512], lhsT=otb[:, c, it * 128:(it + 1) * 128],
                                                                  rhs=wout[:, c, n0:n0 + 512], start=(c == 0), stop=(c == 7)))(c),
                              r=["otb"] + WOUT_KEYS, w=[bk(bb)])
                    P.add("dve", lambda e: e.tensor_tensor(out=otile[:, n0:n0 + 512], in0=banks_f[bb][:, 0:512],
                                                           in1=gab[:, n0:n0 + 512], op=ALU.mult),
                          r=[bk(bb), "gab"], w=["otileh%d" % n0])
                    P.add("pool", lambda e: e.tensor_tensor(out=x1[it][:, n0:n0 + 512], in0=x1[it][:, n0:n0 + 512],
                                                            in1=otile[:, n0:n0 + 512], op=ALU.add),
                          r=[kx1 + "h%d" % n0, "otileh%d" % n0], w=[kx1 + "h%d" % n0])
                    BK.release(bb)
                half(bo0, 0)
                half(bo1, 512)
                P.add("act", lambda e: e.activation(out=xn2[:], in_=x1[it][:], func=AF.Square, accum_out=stC[:, 0:1]),
                      r=kxs, w=["stC", "xn2"])
                P.add("act", lambda e: e.activation(out=stC[:, 0:1], in_=stC[:, 0:1], func=AF.Sqrt, bias=float(D * EPS), scale=1.0),
                      r=["stC"], w=["stC"])
                P.add("dve", lambda e: e.reciprocal(out=stC[:, 0:1], in_=stC[:, 0:1]), r=["stC"], w=["stC"])
                P.add("act", lambda e: e.activation(out=xn2[:], in_=x1[it][:], func=AF.Copy, scale=stC[:, 0:1]),
                      r=kxs + ["stC"], w=["xn2"])
                bt = BK.get()
                for c in range(8):
                    P.add("pe", (lambda c: lambda e: e.transpose(out=banks_b[bt][:, c * 128:(c + 1) * 128],
                                                                 in_=xn2[:, c * 128:(c + 1) * 128], identity=ident[:]))(c),
                          r=["xn2", "ident"], w=[bk(bt)])
                tp3 = banks_b[bt][:, 0:1024].rearrange("p (c t) -> p c t", c=8)
                kh = "hT2_%d" % it
                P.add("dve", lambda e: e.tensor_tensor(out=hT2[:, :, 2 + it * 128:2 + (it + 1) * 128], in0=tp3,
                                                       in1=AB[:, s, 2, :].unsqueeze(2).to_broadcast([128, 8, 128]), op=ALU.mult),
                      r=[bk(bt), "AB"], w=[kh])
                P.add("pool", lambda e: e.tensor_tensor(out=hT2[:, :, 2 + it * 128:2 + (it + 1) * 128], in0=hT2[:, :, 2 + it * 128:2 + (it + 1) * 128],
                                                        in1=AB[:, s, 3, :].unsqueeze(2).to_broadcast([128, 8, 128]), op=ALU.add),
                      r=[kh, "AB"], w=[kh])

            def ffn_up(f, khs):
                ui = ucnt["u"] % NUB
                ucnt["u"] += 1
                bg, bv = BK.get(hold=True), BK.get(hold=True)

                def up_mm(bb, ch):
                    for k in range(8):
                        P.add("pe", (lambda k: lambda e: e.matmul(banks_f[bb][:, 0:TB + 2],
                                                                  lhsT=wup[:, k, ch * 128:(ch + 1) * 128], rhs=hT2[:, k, :],
                                                                  start=(k == 0), stop=(k == 7)))(k),
                              r=khs + ["hT2_halo"] + WUP_KEYS, w=[bk(bb)])
                up_mm(bg, f)
                up_mm(bv, NFF + f)
                kcg, kcv = "cg%d" % ui, "cv%d" % ui

                def tap2(bb, ch, cb, kcb):
                    P.add("act", lambda e: e.activation(out=cb[ui][:], in_=banks_f[bb][:, 2:TB + 2], func=AF.Identity,
                                                        scale=convp[:, 2, ch:ch + 1], bias=convp[:, 3, ch:ch + 1]),
                          r=[bk(bb), "convp"], w=[kcb])

                def tap(bb, ch, cb, kcb, j):
                    P.add("dve", lambda e: e.scalar_tensor_tensor(out=cb[ui][:], in0=banks_f[bb][:, j:TB + j], scalar=convp[:, j, ch:ch + 1],
                                                                  in1=cb[ui][:], op0=ALU.mult, op1=ALU.add),
                          r=[bk(bb), kcb, "convp"], w=[kcb])
                tap2(bg, f, cg, kcg)
                tap2(bv, NFF + f, cv, kcv)
                tap(bg, f, cg, kcg, 1)
                tap(bv, NFF + f, cv, kcv, 1)
                tap(bg, f, cg, kcg, 0)
                tap(bv, NFF + f, cv, kcv, 0)
                BK.release(bg)
                BK.release(bv)
                return (f, ui)

            def ffn_gate(f, ui):
                kcg, kcv = "cg%d" % ui, "cv%d" % ui
                P.add("act", lambda e: e.activation(out=cg[ui][:], in_=cg[ui][:], func=AF.Silu), r=[kcg], w=[kcg])
                P.add("pool", lambda e: e.tensor_tensor(out=gT[:, f, :], in0=cg[ui][:], in1=cv[ui][:], op=ALU.mult),
                      r=[kcg, kcv], w=["gT%d" % f])

            def down_tile(s, t0, it, kgs):
                tk = t0 + it * 128
                bd0, bd1 = BK.get(hold=True), BK.get(hold=True)

                def half(bb, n0):
                    for f in range(NFF):
                        P.add("pe", (lambda f: lambda e: e.matmul(banks_f[bb][:, 0:512], lhsT=gT[:, f, it * 128:(it + 1) * 128],
                                                                  rhs=wdn[:, f, n0:n0 + 512], start=(f == 0), stop=(f == NFF - 1)))(f),
                              r=kgs + WDN_KEYS, w=[bk(bb)])
                    P.add("dve", lambda e: e.tensor_tensor(out=otile[:, n0:n0 + 512], in0=banks_f[bb][:, 0:512],
                                                           in1=gmb[:, n0:n0 + 512], op=ALU.mult),
                          r=[bk(bb), "gmb"], w=["otileh%d" % n0])
                    P.add("pool", lambda e: e.tensor_tensor(out=otile[:, n0:n0 + 512], in0=otile[:, n0:n0 + 512],
                                                            in1=x1[it][:, n0:n0 + 512], op=ALU.add),
                          r=["otileh%d" % n0, "x1_%dh%d" % (it, n0)], w=["otileh%d" % n0])
                    BK.release(bb)
                half(bd0, 0)
                half(bd1, 512)
                P.add("sp", lambda e: e.dma_start(out=out_d[s, tk:tk + 128, :], in_=otile[:]),
                      r=["otileh0", "otileh512"], w=["out_d"], chan="stout")

            def ffn_block(s, tb):
                t0 = tb * TB
                jblk = t0 // 512
                P.add("sp", lambda e: e.dma_start(out=otb[:], in_=otn_d[s, :, t0:t0 + TB].rearrange("(c p) t -> p c t", p=128)),
                      r=["otn_d%d_%d" % (s, jblk)], w=["otb"], chan="ldot")
                if tb == 0:
                    P.add("pool", lambda e: e.memset(hT2[:, :, 0:2], 0.0), r=["hT2_%d" % (NTB - 1)], w=["hT2_halo"])
                else:
                    P.add("pool", lambda e: e.tensor_copy(out=hT2[:, :, 0:2], in_=hT2[:, :, TB:TB + 2]), r=["hT2_%d" % (NTB - 1)], w=["hT2_halo"])
                for it in range(NTB):
                    outproj_tile(s, t0, it)
                khs = ["hT2_%d" % it for it in range(NTB)]
                prev = None
                for f in range(NFF):
                    cur = ffn_up(f, khs)
                    if prev is not None:
                        ffn_gate(*prev)
                    prev = cur
                ffn_gate(*prev)
                kgs = ["gT%d" % f for f in range(NFF)]
                for it in range(NTB):
                    down_tile(s, t0, it, kgs)

            for s in range(NSEQ if stop not in ("setup", "A", "B") else 0):
                seq_start(s)
                for tb in range(S // TB):
                    ffn_block(s, tb)
            P.emit()
    return nc, P.stats


_CACHE = {}


def _feat_major(v):
    v = np.asarray(v, np.float32)
    return np.ascontiguousarray(v.reshape(-1, 128).T)


def _prepare(x, c, positions, w_ada, b_ada, g_attn_norm, w_in, g_q_latent, g_kv_latent, w_q_up, w_kv_up,
             g_mla_q, g_mla_k, g_ca_q, g_ca_k, rel_bias, w_out, g_mlp_norm, w_up, conv_w, conv_b, w_down):
    f = lambda a: np.ascontiguousarray(np.asarray(a))
    x = f(x); c = f(c); positions = f(positions)
    if "nc" not in _CACHE:
        _CACHE["nc"], _CACHE["stats"] = build_program(stop=_CACHE.get("stop"))
    nc = _CACHE["nc"]
    gfeat = np.concatenate([_feat_major(g_attn_norm[0]), _feat_major(g_mlp_norm[0]), _feat_major(g_q_latent[0]),
                            _feat_major(g_kv_latent[0])], axis=1).astype(np.float32)
    convp = np.zeros((128, 4, 2 * NFF), np.float32)
    for t in range(3):
        convp[:, t, :] = _feat_major(conv_w[0, t])
    convp[:, 3, :] = _feat_major(conv_b[0])
    grow = np.concatenate([np.asarray(g_mla_q[0]), np.asarray(g_mla_k[0]), np.asarray(g_ca_q[0]), np.asarray(g_ca_k[0])]).astype(np.float32)
    grow = np.ascontiguousarray(np.broadcast_to(grow[None, :], (128, 320)))
    rb = np.asarray(rel_bias[0], np.float32)
    kj = np.arange(128)[:, None]
    qi = np.arange(128)[None, :]
    bias34 = np.zeros((128, 8, 2, 128), np.float32)
    for ti, t in enumerate((3, 4)):
        idx = np.clip(128 * (4 - t) + qi - kj, -128, 128) + 128
        bias34[:, :, ti, :] = np.transpose(rb[:, idx], (1, 0, 2))
    bfar = np.ascontiguousarray(np.broadcast_to(rb[:, 256][None, :], (128, 8))).astype(np.float32)
    half = 16
    invf = np.power(np.float32(10000.0), -np.arange(half, dtype=np.float32) / np.float32(half)).astype(np.float32)
    invf = np.ascontiguousarray(np.broadcast_to(invf[None, :], (128, 16)))
    shared = {
        "w_ada": f(w_ada[0]), "w_in": f(w_in[0]), "w_q_up": f(w_q_up[0]), "w_kv_up": f(w_kv_up[0]), "w_out": f(w_out[0]),
        "w_up": f(w_up[0]), "w_down": f(w_down[0]), "gfeat": gfeat, "convp": convp, "grow": grow, "bias34": bias34,
        "bfar": bfar, "invf": invf,
    }
    in_maps = []
    for i in range(NCORES):
        b0 = NSEQ * i
        m = dict(shared)
        m["x"] = f(x[b0:b0 + NSEQ])
        m["cT"] = np.ascontiguousarray(c[b0:b0 + NSEQ].reshape(NSEQ, 8, 128).transpose(2, 1, 0)).astype(np.float32)
        pl = positions[b0:b0 + NSEQ].reshape(NSEQ, NT, 128).transpose(2, 0, 1).reshape(128, NSEQ * NT)
        m["posl"] = np.ascontiguousarray(pl).astype(np.int32)
        m["b_ada2"] = np.ascontiguousarray(np.broadcast_to(np.asarray(b_ada[0], np.float32)[None, :], (NSEQ, 6 * D)))
        in_maps.append(m)
    return nc, in_maps


def kernel(**inputs):
    nc, in_maps = _prepare(**inputs)
    res = run_bass_kernel_spmd(nc, in_maps, core_ids=list(range(NCORES)))
    out = np.concatenate([np.asarray(r["out"]) for r in res.results], axis=0)
    return out.astype(np.float32)
```

```python
import math
from contextlib import ExitStack

import numpy as np
import concourse.bass as bass
import concourse.mybir as mybir
from concourse.bass_utils import run_bass_kernel_spmd

F32 = mybir.dt.float32
BF16 = mybir.dt.bfloat16
I32 = mybir.dt.int32
AF = mybir.ActivationFunctionType
ALU = mybir.AluOpType
AX = mybir.AxisListType

NCORES = 8
NSEQ = 2
S = 2048
D = 1024
NT = S // 128
D_IN = 1952
D_FF = 2816
NFF = D_FF // 128
EPS = 1e-6
TWO_PI = 2.0 * math.pi


class Op:
    __slots__ = ("eng", "fn", "r", "w", "chan", "deps", "signal", "sigval", "waitall")

    def __init__(self, eng, fn, r, w, chan, waitall):
        self.eng, self.fn, self.r, self.w, self.chan = eng, fn, tuple(r), tuple(w), chan
        self.deps = set()
        self.signal = False
        self.sigval = 0
        self.waitall = waitall


class Prog:
    def __init__(self, nc, es):
        self.nc = nc
        self.es = es
        self.ops = []
        self.last_w = {}
        self.readers = {}
        self.waitall_chans = set()

    tag = None

    def add(self, eng, fn, r=(), w=(), chan=None, waitall=False):
        if self.tag is not None and self.tag in SKIP:
            return None
        op = Op(eng, fn, r, w, chan, waitall)
        idx = len(self.ops)
        deps = set()
        for k in op.r:
            lw = self.last_w.get(k)
            if lw is not None:
                deps.add(lw)
            if isinstance(k, str) and k.startswith("bank"):
                for rd in self.readers.get(k, ()):
                    if self.ops[rd].eng != eng:
                        deps.add(rd)
        for k in op.w:
            lw = self.last_w.get(k)
            if lw is not None:
                deps.add(lw)
            for rd in self.readers.get(k, ()):
                deps.add(rd)
        deps.discard(idx)
        if chan is not None and waitall:
            deps = {d for d in deps if self.ops[d].chan != chan}
        op.deps = deps
        for k in op.w:
            self.last_w[k] = idx
            self.readers[k] = []
        for k in op.r:
            if k in op.w:
                continue
            self.readers.setdefault(k, []).append(idx)
        if chan is not None and waitall:
            self.waitall_chans.add(chan)
        self.ops.append(op)
        return idx

    def barrier(self):
        n = len(self.ops)
        last = {}
        for i, op in enumerate(self.ops):
            key = op.chan if op.chan is not None else ("E", op.eng)
            last[key] = i
        alld = set(last.values())
        for eng in ("pe", "act", "dve", "pool", "sp"):
            op = Op(eng, None, (), (), None, False)
            op.deps = set(alld)
            self.ops.append(op)

    def emit(self):
        nc = self.nc
        ops = self.ops
        engobj = {"pe": nc.tensor, "act": nc.scalar, "dve": nc.vector, "pool": nc.gpsimd, "sp": nc.sync}
        for op in ops:
            for d in op.deps:
                x = ops[d]
                if x.chan is None and x.eng == "pe" and op.eng == "pe" and op.chan is None:
                    continue
                x.signal = True
        sems = {}

        def sem(name):
            if name not in sems:
                sems[name] = self.es.enter_context(nc.semaphore("s_" + str(name)))
            return sems[name]

        cnt = {}
        chan_total = {}
        for op in ops:
            if op.fn is None:
                continue
            if op.chan is not None:
                c = ("C", op.chan)
                cnt[c] = cnt.get(c, 0) + 16
                op.sigval = cnt[c]
                op.signal = True
                chan_total[op.chan] = cnt[c]
            elif op.signal:
                c = ("E", op.eng)
                cnt[c] = cnt.get(c, 0) + 1
                op.sigval = cnt[c]
        known = {e: {} for e in engobj}
        vcs = [None] * len(ops)
        ecount = {}
        nwaits = 0

        def merge(dst, src):
            for k_, v_ in src.items():
                if dst.get(k_, 0) < v_:
                    dst[k_] = v_

        for i, op in enumerate(ops):
            need = []
            for d in op.deps:
                x = ops[d]
                if x.fn is None:
                    if vcs[d] is not None:
                        need.append((None, 0, d))
                    continue
                if x.chan is not None:
                    key = ("C", x.chan)
                    val = chan_total[x.chan] if x.chan in self.waitall_chans else x.sigval
                else:
                    if x.eng == "pe" and op.eng == "pe" and op.chan is None:
                        continue
                    key = ("E", x.eng)
                    val = x.sigval
                need.append((key, val, d))
            need.sort(key=lambda t: -t[2])
            e = engobj[op.eng]
            kn = known[op.eng]
            for key, val, d in need:
                if key is None:
                    continue
                if kn.get(key, 0) >= val:
                    continue
                e.wait_ge(sem(key), val)
                kn[key] = val
                nwaits += 1
                if vcs[d] is not None and not (ops[d].chan in self.waitall_chans):
                    merge(kn, vcs[d])
            vc = dict(kn)
            if op.fn is not None:
                inst = op.fn(e)
                if op.chan is not None:
                    inst.then_inc(sem(("C", op.chan)), 16)
                    if op.chan not in self.waitall_chans:
                        vc[("C", op.chan)] = max(vc.get(("C", op.chan), 0), op.sigval)
                else:
                    if op.signal:
                        inst.then_inc(sem(("E", op.eng)), 1)
                        ecount[op.eng] = op.sigval
                    if op.eng != "pe" or True:
                        vc[("E", op.eng)] = max(vc.get(("E", op.eng), 0), ecount.get(op.eng, 0))
            vcs[i] = vc
        for chan, tot in chan_total.items():
            if known["sp"].get(("C", chan), 0) < tot:
                nc.sync.wait_ge(sem(("C", chan)), tot)
        self.stats = dict(n_ops=len(ops), n_waits=nwaits, n_sems=len(sems))


class Banks:
    def __init__(self, banks):
        self.banks = banks
        self.ptr = 0
        self.held = set()

    def get(self, hold=False):
        for _ in range(16):
            b = self.ptr
            self.ptr = (self.ptr + 1) % len(self.banks)
            if b not in self.held:
                if hold:
                    self.held.add(b)
                return b
        raise RuntimeError("no free PSUM bank")

    def release(self, b):
        self.held.discard(b)


import os
SKIP = set(os.environ.get('KSKIP', '').split(','))


def build_program(debug=None, stop=None):
    nc = bass.Bass("TRN2", target_bir_lowering=False)

    def din(name, shape, dt=F32):
        return nc.dram_tensor(name, list(shape), dt, kind="ExternalInput").ap()

    def dscr(name, shape, dt):
        return nc.dram_tensor(name, list(shape), dt, kind="Internal").ap()

    x_d = din("x", [NSEQ, S, D])
    cT_d = din("cT", [128, 8, NSEQ])
    pos_d = din("posl", [128, NSEQ * NT], I32)
    wada_d = din("w_ada", [D, 6 * D])
    bada_d = din("b_ada2", [NSEQ, 6 * D])
    win_d = din("w_in", [D, D_IN])
    wq_d = din("w_q_up", [256, 768])
    wkv_d = din("w_kv_up", [128, 1024])
    wout_d = din("w_out", [D, D])
    wup_d = din("w_up", [D, 2 * D_FF])
    wdn_d = din("w_down", [D_FF, D])
    gfeat_d = din("gfeat", [128, 19])
    convp_d = din("convp", [128, 4, 2 * NFF])
    grow_d = din("grow", [128, 320])
    bias34_d = din("bias34", [128, 8, 2, 128])
    bfar_d = din("bfar", [128, 8])
    invf_d = din("invf", [128, 16])
    out_d = nc.dram_tensor("out", [NSEQ, S, D], F32, kind="ExternalOutput").ap()

    mod_d = dscr("mod_scr", [NSEQ, 6 * D], F32)
    qT_d = dscr("qT_scr", [NSEQ, 8, 96, S], BF16)
    kT_d = dscr("kT_scr", [NSEQ, 8, 96, S], BF16)
    cqT_d = dscr("cqT_scr", [NSEQ, 8, 64, S], BF16)
    ckT_d = dscr("ckT_scr", [NSEQ, 8, 64, S], BF16)
    V_d = dscr("V_scr", [NSEQ, S, 8 * 65], BF16)
    cV_d = dscr("cV_scr", [NSEQ, S, 8 * 65], BF16)
    otn_d = dscr("otn_scr", [NSEQ, D, S], BF16)

    with ExitStack() as es:
        P = Prog(nc, es)

        def sb(stack, name, shape, dt):
            return stack.enter_context(nc.sbuf_tensor("sb_" + name, list(shape), dt))

        banks_f = [es.enter_context(nc.psum_tensor("bank%d" % i, [128, 512], F32)) for i in range(8)]
        banks_b = [b[:].bitcast(BF16) for b in banks_f]
        BK = Banks(banks_f)

        def bk(b):
            return "bank%d" % b

        ident = sb(es, "ident", [128, 128], BF16)
        identf = sb(es, "identf", [128, 128], F32)
        sel64 = sb(es, "sel64", [128, 64], F32)
        gfeat = sb(es, "gfeat", [128, 19], F32)
        grow = sb(es, "grow", [128, 320], F32)
        AB = sb(es, "AB", [128, NSEQ, 4, 8], F32)
        sa = es.enter_context(ExitStack())
        cs_all = sb(sa, "cs_all", [128, NSEQ * NT, 32], F32)
        junk = sb(sa, "junk", [128, 1024], BF16)

        P.add("pool", lambda e: e.memset(identf[:], 1.0), w=["identf"])
        P.add("pool", lambda e: e.affine_select(out=identf[:], in_=identf[:], pattern=[[-1, 128]],
                                                compare_op=ALU.is_equal, fill=0.0, base=0, channel_multiplier=1),
              r=["identf"], w=["identf"])
        P.add("pool", lambda e: e.tensor_copy(out=ident[:], in_=identf[:]), r=["identf"], w=["ident"])
        P.add("pool", lambda e: e.memset(sel64[:], 0.0), w=["sel64"])
        P.add("pool", lambda e: e.memset(sel64[64:65, :], 1.0), r=["sel64"], w=["sel64"])
        P.add("sp", lambda e: e.dma_start(out=gfeat[:], in_=gfeat_d), w=["gfeat"], chan="small", waitall=True)
        P.add("sp", lambda e: e.dma_start(out=grow[:], in_=grow_d), w=["grow"], chan="small", waitall=True)
        s0 = es.enter_context(ExitStack())
        cT = sb(s0, "cT", [128, 8, NSEQ], F32)
        bada = sb(s0, "bada", [NSEQ, 6 * D], F32)
        posi = sb(s0, "posi", [128, NSEQ * NT], I32)
        invf = sb(s0, "invf", [128, 16], F32)
        P.add("sp", lambda e: e.dma_start(out=cT[:], in_=cT_d), w=["cT"], chan="small", waitall=True)
        P.add("sp", lambda e: e.dma_start(out=bada[:], in_=bada_d), w=["bada"], chan="small", waitall=True)
        P.add("sp", lambda e: e.dma_start(out=posi[:], in_=pos_d), w=["posi"], chan="small", waitall=True)
        P.add("sp", lambda e: e.dma_start(out=invf[:], in_=invf_d), w=["invf"], chan="small", waitall=True)
        P.add("dve", lambda e: e.tensor_scalar(out=grow[:, 96:192], in0=grow[:, 96:192], scalar1=math.sqrt(96.0),
                                               scalar2=None, op0=ALU.mult), r=["grow"], w=["grow"])
        P.add("dve", lambda e: e.tensor_scalar(out=grow[:, 256:320], in0=grow[:, 256:320], scalar1=8.0,
                                               scalar2=None, op0=ALU.mult), r=["grow"], w=["grow"])

        if True:
            scb = sb(s0, "scb", [128, 8, NSEQ], BF16)
            wab = [sb(s0, "wab%d" % i, [128, 8, 512], BF16) for i in range(2)]
            modsb = sb(s0, "modsb", [NSEQ, 6 * D], F32)
            P.add("act", lambda e: e.activation(out=scb[:], in_=cT[:], func=AF.Silu), r=["cT"], w=["scb"])
            for nb in range(12):
                sl = nb % 2
                P.add("pool", (lambda nb, sl: lambda e: e.dma_start(
                    out=wab[sl][:], in_=wada_d[:, nb * 512:(nb + 1) * 512].rearrange("(j p) n -> p j n", p=128)))(nb, sl),
                    w=["wab%d" % sl], chan="wab%d" % sl)
                b = BK.get()
                for k in range(8):
                    P.add("pe", (lambda b, sl, k: lambda e: e.matmul(
                        banks_f[b][0:NSEQ, 0:512], lhsT=scb[:, k, :], rhs=wab[sl][:, k, :], start=(k == 0), stop=(k == 7)))(b, sl, k),
                        r=["scb", "wab%d" % sl], w=[bk(b)])
                P.add("dve", (lambda b, nb: lambda e: e.tensor_tensor(
                    out=modsb[:, nb * 512:(nb + 1) * 512], in0=banks_f[b][0:NSEQ, 0:512],
                    in1=bada[:, nb * 512:(nb + 1) * 512], op=ALU.add))(b, nb),
                    r=[bk(b), "bada"], w=["modsb"])
            P.add("sp", lambda e: e.dma_start(out=mod_d, in_=modsb[:]), r=["modsb"], w=["mod_d"], chan="modst")
            P.tag = "modT"
            modT = sb(s0, "modT", [128, NSEQ, 4, 8], F32)
            bT = BK.get()

            def modtr(qi, c, j):
                col = (qi * 8 + j) * NSEQ
                P.add("pe", lambda e: e.matmul(banks_f[bT][:, col:col + NSEQ], lhsT=modsb[0:NSEQ, c * D + j * 128:c * D + (j + 1) * 128],
                                               rhs=identf[0:NSEQ, 0:NSEQ], start=True, stop=True),
                      r=["modsb", "identf"], w=[bk(bT)])
            for qi, c in enumerate((0, 1, 3, 4)):
                for j in range(8):
                    modtr(qi, c, j)
            P.add("dve", lambda e: e.tensor_copy(out=modT[:].rearrange("p s k j -> p k j s"),
                                                 in_=banks_f[bT][:, 0:32 * NSEQ].rearrange("p (k j s) -> p k j s", k=4, j=8)),
                  r=[bk(bT)], w=["modT"])
            for s in range(NSEQ):
                for (dst, srcq, gcol) in ((0, 1, 0), (2, 3, 8)):
                    P.add("dve", (lambda s, dst, srcq, gcol: lambda e: e.scalar_tensor_tensor(
                        out=AB[:, s, dst, :], in0=modT[:, s, srcq, :], scalar=1.0, in1=gfeat[:, gcol:gcol + 8],
                        op0=ALU.add, op1=ALU.mult))(s, dst, srcq, gcol),
                        r=["modT", "gfeat"], w=["AB"])
                    P.add("dve", (lambda s, dst: lambda e: e.tensor_scalar(
                        out=AB[:, s, dst, :], in0=AB[:, s, dst, :], scalar1=32.0, scalar2=None, op0=ALU.mult))(s, dst),
                        r=["AB"], w=["AB"])
                    P.add("dve", (lambda s, dst, srcq: lambda e: e.tensor_copy(
                        out=AB[:, s, dst + 1, :], in_=modT[:, s, srcq - 1, :]))(s, dst, srcq),
                        r=["modT"], w=["AB"])
            P.tag = "rot"
            posf = sb(s0, "posf", [128, NSEQ * NT], F32)
            ang = sb(s0, "ang", [128, NSEQ * NT, 32], F32)
            kf = sb(s0, "kf", [128, NSEQ * NT, 32], F32)
            ki = sb(s0, "ki", [128, NSEQ * NT, 32], I32)
            mk = sb(s0, "mk", [128, NSEQ * NT, 32], F32)
            P.add("dve", lambda e: e.tensor_copy(out=posf[:], in_=posi[:]), r=["posi"], w=["posf"])
            NTT = NSEQ * NT
            P.add("dve", lambda e: e.tensor_tensor(out=ang[:, :, 16:32], in0=posf[:].unsqueeze(2).to_broadcast([128, NTT, 16]),
                                                   in1=invf[:].unsqueeze(1).to_broadcast([128, NTT, 16]), op=ALU.mult),
                  r=["posf", "invf"], w=["ang"])
            P.add("dve", lambda e: e.tensor_scalar(out=ang[:, :, 0:16], in0=ang[:, :, 16:32], scalar1=math.pi / 2.0,
                                                   scalar2=None, op0=ALU.add), r=["ang"], w=["ang"])
            P.add("dve", lambda e: e.tensor_scalar(out=kf[:], in0=ang[:], scalar1=1.0 / TWO_PI, scalar2=None, op0=ALU.mult),
                  r=["ang"], w=["kf"])
            P.add("dve", lambda e: e.tensor_copy(out=ki[:], in_=kf[:]), r=["kf"], w=["ki"])
            P.add("dve", lambda e: e.tensor_copy(out=kf[:], in_=ki[:]), r=["ki"], w=["kf"])
            P.add("dve", lambda e: e.scalar_tensor_tensor(out=ang[:], in0=kf[:], scalar=-TWO_PI, in1=ang[:],
                                                          op0=ALU.mult, op1=ALU.add), r=["kf", "ang"], w=["ang"])
            P.add("dve", lambda e: e.tensor_scalar(out=mk[:], in0=ang[:], scalar1=math.pi, scalar2=-TWO_PI,
                                                   op0=ALU.is_gt, op1=ALU.mult), r=["ang"], w=["mk"])
            P.add("dve", lambda e: e.tensor_tensor(out=ang[:], in0=ang[:], in1=mk[:], op=ALU.add), r=["ang", "mk"], w=["ang"])
            P.add("dve", lambda e: e.tensor_scalar(out=mk[:], in0=ang[:], scalar1=-math.pi, scalar2=TWO_PI,
                                                   op0=ALU.is_lt, op1=ALU.mult), r=["ang"], w=["mk"])
            P.add("dve", lambda e: e.tensor_tensor(out=ang[:], in0=ang[:], in1=mk[:], op=ALU.add), r=["ang", "mk"], w=["ang"])
            P.add("dve", lambda e: e.tensor_scalar(out=ang[:], in0=ang[:], scalar1=math.pi, scalar2=-math.pi,
                                                   op0=ALU.min, op1=ALU.max), r=["ang"], w=["ang"])
            P.add("act", lambda e: e.activation(out=cs_all[:], in_=ang[:], func=AF.Sin), r=["ang"], w=["cs_all"])
            P.tag = None
            P.barrier()
            s0.close()

        if True:
            P.tag = "wlA"
            win = sb(sa, "win", [128, 8, D_IN], BF16)
            wq = sb(sa, "wq", [128, 2, 768], BF16)
            wkv = sb(sa, "wkv", [128, 1024], BF16)
            wstage = sb(sa, "wstage", [128, 2, 768], F32)
            wstage2 = sb(sa, "wstage2", [128, 1024], F32)
            WIN_KEYS = []
            for pi_, (c0, c1) in enumerate(((0, 512), (512, 1024), (1024, 1536), (1536, 1952))):
                P.add("pool", (lambda c0, c1: lambda e: e.dma_start(out=win[:, :, c0:c1], in_=win_d[:, c0:c1].rearrange("(j p) n -> p j n", p=128)))(c0, c1),
                      r=(["win_p%d" % (pi_ - 2)] if pi_ >= 2 else []), w=["win_p%d" % pi_], chan="win_c%d" % (pi_ % 2))
                WIN_KEYS.append("win_p%d" % pi_)
            P.tag = "wlA2"
            P.add("sp", lambda e: e.dma_start(out=wstage[:], in_=wq_d.rearrange("(j p) n -> p j n", p=128)),
                  w=["wstage"], chan="small3", waitall=True)
            P.add("sp", lambda e: e.dma_start(out=wstage2[:], in_=wkv_d), w=["wstage2"], chan="small3", waitall=True)
            for j in range(2):
                P.add("dve", (lambda j: lambda e: e.tensor_scalar(
                    out=wq[:, j, :], in0=wstage[:, j, :], scalar1=gfeat[:, 16 + j:17 + j], scalar2=16.0,
                    op0=ALU.mult, op1=ALU.mult))(j), r=["wstage", "gfeat"], w=["wq"])
            P.add("dve", lambda e: e.tensor_scalar(out=wkv[:], in0=wstage2[:], scalar1=gfeat[:, 18:19],
                                                   scalar2=math.sqrt(128.0), op0=ALU.mult, op1=ALU.mult),
                  r=["wstage2", "gfeat"], w=["wkv"])

            P.tag = "wlA3"
            xt = [sb(sa, "xt%d" % i, [128, D], F32) for i in range(2)]
            xn = [sb(sa, "xn%d" % i, [128, D], BF16) for i in range(2)]
            hT = [sb(sa, "hT%d" % i, [128, 8, 128], BF16) for i in range(2)]
            st = sb(sa, "stA", [128, 2, 40], F32)
            lat = [sb(sa, "lat%d" % i, [128, 384], BF16) for i in range(2)]
            latT = [sb(sa, "latT%d" % i, [128, 3, 128], BF16) for i in range(2)]
            kr = [sb(sa, "kr%d" % i, [128, 4, 32], F32) for i in range(2)]
            qn = [sb(sa, "qn%d" % i, [128, 8, 96], F32) for i in range(2)]
            qsq = [sb(sa, "qsq%d" % i, [128, 8, 96], F32) for i in range(2)]
            qb = [sb(sa, "qb%d" % i, [128, 8, 96], BF16) for i in range(2)]
            kn = [sb(sa, "kn%d" % i, [128, 8, 64], F32) for i in range(2)]
            ksq = [sb(sa, "ksq%d" % i, [128, 8, 64], F32) for i in range(2)]
            kb = [sb(sa, "kb%d" % i, [128, 8, 96], BF16) for i in range(2)]
            rt = [sb(sa, "rt%d" % i, [128, 8, 32], F32) for i in range(2)]
            cqn = [sb(sa, "cqn%d" % i, [128, 8, 64], F32) for i in range(2)]
            cqs = [sb(sa, "cqs%d" % i, [128, 8, 64], F32) for i in range(2)]
            cqb = [sb(sa, "cqb%d" % i, [128, 8, 64], BF16) for i in range(2)]
            ckn = [sb(sa, "ckn%d" % i, [128, 8, 64], F32) for i in range(2)]
            cks = [sb(sa, "cks%d" % i, [128, 8, 64], F32) for i in range(2)]
            ckb = [sb(sa, "ckb%d" % i, [128, 8, 64], BF16) for i in range(2)]
            qT_st = [sb(sa, "qTst%d" % i, [96, 8, 512], BF16) for i in range(2)]
            kT_st = [sb(sa, "kTst%d" % i, [96, 8, 512], BF16) for i in range(2)]
            cqT_st = [sb(sa, "cqTst%d" % i, [64, 8, 512], BF16) for i in range(2)]
            ckT_st = [sb(sa, "ckTst%d" % i, [64, 8, 512], BF16) for i in range(2)]
            V_st = [sb(sa, "Vst%d" % i, [128, 4, 8, 65], BF16) for i in range(2)]
            cV_st = [sb(sa, "cVst%d" % i, [128, 4, 8, 65], BF16) for i in range(2)]
            for i in range(2):
                P.add("pool", (lambda i: lambda e: e.memset(V_st[i][:, :, :, 64:65], 1.0))(i), w=["Vst%d" % i])
                P.add("pool", (lambda i: lambda e: e.memset(cV_st[i][:, :, :, 64:65], 1.0))(i), w=["cVst%d" % i])

            P.tag = None


            def rstd_from_ssq(sl, col, n, tag):
                kst = "st%d_%s" % (sl, tag)
                P.add("act", lambda e: e.activation(out=st[:, sl, col:col + 1], in_=st[:, sl, col:col + 1], func=AF.Sqrt,
                                                    bias=float(n * EPS), scale=1.0), r=[kst], w=[kst])
                P.add("dve", lambda e: e.reciprocal(out=st[:, sl, col:col + 1], in_=st[:, sl, col:col + 1]), r=[kst], w=[kst])
                return kst

            def rstd_vec(sl, c0, nh, n, tag):
                kst = "st%d_%s" % (sl, tag)
                P.add("act", lambda e: e.activation(out=st[:, sl, c0:c0 + nh], in_=st[:, sl, c0:c0 + nh], func=AF.Sqrt,
                                                    bias=float(n * EPS), scale=1.0), r=[kst], w=[kst])
                P.add("dve", lambda e: e.reciprocal(out=st[:, sl, c0:c0 + nh], in_=st[:, sl, c0:c0 + nh]), r=[kst], w=[kst])
                return kst

            def rope(eng, src3, dst3, cs, keys_r, keys_w, tmp, nh, ktmp):
                cosb = cs[:, 0:16].unsqueeze(1).to_broadcast([128, nh, 16])
                sinb = cs[:, 16:32].unsqueeze(1).to_broadcast([128, nh, 16])
                P.add(eng, lambda e: e.tensor_tensor(out=tmp[:, :, 0:16], in0=src3[:, :, 16:32], in1=sinb, op=ALU.mult),
                      r=keys_r, w=[ktmp])
                P.add(eng, lambda e: e.tensor_tensor(out=tmp[:, :, 16:32], in0=src3[:, :, 0:16], in1=sinb, op=ALU.mult),
                      r=keys_r + [ktmp], w=[ktmp])
                P.add(eng, lambda e: e.tensor_tensor(out=src3[:, :, 0:16], in0=src3[:, :, 0:16], in1=cosb, op=ALU.mult),
                      r=keys_r + [ktmp], w=keys_r[:1])
                P.add(eng, lambda e: e.tensor_tensor(out=src3[:, :, 16:32], in0=src3[:, :, 16:32], in1=cosb, op=ALU.mult),
                      r=keys_r + [ktmp], w=keys_r[:1])
                P.add(eng, lambda e: e.tensor_tensor(out=dst3[:, :, 0:16], in0=src3[:, :, 0:16], in1=tmp[:, :, 0:16], op=ALU.subtract),
                      r=keys_r + [ktmp], w=keys_w)
                P.add(eng, lambda e: e.tensor_tensor(out=dst3[:, :, 16:32], in0=src3[:, :, 16:32], in1=tmp[:, :, 16:32], op=ALU.add),
                      r=keys_r + [ktmp], w=keys_w)

            def prep_tile(g):
                s, tt = divmod(g, NT)
                jb, i4 = divmod(tt, 4)
                sl = g % 2
                bs = (g // 4) % 2
                S_ = str(sl)
                kxt, kxn, khT = "xt" + S_, "xn" + S_, "hT" + S_
                P.add("sp", lambda e: e.dma_start(out=xt[sl][:], in_=x_d[s, tt * 128:(tt + 1) * 128, :]), w=[kxt], chan="xt" + S_)
                P.add("act", lambda e: e.activation(out=junk[:], in_=xt[sl][:], func=AF.Square, accum_out=st[:, sl, 0:1]),
                      r=[kxt], w=["st%s_x" % S_])
                kst = rstd_from_ssq(sl, 0, D, "x")
                P.add("act", lambda e: e.activation(out=xn[sl][:], in_=xt[sl][:], func=AF.Copy, scale=st[:, sl, 0:1]),
                      r=[kxt, kst], w=[kxn])
                yield
                b0 = BK.get(hold=True)
                for c in range(8):
                    P.add("pe", (lambda c: lambda e: e.transpose(out=banks_b[b0][:, c * 128:(c + 1) * 128],
                                                                  in_=xn[sl][:, c * 128:(c + 1) * 128], identity=ident[:]))(c),
                          r=[kxn, "ident"], w=[bk(b0)])
                tp3 = banks_b[b0][:, 0:1024].rearrange("p (c t) -> p c t", c=8)
                P.add("dve", lambda e: e.tensor_tensor(out=hT[sl][:], in0=tp3, in1=AB[:, s, 0, :].unsqueeze(2).to_broadcast([128, 8, 128]),
                                                       op=ALU.mult), r=[bk(b0), "AB"], w=[khT])
                BK.release(b0)
                P.add("pool", lambda e: e.tensor_tensor(out=hT[sl][:], in0=hT[sl][:], in1=AB[:, s, 1, :].unsqueeze(2).to_broadcast([128, 8, 128]),
                                                        op=ALU.add), r=[khT, "AB"], w=[khT])
                cols = [(0, 416), (416, 928), (928, 1440), (1440, 1952)]

                def proj_group(bi):
                    bb_ = BK.get(hold=True)
                    c0, c1 = cols[bi]
                    for k in range(8):
                        P.add("pe", (lambda k: lambda e: e.matmul(
                            banks_f[bb_][:, 0:c1 - c0], lhsT=hT[sl][:, k, :], rhs=win[:, k, c0:c1],
                            start=(k == 0), stop=(k == 7)))(k),
                            r=[khT] + WIN_KEYS, w=[bk(bb_)])
                    return bb_
                yield
                pb0 = proj_group(0)
                pl = banks_f[pb0]
                kl = bk(pb0)
                yield
                P.add("act", lambda e: e.activation(out=junk[:, 0:256], in_=pl[:, 0:256], func=AF.Square, accum_out=st[:, sl, 1:2]),
                      r=[kl], w=["st%s_ql" % S_])
                P.add("act", lambda e: e.activation(out=junk[:, 256:384], in_=pl[:, 256:384], func=AF.Square, accum_out=st[:, sl, 2:3]),
                      r=[kl], w=["st%s_kvl" % S_])
                k1 = rstd_from_ssq(sl, 1, 256, "ql")
                k2 = rstd_from_ssq(sl, 2, 128, "kvl")
                klat = "lat" + S_
                P.add("dve", lambda e: e.tensor_scalar(out=lat[sl][:, 0:256], in0=pl[:, 0:256], scalar1=st[:, sl, 1:2], scalar2=None,
                                                       op0=ALU.mult), r=[kl, k1], w=[klat + "a"])
                P.add("dve", lambda e: e.tensor_scalar(out=lat[sl][:, 256:384], in0=pl[:, 256:384], scalar1=st[:, sl, 2:3], scalar2=None,
                                                       op0=ALU.mult), r=[kl, k2], w=[klat + "b"])
                kkr = "kr" + S_
                P.add("dve", lambda e: e.tensor_tensor(out=kr[sl][:, 0, :], in0=pl[:, 384:416], in1=grow[:, 160:192], op=ALU.mult),
                      r=[kl, "grow"], w=[kkr])
                P.add("act", lambda e: e.activation(out=junk[:, 512:544], in_=pl[:, 384:416], func=AF.Square, accum_out=st[:, sl, 3:4]),
                      r=[kl], w=["st%s_kr" % S_])
                BK.release(pb0)
                yield
                b1 = BK.get(hold=True)
                for c in range(3):
                    P.add("pe", (lambda c: lambda e: e.transpose(out=banks_b[b1][:, c * 128:(c + 1) * 128],
                                                                  in_=lat[sl][:, c * 128:(c + 1) * 128], identity=ident[:]))(c),
                          r=[klat + "a", klat + "b", "ident"], w=[bk(b1)])
                klT = "latT" + S_
                P.add("act", lambda e: e.activation(out=latT[sl][:].rearrange("p c t -> p (c t)"), in_=banks_b[b1][:, 0:384], func=AF.Copy),
                      r=[bk(b1)], w=[klT])
                BK.release(b1)
                yield
                bq0, bq1 = BK.get(hold=True), BK.get(hold=True)
                for (bb, c0, c1) in ((bq0, 0, 480), (bq1, 480, 768)):
                    for k in range(2):
                        P.add("pe", (lambda bb, c0, c1, k: lambda e: e.matmul(
                            banks_f[bb][:, 0:c1 - c0], lhsT=latT[sl][:, k, :], rhs=wq[:, k, c0:c1], start=(k == 0), stop=(k == 1)))(bb, c0, c1, k),
                            r=[klT, "wq"], w=[bk(bb)])
                bkv0, bkv1 = BK.get(hold=True), BK.get(hold=True)
                for (bb, c0) in ((bkv0, 0), (bkv1, 512)):
                    P.add("pe", (lambda bb, c0: lambda e: e.matmul(
                        banks_f[bb][:, 0:512], lhsT=latT[sl][:, 2, :], rhs=wkv[:, c0:c0 + 512], start=True, stop=True))(bb, c0),
                        r=[klT, "wkv"], w=[bk(bb)])
                yield
                cs = cs_all[:, g, :]
                kqn, kqs, kqb = "qn" + S_, "qsq" + S_, "qb" + S_
                q0 = banks_f[bq0][:, 0:480].rearrange("p (h d) -> p h d", h=5)
                q1 = banks_f[bq1][:, 0:288].rearrange("p (h d) -> p h d", h=3)
                P.add("act", lambda e: e.activation(out=qn[sl][:, 0:5, :], in_=q0, func=AF.Copy), r=[bk(bq0)], w=[kqn + "a"])
                P.add("act", lambda e: e.activation(out=qn[sl][:, 5:8, :], in_=q1, func=AF.Copy), r=[bk(bq1)], w=[kqn + "b"])
                BK.release(bq0)
                BK.release(bq1)
                yield
                P.add("pool", lambda e: e.tensor_tensor(out=qsq[sl][:], in0=qn[sl][:], in1=qn[sl][:], op=ALU.mult),
                      r=[kqn + "a", kqn + "b"], w=[kqs])
                P.add("dve", lambda e: e.tensor_reduce(out=st[:, sl, 8:16], in_=qsq[sl][:], axis=AX.X, op=ALU.add),
                      r=[kqs], w=["st%s_q" % S_])
                k3 = rstd_vec(sl, 8, 8, 96, "q")
                P.add("dve", lambda e: e.tensor_tensor(out=qn[sl][:], in0=qn[sl][:], in1=st[:, sl, 8:16].unsqueeze(2).to_broadcast([128, 8, 96]),
                                                       op=ALU.mult), r=[kqn + "a", kqn + "b", k3], w=[kqn])
                P.add("pool", lambda e: e.tensor_tensor(out=qn[sl][:], in0=qn[sl][:], in1=grow[:, 0:96].unsqueeze(1).to_broadcast([128, 8, 96]),
                                                        op=ALU.mult), r=[kqn, "grow"], w=[kqn])
                P.add("act", lambda e: e.activation(out=qb[sl][:, :, 0:64], in_=qn[sl][:, :, 0:64], func=AF.Copy), r=[kqn], w=[kqb + "n"])
                yield
                rope("pool", qn[sl][:, :, 64:96], qb[sl][:, :, 64:96], cs, [kqn, "cs_all"], [kqb + "r"], rt[sl], 8, "rt" + S_)
                kkn, kks, kkb = "kn" + S_, "ksq" + S_, "kb" + S_
                kv0 = banks_f[bkv0][:, 0:512].rearrange("p (h d) -> p h d", h=4)
                kv1 = banks_f[bkv1][:, 0:512].rearrange("p (h d) -> p h d", h=4)
                P.add("act", lambda e: e.activation(out=kn[sl][:, 0:4, :], in_=kv0[:, :, 0:64], func=AF.Copy), r=[bk(bkv0)], w=[kkn + "a"])
                P.add("act", lambda e: e.activation(out=kn[sl][:, 4:8, :], in_=kv1[:, :, 0:64], func=AF.Copy), r=[bk(bkv1)], w=[kkn + "b"])
                P.add("act", lambda e: e.activation(out=V_st[bs][:, i4, 0:4, 0:64], in_=kv0[:, :, 64:128], func=AF.Copy),
                      r=[bk(bkv0)], w=["Vst%d" % bs])
                P.add("act", lambda e: e.activation(out=V_st[bs][:, i4, 4:8, 0:64], in_=kv1[:, :, 64:128], func=AF.Copy),
                      r=[bk(bkv1)], w=["Vst%d" % bs])
                BK.release(bkv0)
                BK.release(bkv1)
                yield
                P.add("pool", lambda e: e.tensor_tensor(out=ksq[sl][:], in0=kn[sl][:], in1=kn[sl][:], op=ALU.mult),
                      r=[kkn + "a", kkn + "b"], w=[kks])
                P.add("dve", lambda e: e.tensor_reduce(out=st[:, sl, 16:24], in_=ksq[sl][:], axis=AX.X, op=ALU.add),
                      r=[kks], w=["st%s_k" % S_])
                P.add("dve", lambda e: e.tensor_scalar(out=st[:, sl, 16:24], in0=st[:, sl, 16:24], scalar1=st[:, sl, 3:4], scalar2=None,
                                                       op0=ALU.add), r=["st%s_k" % S_, "st%s_kr" % S_], w=["st%s_k" % S_])
                k4 = rstd_vec(sl, 16, 8, 96, "k")
                yield
                P.add("dve", lambda e: e.tensor_tensor(out=kn[sl][:], in0=kn[sl][:], in1=st[:, sl, 16:24].unsqueeze(2).to_broadcast([128, 8, 64]),
                                                       op=ALU.mult), r=[kkn + "a", kkn + "b", k4], w=[kkn])
                P.add("pool", lambda e: e.tensor_tensor(out=kb[sl][:, :, 0:64], in0=kn[sl][:], in1=grow[:, 96:160].unsqueeze(1).to_broadcast([128, 8, 64]),
                                                        op=ALU.mult), r=[kkn, "grow"], w=[kkb + "n"])
                rope("dve", kr[sl][:, 0:1, :], kr[sl][:, 1:2, :], cs, [kkr, "cs_all"], [kkr + "o"], kr[sl][:, 2:3, :], 1, kkr + "t")
                P.add("dve", lambda e: e.tensor_tensor(out=kb[sl][:, :, 64:96], in0=kr[sl][:, 1:2, :].to_broadcast([128, 8, 32]),
                                                       in1=st[:, sl, 16:24].unsqueeze(2).to_broadcast([128, 8, 32]), op=ALU.mult),
                      r=[kkr + "o", k4], w=[kkb + "r"])
                for (bi_, dn, dsq, db, col, gc0, tag) in (
                        (1, cqn, cqs, cqb, 24, 192, "cq"), (2, ckn, cks, ckb, 32, 256, "ck")):
                    kdn, kds, kdb = tag + "n" + S_, tag + "s" + S_, tag + "b" + S_
                    yield
                    bbx = proj_group(bi_)
                    P.add("act", (lambda bbx, dn: lambda e: e.activation(out=dn[sl][:].rearrange("p h d -> p (h d)"), in_=banks_f[bbx][:, 0:512], func=AF.Copy))(bbx, dn),
                          r=[bk(bbx)], w=[kdn])
                    BK.release(bbx)
                    yield
                    P.add("pool", (lambda dn, dsq: lambda e: e.tensor_tensor(out=dsq[sl][:], in0=dn[sl][:], in1=dn[sl][:], op=ALU.mult))(dn, dsq),
                          r=[kdn], w=[kds])
                    P.add("dve", (lambda dsq, col: lambda e: e.tensor_reduce(out=st[:, sl, col:col + 8], in_=dsq[sl][:], axis=AX.X, op=ALU.add))(dsq, col),
                          r=[kds], w=["st%s_%s" % (S_, tag)])
                    k5 = rstd_vec(sl, col, 8, 64, tag)
                    P.add("dve", (lambda dn, col: lambda e: e.tensor_tensor(out=dn[sl][:], in0=dn[sl][:],
                                                                           in1=st[:, sl, col:col + 8].unsqueeze(2).to_broadcast([128, 8, 64]), op=ALU.mult))(dn, col),
                          r=[kdn, k5], w=[kdn])
                    P.add("pool", (lambda dn, db, gc0: lambda e: e.tensor_tensor(out=db[sl][:], in0=dn[sl][:],
                                                                                in1=grow[:, gc0:gc0 + 64].unsqueeze(1).to_broadcast([128, 8, 64]), op=ALU.mult))(dn, db, gc0),
                          r=[kdn, "grow"], w=[kdb])
                yield
                bbv = proj_group(3)
                P.add("act", lambda e: e.activation(out=cV_st[bs][:, i4, :, 0:64], in_=banks_f[bbv][:, 0:512].rearrange("p (h d) -> p h d", h=8), func=AF.Copy),
                      r=[bk(bbv)], w=["cVst%d" % bs])
                BK.release(bbv)
                for (srcb, keys, dst, kdst, dd) in ((qb, [kqb + "n", kqb + "r"], qT_st, "qTst%d" % bs, 96),
                                                    (kb, [kkb + "n", kkb + "r"], kT_st, "kTst%d" % bs, 96),
                                                    (cqb, ["cqb" + S_], cqT_st, "cqTst%d" % bs, 64),
                                                    (ckb, ["ckb" + S_], ckT_st, "ckTst%d" % bs, 64)):
                    yield
                    bt = BK.get(hold=True)
                    for h in range(8):
                        P.add("pe", (lambda srcb, bt, h, dd: lambda e: e.transpose(
                            out=banks_b[bt][0:dd, h * 128:(h + 1) * 128], in_=srcb[sl][:, h, :], identity=ident[:]))(srcb, bt, h, dd),
                            r=keys + ["ident"], w=[bk(bt)])
                    P.add("act", (lambda bt, dst, dd: lambda e: e.activation(
                        out=dst[bs][:, :, i4 * 128:(i4 + 1) * 128], in_=banks_b[bt][0:dd, 0:1024].rearrange("p (h t) -> p h t", h=8), func=AF.Copy))(bt, dst, dd),
                        r=[bk(bt)], w=[kdst])
                    BK.release(bt)
                if i4 == 3:
                    t0 = jb * 512
                    P.add("sp", lambda e: e.dma_start(out=qT_d[s, :, :, t0:t0 + 512].rearrange("h d t -> d h t"), in_=qT_st[bs][:]),
                          r=["qTst%d" % bs], w=["qT_d%d" % s], chan="stq%d" % bs)
                    P.add("sp", lambda e: e.dma_start(out=kT_d[s, :, :, t0:t0 + 512].rearrange("h d t -> d h t"), in_=kT_st[bs][:]),
                          r=["kTst%d" % bs], w=["kT_d%d" % s], chan="stk%d" % bs)
                    P.add("sp", lambda e: e.dma_start(out=cqT_d[s, :, :, t0:t0 + 512].rearrange("h d t -> d h t"), in_=cqT_st[bs][:]),
                          r=["cqTst%d" % bs], w=["cqT_d%d" % s], chan="stcq%d" % bs)
                    P.add("sp", lambda e: e.dma_start(out=ckT_d[s, :, :, t0:t0 + 512].rearrange("h d t -> d h t"), in_=ckT_st[bs][:]),
                          r=["ckTst%d" % bs], w=["ckT_d%d" % s], chan="stck%d" % bs)
                    P.add("sp", lambda e: e.dma_start(out=V_d[s, t0:t0 + 512, :].rearrange("(k p) c -> p k c", p=128),
                                                      in_=V_st[bs][:].rearrange("p k h c -> p k (h c)")),
                          r=["Vst%d" % bs], w=["V_d%d" % s], chan="stv%d" % bs)
                    P.add("sp", lambda e: e.dma_start(out=cV_d[s, t0:t0 + 512, :].rearrange("(k p) c -> p k c", p=128),
                                                      in_=cV_st[bs][:].rearrange("p k h c -> p k (h c)")),
                          r=["cVst%d" % bs], w=["cV_d%d" % s], chan="stcv%d" % bs)

            ntiles = NSEQ * NT if stop not in ("setup",) else 0
            active = []
            nxt = 0
            INFL = 2
            while nxt < ntiles or active:
                while nxt < ntiles and len(active) < INFL:
                    active.append(prep_tile(nxt))
                    nxt += 1
                for gen in list(active):
                    try:
                        next(gen)
                    except StopIteration:
                        active.remove(gen)
            P.barrier()
            sa.close()

        with ExitStack() as sbk:
            kT_all = sb(sbk, "kT_all", [96, 8, S], BF16)
            V_all = sb(sbk, "V_all", [128, NT, 8 * 65], BF16)
            ckT_all = sb(sbk, "ckT_all", [64, 8, S], BF16)
            cV_all = sb(sbk, "cV_all", [128, NT, 8 * 65], BF16)
            qT_b = [sb(sbk, "qT_b%d" % i, [96, 8, 512], BF16) for i in range(2)]
            cqT_b = [sb(sbk, "cqT_b%d" % i, [64, 8, 512], BF16) for i in range(2)]
            otn = [sb(sbk, "otn%d" % i, [64, 16, 512], BF16) for i in range(1)]
            Ef = sb(sbk, "Ef", [128, 8, 2, 128], F32)
            bfar = sb(sbk, "bfar", [128, 8], F32)
            NPT = 4
            Pt = [sb(sbk, "Pt%d" % i, [128, 512], BF16) for i in range(NPT)]
            PA = [sb(sbk, "PA%d" % i, [128, 384], BF16) for i in range(2)]
            PB = [sb(sbk, "PB%d" % i, [128, 256], F32) for i in range(2)]
            PBb = [sb(sbk, "PBb%d" % i, [128, 256], BF16) for i in range(2)]
            NRZ = 4
            ots = [sb(sbk, "ots%d" % i, [128, 512], F32) for i in range(NRZ)]
            rzb = [sb(sbk, "rzb%d" % i, [128, 2, 512], BF16) for i in range(NRZ)]
            sel64b = sb(sbk, "sel64b", [128, 64], BF16)
            P.tag = "wlB"
            P.add("sp", lambda e: e.dma_start(out=Ef[:], in_=bias34_d), w=["Ef"], chan="small4", waitall=True)
            P.add("sp", lambda e: e.dma_start(out=bfar[:], in_=bfar_d), w=["bfar"], chan="small4", waitall=True)
            P.add("act", lambda e: e.activation(out=Ef[:], in_=Ef[:], func=AF.Exp), r=["Ef"], w=["Ef"])
            P.add("pool", lambda e: e.memset(Ef[64:128, :, 1, 0:64], 0.0), r=["Ef"], w=["Ef"])
            for i in range(NRZ):
                P.add("pool", (lambda i: lambda e: e.memset(rzb[i][:], 0.0))(i), w=["rzb%d" % i])
            P.add("pool", lambda e: e.tensor_copy(out=sel64b[:], in_=sel64[:]), r=["sel64"], w=["sel64b"])
            P.tag = None
            cnt = {"pt": 0, "pa": 0, "rz": 0}

            def normalize_head(bo, width):
                ri = cnt["rz"] % NRZ
                cnt["rz"] += 1
                P.add("dve", lambda e: e.tensor_copy(out=ots[ri][0:65, 0:width], in_=banks_f[bo][0:65, 0:width]),
                      r=[bk(bo)], w=["ots%d" % ri])
                BK.release(bo)
                P.add("act", lambda e: e.activation(out=ots[ri][64:65, 0:width], in_=ots[ri][64:65, 0:width], func=AF.Ln),
                      r=["ots%d" % ri], w=["ots%d" % ri])
                P.add("act", lambda e: e.activation(out=ots[ri][64:65, 0:width], in_=ots[ri][64:65, 0:width], func=AF.Exp, scale=-1.0),
                      r=["ots%d" % ri], w=["ots%d" % ri])
                P.add("dve", lambda e: e.tensor_copy(out=rzb[ri][64:65, 0, 0:width], in_=ots[ri][64:65, 0:width]),
                      r=["ots%d" % ri], w=["rzb%d" % ri])
                P.add("dve", lambda e: e.tensor_tensor(out=rzb[ri][64:65, 1, 0:width], in0=ots[ri][64:65, 0:width],
                                                       in1=rzb[ri][64:65, 0, 0:width], op=ALU.subtract),
                      r=["ots%d" % ri, "rzb%d" % ri], w=["rzb%d" % ri])
                return ri

            def normalize_tail(ri, width, dst_ap, kdst):
                bb = BK.get(hold=True)
                for pl_ in range(2):
                    P.add("pe", (lambda pl_: lambda e: e.matmul(banks_f[bb][0:64, 0:width], lhsT=sel64b[:, 0:64], rhs=rzb[ri][:, pl_, 0:width],
                                                                start=(pl_ == 0), stop=(pl_ == 1)))(pl_),
                          r=["rzb%d" % ri, "sel64b"], w=[bk(bb)])
                P.add("dve", lambda e: e.tensor_tensor(out=dst_ap, in0=banks_f[bb][0:64, 0:width], in1=ots[ri][0:64, 0:width], op=ALU.mult),
                      r=[bk(bb), "ots%d" % ri], w=[kdst])
                BK.release(bb)

            def mla_gen(s, j, h, qs):
                kq = "qT_b%d" % qs
                nkt = 4 * j + 4
                bo = BK.get(hold=True)
                tiles = []
                for kt in range(nkt):
                    r_ = kt - 4 * j
                    c0 = 128 * r_ if r_ > 0 else 0
                    tiles.append((kt, c0, r_ >= 0))
                sbank = {}

                def emit_s(idx):
                    kt, c0, diag = tiles[idx]
                    b = BK.get(hold=True)
                    sbank[idx] = b
                    P.add("pe", lambda e: e.matmul(banks_f[b][:, 0:512 - c0], lhsT=kT_all[:, h, kt * 128:(kt + 1) * 128],
                                                   rhs=qT_b[qs][:, h, c0:512], start=True, stop=True),
                          r=["kT_all%d" % (h // 4), kq], w=[bk(b)])

                def emit_rest(idx):
                    kt, c0, diag = tiles[idx]
                    b = sbank[idx]
                    pi = cnt["pt"] % NPT
                    cnt["pt"] += 1
                    kp = "Pt%d" % pi
                    w_ = 512 - c0
                    P.add("act", lambda e: e.activation(out=Pt[pi][:, 0:w_], in_=banks_f[b][:, 0:w_], func=AF.Exp), r=[bk(b)], w=[kp])
                    BK.release(b)
                    if diag:
                        P.add("pool", lambda e: e.memset(Pt[pi][64:128, 0:64], 0.0), r=[kp], w=[kp])
                    P.add("pe", lambda e: e.matmul(banks_f[bo][0:65, c0:512], lhsT=V_all[:, kt, h * 65:(h + 1) * 65], rhs=Pt[pi][:, 0:w_],
                                                   start=(idx == 0), stop=(idx == nkt - 1)),
                          r=[kp, "V_all%d" % (kt // 4)], w=[bk(bo)])

                LOOK = 2
                for idx in range(min(LOOK, nkt)):
                    emit_s(idx)
                yield
                for idx in range(nkt):
                    emit_rest(idx)
                    if idx + LOOK < nkt:
                        emit_s(idx + LOOK)
                    yield
                ri = normalize_head(bo, 512)
                pend_m.append((ri, 512, otn[0][:, h, :], "otn0h%d" % h))
                yield

            def ca_qtile(j, h, qs, bo, i):
                kq = "cqT_b%d" % qs
                gi = 4 * j + i
                tmin = max(0, 4 - gi)
                pai = cnt["pa"] % 2
                cnt["pa"] += 1
                ba = BK.get(hold=True) if tmin <= 2 else None
                bb = BK.get(hold=True)

                def s_mm(t):
                    ktile = gi - 4 + t
                    if t <= 2:
                        dst = banks_f[ba][:, t * 128:(t + 1) * 128]
                        kb_ = bk(ba)
                    else:
                        dst = banks_f[bb][:, (t - 3) * 128:(t - 2) * 128]
                        kb_ = bk(bb)
                    P.add("pe", lambda e: e.matmul(dst, lhsT=ckT_all[:, h, ktile * 128:(ktile + 1) * 128],
                                                   rhs=cqT_b[qs][:, h, i * 128:(i + 1) * 128], start=True, stop=True),
                          r=["ckT_all%d" % (h // 4), kq], w=[kb_])

                def pv_mm(t):
                    ktile = gi - 4 + t
                    if t <= 2:
                        rhs = PA[pai][:, t * 128:(t + 1) * 128]
                        kr_ = "PA%d" % pai
                    else:
                        rhs = PBb[pai][:, (t - 3) * 128:(t - 2) * 128]
                        kr_ = "PBb%d" % pai
                    P.add("pe", lambda e: e.matmul(banks_f[bo][0:65, i * 128:(i + 1) * 128],
                                                   lhsT=cV_all[:, ktile, h * 65:(h + 1) * 65], rhs=rhs,
                                                   start=(t == tmin), stop=(t == 4)),
                          r=[kr_, "cV_all%d" % (ktile // 4)], w=[bk(bo)])

                for t in range(tmin, 5):
                    s_mm(t)
                yield
                if ba is not None:
                    a0 = tmin * 128
                    P.add("act", lambda e: e.activation(out=PA[pai][:, a0:384], in_=banks_f[ba][:, a0:384], func=AF.Exp,
                                                        bias=bfar[:, h:h + 1], scale=1.0), r=[bk(ba), "bfar"], w=["PA%d" % pai])
                    BK.release(ba)
                    if tmin == 0:
                        P.add("pool", lambda e: e.memset(PA[pai][0:64, 64:128], 0.0), r=["PA%d" % pai], w=["PA%d" % pai])
                b0 = 0 if tmin <= 3 else 128
                P.add("act", lambda e: e.activation(out=PB[pai][:, b0:256], in_=banks_f[bb][:, b0:256], func=AF.Exp), r=[bk(bb)], w=["PB%d" % pai])
                BK.release(bb)
                P.add("pool", lambda e: e.tensor_tensor(out=PBb[pai][:, b0:256], in0=PB[pai][:, b0:256],
                                                        in1=Ef[:, h, :, :].rearrange("p t q -> p (t q)")[:, b0:256], op=ALU.mult),
                      r=["PB%d" % pai, "Ef"], w=["PBb%d" % pai])
                yield
                for t in range(tmin, 5):
                    pv_mm(t)
                yield

            def ca_gen(s, j, h, qs):
                bo = BK.get(hold=True)
                for i in range(4):
                    yield from ca_qtile(j, h, qs, bo, i)
                ri = normalize_head(bo, 512)
                pend_c.append((ri, 512, otn[0][:, 8 + h, :], "otn0h%d" % (8 + h)))
                yield

            pend_m, pend_c = [], []

            def stream(gens, pend, delay):
                for g in gens:
                    n = 0
                    old = list(pend)
                    del pend[:]
                    for _ in g:
                        n += 1
                        yield
                        if n == delay and old:
                            for t in old:
                                normalize_tail(*t)
                            old = []
                            yield
                    if old:
                        for t in old:
                            normalize_tail(*t)
                        yield
                for t in pend:
                    normalize_tail(*t)
                del pend[:]
                yield

            def interleave(ga, gb, ra, rb):
                alive_a = alive_b = True
                while alive_a or alive_b:
                    for _ in range(ra):
                        if alive_a:
                            try:
                                next(ga)
                            except StopIteration:
                                alive_a = False
                    for _ in range(rb):
                        if alive_b:
                            try:
                                next(gb)
                            except StopIteration:
                                alive_b = False

            def load_seq(s):
                for hh in range(2):
                    P.add("sp", lambda e: e.dma_start(out=kT_all[:, 4 * hh:4 * hh + 4, :], in_=kT_d[s, 4 * hh:4 * hh + 4].rearrange("h d t -> d h t")),
                          r=["kT_d%d" % s], w=["kT_all%d" % hh], chan="ldk%d" % hh)
                    P.add("sp", lambda e: e.dma_start(out=ckT_all[:, 4 * hh:4 * hh + 4, :], in_=ckT_d[s, 4 * hh:4 * hh + 4].rearrange("h d t -> d h t")),
                          r=["ckT_d%d" % s], w=["ckT_all%d" % hh], chan="ldck%d" % hh)
                for q4 in range(4):
                    P.add("sp", lambda e: e.dma_start(out=V_all[:, 4 * q4:4 * q4 + 4, :], in_=V_d[s, 512 * q4:512 * q4 + 512, :].rearrange("(k p) c -> p k c", p=128)),
                          r=["V_d%d" % s], w=["V_all%d" % q4], chan="ldv%d" % q4)
                    P.add("sp", lambda e: e.dma_start(out=cV_all[:, 4 * q4:4 * q4 + 4, :], in_=cV_d[s, 512 * q4:512 * q4 + 512, :].rearrange("(k p) c -> p k c", p=128)),
                          r=["cV_d%d" % s], w=["cV_all%d" % q4], chan="ldcv%d" % q4)

            def load_seq_part(fn, *a):
                fn(*a)

            def load_q(s, j, qs):
                t0 = 512 * j
                P.add("sp", lambda e: e.dma_start(out=qT_b[qs][:], in_=qT_d[s, :, :, t0:t0 + 512].rearrange("h d t -> d h t")),
                      r=["qT_d%d" % s], w=["qT_b%d" % qs], chan="ldq%d" % qs)
                P.add("sp", lambda e: e.dma_start(out=cqT_b[qs][:], in_=cqT_d[s, :, :, t0:t0 + 512].rearrange("h d t -> d h t")),
                      r=["cqT_d%d" % s], w=["cqT_b%d" % qs], chan="ldcq%d" % qs)

            def attn_block(s, j, qs):
                t0 = 512 * j
                nb_ = s * 4 + j + 1
                if nb_ < NSEQ * 4:
                    load_q(nb_ // 4, nb_ % 4, 1 - qs)
                gm = stream([mla_gen(s, j, h, qs) for h in range(8)], pend_m, 3)
                gc = stream([ca_gen(s, j, h, qs) for h in range(8)], pend_c, 4)
                ra, rb = {0: (1, 2), 1: (3, 4), 2: (1, 1), 3: (3, 2)}[j]
                interleave(gm, gc, ra, rb)
                P.add("sp", lambda e: e.dma_start(out=otn_d[s, :, t0:t0 + 512].rearrange("(h d) t -> d h t", d=64), in_=otn[0][:]),
                      r=["otn0h%d" % hh for hh in range(16)], w=["otn_d%d_%d" % (s, j)], chan="stotn")

            def load_seq_safe(s):
                def ldk(hh):
                    P.add("sp", lambda e: e.dma_start(out=kT_all[:, 4 * hh:4 * hh + 4, :], in_=kT_d[s, 4 * hh:4 * hh + 4].rearrange("h d t -> d h t")),
                          r=["kT_d%d" % s], w=["kT_all%d" % hh], chan="ldk%d" % hh)
                    P.add("sp", lambda e: e.dma_start(out=ckT_all[:, 4 * hh:4 * hh + 4, :], in_=ckT_d[s, 4 * hh:4 * hh + 4].rearrange("h d t -> d h t")),
                          r=["ckT_d%d" % s], w=["ckT_all%d" % hh], chan="ldck%d" % hh)

                def ldv(q4):
                    P.add("sp", lambda e: e.dma_start(out=V_all[:, 4 * q4:4 * q4 + 4, :], in_=V_d[s, 512 * q4:512 * q4 + 512, :].rearrange("(k p) c -> p k c", p=128)),
                          r=["V_d%d" % s], w=["V_all%d" % q4], chan="ldv%d" % q4)
                    P.add("sp", lambda e: e.dma_start(out=cV_all[:, 4 * q4:4 * q4 + 4, :], in_=cV_d[s, 512 * q4:512 * q4 + 512, :].rearrange("(k p) c -> p k c", p=128)),
                          r=["cV_d%d" % s], w=["cV_all%d" % q4], chan="ldcv%d" % q4)
                ldk(0)
                ldv(0)
                ldk(1)
                ldv(1)
                ldv(2)
                ldv(3)

            blk = 0
            if stop not in ("setup", "A"):
                load_q(0, 0, 0)
            for s in range(NSEQ if stop not in ("setup", "A") else 0):
                load_seq_safe(s)
                for j in range(4):
                    attn_block(s, j, blk % 2)
                    blk += 1
            P.barrier()

        with ExitStack() as sc:
            TB = 256
            NTB = TB // 128
            wdn = sb(sc, "wdn", [128, NFF, D], BF16)
            wout = sb(sc, "wout", [128, 8, D], BF16)
            wup = sb(sc, "wup", [128, 8, 2 * D_FF], BF16)
            convp = sb(sc, "convp", [128, 4, 2 * NFF], F32)
            gab = sb(sc, "gab", [128, D], F32)
            gmb = sb(sc, "gmb", [128, D], F32)
            otb = sb(sc, "otb", [128, 8, TB], BF16)
            x1 = [sb(sc, "x1_%d" % i, [128, D], F32) for i in range(NTB)]
            xn2s = [sb(sc, "xn2_%d" % i, [128, D], BF16) for i in range(NTB)]
            tmph = [sb(sc, "tmph%d" % i, [128, 512], F32) for i in range(NTB)]
            hT2 = sb(sc, "hT2", [128, 8, TB + 2], BF16)
            gT = sb(sc, "gT", [128, NFF, TB], BF16)
            NUB = 3
            cg = [sb(sc, "cg%d" % i, [128, TB], F32) for i in range(NUB)]
            cv = [sb(sc, "cv%d" % i, [128, TB], F32) for i in range(NUB)]
            stC = sb(sc, "stC", [128, 4], F32)
            otile = sb(sc, "otile", [128, D], F32)
            P.tag = "wlC"
            P.add("sp", lambda e: e.dma_start(out=convp[:], in_=convp_d), w=["convp"], chan="small5", waitall=True)

            wl_cnt = {"n": 0}
            WUP_KEYS, WOUT_KEYS, WDN_KEYS = [], [], []

            def wl_piece(fn, keylist, name):
                n = wl_cnt["n"]
                wl_cnt["n"] += 1
                key = "%s_p%d" % (name, len(keylist))
                P.add("pool", fn, r=([wl_cnt["prev2"]] if n >= 2 else []), w=[key], chan="wlc%d" % (n % 2))
                wl_cnt["prev2"] = wl_cnt.get("prev1")
                wl_cnt["prev1"] = key
                keylist.append(key)

            def ldw(k):
                wl_piece(lambda e: e.dma_start(out=wup[:, :, k * 512:(k + 1) * 512],
                                               in_=wup_d[:, k * 512:(k + 1) * 512].rearrange("(j p) n -> p j n", p=128)), WUP_KEYS, "wup")

            def ldwo(k):
                wl_piece(lambda e: e.dma_start(out=wout[:, :, k * 512:(k + 1) * 512],
                                               in_=wout_d[:, k * 512:(k + 1) * 512].rearrange("(j p) n -> p j n", p=128)), WOUT_KEYS, "wout")

            def ldwd(k, jg):
                wl_piece(lambda e: e.dma_start(out=wdn[:, 11 * jg:11 * jg + 11, k * 512:(k + 1) * 512],
                                               in_=wdn_d[1408 * jg:1408 * jg + 1408, k * 512:(k + 1) * 512].rearrange("(j p) n -> p j n", p=128)), WDN_KEYS, "wdn")
            for k in range(2):
                ldwo(k)
            for k in range(11):
                ldw(k)
            for k in range(2):
                for jg in range(2):
                    ldwd(k, jg)
            P.tag = None
            ucnt = {"u": 0}
            HALO_KEYS = ["halo%d" % ch for ch in range(2 * NFF)]

            def seq_start(s):
                P.add("sp", lambda e: e.dma_start(out=gab[:], in_=mod_d[s, 2 * D:3 * D].partition_broadcast(128)), r=["mod_d"], w=["gab"], chan="ldga")
                P.add("sp", lambda e: e.dma_start(out=gmb[:], in_=mod_d[s, 5 * D:6 * D].partition_broadcast(128)), r=["mod_d"], w=["gmb"], chan="ldgm")

            def outproj_tile(s, t0, it):
                tk = t0 + it * 128
                kx1 = "x1_%d" % it
                kxs = [kx1 + "h0", kx1 + "h512"]
                ktm, kxn2, kst = "tmph%d" % it, "xn2_%d" % it, "stC%d" % it
                P.add("sp", lambda e: e.dma_start(out=x1[it][:], in_=x_d[s, tk:tk + 128, :]), w=kxs, chan="ldx%d" % it)
                bo0, bo1 = BK.get(hold=True), BK.get(hold=True)

                def mm(bb, n0):
                    for c in range(8):
                        P.add("pe", (lambda c: lambda e: e.matmul(banks_f[bb][:, 0:512], lhsT=otb[:, c, it * 128:(it + 1) * 128],
                                                                  rhs=wout[:, c, n0:n0 + 512], start=(c == 0), stop=(c == 7)))(c),
                              r=["otb"] + WOUT_KEYS, w=[bk(bb)])

                def epi(bb, n0):
                    P.add("dve", lambda e: e.tensor_tensor(out=tmph[it][:], in0=banks_f[bb][:, 0:512],
                                                           in1=gab[:, n0:n0 + 512], op=ALU.mult),
                          r=[bk(bb), "gab"], w=[ktm])
                    BK.release(bb)
                    P.add("pool", lambda e: e.tensor_tensor(out=x1[it][:, n0:n0 + 512], in0=x1[it][:, n0:n0 + 512],
                                                            in1=tmph[it][:], op=ALU.add),
                          r=[kx1 + "h%d" % n0, ktm], w=[kx1 + "h%d" % n0])
                mm(bo0, 0)
                mm(bo1, 512)
                yield
                epi(bo0, 0)
                yield
                epi(bo1, 512)
                yield
                P.add("act", lambda e: e.activation(out=xn2s[it][:], in_=x1[it][:], func=AF.Square, accum_out=stC[:, it:it + 1]),
                      r=kxs, w=[kst, kxn2])
                P.add("act", lambda e: e.activation(out=stC[:, it:it + 1], in_=stC[:, it:it + 1], func=AF.Sqrt, bias=float(D * EPS), scale=1.0),
                      r=[kst], w=[kst])
                yield
                P.add("dve", lambda e: e.reciprocal(out=stC[:, it:it + 1], in_=stC[:, it:it + 1]), r=[kst], w=[kst])
                P.add("act", lambda e: e.activation(out=xn2s[it][:], in_=x1[it][:], func=AF.Copy, scale=stC[:, it:it + 1]),
                      r=kxs + [kst], w=[kxn2])
                yield
                bt = BK.get(hold=True)
                for c in range(8):
                    P.add("pe", (lambda c: lambda e: e.transpose(out=banks_b[bt][:, c * 128:(c + 1) * 128],
                                                                 in_=xn2s[it][:, c * 128:(c + 1) * 128], identity=ident[:]))(c),
                          r=[kxn2, "ident"], w=[bk(bt)])
                yield
                tp3 = banks_b[bt][:, 0:1024].rearrange("p (c t) -> p c t", c=8)
                kh = "hT2_%d" % it
                P.add("dve", lambda e: e.tensor_tensor(out=hT2[:, :, 2 + it * 128:2 + (it + 1) * 128], in0=tp3,
                                                       in1=AB[:, s, 2, :].unsqueeze(2).to_broadcast([128, 8, 128]), op=ALU.mult),
                      r=[bk(bt), "AB"], w=[kh])
                BK.release(bt)
                P.add("pool", lambda e: e.tensor_tensor(out=hT2[:, :, 2 + it * 128:2 + (it + 1) * 128], in0=hT2[:, :, 2 + it * 128:2 + (it + 1) * 128],
                                                        in1=AB[:, s, 3, :].unsqueeze(2).to_broadcast([128, 8, 128]), op=ALU.add),
                      r=[kh, "AB"], w=[kh])
                yield

            def ffn_up(f, khs):
                ui = ucnt["u"] % NUB
                ucnt["u"] += 1
                bg, bv = BK.get(hold=True), BK.get(hold=True)

                def up_mm(bb, ch):
                    for k in range(8):
                        P.add("pe", (lambda k: lambda e: e.matmul(banks_f[bb][:, 0:TB + 2],
                                                                  lhsT=wup[:, k, ch * 128:(ch + 1) * 128], rhs=hT2[:, k, :],
                                                                  start=(k == 0), stop=(k == 7)))(k),
                              r=khs + ["hT2_halo"] + WUP_KEYS, w=[bk(bb)])
                up_mm(bg, f)
                up_mm(bv, NFF + f)
                kcg, kcv = "cg%d" % ui, "cv%d" % ui

                def tap2(bb, ch, cb, kcb):
                    P.add("act", lambda e: e.activation(out=cb[ui][:], in_=banks_f[bb][:, 2:TB + 2], func=AF.Identity,
                                                        scale=convp[:, 2, ch:ch + 1], bias=convp[:, 3, ch:ch + 1]),
                          r=[bk(bb), "convp"], w=[kcb])

                def tap(bb, ch, cb, kcb, j):
                    P.add("dve", lambda e: e.scalar_tensor_tensor(out=cb[ui][:], in0=banks_f[bb][:, j:TB + j], scalar=convp[:, j, ch:ch + 1],
                                                                  in1=cb[ui][:], op0=ALU.mult, op1=ALU.add),
                          r=[bk(bb), kcb, "convp"], w=[kcb])
                tap2(bg, f, cg, kcg)
                tap2(bv, NFF + f, cv, kcv)
                tap(bg, f, cg, kcg, 1)
                tap(bv, NFF + f, cv, kcv, 1)
                tap(bg, f, cg, kcg, 0)
                tap(bv, NFF + f, cv, kcv, 0)
                BK.release(bg)
                BK.release(bv)
                return (f, ui)

            def ffn_gate(f, ui):
                kcg, kcv = "cg%d" % ui, "cv%d" % ui
                P.add("act", lambda e: e.activation(out=cg[ui][:], in_=cg[ui][:], func=AF.Silu), r=[kcg], w=[kcg])
                P.add("pool", lambda e: e.tensor_tensor(out=gT[:, f, :], in0=cg[ui][:], in1=cv[ui][:], op=ALU.mult),
                      r=[kcg, kcv], w=["gT%d" % f])

            def down_tile(s, t0, it, kgs):
                tk = t0 + it * 128
                bd0, bd1 = BK.get(hold=True), BK.get(hold=True)

                def half(bb, n0):
                    for f in range(NFF):
                        P.add("pe", (lambda f: lambda e: e.matmul(banks_f[bb][:, 0:512], lhsT=gT[:, f, it * 128:(it + 1) * 128],
                                                                  rhs=wdn[:, f, n0:n0 + 512], start=(f == 0), stop=(f == NFF - 1)))(f),
                              r=kgs + WDN_KEYS, w=[bk(bb)])
                    P.add("dve", lambda e: e.tensor_tensor(out=otile[:, n0:n0 + 512], in0=banks_f[bb][:, 0:512],
                                                           in1=gmb[:, n0:n0 + 512], op=ALU.mult),
                          r=[bk(bb), "gmb"], w=["otileh%d" % n0])
                    P.add("pool", lambda e: e.tensor_tensor(out=otile[:, n0:n0 + 512], in0=otile[:, n0:n0 + 512],
                                                            in1=x1[it][:, n0:n0 + 512], op=ALU.add),
                          r=["otileh%d" % n0, "x1_%dh%d" % (it, n0)], w=["otileh%d" % n0])
                    BK.release(bb)
                half(bd0, 0)
                half(bd1, 512)
                P.add("sp", lambda e: e.dma_start(out=out_d[s, tk:tk + 128, :], in_=otile[:]),
                      r=["otileh0", "otileh512"], w=["out_d"], chan="stout")

            def ffn_block(s, tb):
                t0 = tb * TB
                jblk = t0 // 512
                P.add("sp", lambda e: e.dma_start(out=otb[:], in_=otn_d[s, :, t0:t0 + TB].rearrange("(c p) t -> p c t", p=128)),
                      r=["otn_d%d_%d" % (s, jblk)], w=["otb"], chan="ldot")
                if tb == 0:
                    P.add("pool", lambda e: e.memset(hT2[:, :, 0:2], 0.0), r=["hT2_%d" % (NTB - 1)], w=["hT2_halo"])
                else:
                    P.add("pool", lambda e: e.tensor_copy(out=hT2[:, :, 0:2], in_=hT2[:, :, TB:TB + 2]), r=["hT2_%d" % (NTB - 1)], w=["hT2_halo"])
                gens_ = [outproj_tile(s, t0, it) for it in range(NTB)]
                live = []
                pending_ = list(gens_)
                while pending_ or live:
                    if pending_:
                        live.append(pending_.pop(0))
                    for g_ in list(live):
                        try:
                            next(g_)
                        except StopIteration:
                            live.remove(g_)
                khs = ["hT2_%d" % it for it in range(NTB)]
                prev = None
                for f in range(NFF):
                    cur = ffn_up(f, khs)
                    if prev is not None:
                        ffn_gate(*prev)
                    prev = cur
                ffn_gate(*prev)
                kgs = ["gT%d" % f for f in range(NFF)]
                for it in range(NTB):
                    down_tile(s, t0, it, kgs)

            for s in range(NSEQ if stop not in ("setup", "A", "B") else 0):
                seq_start(s)
                for tb in range(S // TB):
                    ffn_block(s, tb)
            P.barrier()
            P.emit()
    return nc, P.stats


_CACHE = {}


def _feat_major(v):
    v = np.asarray(v, np.float32)
    return np.ascontiguousarray(v.reshape(-1, 128).T)


def _prepare(x, c, positions, w_ada, b_ada, g_attn_norm, w_in, g_q_latent, g_kv_latent, w_q_up, w_kv_up,
             g_mla_q, g_mla_k, g_ca_q, g_ca_k, rel_bias, w_out, g_mlp_norm, w_up, conv_w, conv_b, w_down):
    f = lambda a: np.ascontiguousarray(np.asarray(a))
    x = f(x); c = f(c); positions = f(positions)
    if "nc" not in _CACHE:
        _CACHE["nc"], _CACHE["stats"] = build_program(stop=_CACHE.get("stop"))
    nc = _CACHE["nc"]
    gfeat = np.concatenate([_feat_major(g_attn_norm[0]), _feat_major(g_mlp_norm[0]), _feat_major(g_q_latent[0]),
                            _feat_major(g_kv_latent[0])], axis=1).astype(np.float32)
    convp = np.zeros((128, 4, 2 * NFF), np.float32)
    for t in range(3):
        convp[:, t, :] = _feat_major(conv_w[0, t])
    convp[:, 3, :] = _feat_major(conv_b[0])
    grow = np.concatenate([np.asarray(g_mla_q[0]), np.asarray(g_mla_k[0]), np.asarray(g_ca_q[0]), np.asarray(g_ca_k[0])]).astype(np.float32)
    grow = np.ascontiguousarray(np.broadcast_to(grow[None, :], (128, 320)))
    rb = np.asarray(rel_bias[0], np.float32)
    kj = np.arange(128)[:, None]
    qi = np.arange(128)[None, :]
    bias34 = np.zeros((128, 8, 2, 128), np.float32)
    for ti, t in enumerate((3, 4)):
        idx = np.clip(128 * (4 - t) + qi - kj, -128, 128) + 128
        bias34[:, :, ti, :] = np.transpose(rb[:, idx], (1, 0, 2))
    bfar = np.ascontiguousarray(np.broadcast_to(rb[:, 256][None, :], (128, 8))).astype(np.float32)
    half = 16
    invf = np.power(np.float32(10000.0), -np.arange(half, dtype=np.float32) / np.float32(half)).astype(np.float32)
    invf = np.ascontiguousarray(np.broadcast_to(invf[None, :], (128, 16)))
    shared = {
        "w_ada": f(w_ada[0]), "w_in": f(w_in[0]), "w_q_up": f(w_q_up[0]), "w_kv_up": f(w_kv_up[0]), "w_out": f(w_out[0]),
        "w_up": f(w_up[0]), "w_down": f(w_down[0]), "gfeat": gfeat, "convp": convp, "grow": grow, "bias34": bias34,
        "bfar": bfar, "invf": invf,
    }
    in_maps = []
    for i in range(NCORES):
        b0 = NSEQ * i
        m = dict(shared)
        m["x"] = f(x[b0:b0 + NSEQ])
        m["cT"] = np.ascontiguousarray(c[b0:b0 + NSEQ].reshape(NSEQ, 8, 128).transpose(2, 1, 0)).astype(np.float32)
        pl = positions[b0:b0 + NSEQ].reshape(NSEQ, NT, 128).transpose(2, 0, 1).reshape(128, NSEQ * NT)
        m["posl"] = np.ascontiguousarray(pl).astype(np.int32)
        m["b_ada2"] = np.ascontiguousarray(np.broadcast_to(np.asarray(b_ada[0], np.float32)[None, :], (NSEQ, 6 * D)))
        in_maps.append(m)
    return nc, in_maps


def kernel(**inputs):
    nc, in_maps = _prepare(**inputs)
    res = run_bass_kernel_spmd(nc, in_maps, core_ids=list(range(NCORES)))
    out = np.concatenate([np.asarray(r["out"]) for r in res.results], axis=0)
    return out.astype(np.float32)
```

```python
import math
from contextlib import ExitStack

import numpy as np
import concourse.bass as bass
import concourse.mybir as mybir
from concourse.bass_utils import run_bass_kernel_spmd

F32 = mybir.dt.float32
BF16 = mybir.dt.bfloat16
I32 = mybir.dt.int32
AF = mybir.ActivationFunctionType
ALU = mybir.AluOpType
AX = mybir.AxisListType

NCORES = 8
NSEQ = 2
S = 2048
D = 1024
NT = S // 128
D_IN = 1952
D_FF = 2816
NFF = D_FF // 128
EPS = 1e-6
TWO_PI = 2.0 * math.pi


class Op:
    __slots__ = ("eng", "fn", "r", "w", "chan", "deps", "signal", "sigval", "waitall")

    def __init__(self, eng, fn, r, w, chan, waitall):
        self.eng, self.fn, self.r, self.w, self.chan = eng, fn, tuple(r), tuple(w), chan
        self.deps = set()
        self.signal = False
        self.sigval = 0
        self.waitall = waitall


class Prog:
    def __init__(self, nc, es):
        self.nc = nc
        self.es = es
        self.ops = []
        self.last_w = {}
        self.readers = {}
        self.waitall_chans = set()

    tag = None

    def add(self, eng, fn, r=(), w=(), chan=None, waitall=False):
        if self.tag is not None and self.tag in SKIP:
            return None
        op = Op(eng, fn, r, w, chan, waitall)
        idx = len(self.ops)
        deps = set()
        for k in op.r:
            lw = self.last_w.get(k)
            if lw is not None:
                deps.add(lw)
            if isinstance(k, str) and k.startswith("bank"):
                for rd in self.readers.get(k, ()):
                    if self.ops[rd].eng != eng:
                        deps.add(rd)
        for k in op.w:
            lw = self.last_w.get(k)
            if lw is not None:
                deps.add(lw)
            for rd in self.readers.get(k, ()):
                deps.add(rd)
        deps.discard(idx)
        if chan is not None and waitall:
            deps = {d for d in deps if self.ops[d].chan != chan}
        op.deps = deps
        for k in op.w:
            self.last_w[k] = idx
            self.readers[k] = []
        for k in op.r:
            if k in op.w:
                continue
            self.readers.setdefault(k, []).append(idx)
        if chan is not None and waitall:
            self.waitall_chans.add(chan)
        self.ops.append(op)
        return idx

    def barrier(self):
        n = len(self.ops)
        last = {}
        for i, op in enumerate(self.ops):
            key = op.chan if op.chan is not None else ("E", op.eng)
            last[key] = i
        alld = set(last.values())
        for eng in ("pe", "act", "dve", "pool", "sp"):
            op = Op(eng, None, (), (), None, False)
            op.deps = set(alld)
            self.ops.append(op)

    def emit(self):
        nc = self.nc
        ops = self.ops
        engobj = {"pe": nc.tensor, "act": nc.scalar, "dve": nc.vector, "pool": nc.gpsimd, "sp": nc.sync}
        for op in ops:
            for d in op.deps:
                x = ops[d]
                if x.chan is None and x.eng == "pe" and op.eng == "pe" and op.chan is None:
                    continue
                x.signal = True
        sems = {}

        def sem(name):
            if name not in sems:
                sems[name] = self.es.enter_context(nc.semaphore("s_" + str(name)))
            return sems[name]

        cnt = {}
        chan_total = {}
        for op in ops:
            if op.fn is None:
                continue
            if op.chan is not None:
                c = ("C", op.chan)
                cnt[c] = cnt.get(c, 0) + 16
                op.sigval = cnt[c]
                op.signal = True
                chan_total[op.chan] = cnt[c]
            elif op.signal:
                c = ("E", op.eng)
                cnt[c] = cnt.get(c, 0) + 1
                op.sigval = cnt[c]
        known = {e: {} for e in engobj}
        vcs = [None] * len(ops)
        ecount = {}
        nwaits = 0

        def merge(dst, src):
            for k_, v_ in src.items():
                if dst.get(k_, 0) < v_:
                    dst[k_] = v_

        for i, op in enumerate(ops):
            need = []
            for d in op.deps:
                x = ops[d]
                if x.fn is None:
                    if vcs[d] is not None:
                        need.append((None, 0, d))
                    continue
                if x.chan is not None:
                    key = ("C", x.chan)
                    val = chan_total[x.chan] if x.chan in self.waitall_chans else x.sigval
                else:
                    if x.eng == "pe" and op.eng == "pe" and op.chan is None:
                        continue
                    key = ("E", x.eng)
                    val = x.sigval
                need.append((key, val, d))
            need.sort(key=lambda t: -t[2])
            e = engobj[op.eng]
            kn = known[op.eng]
            for key, val, d in need:
                if key is None:
                    continue
                if kn.get(key, 0) >= val:
                    continue
                e.wait_ge(sem(key), val)
                kn[key] = val
                nwaits += 1
                if vcs[d] is not None and not (ops[d].chan in self.waitall_chans):
                    merge(kn, vcs[d])
            vc = dict(kn)
            if op.fn is not None:
                inst = op.fn(e)
                if op.chan is not None:
                    inst.then_inc(sem(("C", op.chan)), 16)
                    if op.chan not in self.waitall_chans:
                        vc[("C", op.chan)] = max(vc.get(("C", op.chan), 0), op.sigval)
                else:
                    if op.signal:
                        inst.then_inc(sem(("E", op.eng)), 1)
                        ecount[op.eng] = op.sigval
                    if op.eng != "pe" or True:
                        vc[("E", op.eng)] = max(vc.get(("E", op.eng), 0), ecount.get(op.eng, 0))
            vcs[i] = vc
        for chan, tot in chan_total.items():
            if known["sp"].get(("C", chan), 0) < tot:
                nc.sync.wait_ge(sem(("C", chan)), tot)
        self.stats = dict(n_ops=len(ops), n_waits=nwaits, n_sems=len(sems))


class Banks:
    def __init__(self, banks):
        self.banks = banks
        self.ptr = 0
        self.held = set()

    def get(self, hold=False):
        for _ in range(16):
            b = self.ptr
            self.ptr = (self.ptr + 1) % len(self.banks)
            if b not in self.held:
                if hold:
                    self.held.add(b)
                return b
        raise RuntimeError("no free PSUM bank")

    def release(self, b):
        self.held.discard(b)


import os
SKIP = set(os.environ.get('KSKIP', '').split(','))


def build_program(debug=None, stop=None):
    nc = bass.Bass("TRN2", target_bir_lowering=False)

    def din(name, shape, dt=F32):
        return nc.dram_tensor(name, list(shape), dt, kind="ExternalInput").ap()

    def dscr(name, shape, dt):
        return nc.dram_tensor(name, list(shape), dt, kind="Internal").ap()

    x_d = din("x", [NSEQ, S, D])
    cT_d = din("cT", [128, 8, NSEQ])
    pos_d = din("posl", [128, NSEQ * NT], I32)
    wada_d = din("w_ada", [D, 6 * D])
    bada_d = din("b_ada2", [NSEQ, 6 * D])
    win_d = din("w_in", [D, D_IN])
    wq_d = din("w_q_up", [256, 768])
    wkv_d = din("w_kv_up", [128, 1024])
    wout_d = din("w_out", [D, D])
    wup_d = din("w_up", [D, 2 * D_FF])
    wdn_d = din("w_down", [D_FF, D])
    gfeat_d = din("gfeat", [128, 19])
    convp_d = din("convp", [128, 4, 2 * NFF])
    grow_d = din("grow", [128, 320])
    bias34_d = din("bias34", [128, 8, 2, 128])
    bfar_d = din("bfar", [128, 8])
    invf_d = din("invf", [128, 16])
    out_d = nc.dram_tensor("out", [NSEQ, S, D], F32, kind="ExternalOutput").ap()

    mod_d = dscr("mod_scr", [NSEQ, 6 * D], F32)
    qT_d = dscr("qT_scr", [NSEQ, 8, 96, S], BF16)
    kT_d = dscr("kT_scr", [NSEQ, 8, 96, S], BF16)
    cqT_d = dscr("cqT_scr", [NSEQ, 8, 64, S], BF16)
    ckT_d = dscr("ckT_scr", [NSEQ, 8, 64, S], BF16)
    V_d = dscr("V_scr", [NSEQ, S, 8 * 65], BF16)
    cV_d = dscr("cV_scr", [NSEQ, S, 8 * 65], BF16)
    otn_d = dscr("otn_scr", [NSEQ, D, S], BF16)

    with ExitStack() as es:
        P = Prog(nc, es)

        def sb(stack, name, shape, dt):
            return stack.enter_context(nc.sbuf_tensor("sb_" + name, list(shape), dt))

        banks_f = [es.enter_context(nc.psum_tensor("bank%d" % i, [128, 512], F32)) for i in range(8)]
        banks_b = [b[:].bitcast(BF16) for b in banks_f]
        BK = Banks(banks_f)

        def bk(b):
            return "bank%d" % b

        ident = sb(es, "ident", [128, 128], BF16)
        identf = sb(es, "identf", [128, 128], F32)
        sel64 = sb(es, "sel64", [128, 64], F32)
        gfeat = sb(es, "gfeat", [128, 19], F32)
        grow = sb(es, "grow", [128, 320], F32)
        AB = sb(es, "AB", [128, NSEQ, 4, 8], F32)
        sa = es.enter_context(ExitStack())
        cs_all = sb(sa, "cs_all", [128, NSEQ * NT, 32], F32)
        junk = sb(sa, "junk", [128, 1024], BF16)

        P.add("pool", lambda e: e.memset(identf[:], 1.0), w=["identf"])
        P.add("pool", lambda e: e.affine_select(out=identf[:], in_=identf[:], pattern=[[-1, 128]],
                                                compare_op=ALU.is_equal, fill=0.0, base=0, channel_multiplier=1),
              r=["identf"], w=["identf"])
        P.add("pool", lambda e: e.tensor_copy(out=ident[:], in_=identf[:]), r=["identf"], w=["ident"])
        P.add("pool", lambda e: e.memset(sel64[:], 0.0), w=["sel64"])
        P.add("pool", lambda e: e.memset(sel64[64:65, :], 1.0), r=["sel64"], w=["sel64"])
        P.add("sp", lambda e: e.dma_start(out=gfeat[:], in_=gfeat_d), w=["gfeat"], chan="small", waitall=True)
        P.add("sp", lambda e: e.dma_start(out=grow[:], in_=grow_d), w=["grow"], chan="small", waitall=True)
        s0 = es.enter_context(ExitStack())
        cT = sb(s0, "cT", [128, 8, NSEQ], F32)
        bada = sb(s0, "bada", [NSEQ, 6 * D], F32)
        posi = sb(s0, "posi", [128, NSEQ * NT], I32)
        invf = sb(s0, "invf", [128, 16], F32)
        P.add("sp", lambda e: e.dma_start(out=cT[:], in_=cT_d), w=["cT"], chan="small", waitall=True)
        P.add("sp", lambda e: e.dma_start(out=bada[:], in_=bada_d), w=["bada"], chan="small", waitall=True)
        P.add("sp", lambda e: e.dma_start(out=posi[:], in_=pos_d), w=["posi"], chan="small", waitall=True)
        P.add("sp", lambda e: e.dma_start(out=invf[:], in_=invf_d), w=["invf"], chan="small", waitall=True)
        P.add("dve", lambda e: e.tensor_scalar(out=grow[:, 96:192], in0=grow[:, 96:192], scalar1=math.sqrt(96.0),
                                               scalar2=None, op0=ALU.mult), r=["grow"], w=["grow"])
        P.add("dve", lambda e: e.tensor_scalar(out=grow[:, 256:320], in0=grow[:, 256:320], scalar1=8.0,
                                               scalar2=None, op0=ALU.mult), r=["grow"], w=["grow"])

        if True:
            scb = sb(s0, "scb", [128, 8, NSEQ], BF16)
            wab = [sb(s0, "wab%d" % i, [128, 8, 512], BF16) for i in range(2)]
            modsb = sb(s0, "modsb", [NSEQ, 6 * D], F32)
            P.add("act", lambda e: e.activation(out=scb[:], in_=cT[:], func=AF.Silu), r=["cT"], w=["scb"])
            for nb in range(12):
                sl = nb % 2
                P.add("pool", (lambda nb, sl: lambda e: e.dma_start(
                    out=wab[sl][:], in_=wada_d[:, nb * 512:(nb + 1) * 512].rearrange("(j p) n -> p j n", p=128)))(nb, sl),
                    w=["wab%d" % sl], chan="wab%d" % sl)
                b = BK.get()
                for k in range(8):
                    P.add("pe", (lambda b, sl, k: lambda e: e.matmul(
                        banks_f[b][0:NSEQ, 0:512], lhsT=scb[:, k, :], rhs=wab[sl][:, k, :], start=(k == 0), stop=(k == 7)))(b, sl, k),
                        r=["scb", "wab%d" % sl], w=[bk(b)])
                P.add("dve", (lambda b, nb: lambda e: e.tensor_tensor(
                    out=modsb[:, nb * 512:(nb + 1) * 512], in0=banks_f[b][0:NSEQ, 0:512],
                    in1=bada[:, nb * 512:(nb + 1) * 512], op=ALU.add))(b, nb),
                    r=[bk(b), "bada"], w=["modsb"])
            P.add("sp", lambda e: e.dma_start(out=mod_d, in_=modsb[:]), r=["modsb"], w=["mod_d"], chan="modst")
            P.tag = "modT"
            modT = sb(s0, "modT", [128, NSEQ, 4, 8], F32)
            bT = BK.get()

            def modtr(qi, c, j):
                col = (qi * 8 + j) * NSEQ
                P.add("pe", lambda e: e.matmul(banks_f[bT][:, col:col + NSEQ], lhsT=modsb[0:NSEQ, c * D + j * 128:c * D + (j + 1) * 128],
                                               rhs=identf[0:NSEQ, 0:NSEQ], start=True, stop=True),
                      r=["modsb", "identf"], w=[bk(bT)])
            for qi, c in enumerate((0, 1, 3, 4)):
                for j in range(8):
                    modtr(qi, c, j)
            P.add("dve", lambda e: e.tensor_copy(out=modT[:].rearrange("p s k j -> p k j s"),
                                                 in_=banks_f[bT][:, 0:32 * NSEQ].rearrange("p (k j s) -> p k j s", k=4, j=8)),
                  r=[bk(bT)], w=["modT"])
            for s in range(NSEQ):
                for (dst, srcq, gcol) in ((0, 1, 0), (2, 3, 8)):
                    P.add("dve", (lambda s, dst, srcq, gcol: lambda e: e.scalar_tensor_tensor(
                        out=AB[:, s, dst, :], in0=modT[:, s, srcq, :], scalar=1.0, in1=gfeat[:, gcol:gcol + 8],
                        op0=ALU.add, op1=ALU.mult))(s, dst, srcq, gcol),
                        r=["modT", "gfeat"], w=["AB"])
                    P.add("dve", (lambda s, dst: lambda e: e.tensor_scalar(
                        out=AB[:, s, dst, :], in0=AB[:, s, dst, :], scalar1=32.0, scalar2=None, op0=ALU.mult))(s, dst),
                        r=["AB"], w=["AB"])
                    P.add("dve", (lambda s, dst, srcq: lambda e: e.tensor_copy(
                        out=AB[:, s, dst + 1, :], in_=modT[:, s, srcq - 1, :]))(s, dst, srcq),
                        r=["modT"], w=["AB"])
            P.tag = "rot"
            posf = sb(s0, "posf", [128, NSEQ * NT], F32)
            ang = sb(s0, "ang", [128, NSEQ * NT, 32], F32)
            kf = sb(s0, "kf", [128, NSEQ * NT, 32], F32)
            ki = sb(s0, "ki", [128, NSEQ * NT, 32], I32)
            mk = sb(s0, "mk", [128, NSEQ * NT, 32], F32)
            P.add("dve", lambda e: e.tensor_copy(out=posf[:], in_=posi[:]), r=["posi"], w=["posf"])
            NTT = NSEQ * NT
            P.add("dve", lambda e: e.tensor_tensor(out=ang[:, :, 16:32], in0=posf[:].unsqueeze(2).to_broadcast([128, NTT, 16]),
                                                   in1=invf[:].unsqueeze(1).to_broadcast([128, NTT, 16]), op=ALU.mult),
                  r=["posf", "invf"], w=["ang"])
            P.add("dve", lambda e: e.tensor_scalar(out=ang[:, :, 0:16], in0=ang[:, :, 16:32], scalar1=math.pi / 2.0,
                                                   scalar2=None, op0=ALU.add), r=["ang"], w=["ang"])
            P.add("dve", lambda e: e.tensor_scalar(out=kf[:], in0=ang[:], scalar1=1.0 / TWO_PI, scalar2=None, op0=ALU.mult),
                  r=["ang"], w=["kf"])
            P.add("dve", lambda e: e.tensor_copy(out=ki[:], in_=kf[:]), r=["kf"], w=["ki"])
            P.add("dve", lambda e: e.tensor_copy(out=kf[:], in_=ki[:]), r=["ki"], w=["kf"])
            P.add("dve", lambda e: e.scalar_tensor_tensor(out=ang[:], in0=kf[:], scalar=-TWO_PI, in1=ang[:],
                                                          op0=ALU.mult, op1=ALU.add), r=["kf", "ang"], w=["ang"])
            P.add("dve", lambda e: e.tensor_scalar(out=mk[:], in0=ang[:], scalar1=math.pi, scalar2=-TWO_PI,
                                                   op0=ALU.is_gt, op1=ALU.mult), r=["ang"], w=["mk"])
            P.add("dve", lambda e: e.tensor_tensor(out=ang[:], in0=ang[:], in1=mk[:], op=ALU.add), r=["ang", "mk"], w=["ang"])
            P.add("dve", lambda e: e.tensor_scalar(out=mk[:], in0=ang[:], scalar1=-math.pi, scalar2=TWO_PI,
                                                   op0=ALU.is_lt, op1=ALU.mult), r=["ang"], w=["mk"])
            P.add("dve", lambda e: e.tensor_tensor(out=ang[:], in0=ang[:], in1=mk[:], op=ALU.add), r=["ang", "mk"], w=["ang"])
            P.add("dve", lambda e: e.tensor_scalar(out=ang[:], in0=ang[:], scalar1=math.pi, scalar2=-math.pi,
                                                   op0=ALU.min, op1=ALU.max), r=["ang"], w=["ang"])
            P.add("act", lambda e: e.activation(out=cs_all[:], in_=ang[:], func=AF.Sin), r=["ang"], w=["cs_all"])
            P.tag = None
            P.barrier()
            s0.close()

        if True:
            P.tag = "wlA"
            win = sb(sa, "win", [128, 8, D_IN], BF16)
            wq = sb(sa, "wq", [128, 2, 768], BF16)
            wkv = sb(sa, "wkv", [128, 1024], BF16)
            wstage = sb(sa, "wstage", [128, 2, 768], F32)
            wstage2 = sb(sa, "wstage2", [128, 1024], F32)
            WIN_KEYS = []
            for pi_, (c0, c1) in enumerate(((0, 512), (512, 1024), (1024, 1536), (1536, 1952))):
                P.add("pool", (lambda c0, c1: lambda e: e.dma_start(out=win[:, :, c0:c1], in_=win_d[:, c0:c1].rearrange("(j p) n -> p j n", p=128)))(c0, c1),
                      r=(["win_p%d" % (pi_ - 2)] if pi_ >= 2 else []), w=["win_p%d" % pi_], chan="win_c%d" % (pi_ % 2))
                WIN_KEYS.append("win_p%d" % pi_)
            P.tag = "wlA2"
            P.add("sp", lambda e: e.dma_start(out=wstage[:], in_=wq_d.rearrange("(j p) n -> p j n", p=128)),
                  w=["wstage"], chan="small3", waitall=True)
            P.add("sp", lambda e: e.dma_start(out=wstage2[:], in_=wkv_d), w=["wstage2"], chan="small3", waitall=True)
            for j in range(2):
                P.add("dve", (lambda j: lambda e: e.tensor_scalar(
                    out=wq[:, j, :], in0=wstage[:, j, :], scalar1=gfeat[:, 16 + j:17 + j], scalar2=16.0,
                    op0=ALU.mult, op1=ALU.mult))(j), r=["wstage", "gfeat"], w=["wq"])
            P.add("dve", lambda e: e.tensor_scalar(out=wkv[:], in0=wstage2[:], scalar1=gfeat[:, 18:19],
                                                   scalar2=math.sqrt(128.0), op0=ALU.mult, op1=ALU.mult),
                  r=["wstage2", "gfeat"], w=["wkv"])

            P.tag = "wlA3"
            xt = [sb(sa, "xt%d" % i, [128, D], F32) for i in range(2)]
            xn = [sb(sa, "xn%d" % i, [128, D], BF16) for i in range(2)]
            hT = [sb(sa, "hT%d" % i, [128, 8, 128], BF16) for i in range(2)]
            st = sb(sa, "stA", [128, 2, 40], F32)
            lat = [sb(sa, "lat%d" % i, [128, 384], BF16) for i in range(2)]
            latT = [sb(sa, "latT%d" % i, [128, 3, 128], BF16) for i in range(2)]
            kr = [sb(sa, "kr%d" % i, [128, 4, 32], F32) for i in range(2)]
            qn = [sb(sa, "qn%d" % i, [128, 8, 96], F32) for i in range(2)]
            qsq = [sb(sa, "qsq%d" % i, [128, 8, 96], F32) for i in range(2)]
            qb = [sb(sa, "qb%d" % i, [128, 8, 96], BF16) for i in range(2)]
            kn = [sb(sa, "kn%d" % i, [128, 8, 64], F32) for i in range(2)]
            ksq = [sb(sa, "ksq%d" % i, [128, 8, 64], F32) for i in range(2)]
            kb = [sb(sa, "kb%d" % i, [128, 8, 96], BF16) for i in range(2)]
            rt = [sb(sa, "rt%d" % i, [128, 8, 32], F32) for i in range(2)]
            cqn = [sb(sa, "cqn%d" % i, [128, 8, 64], F32) for i in range(2)]
            cqs = [sb(sa, "cqs%d" % i, [128, 8, 64], F32) for i in range(2)]
            cqb = [sb(sa, "cqb%d" % i, [128, 8, 64], BF16) for i in range(2)]
            ckn = [sb(sa, "ckn%d" % i, [128, 8, 64], F32) for i in range(2)]
            cks = [sb(sa, "cks%d" % i, [128, 8, 64], F32) for i in range(2)]
            ckb = [sb(sa, "ckb%d" % i, [128, 8, 64], BF16) for i in range(2)]
            qT_st = [sb(sa, "qTst%d" % i, [96, 8, 512], BF16) for i in range(2)]
            kT_st = [sb(sa, "kTst%d" % i, [96, 8, 512], BF16) for i in range(2)]
            cqT_st = [sb(sa, "cqTst%d" % i, [64, 8, 512], BF16) for i in range(2)]
            ckT_st = [sb(sa, "ckTst%d" % i, [64, 8, 512], BF16) for i in range(2)]
            V_st = [sb(sa, "Vst%d" % i, [128, 4, 8, 65], BF16) for i in range(2)]
            cV_st = [sb(sa, "cVst%d" % i, [128, 4, 8, 65], BF16) for i in range(2)]
            for i in range(2):
                P.add("pool", (lambda i: lambda e: e.memset(V_st[i][:, :, :, 64:65], 1.0))(i), w=["Vst%d" % i])
                P.add("pool", (lambda i: lambda e: e.memset(cV_st[i][:, :, :, 64:65], 1.0))(i), w=["cVst%d" % i])

            P.tag = None


            def rstd_from_ssq(sl, col, n, tag):
                kst = "st%d_%s" % (sl, tag)
                P.add("act", lambda e: e.activation(out=st[:, sl, col:col + 1], in_=st[:, sl, col:col + 1], func=AF.Sqrt,
                                                    bias=float(n * EPS), scale=1.0), r=[kst], w=[kst])
                P.add("dve", lambda e: e.reciprocal(out=st[:, sl, col:col + 1], in_=st[:, sl, col:col + 1]), r=[kst], w=[kst])
                return kst

            def rstd_vec(sl, c0, nh, n, tag):
                kst = "st%d_%s" % (sl, tag)
                P.add("act", lambda e: e.activation(out=st[:, sl, c0:c0 + nh], in_=st[:, sl, c0:c0 + nh], func=AF.Sqrt,
                                                    bias=float(n * EPS), scale=1.0), r=[kst], w=[kst])
                P.add("dve", lambda e: e.reciprocal(out=st[:, sl, c0:c0 + nh], in_=st[:, sl, c0:c0 + nh]), r=[kst], w=[kst])
                return kst

            def rope(eng, src3, dst3, cs, keys_r, keys_w, tmp, nh, ktmp):
                cosb = cs[:, 0:16].unsqueeze(1).to_broadcast([128, nh, 16])
                sinb = cs[:, 16:32].unsqueeze(1).to_broadcast([128, nh, 16])
                P.add(eng, lambda e: e.tensor_tensor(out=tmp[:, :, 0:16], in0=src3[:, :, 16:32], in1=sinb, op=ALU.mult),
                      r=keys_r, w=[ktmp])
                P.add(eng, lambda e: e.tensor_tensor(out=tmp[:, :, 16:32], in0=src3[:, :, 0:16], in1=sinb, op=ALU.mult),
                      r=keys_r + [ktmp], w=[ktmp])
                P.add(eng, lambda e: e.tensor_tensor(out=src3[:, :, 0:16], in0=src3[:, :, 0:16], in1=cosb, op=ALU.mult),
                      r=keys_r + [ktmp], w=keys_r[:1])
                P.add(eng, lambda e: e.tensor_tensor(out=src3[:, :, 16:32], in0=src3[:, :, 16:32], in1=cosb, op=ALU.mult),
                      r=keys_r + [ktmp], w=keys_r[:1])
                P.add(eng, lambda e: e.tensor_tensor(out=dst3[:, :, 0:16], in0=src3[:, :, 0:16], in1=tmp[:, :, 0:16], op=ALU.subtract),
                      r=keys_r + [ktmp], w=keys_w)
                P.add(eng, lambda e: e.tensor_tensor(out=dst3[:, :, 16:32], in0=src3[:, :, 16:32], in1=tmp[:, :, 16:32], op=ALU.add),
                      r=keys_r + [ktmp], w=keys_w)

            def prep_tile(g):
                s, tt = divmod(g, NT)
                jb, i4 = divmod(tt, 4)
                sl = g % 2
                bs = (g // 4) % 2
                S_ = str(sl)
                kxt, kxn, khT = "xt" + S_, "xn" + S_, "hT" + S_
                P.add("sp", lambda e: e.dma_start(out=xt[sl][:], in_=x_d[s, tt * 128:(tt + 1) * 128, :]), w=[kxt], chan="xt" + S_)
                P.add("act", lambda e: e.activation(out=junk[:], in_=xt[sl][:], func=AF.Square, accum_out=st[:, sl, 0:1]),
                      r=[kxt], w=["st%s_x" % S_])
                kst = rstd_from_ssq(sl, 0, D, "x")
                P.add("act", lambda e: e.activation(out=xn[sl][:], in_=xt[sl][:], func=AF.Copy, scale=st[:, sl, 0:1]),
                      r=[kxt, kst], w=[kxn])
                yield
                b0 = BK.get(hold=True)
                for c in range(8):
                    P.add("pe", (lambda c: lambda e: e.transpose(out=banks_b[b0][:, c * 128:(c + 1) * 128],
                                                                  in_=xn[sl][:, c * 128:(c + 1) * 128], identity=ident[:]))(c),
                          r=[kxn, "ident"], w=[bk(b0)])
                tp3 = banks_b[b0][:, 0:1024].rearrange("p (c t) -> p c t", c=8)
                P.add("dve", lambda e: e.tensor_tensor(out=hT[sl][:], in0=tp3, in1=AB[:, s, 0, :].unsqueeze(2).to_broadcast([128, 8, 128]),
                                                       op=ALU.mult), r=[bk(b0), "AB"], w=[khT])
                BK.release(b0)
                P.add("pool", lambda e: e.tensor_tensor(out=hT[sl][:], in0=hT[sl][:], in1=AB[:, s, 1, :].unsqueeze(2).to_broadcast([128, 8, 128]),
                                                        op=ALU.add), r=[khT, "AB"], w=[khT])
                cols = [(0, 416), (416, 928), (928, 1440), (1440, 1952)]

                def proj_group(bi):
                    bb_ = BK.get(hold=True)
                    c0, c1 = cols[bi]
                    for k in range(8):
                        P.add("pe", (lambda k: lambda e: e.matmul(
                            banks_f[bb_][:, 0:c1 - c0], lhsT=hT[sl][:, k, :], rhs=win[:, k, c0:c1],
                            start=(k == 0), stop=(k == 7)))(k),
                            r=[khT] + WIN_KEYS, w=[bk(bb_)])
                    return bb_
                yield
                pb0 = proj_group(0)
                pl = banks_f[pb0]
                kl = bk(pb0)
                yield
                P.add("act", lambda e: e.activation(out=junk[:, 0:256], in_=pl[:, 0:256], func=AF.Square, accum_out=st[:, sl, 1:2]),
                      r=[kl], w=["st%s_ql" % S_])
                P.add("act", lambda e: e.activation(out=junk[:, 256:384], in_=pl[:, 256:384], func=AF.Square, accum_out=st[:, sl, 2:3]),
                      r=[kl], w=["st%s_kvl" % S_])
                k1 = rstd_from_ssq(sl, 1, 256, "ql")
                k2 = rstd_from_ssq(sl, 2, 128, "kvl")
                klat = "lat" + S_
                P.add("dve", lambda e: e.tensor_scalar(out=lat[sl][:, 0:256], in0=pl[:, 0:256], scalar1=st[:, sl, 1:2], scalar2=None,
                                                       op0=ALU.mult), r=[kl, k1], w=[klat + "a"])
                P.add("dve", lambda e: e.tensor_scalar(out=lat[sl][:, 256:384], in0=pl[:, 256:384], scalar1=st[:, sl, 2:3], scalar2=None,
                                                       op0=ALU.mult), r=[kl, k2], w=[klat + "b"])
                kkr = "kr" + S_
                P.add("dve", lambda e: e.tensor_tensor(out=kr[sl][:, 0, :], in0=pl[:, 384:416], in1=grow[:, 160:192], op=ALU.mult),
                      r=[kl, "grow"], w=[kkr])
                P.add("act", lambda e: e.activation(out=junk[:, 512:544], in_=pl[:, 384:416], func=AF.Square, accum_out=st[:, sl, 3:4]),
                      r=[kl], w=["st%s_kr" % S_])
                BK.release(pb0)
                yield
                b1 = BK.get(hold=True)
                for c in range(3):
                    P.add("pe", (lambda c: lambda e: e.transpose(out=banks_b[b1][:, c * 128:(c + 1) * 128],
                                                                  in_=lat[sl][:, c * 128:(c + 1) * 128], identity=ident[:]))(c),
                          r=[klat + "a", klat + "b", "ident"], w=[bk(b1)])
                klT = "latT" + S_
                P.add("act", lambda e: e.activation(out=latT[sl][:].rearrange("p c t -> p (c t)"), in_=banks_b[b1][:, 0:384], func=AF.Copy),
                      r=[bk(b1)], w=[klT])
                BK.release(b1)
                yield
                bq0, bq1 = BK.get(hold=True), BK.get(hold=True)
                for (bb, c0, c1) in ((bq0, 0, 480), (bq1, 480, 768)):
                    for k in range(2):
                        P.add("pe", (lambda bb, c0, c1, k: lambda e: e.matmul(
                            banks_f[bb][:, 0:c1 - c0], lhsT=latT[sl][:, k, :], rhs=wq[:, k, c0:c1], start=(k == 0), stop=(k == 1)))(bb, c0, c1, k),
                            r=[klT, "wq"], w=[bk(bb)])
                bkv0, bkv1 = BK.get(hold=True), BK.get(hold=True)
                for (bb, c0) in ((bkv0, 0), (bkv1, 512)):
                    P.add("pe", (lambda bb, c0: lambda e: e.matmul(
                        banks_f[bb][:, 0:512], lhsT=latT[sl][:, 2, :], rhs=wkv[:, c0:c0 + 512], start=True, stop=True))(bb, c0),
                        r=[klT, "wkv"], w=[bk(bb)])
                yield
                cs = cs_all[:, g, :]
                kqn, kqs, kqb = "qn" + S_, "qsq" + S_, "qb" + S_
                q0 = banks_f[bq0][:, 0:480].rearrange("p (h d) -> p h d", h=5)
                q1 = banks_f[bq1][:, 0:288].rearrange("p (h d) -> p h d", h=3)
                P.add("act", lambda e: e.activation(out=qn[sl][:, 0:5, :], in_=q0, func=AF.Copy), r=[bk(bq0)], w=[kqn + "a"])
                P.add("act", lambda e: e.activation(out=qn[sl][:, 5:8, :], in_=q1, func=AF.Copy), r=[bk(bq1)], w=[kqn + "b"])
                BK.release(bq0)
                BK.release(bq1)
                yield
                P.add("pool", lambda e: e.tensor_tensor(out=qsq[sl][:], in0=qn[sl][:], in1=qn[sl][:], op=ALU.mult),
                      r=[kqn + "a", kqn + "b"], w=[kqs])
                P.add("dve", lambda e: e.tensor_reduce(out=st[:, sl, 8:16], in_=qsq[sl][:], axis=AX.X, op=ALU.add),
                      r=[kqs], w=["st%s_q" % S_])
                k3 = rstd_vec(sl, 8, 8, 96, "q")
                P.add("dve", lambda e: e.tensor_tensor(out=qn[sl][:], in0=qn[sl][:], in1=st[:, sl, 8:16].unsqueeze(2).to_broadcast([128, 8, 96]),
                                                       op=ALU.mult), r=[kqn + "a", kqn + "b", k3], w=[kqn])
                P.add("pool", lambda e: e.tensor_tensor(out=qn[sl][:], in0=qn[sl][:], in1=grow[:, 0:96].unsqueeze(1).to_broadcast([128, 8, 96]),
                                                        op=ALU.mult), r=[kqn, "grow"], w=[kqn])
                P.add("act", lambda e: e.activation(out=qb[sl][:, :, 0:64], in_=qn[sl][:, :, 0:64], func=AF.Copy), r=[kqn], w=[kqb + "n"])
                yield
                rope("pool", qn[sl][:, :, 64:96], qb[sl][:, :, 64:96], cs, [kqn, "cs_all"], [kqb + "r"], rt[sl], 8, "rt" + S_)
                kkn, kks, kkb = "kn" + S_, "ksq" + S_, "kb" + S_
                kv0 = banks_f[bkv0][:, 0:512].rearrange("p (h d) -> p h d", h=4)
                kv1 = banks_f[bkv1][:, 0:512].rearrange("p (h d) -> p h d", h=4)
                P.add("act", lambda e: e.activation(out=kn[sl][:, 0:4, :], in_=kv0[:, :, 0:64], func=AF.Copy), r=[bk(bkv0)], w=[kkn + "a"])
                P.add("act", lambda e: e.activation(out=kn[sl][:, 4:8, :], in_=kv1[:, :, 0:64], func=AF.Copy), r=[bk(bkv1)], w=[kkn + "b"])
                P.add("act", lambda e: e.activation(out=V_st[bs][:, i4, 0:4, 0:64], in_=kv0[:, :, 64:128], func=AF.Copy),
                      r=[bk(bkv0)], w=["Vst%d" % bs])
                P.add("act", lambda e: e.activation(out=V_st[bs][:, i4, 4:8, 0:64], in_=kv1[:, :, 64:128], func=AF.Copy),
                      r=[bk(bkv1)], w=["Vst%d" % bs])
                BK.release(bkv0)
                BK.release(bkv1)
                yield
                P.add("pool", lambda e: e.tensor_tensor(out=ksq[sl][:], in0=kn[sl][:], in1=kn[sl][:], op=ALU.mult),
                      r=[kkn + "a", kkn + "b"], w=[kks])
                P.add("dve", lambda e: e.tensor_reduce(out=st[:, sl, 16:24], in_=ksq[sl][:], axis=AX.X, op=ALU.add),
                      r=[kks], w=["st%s_k" % S_])
                P.add("dve", lambda e: e.tensor_scalar(out=st[:, sl, 16:24], in0=st[:, sl, 16:24], scalar1=st[:, sl, 3:4], scalar2=None,
                                                       op0=ALU.add), r=["st%s_k" % S_, "st%s_kr" % S_], w=["st%s_k" % S_])
                k4 = rstd_vec(sl, 16, 8, 96, "k")
                yield
                P.add("dve", lambda e: e.tensor_tensor(out=kn[sl][:], in0=kn[sl][:], in1=st[:, sl, 16:24].unsqueeze(2).to_broadcast([128, 8, 64]),
                                                       op=ALU.mult), r=[kkn + "a", kkn + "b", k4], w=[kkn])
                P.add("pool", lambda e: e.tensor_tensor(out=kb[sl][:, :, 0:64], in0=kn[sl][:], in1=grow[:, 96:160].unsqueeze(1).to_broadcast([128, 8, 64]),
                                                        op=ALU.mult), r=[kkn, "grow"], w=[kkb + "n"])
                rope("dve", kr[sl][:, 0:1, :], kr[sl][:, 1:2, :], cs, [kkr, "cs_all"], [kkr + "o"], kr[sl][:, 2:3, :], 1, kkr + "t")
                P.add("dve", lambda e: e.tensor_tensor(out=kb[sl][:, :, 64:96], in0=kr[sl][:, 1:2, :].to_broadcast([128, 8, 32]),
                                                       in1=st[:, sl, 16:24].unsqueeze(2).to_broadcast([128, 8, 32]), op=ALU.mult),
                      r=[kkr + "o", k4], w=[kkb + "r"])
                for (bi_, dn, dsq, db, col, gc0, tag) in (
                        (1, cqn, cqs, cqb, 24, 192, "cq"), (2, ckn, cks, ckb, 32, 256, "ck")):
                    kdn, kds, kdb = tag + "n" + S_, tag + "s" + S_, tag + "b" + S_
                    yield
                    bbx = proj_group(bi_)
                    P.add("act", (lambda bbx, dn: lambda e: e.activation(out=dn[sl][:].rearrange("p h d -> p (h d)"), in_=banks_f[bbx][:, 0:512], func=AF.Copy))(bbx, dn),
                          r=[bk(bbx)], w=[kdn])
                    BK.release(bbx)
                    yield
                    P.add("pool", (lambda dn, dsq: lambda e: e.tensor_tensor(out=dsq[sl][:], in0=dn[sl][:], in1=dn[sl][:], op=ALU.mult))(dn, dsq),
                          r=[kdn], w=[kds])
                    P.add("dve", (lambda dsq, col: lambda e: e.tensor_reduce(out=st[:, sl, col:col + 8], in_=dsq[sl][:], axis=AX.X, op=ALU.add))(dsq, col),
                          r=[kds], w=["st%s_%s" % (S_, tag)])
                    k5 = rstd_vec(sl, col, 8, 64, tag)
                    P.add("dve", (lambda dn, col: lambda e: e.tensor_tensor(out=dn[sl][:], in0=dn[sl][:],
                                                                           in1=st[:, sl, col:col + 8].unsqueeze(2).to_broadcast([128, 8, 64]), op=ALU.mult))(dn, col),
                          r=[kdn, k5], w=[kdn])
                    P.add("pool", (lambda dn, db, gc0: lambda e: e.tensor_tensor(out=db[sl][:], in0=dn[sl][:],
                                                                                in1=grow[:, gc0:gc0 + 64].unsqueeze(1).to_broadcast([128, 8, 64]), op=ALU.mult))(dn, db, gc0),
                          r=[kdn, "grow"], w=[kdb])
                yield
                bbv = proj_group(3)
                P.add("act", lambda e: e.activation(out=cV_st[bs][:, i4, :, 0:64], in_=banks_f[bbv][:, 0:512].rearrange("p (h d) -> p h d", h=8), func=AF.Copy),
                      r=[bk(bbv)], w=["cVst%d" % bs])
                BK.release(bbv)
                for (srcb, keys, dst, kdst, dd) in ((qb, [kqb + "n", kqb + "r"], qT_st, "qTst%d" % bs, 96),
                                                    (kb, [kkb + "n", kkb + "r"], kT_st, "kTst%d" % bs, 96),
                                                    (cqb, ["cqb" + S_], cqT_st, "cqTst%d" % bs, 64),
                                                    (ckb, ["ckb" + S_], ckT_st, "ckTst%d" % bs, 64)):
                    yield
                    bt = BK.get(hold=True)
                    for h in range(8):
                        P.add("pe", (lambda srcb, bt, h, dd: lambda e: e.transpose(
                            out=banks_b[bt][0:dd, h * 128:(h + 1) * 128], in_=srcb[sl][:, h, :], identity=ident[:]))(srcb, bt, h, dd),
                            r=keys + ["ident"], w=[bk(bt)])
                    P.add("act", (lambda bt, dst, dd: lambda e: e.activation(
                        out=dst[bs][:, :, i4 * 128:(i4 + 1) * 128], in_=banks_b[bt][0:dd, 0:1024].rearrange("p (h t) -> p h t", h=8), func=AF.Copy))(bt, dst, dd),
                        r=[bk(bt)], w=[kdst])
                    BK.release(bt)
                if i4 == 3:
                    t0 = jb * 512
                    P.add("sp", lambda e: e.dma_start(out=qT_d[s, :, :, t0:t0 + 512].rearrange("h d t -> d h t"), in_=qT_st[bs][:]),
                          r=["qTst%d" % bs], w=["qT_d%d" % s], chan="stq%d" % bs)
                    P.add("sp", lambda e: e.dma_start(out=kT_d[s, :, :, t0:t0 + 512].rearrange("h d t -> d h t"), in_=kT_st[bs][:]),
                          r=["kTst%d" % bs], w=["kT_d%d" % s], chan="stk%d" % bs)
                    P.add("sp", lambda e: e.dma_start(out=cqT_d[s, :, :, t0:t0 + 512].rearrange("h d t -> d h t"), in_=cqT_st[bs][:]),
                          r=["cqTst%d" % bs], w=["cqT_d%d" % s], chan="stcq%d" % bs)
                    P.add("sp", lambda e: e.dma_start(out=ckT_d[s, :, :, t0:t0 + 512].rearrange("h d t -> d h t"), in_=ckT_st[bs][:]),
                          r=["ckTst%d" % bs], w=["ckT_d%d" % s], chan="stck%d" % bs)
                    P.add("sp", lambda e: e.dma_start(out=V_d[s, t0:t0 + 512, :].rearrange("(k p) c -> p k c", p=128),
                                                      in_=V_st[bs][:].rearrange("p k h c -> p k (h c)")),
                          r=["Vst%d" % bs], w=["V_d%d" % s], chan="stv%d" % bs)
                    P.add("sp", lambda e: e.dma_start(out=cV_d[s, t0:t0 + 512, :].rearrange("(k p) c -> p k c", p=128),
                                                      in_=cV_st[bs][:].rearrange("p k h c -> p k (h c)")),
                          r=["cVst%d" % bs], w=["cV_d%d" % s], chan="stcv%d" % bs)

            ntiles = NSEQ * NT if stop not in ("setup",) else 0
            active = []
            nxt = 0
            INFL = 2
            STAG = 8
            steps = {}
            while nxt < ntiles or active:
                if nxt < ntiles and len(active) < INFL and all(steps[id(g_)] >= STAG for g_ in active):
                    gnew = prep_tile(nxt)
                    steps[id(gnew)] = 0
                    active.append(gnew)
                    nxt += 1
                for gen in list(active):
                    try:
                        next(gen)
                        steps[id(gen)] += 1
                    except StopIteration:
                        active.remove(gen)
            P.barrier()
            sa.close()

        with ExitStack() as sbk:
            kT_all = sb(sbk, "kT_all", [96, 8, S], BF16)
            V_all = sb(sbk, "V_all", [128, NT, 8 * 65], BF16)
            ckT_all = sb(sbk, "ckT_all", [64, 8, S], BF16)
            cV_all = sb(sbk, "cV_all", [128, NT, 8 * 65], BF16)
            qT_b = [sb(sbk, "qT_b%d" % i, [96, 8, 512], BF16) for i in range(2)]
            cqT_b = [sb(sbk, "cqT_b%d" % i, [64, 8, 512], BF16) for i in range(2)]
            otn = [sb(sbk, "otn%d" % i, [64, 16, 512], BF16) for i in range(1)]
            Ef = sb(sbk, "Ef", [128, 8, 2, 128], F32)
            bfar = sb(sbk, "bfar", [128, 8], F32)
            NPT = 6
            Pt = [sb(sbk, "Pt%d" % i, [128, 512], BF16) for i in range(NPT)]
            PA = [sb(sbk, "PA%d" % i, [128, 384], BF16) for i in range(3)]
            PB = [sb(sbk, "PB%d" % i, [128, 256], F32) for i in range(3)]
            PBb = [sb(sbk, "PBb%d" % i, [128, 256], BF16) for i in range(3)]
            NRZ = 4
            ots = [sb(sbk, "ots%d" % i, [128, 512], F32) for i in range(NRZ)]
            rzb = [sb(sbk, "rzb%d" % i, [128, 2, 512], BF16) for i in range(NRZ)]
            sel64b = sb(sbk, "sel64b", [128, 64], BF16)
            P.tag = "wlB"
            P.add("sp", lambda e: e.dma_start(out=Ef[:], in_=bias34_d), w=["Ef"], chan="small4", waitall=True)
            P.add("sp", lambda e: e.dma_start(out=bfar[:], in_=bfar_d), w=["bfar"], chan="small4", waitall=True)
            P.add("act", lambda e: e.activation(out=Ef[:], in_=Ef[:], func=AF.Exp), r=["Ef"], w=["Ef"])
            P.add("pool", lambda e: e.memset(Ef[64:128, :, 1, 0:64], 0.0), r=["Ef"], w=["Ef"])
            for i in range(NRZ):
                P.add("pool", (lambda i: lambda e: e.memset(rzb[i][:], 0.0))(i), w=["rzb%d" % i])
            P.add("pool", lambda e: e.tensor_copy(out=sel64b[:], in_=sel64[:]), r=["sel64"], w=["sel64b"])
            P.tag = None
            cnt = {"pt": 0, "pa": 0, "rz": 0}

            def normalize_head(bo, width):
                ri = cnt["rz"] % NRZ
                cnt["rz"] += 1
                P.add("dve", lambda e: e.tensor_copy(out=ots[ri][0:65, 0:width], in_=banks_f[bo][0:65, 0:width]),
                      r=[bk(bo)], w=["ots%d" % ri])
                BK.release(bo)
                P.add("act", lambda e: e.activation(out=ots[ri][64:65, 0:width], in_=ots[ri][64:65, 0:width], func=AF.Ln),
                      r=["ots%d" % ri], w=["ots%d" % ri])
                P.add("act", lambda e: e.activation(out=ots[ri][64:65, 0:width], in_=ots[ri][64:65, 0:width], func=AF.Exp, scale=-1.0),
                      r=["ots%d" % ri], w=["ots%d" % ri])
                P.add("dve", lambda e: e.tensor_copy(out=rzb[ri][64:65, 0, 0:width], in_=ots[ri][64:65, 0:width]),
                      r=["ots%d" % ri], w=["rzb%d" % ri])
                P.add("dve", lambda e: e.tensor_tensor(out=rzb[ri][64:65, 1, 0:width], in0=ots[ri][64:65, 0:width],
                                                       in1=rzb[ri][64:65, 0, 0:width], op=ALU.subtract),
                      r=["ots%d" % ri, "rzb%d" % ri], w=["rzb%d" % ri])
                return ri

            def normalize_tail(ri, width, dst_ap, kdst):
                bb = BK.get(hold=True)
                for pl_ in range(2):
                    P.add("pe", (lambda pl_: lambda e: e.matmul(banks_f[bb][0:64, 0:width], lhsT=sel64b[:, 0:64], rhs=rzb[ri][:, pl_, 0:width],
                                                                start=(pl_ == 0), stop=(pl_ == 1)))(pl_),
                          r=["rzb%d" % ri, "sel64b"], w=[bk(bb)])
                P.add("dve", lambda e: e.tensor_tensor(out=dst_ap, in0=banks_f[bb][0:64, 0:width], in1=ots[ri][0:64, 0:width], op=ALU.mult),
                      r=[bk(bb), "ots%d" % ri], w=[kdst])
                BK.release(bb)

            def mla_gen(s, j, h, qs):
                kq = "qT_b%d" % qs
                nkt = 4 * j + 4
                bo = BK.get(hold=True)
                tiles = []
                for kt in range(nkt):
                    r_ = kt - 4 * j
                    c0 = 128 * r_ if r_ > 0 else 0
                    tiles.append((kt, c0, r_ >= 0))
                sbank = {}

                def emit_s(idx):
                    kt, c0, diag = tiles[idx]
                    b = BK.get(hold=True)
                    sbank[idx] = b
                    P.add("pe", lambda e: e.matmul(banks_f[b][:, 0:512 - c0], lhsT=kT_all[:, h, kt * 128:(kt + 1) * 128],
                                                   rhs=qT_b[qs][:, h, c0:512], start=True, stop=True),
                          r=["kT_all%d" % (h // 4), kq], w=[bk(b)])

                def emit_rest(idx):
                    kt, c0, diag = tiles[idx]
                    b = sbank[idx]
                    pi = cnt["pt"] % NPT
                    cnt["pt"] += 1
                    kp = "Pt%d" % pi
                    w_ = 512 - c0
                    P.add("act", lambda e: e.activation(out=Pt[pi][:, 0:w_], in_=banks_f[b][:, 0:w_], func=AF.Exp), r=[bk(b)], w=[kp])
                    BK.release(b)
                    if diag:
                        P.add("pool", lambda e: e.memset(Pt[pi][64:128, 0:64], 0.0), r=[kp], w=[kp])
                    P.add("pe", lambda e: e.matmul(banks_f[bo][0:65, c0:512], lhsT=V_all[:, kt, h * 65:(h + 1) * 65], rhs=Pt[pi][:, 0:w_],
                                                   start=(idx == 0), stop=(idx == nkt - 1)),
                          r=[kp, "V_all%d" % (kt // 4)], w=[bk(bo)])

                LOOK = 3
                for idx in range(min(LOOK, nkt)):
                    emit_s(idx)
                yield
                for idx in range(nkt):
                    emit_rest(idx)
                    if idx + LOOK < nkt:
                        emit_s(idx + LOOK)
                    yield
                ri = normalize_head(bo, 512)
                pend_m.append((ri, 512, otn[0][:, h, :], "otn0h%d" % h))
                yield

            def ca_qtile(j, h, qs, bo, i):
                kq = "cqT_b%d" % qs
                gi = 4 * j + i
                tmin = max(0, 4 - gi)
                pai = cnt["pa"] % 3
                cnt["pa"] += 1
                ba = BK.get(hold=True) if tmin <= 2 else None
                bb = BK.get(hold=True)

                def s_mm(t):
                    ktile = gi - 4 + t
                    if t <= 2:
                        dst = banks_f[ba][:, t * 128:(t + 1) * 128]
                        kb_ = bk(ba)
                    else:
                        dst = banks_f[bb][:, (t - 3) * 128:(t - 2) * 128]
                        kb_ = bk(bb)
                    P.add("pe", lambda e: e.matmul(dst, lhsT=ckT_all[:, h, ktile * 128:(ktile + 1) * 128],
                                                   rhs=cqT_b[qs][:, h, i * 128:(i + 1) * 128], start=True, stop=True),
                          r=["ckT_all%d" % (h // 4), kq], w=[kb_])

                def pv_mm(t):
                    ktile = gi - 4 + t
                    if t <= 2:
                        rhs = PA[pai][:, t * 128:(t + 1) * 128]
                        kr_ = "PA%d" % pai
                    else:
                        rhs = PBb[pai][:, (t - 3) * 128:(t - 2) * 128]
                        kr_ = "PBb%d" % pai
                    P.add("pe", lambda e: e.matmul(banks_f[bo][0:65, i * 128:(i + 1) * 128],
                                                   lhsT=cV_all[:, ktile, h * 65:(h + 1) * 65], rhs=rhs,
                                                   start=(t == tmin), stop=(t == 4)),
                          r=[kr_, "cV_all%d" % (ktile // 4)], w=[bk(bo)])

                for t in range(tmin, 5):
                    s_mm(t)
                yield
                if ba is not None:
                    a0 = tmin * 128
                    P.add("act", lambda e: e.activation(out=PA[pai][:, a0:384], in_=banks_f[ba][:, a0:384], func=AF.Exp,
                                                        bias=bfar[:, h:h + 1], scale=1.0), r=[bk(ba), "bfar"], w=["PA%d" % pai])
                    BK.release(ba)
                    if tmin == 0:
                        P.add("pool", lambda e: e.memset(PA[pai][0:64, 64:128], 0.0), r=["PA%d" % pai], w=["PA%d" % pai])
                b0 = 0 if tmin <= 3 else 128
                P.add("act", lambda e: e.activation(out=PB[pai][:, b0:256], in_=banks_f[bb][:, b0:256], func=AF.Exp), r=[bk(bb)], w=["PB%d" % pai])
                BK.release(bb)
                P.add("pool", lambda e: e.tensor_tensor(out=PBb[pai][:, b0:256], in0=PB[pai][:, b0:256],
                                                        in1=Ef[:, h, :, :].rearrange("p t q -> p (t q)")[:, b0:256], op=ALU.mult),
                      r=["PB%d" % pai, "Ef"], w=["PBb%d" % pai])
                yield
                for t in range(tmin, 5):
                    pv_mm(t)
                yield

            def ca_gen(s, j, h, qs):
                bo = BK.get(hold=True)
                for i in range(4):
                    yield from ca_qtile(j, h, qs, bo, i)
                ri = normalize_head(bo, 512)
                pend_c.append((ri, 512, otn[0][:, 8 + h, :], "otn0h%d" % (8 + h)))
                yield

            pend_m, pend_c = [], []

            def stream(gens, pend, delay):
                for g in gens:
                    n = 0
                    old = list(pend)
                    del pend[:]
                    for _ in g:
                        n += 1
                        yield
                        if n == delay and old:
                            for t in old:
                                normalize_tail(*t)
                            old = []
                            yield
                    if old:
                        for t in old:
                            normalize_tail(*t)
                        yield
                for t in pend:
                    normalize_tail(*t)
                del pend[:]
                yield

            def interleave(ga, gb, ra, rb):
                alive_a = alive_b = True
                while alive_a or alive_b:
                    for _ in range(ra):
                        if alive_a:
                            try:
                                next(ga)
                            except StopIteration:
                                alive_a = False
                    for _ in range(rb):
                        if alive_b:
                            try:
                                next(gb)
                            except StopIteration:
                                alive_b = False

            def load_seq(s):
                for hh in range(2):
                    P.add("sp", lambda e: e.dma_start(out=kT_all[:, 4 * hh:4 * hh + 4, :], in_=kT_d[s, 4 * hh:4 * hh + 4].rearrange("h d t -> d h t")),
                          r=["kT_d%d" % s], w=["kT_all%d" % hh], chan="ldk%d" % hh)
                    P.add("sp", lambda e: e.dma_start(out=ckT_all[:, 4 * hh:4 * hh + 4, :], in_=ckT_d[s, 4 * hh:4 * hh + 4].rearrange("h d t -> d h t")),
                          r=["ckT_d%d" % s], w=["ckT_all%d" % hh], chan="ldck%d" % hh)
                for q4 in range(4):
                    P.add("sp", lambda e: e.dma_start(out=V_all[:, 4 * q4:4 * q4 + 4, :], in_=V_d[s, 512 * q4:512 * q4 + 512, :].rearrange("(k p) c -> p k c", p=128)),
                          r=["V_d%d" % s], w=["V_all%d" % q4], chan="ldv%d" % q4)
                    P.add("sp", lambda e: e.dma_start(out=cV_all[:, 4 * q4:4 * q4 + 4, :], in_=cV_d[s, 512 * q4:512 * q4 + 512, :].rearrange("(k p) c -> p k c", p=128)),
                          r=["cV_d%d" % s], w=["cV_all%d" % q4], chan="ldcv%d" % q4)

            def load_seq_part(fn, *a):
                fn(*a)

            def load_q(s, j, qs):
                t0 = 512 * j
                P.add("sp", lambda e: e.dma_start(out=qT_b[qs][:], in_=qT_d[s, :, :, t0:t0 + 512].rearrange("h d t -> d h t")),
                      r=["qT_d%d" % s], w=["qT_b%d" % qs], chan="ldq%d" % qs)
                P.add("sp", lambda e: e.dma_start(out=cqT_b[qs][:], in_=cqT_d[s, :, :, t0:t0 + 512].rearrange("h d t -> d h t")),
                      r=["cqT_d%d" % s], w=["cqT_b%d" % qs], chan="ldcq%d" % qs)

            def attn_block(s, j, qs):
                t0 = 512 * j
                nb_ = s * 4 + j + 1
                if nb_ < NSEQ * 4:
                    load_q(nb_ // 4, nb_ % 4, 1 - qs)
                gm = stream([mla_gen(s, j, h, qs) for h in range(8)], pend_m, 3)
                gc = stream([ca_gen(s, j, h, qs) for h in range(8)], pend_c, 4)
                ra, rb = {0: (1, 2), 1: (3, 4), 2: (1, 1), 3: (3, 2)}[j]
                interleave(gm, gc, ra, rb)
                P.add("sp", lambda e: e.dma_start(out=otn_d[s, :, t0:t0 + 512].rearrange("(h d) t -> d h t", d=64), in_=otn[0][:]),
                      r=["otn0h%d" % hh for hh in range(16)], w=["otn_d%d_%d" % (s, j)], chan="stotn")

            def load_seq_safe(s):
                def ldk(hh):
                    P.add("sp", lambda e: e.dma_start(out=kT_all[:, 4 * hh:4 * hh + 4, :], in_=kT_d[s, 4 * hh:4 * hh + 4].rearrange("h d t -> d h t")),
                          r=["kT_d%d" % s], w=["kT_all%d" % hh], chan="ldk%d" % hh)
                    P.add("sp", lambda e: e.dma_start(out=ckT_all[:, 4 * hh:4 * hh + 4, :], in_=ckT_d[s, 4 * hh:4 * hh + 4].rearrange("h d t -> d h t")),
                          r=["ckT_d%d" % s], w=["ckT_all%d" % hh], chan="ldck%d" % hh)

                def ldv(q4):
                    P.add("sp", lambda e: e.dma_start(out=V_all[:, 4 * q4:4 * q4 + 4, :], in_=V_d[s, 512 * q4:512 * q4 + 512, :].rearrange("(k p) c -> p k c", p=128)),
                          r=["V_d%d" % s], w=["V_all%d" % q4], chan="ldv%d" % q4)
                    P.add("sp", lambda e: e.dma_start(out=cV_all[:, 4 * q4:4 * q4 + 4, :], in_=cV_d[s, 512 * q4:512 * q4 + 512, :].rearrange("(k p) c -> p k c", p=128)),
                          r=["cV_d%d" % s], w=["cV_all%d" % q4], chan="ldcv%d" % q4)
                ldk(0)
                ldv(0)
                ldk(1)
                ldv(1)
                ldv(2)
                ldv(3)

            blk = 0
            if stop not in ("setup", "A"):
                load_q(0, 0, 0)
            for s in range(NSEQ if stop not in ("setup", "A") else 0):
                load_seq_safe(s)
                for j in range(4):
                    attn_block(s, j, blk % 2)
                    blk += 1
            P.barrier()

        with ExitStack() as sc:
            TB = 256
            NTB = TB // 128
            wdn = sb(sc, "wdn", [128, NFF, D], BF16)
            wout = sb(sc, "wout", [128, 8, D], BF16)
            wup = sb(sc, "wup", [128, 8, 2 * D_FF], BF16)
            convp = sb(sc, "convp", [128, 4, 2 * NFF], F32)
            gab = sb(sc, "gab", [128, D], F32)
            gmb = sb(sc, "gmb", [128, D], F32)
            otb = sb(sc, "otb", [128, 8, TB], BF16)
            x1 = [sb(sc, "x1_%d" % i, [128, D], F32) for i in range(NTB)]
            xn2s = [sb(sc, "xn2_%d" % i, [128, D], BF16) for i in range(NTB)]
            tmph = [sb(sc, "tmph%d" % i, [128, 512], F32) for i in range(NTB)]
            hT2 = sb(sc, "hT2", [128, 8, TB + 2], BF16)
            gT = sb(sc, "gT", [128, NFF, TB], BF16)
            NUB = 3
            cg = [sb(sc, "cg%d" % i, [128, TB], F32) for i in range(NUB)]
            cv = [sb(sc, "cv%d" % i, [128, TB], F32) for i in range(NUB)]
            stC = sb(sc, "stC", [128, 4], F32)
            otile = sb(sc, "otile", [128, D], F32)
            P.tag = "wlC"
            P.add("sp", lambda e: e.dma_start(out=convp[:], in_=convp_d), w=["convp"], chan="small5", waitall=True)

            wl_cnt = {"n": 0}
            WUP_KEYS, WOUT_KEYS, WDN_KEYS = [], [], []

            def wl_piece(fn, keylist, name):
                n = wl_cnt["n"]
                wl_cnt["n"] += 1
                key = "%s_p%d" % (name, len(keylist))
                P.add("pool", fn, r=([wl_cnt["prev2"]] if n >= 2 else []), w=[key], chan="wlc%d" % (n % 2))
                wl_cnt["prev2"] = wl_cnt.get("prev1")
                wl_cnt["prev1"] = key
                keylist.append(key)

            def ldw(k):
                wl_piece(lambda e: e.dma_start(out=wup[:, :, k * 512:(k + 1) * 512],
                                               in_=wup_d[:, k * 512:(k + 1) * 512].rearrange("(j p) n -> p j n", p=128)), WUP_KEYS, "wup")

            def ldwo(k):
                wl_piece(lambda e: e.dma_start(out=wout[:, :, k * 512:(k + 1) * 512],
                                               in_=wout_d[:, k * 512:(k + 1) * 512].rearrange("(j p) n -> p j n", p=128)), WOUT_KEYS, "wout")

            def ldwd(k, jg):
                wl_piece(lambda e: e.dma_start(out=wdn[:, 11 * jg:11 * jg + 11, k * 512:(k + 1) * 512],
                                               in_=wdn_d[1408 * jg:1408 * jg + 1408, k * 512:(k + 1) * 512].rearrange("(j p) n -> p j n", p=128)), WDN_KEYS, "wdn")
            for k in range(2):
                ldwo(k)
            for k in range(11):
                ldw(k)
            for k in range(2):
                for jg in range(2):
                    ldwd(k, jg)
            P.tag = None
            ucnt = {"u": 0}
            HALO_KEYS = ["halo%d" % ch for ch in range(2 * NFF)]

            def seq_start(s):
                P.add("sp", lambda e: e.dma_start(out=gab[:], in_=mod_d[s, 2 * D:3 * D].partition_broadcast(128)), r=["mod_d"], w=["gab"], chan="ldga")
                P.add("sp", lambda e: e.dma_start(out=gmb[:], in_=mod_d[s, 5 * D:6 * D].partition_broadcast(128)), r=["mod_d"], w=["gmb"], chan="ldgm")

            def outproj_tile(s, t0, it):
                tk = t0 + it * 128
                kx1 = "x1_%d" % it
                kxs = [kx1 + "h0", kx1 + "h512"]
                ktm, kxn2, kst = "tmph%d" % it, "xn2_%d" % it, "stC%d" % it
                P.add("sp", lambda e: e.dma_start(out=x1[it][:], in_=x_d[s, tk:tk + 128, :]), w=kxs, chan="ldx%d" % it)
                bo0, bo1 = BK.get(hold=True), BK.get(hold=True)

                def mm(bb, n0):
                    for c in range(8):
                        P.add("pe", (lambda c: lambda e: e.matmul(banks_f[bb][:, 0:512], lhsT=otb[:, c, it * 128:(it + 1) * 128],
                                                                  rhs=wout[:, c, n0:n0 + 512], start=(c == 0), stop=(c == 7)))(c),
                              r=["otb"] + WOUT_KEYS, w=[bk(bb)])

                def epi(bb, n0):
                    P.add("dve", lambda e: e.tensor_tensor(out=tmph[it][:], in0=banks_f[bb][:, 0:512],
                                                           in1=gab[:, n0:n0 + 512], op=ALU.mult),
                          r=[bk(bb), "gab"], w=[ktm])
                    BK.release(bb)
                    P.add("pool", lambda e: e.tensor_tensor(out=x1[it][:, n0:n0 + 512], in0=x1[it][:, n0:n0 + 512],
                                                            in1=tmph[it][:], op=ALU.add),
                          r=[kx1 + "h%d" % n0, ktm], w=[kx1 + "h%d" % n0])
                mm(bo0, 0)
                mm(bo1, 512)
                yield
                epi(bo0, 0)
                yield
                epi(bo1, 512)
                yield
                P.add("act", lambda e: e.activation(out=xn2s[it][:], in_=x1[it][:], func=AF.Square, accum_out=stC[:, it:it + 1]),
                      r=kxs, w=[kst, kxn2])
                P.add("act", lambda e: e.activation(out=stC[:, it:it + 1], in_=stC[:, it:it + 1], func=AF.Sqrt, bias=float(D * EPS), scale=1.0),
                      r=[kst], w=[kst])
                yield
                P.add("dve", lambda e: e.reciprocal(out=stC[:, it:it + 1], in_=stC[:, it:it + 1]), r=[kst], w=[kst])
                P.add("act", lambda e: e.activation(out=xn2s[it][:], in_=x1[it][:], func=AF.Copy, scale=stC[:, it:it + 1]),
                      r=kxs + [kst], w=[kxn2])
                yield
                bt = BK.get(hold=True)
                for c in range(8):
                    P.add("pe", (lambda c: lambda e: e.transpose(out=banks_b[bt][:, c * 128:(c + 1) * 128],
                                                                 in_=xn2s[it][:, c * 128:(c + 1) * 128], identity=ident[:]))(c),
                          r=[kxn2, "ident"], w=[bk(bt)])
                yield
                tp3 = banks_b[bt][:, 0:1024].rearrange("p (c t) -> p c t", c=8)
                kh = "hT2_%d" % it
                P.add("dve", lambda e: e.tensor_tensor(out=hT2[:, :, 2 + it * 128:2 + (it + 1) * 128], in0=tp3,
                                                       in1=AB[:, s, 2, :].unsqueeze(2).to_broadcast([128, 8, 128]), op=ALU.mult),
                      r=[bk(bt), "AB"], w=[kh])
                BK.release(bt)
                P.add("pool", lambda e: e.tensor_tensor(out=hT2[:, :, 2 + it * 128:2 + (it + 1) * 128], in0=hT2[:, :, 2 + it * 128:2 + (it + 1) * 128],
                                                        in1=AB[:, s, 3, :].unsqueeze(2).to_broadcast([128, 8, 128]), op=ALU.add),
                      r=[kh, "AB"], w=[kh])
                yield

            def ffn_up(f, khs):
                ui = ucnt["u"] % NUB
                ucnt["u"] += 1
                bg, bv = BK.get(hold=True), BK.get(hold=True)

                def up_mm(bb, ch):
                    for k in range(8):
                        P.add("pe", (lambda k: lambda e: e.matmul(banks_f[bb][:, 0:TB + 2],
                                                                  lhsT=wup[:, k, ch * 128:(ch + 1) * 128], rhs=hT2[:, k, :],
                                                                  start=(k == 0), stop=(k == 7)))(k),
                              r=khs + ["hT2_halo"] + WUP_KEYS, w=[bk(bb)])
                up_mm(bg, f)
                up_mm(bv, NFF + f)
                kcg, kcv = "cg%d" % ui, "cv%d" % ui

                def tap2(bb, ch, cb, kcb):
                    P.add("act", lambda e: e.activation(out=cb[ui][:], in_=banks_f[bb][:, 2:TB + 2], func=AF.Identity,
                                                        scale=convp[:, 2, ch:ch + 1], bias=convp[:, 3, ch:ch + 1]),
                          r=[bk(bb), "convp"], w=[kcb])

                def tap(bb, ch, cb, kcb, j):
                    P.add("dve", lambda e: e.scalar_tensor_tensor(out=cb[ui][:], in0=banks_f[bb][:, j:TB + j], scalar=convp[:, j, ch:ch + 1],
                                                                  in1=cb[ui][:], op0=ALU.mult, op1=ALU.add),
                          r=[bk(bb), kcb, "convp"], w=[kcb])
                tap2(bg, f, cg, kcg)
                tap2(bv, NFF + f, cv, kcv)
                tap(bg, f, cg, kcg, 1)
                tap(bv, NFF + f, cv, kcv, 1)
                tap(bg, f, cg, kcg, 0)
                tap(bv, NFF + f, cv, kcv, 0)
                BK.release(bg)
                BK.release(bv)
                return (f, ui)

            def ffn_gate(f, ui):
                kcg, kcv = "cg%d" % ui, "cv%d" % ui
                P.add("act", lambda e: e.activation(out=cg[ui][:], in_=cg[ui][:], func=AF.Silu), r=[kcg], w=[kcg])
                P.add("pool", lambda e: e.tensor_tensor(out=gT[:, f, :], in0=cg[ui][:], in1=cv[ui][:], op=ALU.mult),
                      r=[kcg, kcv], w=["gT%d" % f])

            def down_tile(s, t0, it, kgs):
                tk = t0 + it * 128
                bd0, bd1 = BK.get(hold=True), BK.get(hold=True)

                def half(bb, n0):
                    for f in range(NFF):
                        P.add("pe", (lambda f: lambda e: e.matmul(banks_f[bb][:, 0:512], lhsT=gT[:, f, it * 128:(it + 1) * 128],
                                                                  rhs=wdn[:, f, n0:n0 + 512], start=(f == 0), stop=(f == NFF - 1)))(f),
                              r=kgs + WDN_KEYS, w=[bk(bb)])
                    P.add("dve", lambda e: e.tensor_tensor(out=otile[:, n0:n0 + 512], in0=banks_f[bb][:, 0:512],
                                                           in1=gmb[:, n0:n0 + 512], op=ALU.mult),
                          r=[bk(bb), "gmb"], w=["otileh%d" % n0])
                    P.add("pool", lambda e: e.tensor_tensor(out=otile[:, n0:n0 + 512], in0=otile[:, n0:n0 + 512],
                                                            in1=x1[it][:, n0:n0 + 512], op=ALU.add),
                          r=["otileh%d" % n0, "x1_%dh%d" % (it, n0)], w=["otileh%d" % n0])
                    BK.release(bb)
                half(bd0, 0)
                half(bd1, 512)
                P.add("sp", lambda e: e.dma_start(out=out_d[s, tk:tk + 128, :], in_=otile[:]),
                      r=["otileh0", "otileh512"], w=["out_d"], chan="stout")

            def ffn_block(s, tb):
                t0 = tb * TB
                jblk = t0 // 512
                P.add("sp", lambda e: e.dma_start(out=otb[:], in_=otn_d[s, :, t0:t0 + TB].rearrange("(c p) t -> p c t", p=128)),
                      r=["otn_d%d_%d" % (s, jblk)], w=["otb"], chan="ldot")
                if tb == 0:
                    P.add("pool", lambda e: e.memset(hT2[:, :, 0:2], 0.0), r=["hT2_%d" % (NTB - 1)], w=["hT2_halo"])
                else:
                    P.add("pool", lambda e: e.tensor_copy(out=hT2[:, :, 0:2], in_=hT2[:, :, TB:TB + 2]), r=["hT2_%d" % (NTB - 1)], w=["hT2_halo"])
                gens_ = [outproj_tile(s, t0, it) for it in range(NTB)]
                live = []
                pending_ = list(gens_)
                while pending_ or live:
                    if pending_:
                        live.append(pending_.pop(0))
                    for g_ in list(live):
                        try:
                            next(g_)
                        except StopIteration:
                            live.remove(g_)
                khs = ["hT2_%d" % it for it in range(NTB)]
                prev = None
                for f in range(NFF):
                    cur = ffn_up(f, khs)
                    if prev is not None:
                        ffn_gate(*prev)
                    prev = cur
                ffn_gate(*prev)
                kgs = ["gT%d" % f for f in range(NFF)]
                for it in range(NTB):
                    down_tile(s, t0, it, kgs)

            for s in range(NSEQ if stop not in ("setup", "A", "B") else 0):
                seq_start(s)
                for tb in range(S // TB):
                    ffn_block(s, tb)
            P.barrier()
            P.emit()
    return nc, P.stats


_CACHE = {}


def _feat_major(v):
    v = np.asarray(v, np.float32)
    return np.ascontiguousarray(v.reshape(-1, 128).T)


def _prepare(x, c, positions, w_ada, b_ada, g_attn_norm, w_in, g_q_latent, g_kv_latent, w_q_up, w_kv_up,
             g_mla_q, g_mla_k, g_ca_q, g_ca_k, rel_bias, w_out, g_mlp_norm, w_up, conv_w, conv_b, w_down):
    f = lambda a: np.ascontiguousarray(np.asarray(a))
    x = f(x); c = f(c); positions = f(positions)
    if "nc" not in _CACHE:
        _CACHE["nc"], _CACHE["stats"] = build_program(stop=_CACHE.get("stop"))
    nc = _CACHE["nc"]
    gfeat = np.concatenate([_feat_major(g_attn_norm[0]), _feat_major(g_mlp_norm[0]), _feat_major(g_q_latent[0]),
                            _feat_major(g_kv_latent[0])], axis=1).astype(np.float32)
    convp = np.zeros((128, 4, 2 * NFF), np.float32)
    for t in range(3):
        convp[:, t, :] = _feat_major(conv_w[0, t])
    convp[:, 3, :] = _feat_major(conv_b[0])
    grow = np.concatenate([np.asarray(g_mla_q[0]), np.asarray(g_mla_k[0]), np.asarray(g_ca_q[0]), np.asarray(g_ca_k[0])]).astype(np.float32)
    grow = np.ascontiguousarray(np.broadcast_to(grow[None, :], (128, 320)))
    rb = np.asarray(rel_bias[0], np.float32)
    kj = np.arange(128)[:, None]
    qi = np.arange(128)[None, :]
    bias34 = np.zeros((128, 8, 2, 128), np.float32)
    for ti, t in enumerate((3, 4)):
        idx = np.clip(128 * (4 - t) + qi - kj, -128, 128) + 128
        bias34[:, :, ti, :] = np.transpose(rb[:, idx], (1, 0, 2))
    bfar = np.ascontiguousarray(np.broadcast_to(rb[:, 256][None, :], (128, 8))).astype(np.float32)
    half = 16
    invf = np.power(np.float32(10000.0), -np.arange(half, dtype=np.float32) / np.float32(half)).astype(np.float32)
    invf = np.ascontiguousarray(np.broadcast_to(invf[None, :], (128, 16)))
    shared = {
        "w_ada": f(w_ada[0]), "w_in": f(w_in[0]), "w_q_up": f(w_q_up[0]), "w_kv_up": f(w_kv_up[0]), "w_out": f(w_out[0]),
        "w_up": f(w_up[0]), "w_down": f(w_down[0]), "gfeat": gfeat, "convp": convp, "grow": grow, "bias34": bias34,
        "bfar": bfar, "invf": invf,
    }
    in_maps = []
    for i in range(NCORES):
        b0 = NSEQ * i
        m = dict(shared)
        m["x"] = f(x[b0:b0 + NSEQ])
        m["cT"] = np.ascontiguousarray(c[b0:b0 + NSEQ].reshape(NSEQ, 8, 128).transpose(2, 1, 0)).astype(np.float32)
        pl = positions[b0:b0 + NSEQ].reshape(NSEQ, NT, 128).transpose(2, 0, 1).reshape(128, NSEQ * NT)
        m["posl"] = np.ascontiguousarray(pl).astype(np.int32)
        m["b_ada2"] = np.ascontiguousarray(np.broadcast_to(np.asarray(b_ada[0], np.float32)[None, :], (NSEQ, 6 * D)))
        in_maps.append(m)
    return nc, in_maps


def kernel(**inputs):
    nc, in_maps = _prepare(**inputs)
    res = run_bass_kernel_spmd(nc, in_maps, core_ids=list(range(NCORES)))
    out = np.concatenate([np.asarray(r["out"]) for r in res.results], axis=0)
    return out.astype(np.float32)
```

```python
import math
from contextlib import ExitStack

import numpy as np
import concourse.bass as bass
import concourse.mybir as mybir
from concourse.bass_utils import run_bass_kernel_spmd

F32 = mybir.dt.float32
BF16 = mybir.dt.bfloat16
I32 = mybir.dt.int32
AF = mybir.ActivationFunctionType
ALU = mybir.AluOpType
AX = mybir.AxisListType

NCORES = 8
NSEQ = 2
S = 2048
D = 1024
NT = S // 128
D_IN = 1952
D_FF = 2816
NFF = D_FF // 128
EPS = 1e-6
TWO_PI = 2.0 * math.pi


class Op:
    __slots__ = ("eng", "fn", "r", "w", "chan", "deps", "signal", "sigval", "waitall")

    def __init__(self, eng, fn, r, w, chan, waitall):
        self.eng, self.fn, self.r, self.w, self.chan = eng, fn, tuple(r), tuple(w), chan
        self.deps = set()
        self.signal = False
        self.sigval = 0
        self.waitall = waitall


class Prog:
    def __init__(self, nc, es):
        self.nc = nc
        self.es = es
        self.ops = []
        self.last_w = {}
        self.readers = {}
        self.waitall_chans = set()

    tag = None

    def add(self, eng, fn, r=(), w=(), chan=None, waitall=False):
        if self.tag is not None and self.tag in SKIP:
            return None
        op = Op(eng, fn, r, w, chan, waitall)
        idx = len(self.ops)
        deps = set()
        for k in op.r:
            lw = self.last_w.get(k)
            if lw is not None:
                deps.add(lw)
            if isinstance(k, str) and k.startswith("bank"):
                for rd in self.readers.get(k, ()):
                    if self.ops[rd].eng != eng:
                        deps.add(rd)
        for k in op.w:
            lw = self.last_w.get(k)
            if lw is not None:
                deps.add(lw)
            for rd in self.readers.get(k, ()):
                deps.add(rd)
        deps.discard(idx)
        if chan is not None and waitall:
            deps = {d for d in deps if self.ops[d].chan != chan}
        op.deps = deps
        for k in op.w:
            self.last_w[k] = idx
            self.readers[k] = []
        for k in op.r:
            if k in op.w:
                continue
            self.readers.setdefault(k, []).append(idx)
        if chan is not None and waitall:
            self.waitall_chans.add(chan)
        self.ops.append(op)
        return idx

    def barrier(self):
        n = len(self.ops)
        last = {}
        for i, op in enumerate(self.ops):
            key = op.chan if op.chan is not None else ("E", op.eng)
            last[key] = i
        alld = set(last.values())
        for eng in ("pe", "act", "dve", "pool", "sp"):
            op = Op(eng, None, (), (), None, False)
            op.deps = set(alld)
            self.ops.append(op)

    def emit(self):
        nc = self.nc
        ops = self.ops
        engobj = {"pe": nc.tensor, "act": nc.scalar, "dve": nc.vector, "pool": nc.gpsimd, "sp": nc.sync}
        for op in ops:
            for d in op.deps:
                x = ops[d]
                if x.chan is None and x.eng == "pe" and op.eng == "pe" and op.chan is None:
                    continue
                x.signal = True
        sems = {}

        def sem(name):
            if name not in sems:
                sems[name] = self.es.enter_context(nc.semaphore("s_" + str(name)))
            return sems[name]

        cnt = {}
        chan_total = {}
        for op in ops:
            if op.fn is None:
                continue
            if op.chan is not None:
                c = ("C", op.chan)
                cnt[c] = cnt.get(c, 0) + 16
                op.sigval = cnt[c]
                op.signal = True
                chan_total[op.chan] = cnt[c]
            elif op.signal:
                c = ("E", op.eng)
                cnt[c] = cnt.get(c, 0) + 1
                op.sigval = cnt[c]
        known = {e: {} for e in engobj}
        vcs = [None] * len(ops)
        ecount = {}
        nwaits = 0

        def merge(dst, src):
            for k_, v_ in src.items():
                if dst.get(k_, 0) < v_:
                    dst[k_] = v_

        for i, op in enumerate(ops):
            need = []
            for d in op.deps:
                x = ops[d]
                if x.fn is None:
                    if vcs[d] is not None:
                        need.append((None, 0, d))
                    continue
                if x.chan is not None:
                    key = ("C", x.chan)
                    val = chan_total[x.chan] if x.chan in self.waitall_chans else x.sigval
                else:
                    if x.eng == "pe" and op.eng == "pe" and op.chan is None:
                        continue
                    key = ("E", x.eng)
                    val = x.sigval
                need.append((key, val, d))
            need.sort(key=lambda t: -t[2])
            e = engobj[op.eng]
            kn = known[op.eng]
            for key, val, d in need:
                if key is None:
                    continue
                if kn.get(key, 0) >= val:
                    continue
                e.wait_ge(sem(key), val)
                kn[key] = val
                nwaits += 1
                if vcs[d] is not None and not (ops[d].chan in self.waitall_chans):
                    merge(kn, vcs[d])
            vc = dict(kn)
            if op.fn is not None:
                inst = op.fn(e)
                if op.chan is not None:
                    inst.then_inc(sem(("C", op.chan)), 16)
                    if op.chan not in self.waitall_chans:
                        vc[("C", op.chan)] = max(vc.get(("C", op.chan), 0), op.sigval)
                else:
                    if op.signal:
                        inst.then_inc(sem(("E", op.eng)), 1)
                        ecount[op.eng] = op.sigval
                    if op.eng != "pe" or True:
                        vc[("E", op.eng)] = max(vc.get(("E", op.eng), 0), ecount.get(op.eng, 0))
            vcs[i] = vc
        for chan, tot in chan_total.items():
            if known["sp"].get(("C", chan), 0) < tot:
                nc.sync.wait_ge(sem(("C", chan)), tot)
        self.stats = dict(n_ops=len(ops), n_waits=nwaits, n_sems=len(sems))


class Banks:
    def __init__(self, banks):
        self.banks = banks
        self.ptr = 0
        self.held = set()

    def get(self, hold=False):
        for _ in range(16):
            b = self.ptr
            self.ptr = (self.ptr + 1) % len(self.banks)
            if b not in self.held:
                if hold:
                    self.held.add(b)
                return b
        raise RuntimeError("no free PSUM bank")

    def release(self, b):
        self.held.discard(b)


import os
SKIP = set(os.environ.get('KSKIP', '').split(','))


def build_program(debug=None, stop=None):
    nc = bass.Bass("TRN2", target_bir_lowering=False)

    def din(name, shape, dt=F32):
        return nc.dram_tensor(name, list(shape), dt, kind="ExternalInput").ap()

    def dscr(name, shape, dt):
        return nc.dram_tensor(name, list(shape), dt, kind="Internal").ap()

    x_d = din("x", [NSEQ, S, D])
    cT_d = din("cT", [128, 8, NSEQ])
    pos_d = din("posl", [128, NSEQ * NT], I32)
    wada_d = din("w_ada", [D, 6 * D])
    bada_d = din("b_ada2", [NSEQ, 6 * D])
    win_d = din("w_in", [D, D_IN])
    wq_d = din("w_q_up", [256, 768])
    wkv_d = din("w_kv_up", [128, 1024])
    wout_d = din("w_out", [D, D])
    wup_d = din("w_up", [D, 2 * D_FF])
    wdn_d = din("w_down", [D_FF, D])
    gfeat_d = din("gfeat", [128, 19])
    convp_d = din("convp", [128, 4, 2 * NFF])
    grow_d = din("grow", [128, 320])
    bias34_d = din("bias34", [128, 8, 2, 128])
    bfar_d = din("bfar", [128, 8])
    invf_d = din("invf", [128, 16])
    out_d = nc.dram_tensor("out", [NSEQ, S, D], F32, kind="ExternalOutput").ap()

    mod_d = dscr("mod_scr", [NSEQ, 6 * D], F32)
    qT_d = dscr("qT_scr", [NSEQ, 8, 96, S], BF16)
    kT_d = dscr("kT_scr", [NSEQ, 8, 96, S], BF16)
    cqT_d = dscr("cqT_scr", [NSEQ, 8, 64, S], BF16)
    ckT_d = dscr("ckT_scr", [NSEQ, 8, 64, S], BF16)
    V_d = dscr("V_scr", [NSEQ, S, 8 * 65], BF16)
    cV_d = dscr("cV_scr", [NSEQ, S, 8 * 65], BF16)
    otn_d = dscr("otn_scr", [NSEQ, D, S], BF16)

    with ExitStack() as es:
        P = Prog(nc, es)

        def sb(stack, name, shape, dt):
            return stack.enter_context(nc.sbuf_tensor("sb_" + name, list(shape), dt))

        banks_f = [es.enter_context(nc.psum_tensor("bank%d" % i, [128, 512], F32)) for i in range(8)]
        banks_b = [b[:].bitcast(BF16) for b in banks_f]
        BK = Banks(banks_f)

        def bk(b):
            return "bank%d" % b

        ident = sb(es, "ident", [128, 128], BF16)
        identf = sb(es, "identf", [128, 128], F32)
        sel64 = sb(es, "sel64", [128, 64], F32)
        gfeat = sb(es, "gfeat", [128, 19], F32)
        grow = sb(es, "grow", [128, 320], F32)
        AB = sb(es, "AB", [128, NSEQ, 4, 8], F32)
        sa = es.enter_context(ExitStack())
        cs_all = sb(sa, "cs_all", [128, NSEQ * NT, 32], F32)
        junk = sb(sa, "junk", [128, 1024], BF16)

        P.add("pool", lambda e: e.memset(identf[:], 1.0), w=["identf"])
        P.add("pool", lambda e: e.affine_select(out=identf[:], in_=identf[:], pattern=[[-1, 128]],
                                                compare_op=ALU.is_equal, fill=0.0, base=0, channel_multiplier=1),
              r=["identf"], w=["identf"])
        P.add("pool", lambda e: e.tensor_copy(out=ident[:], in_=identf[:]), r=["identf"], w=["ident"])
        P.add("pool", lambda e: e.memset(sel64[:], 0.0), w=["sel64"])
        P.add("pool", lambda e: e.memset(sel64[64:65, :], 1.0), r=["sel64"], w=["sel64"])
        P.add("sp", lambda e: e.dma_start(out=gfeat[:], in_=gfeat_d), w=["gfeat"], chan="small", waitall=True)
        P.add("sp", lambda e: e.dma_start(out=grow[:], in_=grow_d), w=["grow"], chan="small", waitall=True)
        s0 = es.enter_context(ExitStack())
        cT = sb(s0, "cT", [128, 8, NSEQ], F32)
        bada = sb(s0, "bada", [NSEQ, 6 * D], F32)
        posi = sb(s0, "posi", [128, NSEQ * NT], I32)
        invf = sb(s0, "invf", [128, 16], F32)
        P.add("sp", lambda e: e.dma_start(out=cT[:], in_=cT_d), w=["cT"], chan="small", waitall=True)
        P.add("sp", lambda e: e.dma_start(out=bada[:], in_=bada_d), w=["bada"], chan="small", waitall=True)
        P.add("sp", lambda e: e.dma_start(out=posi[:], in_=pos_d), w=["posi"], chan="small", waitall=True)
        P.add("sp", lambda e: e.dma_start(out=invf[:], in_=invf_d), w=["invf"], chan="small", waitall=True)
        P.add("dve", lambda e: e.tensor_scalar(out=grow[:, 96:192], in0=grow[:, 96:192], scalar1=math.sqrt(96.0),
                                               scalar2=None, op0=ALU.mult), r=["grow"], w=["grow"])
        P.add("dve", lambda e: e.tensor_scalar(out=grow[:, 256:320], in0=grow[:, 256:320], scalar1=8.0,
                                               scalar2=None, op0=ALU.mult), r=["grow"], w=["grow"])

        if True:
            scb = sb(s0, "scb", [128, 8, NSEQ], BF16)
            wab = [sb(s0, "wab%d" % i, [128, 8, 512], BF16) for i in range(2)]
            modsb = sb(s0, "modsb", [NSEQ, 6 * D], F32)
            P.add("act", lambda e: e.activation(out=scb[:], in_=cT[:], func=AF.Silu), r=["cT"], w=["scb"])
            for nb in range(12):
                sl = nb % 2
                P.add("pool", (lambda nb, sl: lambda e: e.dma_start(
                    out=wab[sl][:], in_=wada_d[:, nb * 512:(nb + 1) * 512].rearrange("(j p) n -> p j n", p=128)))(nb, sl),
                    w=["wab%d" % sl], chan="wab%d" % sl)
                b = BK.get()
                for k in range(8):
                    P.add("pe", (lambda b, sl, k: lambda e: e.matmul(
                        banks_f[b][0:NSEQ, 0:512], lhsT=scb[:, k, :], rhs=wab[sl][:, k, :], start=(k == 0), stop=(k == 7)))(b, sl, k),
                        r=["scb", "wab%d" % sl], w=[bk(b)])
                P.add("dve", (lambda b, nb: lambda e: e.tensor_tensor(
                    out=modsb[:, nb * 512:(nb + 1) * 512], in0=banks_f[b][0:NSEQ, 0:512],
                    in1=bada[:, nb * 512:(nb + 1) * 512], op=ALU.add))(b, nb),
                    r=[bk(b), "bada"], w=["modsb"])
            P.add("sp", lambda e: e.dma_start(out=mod_d, in_=modsb[:]), r=["modsb"], w=["mod_d"], chan="modst")
            P.tag = "modT"
            modT = sb(s0, "modT", [128, NSEQ, 4, 8], F32)
            bT = BK.get()

            def modtr(qi, c, j):
                col = (qi * 8 + j) * NSEQ
                P.add("pe", lambda e: e.matmul(banks_f[bT][:, col:col + NSEQ], lhsT=modsb[0:NSEQ, c * D + j * 128:c * D + (j + 1) * 128],
                                               rhs=identf[0:NSEQ, 0:NSEQ], start=True, stop=True),
                      r=["modsb", "identf"], w=[bk(bT)])
            for qi, c in enumerate((0, 1, 3, 4)):
                for j in range(8):
                    modtr(qi, c, j)
            P.add("dve", lambda e: e.tensor_copy(out=modT[:].rearrange("p s k j -> p k j s"),
                                                 in_=banks_f[bT][:, 0:32 * NSEQ].rearrange("p (k j s) -> p k j s", k=4, j=8)),
                  r=[bk(bT)], w=["modT"])
            for s in range(NSEQ):
                for (dst, srcq, gcol) in ((0, 1, 0), (2, 3, 8)):
                    P.add("dve", (lambda s, dst, srcq, gcol: lambda e: e.scalar_tensor_tensor(
                        out=AB[:, s, dst, :], in0=modT[:, s, srcq, :], scalar=1.0, in1=gfeat[:, gcol:gcol + 8],
                        op0=ALU.add, op1=ALU.mult))(s, dst, srcq, gcol),
                        r=["modT", "gfeat"], w=["AB"])
                    P.add("dve", (lambda s, dst: lambda e: e.tensor_scalar(
                        out=AB[:, s, dst, :], in0=AB[:, s, dst, :], scalar1=32.0, scalar2=None, op0=ALU.mult))(s, dst),
                        r=["AB"], w=["AB"])
                    P.add("dve", (lambda s, dst, srcq: lambda e: e.tensor_copy(
                        out=AB[:, s, dst + 1, :], in_=modT[:, s, srcq - 1, :]))(s, dst, srcq),
                        r=["modT"], w=["AB"])
            P.tag = "rot"
            posf = sb(s0, "posf", [128, NSEQ * NT], F32)
            ang = sb(s0, "ang", [128, NSEQ * NT, 32], F32)
            kf = sb(s0, "kf", [128, NSEQ * NT, 32], F32)
            ki = sb(s0, "ki", [128, NSEQ * NT, 32], I32)
            mk = sb(s0, "mk", [128, NSEQ * NT, 32], F32)
            P.add("dve", lambda e: e.tensor_copy(out=posf[:], in_=posi[:]), r=["posi"], w=["posf"])
            NTT = NSEQ * NT
            P.add("dve", lambda e: e.tensor_tensor(out=ang[:, :, 16:32], in0=posf[:].unsqueeze(2).to_broadcast([128, NTT, 16]),
                                                   in1=invf[:].unsqueeze(1).to_broadcast([128, NTT, 16]), op=ALU.mult),
                  r=["posf", "invf"], w=["ang"])
            P.add("dve", lambda e: e.tensor_scalar(out=ang[:, :, 0:16], in0=ang[:, :, 16:32], scalar1=math.pi / 2.0,
                                                   scalar2=None, op0=ALU.add), r=["ang"], w=["ang"])
            P.add("dve", lambda e: e.tensor_scalar(out=kf[:], in0=ang[:], scalar1=1.0 / TWO_PI, scalar2=None, op0=ALU.mult),
                  r=["ang"], w=["kf"])
            P.add("dve", lambda e: e.tensor_copy(out=ki[:], in_=kf[:]), r=["kf"], w=["ki"])
            P.add("dve", lambda e: e.tensor_copy(out=kf[:], in_=ki[:]), r=["ki"], w=["kf"])
            P.add("dve", lambda e: e.scalar_tensor_tensor(out=ang[:], in0=kf[:], scalar=-TWO_PI, in1=ang[:],
                                                          op0=ALU.mult, op1=ALU.add), r=["kf", "ang"], w=["ang"])
            P.add("dve", lambda e: e.tensor_scalar(out=mk[:], in0=ang[:], scalar1=math.pi, scalar2=-TWO_PI,
                                                   op0=ALU.is_gt, op1=ALU.mult), r=["ang"], w=["mk"])
            P.add("dve", lambda e: e.tensor_tensor(out=ang[:], in0=ang[:], in1=mk[:], op=ALU.add), r=["ang", "mk"], w=["ang"])
            P.add("dve", lambda e: e.tensor_scalar(out=mk[:], in0=ang[:], scalar1=-math.pi, scalar2=TWO_PI,
                                                   op0=ALU.is_lt, op1=ALU.mult), r=["ang"], w=["mk"])
            P.add("dve", lambda e: e.tensor_tensor(out=ang[:], in0=ang[:], in1=mk[:], op=ALU.add), r=["ang", "mk"], w=["ang"])
            P.add("dve", lambda e: e.tensor_scalar(out=ang[:], in0=ang[:], scalar1=math.pi, scalar2=-math.pi,
                                                   op0=ALU.min, op1=ALU.max), r=["ang"], w=["ang"])
            P.add("act", lambda e: e.activation(out=cs_all[:], in_=ang[:], func=AF.Sin), r=["ang"], w=["cs_all"])
            P.tag = None
            P.barrier()
            s0.close()

        if True:
            P.tag = "wlA"
            win = sb(sa, "win", [128, 8, D_IN], BF16)
            wq = sb(sa, "wq", [128, 2, 768], BF16)
            wkv = sb(sa, "wkv", [128, 1024], BF16)
            wstage = sb(sa, "wstage", [128, 2, 768], F32)
            wstage2 = sb(sa, "wstage2", [128, 1024], F32)
            WIN_KEYS = []
            for pi_, (c0, c1) in enumerate(((0, 512), (512, 1024), (1024, 1536), (1536, 1952))):
                P.add("pool", (lambda c0, c1: lambda e: e.dma_start(out=win[:, :, c0:c1], in_=win_d[:, c0:c1].rearrange("(j p) n -> p j n", p=128)))(c0, c1),
                      r=(["win_p%d" % (pi_ - 2)] if pi_ >= 2 else []), w=["win_p%d" % pi_], chan="win_c%d" % (pi_ % 2))
                WIN_KEYS.append("win_p%d" % pi_)
            P.tag = "wlA2"
            P.add("sp", lambda e: e.dma_start(out=wstage[:], in_=wq_d.rearrange("(j p) n -> p j n", p=128)),
                  w=["wstage"], chan="small3", waitall=True)
            P.add("sp", lambda e: e.dma_start(out=wstage2[:], in_=wkv_d), w=["wstage2"], chan="small3", waitall=True)
            for j in range(2):
                P.add("dve", (lambda j: lambda e: e.tensor_scalar(
                    out=wq[:, j, :], in0=wstage[:, j, :], scalar1=gfeat[:, 16 + j:17 + j], scalar2=16.0,
                    op0=ALU.mult, op1=ALU.mult))(j), r=["wstage", "gfeat"], w=["wq"])
            P.add("dve", lambda e: e.tensor_scalar(out=wkv[:], in0=wstage2[:], scalar1=gfeat[:, 18:19],
                                                   scalar2=math.sqrt(128.0), op0=ALU.mult, op1=ALU.mult),
                  r=["wstage2", "gfeat"], w=["wkv"])

            P.tag = "wlA3"
            xt = [sb(sa, "xt%d" % i, [128, D], F32) for i in range(2)]
            xn = [sb(sa, "xn%d" % i, [128, D], BF16) for i in range(2)]
            hT = [sb(sa, "hT%d" % i, [128, 8, 128], BF16) for i in range(2)]
            st = sb(sa, "stA", [128, 2, 40], F32)
            lat = [sb(sa, "lat%d" % i, [128, 384], BF16) for i in range(2)]
            latT = [sb(sa, "latT%d" % i, [128, 3, 128], BF16) for i in range(2)]
            kr = [sb(sa, "kr%d" % i, [128, 4, 32], F32) for i in range(2)]
            qn = [sb(sa, "qn%d" % i, [128, 8, 96], F32) for i in range(2)]
            qsq = [sb(sa, "qsq%d" % i, [128, 8, 96], F32) for i in range(2)]
            qb = [sb(sa, "qb%d" % i, [128, 8, 96], BF16) for i in range(2)]
            kn = [sb(sa, "kn%d" % i, [128, 8, 64], F32) for i in range(2)]
            ksq = [sb(sa, "ksq%d" % i, [128, 8, 64], F32) for i in range(2)]
            kb = [sb(sa, "kb%d" % i, [128, 8, 96], BF16) for i in range(2)]
            rt = [sb(sa, "rt%d" % i, [128, 8, 32], F32) for i in range(2)]
            cqn = [sb(sa, "cqn%d" % i, [128, 8, 64], F32) for i in range(2)]
            cqs = [sb(sa, "cqs%d" % i, [128, 8, 64], F32) for i in range(2)]
            cqb = [sb(sa, "cqb%d" % i, [128, 8, 64], BF16) for i in range(2)]
            ckn = [sb(sa, "ckn%d" % i, [128, 8, 64], F32) for i in range(2)]
            cks = [sb(sa, "cks%d" % i, [128, 8, 64], F32) for i in range(2)]
            ckb = [sb(sa, "ckb%d" % i, [128, 8, 64], BF16) for i in range(2)]
            qT_st = [sb(sa, "qTst%d" % i, [96, 8, 512], BF16) for i in range(2)]
            kT_st = [sb(sa, "kTst%d" % i, [96, 8, 512], BF16) for i in range(2)]
            cqT_st = [sb(sa, "cqTst%d" % i, [64, 8, 512], BF16) for i in range(2)]
            ckT_st = [sb(sa, "ckTst%d" % i, [64, 8, 512], BF16) for i in range(2)]
            V_st = [sb(sa, "Vst%d" % i, [128, 4, 8, 65], BF16) for i in range(2)]
            cV_st = [sb(sa, "cVst%d" % i, [128, 4, 8, 65], BF16) for i in range(2)]
            for i in range(2):
                P.add("pool", (lambda i: lambda e: e.memset(V_st[i][:, :, :, 64:65], 1.0))(i), w=["Vst%d" % i])
                P.add("pool", (lambda i: lambda e: e.memset(cV_st[i][:, :, :, 64:65], 1.0))(i), w=["cVst%d" % i])

            P.tag = None


            def rstd_from_ssq(sl, col, n, tag):
                kst = "st%d_%s" % (sl, tag)
                P.add("act", lambda e: e.activation(out=st[:, sl, col:col + 1], in_=st[:, sl, col:col + 1], func=AF.Sqrt,
                                                    bias=float(n * EPS), scale=1.0), r=[kst], w=[kst])
                P.add("dve", lambda e: e.reciprocal(out=st[:, sl, col:col + 1], in_=st[:, sl, col:col + 1]), r=[kst], w=[kst])
                return kst

            def rstd_vec(sl, c0, nh, n, tag):
                kst = "st%d_%s" % (sl, tag)
                P.add("act", lambda e: e.activation(out=st[:, sl, c0:c0 + nh], in_=st[:, sl, c0:c0 + nh], func=AF.Sqrt,
                                                    bias=float(n * EPS), scale=1.0), r=[kst], w=[kst])
                P.add("dve", lambda e: e.reciprocal(out=st[:, sl, c0:c0 + nh], in_=st[:, sl, c0:c0 + nh]), r=[kst], w=[kst])
                return kst

            def rope(eng, src3, dst3, cs, keys_r, keys_w, tmp, nh, ktmp):
                cosb = cs[:, 0:16].unsqueeze(1).to_broadcast([128, nh, 16])
                sinb = cs[:, 16:32].unsqueeze(1).to_broadcast([128, nh, 16])
                P.add(eng, lambda e: e.tensor_tensor(out=tmp[:, :, 0:16], in0=src3[:, :, 16:32], in1=sinb, op=ALU.mult),
                      r=keys_r, w=[ktmp])
                P.add(eng, lambda e: e.tensor_tensor(out=tmp[:, :, 16:32], in0=src3[:, :, 0:16], in1=sinb, op=ALU.mult),
                      r=keys_r + [ktmp], w=[ktmp])
                P.add(eng, lambda e: e.tensor_tensor(out=src3[:, :, 0:16], in0=src3[:, :, 0:16], in1=cosb, op=ALU.mult),
                      r=keys_r + [ktmp], w=keys_r[:1])
                P.add(eng, lambda e: e.tensor_tensor(out=src3[:, :, 16:32], in0=src3[:, :, 16:32], in1=cosb, op=ALU.mult),
                      r=keys_r + [ktmp], w=keys_r[:1])
                P.add(eng, lambda e: e.tensor_tensor(out=dst3[:, :, 0:16], in0=src3[:, :, 0:16], in1=tmp[:, :, 0:16], op=ALU.subtract),
                      r=keys_r + [ktmp], w=keys_w)
                P.add(eng, lambda e: e.tensor_tensor(out=dst3[:, :, 16:32], in0=src3[:, :, 16:32], in1=tmp[:, :, 16:32], op=ALU.add),
                      r=keys_r + [ktmp], w=keys_w)

            def prep_tile(g):
                s, tt = divmod(g, NT)
                jb, i4 = divmod(tt, 4)
                sl = g % 2
                bs = (g // 4) % 2
                S_ = str(sl)
                kxt, kxn, khT = "xt" + S_, "xn" + S_, "hT" + S_
                P.add("sp", lambda e: e.dma_start(out=xt[sl][:], in_=x_d[s, tt * 128:(tt + 1) * 128, :]), w=[kxt], chan="xt" + S_)
                P.add("act", lambda e: e.activation(out=junk[:], in_=xt[sl][:], func=AF.Square, accum_out=st[:, sl, 0:1]),
                      r=[kxt], w=["st%s_x" % S_])
                kst = rstd_from_ssq(sl, 0, D, "x")
                P.add("act", lambda e: e.activation(out=xn[sl][:], in_=xt[sl][:], func=AF.Copy, scale=st[:, sl, 0:1]),
                      r=[kxt, kst], w=[kxn])
                yield
                b0 = BK.get(hold=True)
                for c in range(8):
                    P.add("pe", (lambda c: lambda e: e.transpose(out=banks_b[b0][:, c * 128:(c + 1) * 128],
                                                                  in_=xn[sl][:, c * 128:(c + 1) * 128], identity=ident[:]))(c),
                          r=[kxn, "ident"], w=[bk(b0)])
                tp3 = banks_b[b0][:, 0:1024].rearrange("p (c t) -> p c t", c=8)
                P.add("dve", lambda e: e.tensor_tensor(out=hT[sl][:], in0=tp3, in1=AB[:, s, 0, :].unsqueeze(2).to_broadcast([128, 8, 128]),
                                                       op=ALU.mult), r=[bk(b0), "AB"], w=[khT])
                BK.release(b0)
                P.add("pool", lambda e: e.tensor_tensor(out=hT[sl][:], in0=hT[sl][:], in1=AB[:, s, 1, :].unsqueeze(2).to_broadcast([128, 8, 128]),
                                                        op=ALU.add), r=[khT, "AB"], w=[khT])
                cols = [(0, 416), (416, 928), (928, 1440), (1440, 1952)]

                def proj_group(bi):
                    bb_ = BK.get(hold=True)
                    c0, c1 = cols[bi]
                    for k in range(8):
                        P.add("pe", (lambda k: lambda e: e.matmul(
                            banks_f[bb_][:, 0:c1 - c0], lhsT=hT[sl][:, k, :], rhs=win[:, k, c0:c1],
                            start=(k == 0), stop=(k == 7)))(k),
                            r=[khT] + WIN_KEYS, w=[bk(bb_)])
                    return bb_
                yield
                pb0 = proj_group(0)
                pl = banks_f[pb0]
                kl = bk(pb0)
                yield
                P.add("act", lambda e: e.activation(out=junk[:, 0:256], in_=pl[:, 0:256], func=AF.Square, accum_out=st[:, sl, 1:2]),
                      r=[kl], w=["st%s_ql" % S_])
                P.add("act", lambda e: e.activation(out=junk[:, 256:384], in_=pl[:, 256:384], func=AF.Square, accum_out=st[:, sl, 2:3]),
                      r=[kl], w=["st%s_kvl" % S_])
                k1 = rstd_from_ssq(sl, 1, 256, "ql")
                k2 = rstd_from_ssq(sl, 2, 128, "kvl")
                klat = "lat" + S_
                P.add("dve", lambda e: e.tensor_scalar(out=lat[sl][:, 0:256], in0=pl[:, 0:256], scalar1=st[:, sl, 1:2], scalar2=None,
                                                       op0=ALU.mult), r=[kl, k1], w=[klat + "a"])
                P.add("dve", lambda e: e.tensor_scalar(out=lat[sl][:, 256:384], in0=pl[:, 256:384], scalar1=st[:, sl, 2:3], scalar2=None,
                                                       op0=ALU.mult), r=[kl, k2], w=[klat + "b"])
                kkr = "kr" + S_
                P.add("dve", lambda e: e.tensor_tensor(out=kr[sl][:, 0, :], in0=pl[:, 384:416], in1=grow[:, 160:192], op=ALU.mult),
                      r=[kl, "grow"], w=[kkr])
                P.add("act", lambda e: e.activation(out=junk[:, 512:544], in_=pl[:, 384:416], func=AF.Square, accum_out=st[:, sl, 3:4]),
                      r=[kl], w=["st%s_kr" % S_])
                BK.release(pb0)
                yield
                b1 = BK.get(hold=True)
                for c in range(3):
                    P.add("pe", (lambda c: lambda e: e.transpose(out=banks_b[b1][:, c * 128:(c + 1) * 128],
                                                                  in_=lat[sl][:, c * 128:(c + 1) * 128], identity=ident[:]))(c),
                          r=[klat + "a", klat + "b", "ident"], w=[bk(b1)])
                klT = "latT" + S_
                P.add("act", lambda e: e.activation(out=latT[sl][:].rearrange("p c t -> p (c t)"), in_=banks_b[b1][:, 0:384], func=AF.Copy),
                      r=[bk(b1)], w=[klT])
                BK.release(b1)
                yield
                bq0, bq1 = BK.get(hold=True), BK.get(hold=True)
                for (bb, c0, c1) in ((bq0, 0, 480), (bq1, 480, 768)):
                    for k in range(2):
                        P.add("pe", (lambda bb, c0, c1, k: lambda e: e.matmul(
                            banks_f[bb][:, 0:c1 - c0], lhsT=latT[sl][:, k, :], rhs=wq[:, k, c0:c1], start=(k == 0), stop=(k == 1)))(bb, c0, c1, k),
                            r=[klT, "wq"], w=[bk(bb)])
                bkv0, bkv1 = BK.get(hold=True), BK.get(hold=True)
                for (bb, c0) in ((bkv0, 0), (bkv1, 512)):
                    P.add("pe", (lambda bb, c0: lambda e: e.matmul(
                        banks_f[bb][:, 0:512], lhsT=latT[sl][:, 2, :], rhs=wkv[:, c0:c0 + 512], start=True, stop=True))(bb, c0),
                        r=[klT, "wkv"], w=[bk(bb)])
                yield
                cs = cs_all[:, g, :]
                kqn, kqs, kqb = "qn" + S_, "qsq" + S_, "qb" + S_
                q0 = banks_f[bq0][:, 0:480].rearrange("p (h d) -> p h d", h=5)
                q1 = banks_f[bq1][:, 0:288].rearrange("p (h d) -> p h d", h=3)
                P.add("act", lambda e: e.activation(out=qn[sl][:, 0:5, :], in_=q0, func=AF.Copy), r=[bk(bq0)], w=[kqn + "a"])
                P.add("act", lambda e: e.activation(out=qn[sl][:, 5:8, :], in_=q1, func=AF.Copy), r=[bk(bq1)], w=[kqn + "b"])
                BK.release(bq0)
                BK.release(bq1)
                yield
                P.add("pool", lambda e: e.tensor_tensor(out=qsq[sl][:], in0=qn[sl][:], in1=qn[sl][:], op=ALU.mult),
                      r=[kqn + "a", kqn + "b"], w=[kqs])
                P.add("dve", lambda e: e.tensor_reduce(out=st[:, sl, 8:16], in_=qsq[sl][:], axis=AX.X, op=ALU.add),
                      r=[kqs], w=["st%s_q" % S_])
                k3 = rstd_vec(sl, 8, 8, 96, "q")
                P.add("dve", lambda e: e.tensor_tensor(out=qn[sl][:], in0=qn[sl][:], in1=st[:, sl, 8:16].unsqueeze(2).to_broadcast([128, 8, 96]),
                                                       op=ALU.mult), r=[kqn + "a", kqn + "b", k3], w=[kqn])
                P.add("pool", lambda e: e.tensor_tensor(out=qn[sl][:], in0=qn[sl][:], in1=grow[:, 0:96].unsqueeze(1).to_broadcast([128, 8, 96]),
                                                        op=ALU.mult), r=[kqn, "grow"], w=[kqn])
                P.add("act", lambda e: e.activation(out=qb[sl][:, :, 0:64], in_=qn[sl][:, :, 0:64], func=AF.Copy), r=[kqn], w=[kqb + "n"])
                yield
                rope("pool", qn[sl][:, :, 64:96], qb[sl][:, :, 64:96], cs, [kqn, "cs_all"], [kqb + "r"], rt[sl], 8, "rt" + S_)
                kkn, kks, kkb = "kn" + S_, "ksq" + S_, "kb" + S_
                kv0 = banks_f[bkv0][:, 0:512].rearrange("p (h d) -> p h d", h=4)
                kv1 = banks_f[bkv1][:, 0:512].rearrange("p (h d) -> p h d", h=4)
                P.add("act", lambda e: e.activation(out=kn[sl][:, 0:4, :], in_=kv0[:, :, 0:64], func=AF.Copy), r=[bk(bkv0)], w=[kkn + "a"])
                P.add("act", lambda e: e.activation(out=kn[sl][:, 4:8, :], in_=kv1[:, :, 0:64], func=AF.Copy), r=[bk(bkv1)], w=[kkn + "b"])
                P.add("act", lambda e: e.activation(out=V_st[bs][:, i4, 0:4, 0:64], in_=kv0[:, :, 64:128], func=AF.Copy),
                      r=[bk(bkv0)], w=["Vst%d" % bs])
                P.add("act", lambda e: e.activation(out=V_st[bs][:, i4, 4:8, 0:64], in_=kv1[:, :, 64:128], func=AF.Copy),
                      r=[bk(bkv1)], w=["Vst%d" % bs])
                BK.release(bkv0)
                BK.release(bkv1)
                yield
                P.add("pool", lambda e: e.tensor_tensor(out=ksq[sl][:], in0=kn[sl][:], in1=kn[sl][:], op=ALU.mult),
                      r=[kkn + "a", kkn + "b"], w=[kks])
                P.add("dve", lambda e: e.tensor_reduce(out=st[:, sl, 16:24], in_=ksq[sl][:], axis=AX.X, op=ALU.add),
                      r=[kks], w=["st%s_k" % S_])
                P.add("dve", lambda e: e.tensor_scalar(out=st[:, sl, 16:24], in0=st[:, sl, 16:24], scalar1=st[:, sl, 3:4], scalar2=None,
                                                       op0=ALU.add), r=["st%s_k" % S_, "st%s_kr" % S_], w=["st%s_k" % S_])
                k4 = rstd_vec(sl, 16, 8, 96, "k")
                yield
                P.add("dve", lambda e: e.tensor_tensor(out=kn[sl][:], in0=kn[sl][:], in1=st[:, sl, 16:24].unsqueeze(2).to_broadcast([128, 8, 64]),
                                                       op=ALU.mult), r=[kkn + "a", kkn + "b", k4], w=[kkn])
                P.add("pool", lambda e: e.tensor_tensor(out=kb[sl][:, :, 0:64], in0=kn[sl][:], in1=grow[:, 96:160].unsqueeze(1).to_broadcast([128, 8, 64]),
                                                        op=ALU.mult), r=[kkn, "grow"], w=[kkb + "n"])
                rope("dve", kr[sl][:, 0:1, :], kr[sl][:, 1:2, :], cs, [kkr, "cs_all"], [kkr + "o"], kr[sl][:, 2:3, :], 1, kkr + "t")
                P.add("dve", lambda e: e.tensor_tensor(out=kb[sl][:, :, 64:96], in0=kr[sl][:, 1:2, :].to_broadcast([128, 8, 32]),
                                                       in1=st[:, sl, 16:24].unsqueeze(2).to_broadcast([128, 8, 32]), op=ALU.mult),
                      r=[kkr + "o", k4], w=[kkb + "r"])
                for (bi_, dn, dsq, db, col, gc0, tag) in (
                        (1, cqn, cqs, cqb, 24, 192, "cq"), (2, ckn, cks, ckb, 32, 256, "ck")):
                    kdn, kds, kdb = tag + "n" + S_, tag + "s" + S_, tag + "b" + S_
                    yield
                    bbx = proj_group(bi_)
                    P.add("act", (lambda bbx, dn: lambda e: e.activation(out=dn[sl][:].rearrange("p h d -> p (h d)"), in_=banks_f[bbx][:, 0:512], func=AF.Copy))(bbx, dn),
                          r=[bk(bbx)], w=[kdn])
                    BK.release(bbx)
                    yield
                    P.add("pool", (lambda dn, dsq: lambda e: e.tensor_tensor(out=dsq[sl][:], in0=dn[sl][:], in1=dn[sl][:], op=ALU.mult))(dn, dsq),
                          r=[kdn], w=[kds])
                    P.add("dve", (lambda dsq, col: lambda e: e.tensor_reduce(out=st[:, sl, col:col + 8], in_=dsq[sl][:], axis=AX.X, op=ALU.add))(dsq, col),
                          r=[kds], w=["st%s_%s" % (S_, tag)])
                    k5 = rstd_vec(sl, col, 8, 64, tag)
                    P.add("dve", (lambda dn, col: lambda e: e.tensor_tensor(out=dn[sl][:], in0=dn[sl][:],
                                                                           in1=st[:, sl, col:col + 8].unsqueeze(2).to_broadcast([128, 8, 64]), op=ALU.mult))(dn, col),
                          r=[kdn, k5], w=[kdn])
                    P.add("pool", (lambda dn, db, gc0: lambda e: e.tensor_tensor(out=db[sl][:], in0=dn[sl][:],
                                                                                in1=grow[:, gc0:gc0 + 64].unsqueeze(1).to_broadcast([128, 8, 64]), op=ALU.mult))(dn, db, gc0),
                          r=[kdn, "grow"], w=[kdb])
                yield
                bbv = proj_group(3)
                P.add("act", lambda e: e.activation(out=cV_st[bs][:, i4, :, 0:64], in_=banks_f[bbv][:, 0:512].rearrange("p (h d) -> p h d", h=8), func=AF.Copy),
                      r=[bk(bbv)], w=["cVst%d" % bs])
                BK.release(bbv)
                for (srcb, keys, dst, kdst, dd) in ((qb, [kqb + "n", kqb + "r"], qT_st, "qTst%d" % bs, 96),
                                                    (kb, [kkb + "n", kkb + "r"], kT_st, "kTst%d" % bs, 96),
                                                    (cqb, ["cqb" + S_], cqT_st, "cqTst%d" % bs, 64),
                                                    (ckb, ["ckb" + S_], ckT_st, "ckTst%d" % bs, 64)):
                    yield
                    bt = BK.get(hold=True)
                    for h in range(8):
                        P.add("pe", (lambda srcb, bt, h, dd: lambda e: e.transpose(
                            out=banks_b[bt][0:dd, h * 128:(h + 1) * 128], in_=srcb[sl][:, h, :], identity=ident[:]))(srcb, bt, h, dd),
                            r=keys + ["ident"], w=[bk(bt)])
                    P.add("act", (lambda bt, dst, dd: lambda e: e.activation(
                        out=dst[bs][:, :, i4 * 128:(i4 + 1) * 128], in_=banks_b[bt][0:dd, 0:1024].rearrange("p (h t) -> p h t", h=8), func=AF.Copy))(bt, dst, dd),
                        r=[bk(bt)], w=[kdst])
                    BK.release(bt)
                if i4 == 3:
                    t0 = jb * 512
                    P.add("sp", lambda e: e.dma_start(out=qT_d[s, :, :, t0:t0 + 512].rearrange("h d t -> d h t"), in_=qT_st[bs][:]),
                          r=["qTst%d" % bs], w=["qT_d%d" % s], chan="stq%d" % bs)
                    P.add("sp", lambda e: e.dma_start(out=kT_d[s, :, :, t0:t0 + 512].rearrange("h d t -> d h t"), in_=kT_st[bs][:]),
                          r=["kTst%d" % bs], w=["kT_d%d" % s], chan="stk%d" % bs)
                    P.add("sp", lambda e: e.dma_start(out=cqT_d[s, :, :, t0:t0 + 512].rearrange("h d t -> d h t"), in_=cqT_st[bs][:]),
                          r=["cqTst%d" % bs], w=["cqT_d%d" % s], chan="stcq%d" % bs)
                    P.add("sp", lambda e: e.dma_start(out=ckT_d[s, :, :, t0:t0 + 512].rearrange("h d t -> d h t"), in_=ckT_st[bs][:]),
                          r=["ckTst%d" % bs], w=["ckT_d%d" % s], chan="stck%d" % bs)
                    P.add("sp", lambda e: e.dma_start(out=V_d[s, t0:t0 + 512, :].rearrange("(k p) c -> p k c", p=128),
                                                      in_=V_st[bs][:].rearrange("p k h c -> p k (h c)")),
                          r=["Vst%d" % bs], w=["V_d%d" % s], chan="stv%d" % bs)
                    P.add("sp", lambda e: e.dma_start(out=cV_d[s, t0:t0 + 512, :].rearrange("(k p) c -> p k c", p=128),
                                                      in_=cV_st[bs][:].rearrange("p k h c -> p k (h c)")),
                          r=["cVst%d" % bs], w=["cV_d%d" % s], chan="stcv%d" % bs)

            ntiles = NSEQ * NT if stop not in ("setup",) else 0
            active = []
            nxt = 0
            INFL = 2
            STAG = 11
            steps = {}
            while nxt < ntiles or active:
                if nxt < ntiles and len(active) < INFL and all(steps[id(g_)] >= STAG for g_ in active):
                    gnew = prep_tile(nxt)
                    steps[id(gnew)] = 0
                    active.append(gnew)
                    nxt += 1
                for gen in list(active):
                    try:
                        next(gen)
                        steps[id(gen)] += 1
                    except StopIteration:
                        active.remove(gen)
            P.barrier()
            sa.close()

        with ExitStack() as sbk:
            kT_all = sb(sbk, "kT_all", [96, 8, S], BF16)
            V_all = sb(sbk, "V_all", [128, NT, 8 * 65], BF16)
            ckT_all = sb(sbk, "ckT_all", [64, 8, S], BF16)
            cV_all = sb(sbk, "cV_all", [128, NT, 8 * 65], BF16)
            qT_b = [sb(sbk, "qT_b%d" % i, [96, 8, 512], BF16) for i in range(2)]
            cqT_b = [sb(sbk, "cqT_b%d" % i, [64, 8, 512], BF16) for i in range(2)]
            otn = [sb(sbk, "otn%d" % i, [64, 16, 512], BF16) for i in range(1)]
            Ef = sb(sbk, "Ef", [128, 8, 2, 128], F32)
            bfar = sb(sbk, "bfar", [128, 8], F32)
            NPT = 6
            Pt = [sb(sbk, "Pt%d" % i, [128, 512], BF16) for i in range(NPT)]
            PA = [sb(sbk, "PA%d" % i, [128, 384], BF16) for i in range(3)]
            PB = [sb(sbk, "PB%d" % i, [128, 256], F32) for i in range(3)]
            PBb = [sb(sbk, "PBb%d" % i, [128, 256], BF16) for i in range(3)]
            NRZ = 4
            ots = [sb(sbk, "ots%d" % i, [128, 512], F32) for i in range(NRZ)]
            rzb = [sb(sbk, "rzb%d" % i, [128, 2, 512], BF16) for i in range(NRZ)]
            sel64b = sb(sbk, "sel64b", [128, 64], BF16)
            P.tag = "wlB"
            P.add("sp", lambda e: e.dma_start(out=Ef[:], in_=bias34_d), w=["Ef"], chan="small4", waitall=True)
            P.add("sp", lambda e: e.dma_start(out=bfar[:], in_=bfar_d), w=["bfar"], chan="small4", waitall=True)
            P.add("act", lambda e: e.activation(out=Ef[:], in_=Ef[:], func=AF.Exp), r=["Ef"], w=["Ef"])
            P.add("pool", lambda e: e.memset(Ef[64:128, :, 1, 0:64], 0.0), r=["Ef"], w=["Ef"])
            for i in range(NRZ):
                P.add("pool", (lambda i: lambda e: e.memset(rzb[i][:], 0.0))(i), w=["rzb%d" % i])
            P.add("pool", lambda e: e.tensor_copy(out=sel64b[:], in_=sel64[:]), r=["sel64"], w=["sel64b"])
            P.tag = None
            cnt = {"pt": 0, "pa": 0, "rz": 0}

            def normalize_head(bo, width):
                ri = cnt["rz"] % NRZ
                cnt["rz"] += 1
                P.add("dve", lambda e: e.tensor_copy(out=ots[ri][0:65, 0:width], in_=banks_f[bo][0:65, 0:width]),
                      r=[bk(bo)], w=["ots%d" % ri])
                BK.release(bo)
                P.add("act", lambda e: e.activation(out=ots[ri][64:65, 0:width], in_=ots[ri][64:65, 0:width], func=AF.Ln),
                      r=["ots%d" % ri], w=["ots%d" % ri])
                P.add("act", lambda e: e.activation(out=ots[ri][64:65, 0:width], in_=ots[ri][64:65, 0:width], func=AF.Exp, scale=-1.0),
                      r=["ots%d" % ri], w=["ots%d" % ri])
                P.add("dve", lambda e: e.tensor_copy(out=rzb[ri][64:65, 0, 0:width], in_=ots[ri][64:65, 0:width]),
                      r=["ots%d" % ri], w=["rzb%d" % ri])
                P.add("dve", lambda e: e.tensor_tensor(out=rzb[ri][64:65, 1, 0:width], in0=ots[ri][64:65, 0:width],
                                                       in1=rzb[ri][64:65, 0, 0:width], op=ALU.subtract),
                      r=["ots%d" % ri, "rzb%d" % ri], w=["rzb%d" % ri])
                return ri

            def normalize_tail(ri, width, dst_ap, kdst):
                bb = BK.get(hold=True)
                for pl_ in range(2):
                    P.add("pe", (lambda pl_: lambda e: e.matmul(banks_f[bb][0:64, 0:width], lhsT=sel64b[:, 0:64], rhs=rzb[ri][:, pl_, 0:width],
                                                                start=(pl_ == 0), stop=(pl_ == 1)))(pl_),
                          r=["rzb%d" % ri, "sel64b"], w=[bk(bb)])
                P.add("dve", lambda e: e.tensor_tensor(out=dst_ap, in0=banks_f[bb][0:64, 0:width], in1=ots[ri][0:64, 0:width], op=ALU.mult),
                      r=[bk(bb), "ots%d" % ri], w=[kdst])
                BK.release(bb)

            def mla_gen(s, j, h, qs):
                kq = "qT_b%d" % qs
                nkt = 4 * j + 4
                bo = BK.get(hold=True)
                tiles = []
                for kt in range(nkt):
                    r_ = kt - 4 * j
                    c0 = 128 * r_ if r_ > 0 else 0
                    tiles.append((kt, c0, r_ >= 0))
                sbank = {}

                def emit_s(idx):
                    kt, c0, diag = tiles[idx]
                    b = BK.get(hold=True)
                    sbank[idx] = b
                    P.add("pe", lambda e: e.matmul(banks_f[b][:, 0:512 - c0], lhsT=kT_all[:, h, kt * 128:(kt + 1) * 128],
                                                   rhs=qT_b[qs][:, h, c0:512], start=True, stop=True),
                          r=["kT_all%d" % (h // 4), kq], w=[bk(b)])

                def emit_rest(idx):
                    kt, c0, diag = tiles[idx]
                    b = sbank[idx]
                    pi = cnt["pt"] % NPT
                    cnt["pt"] += 1
                    kp = "Pt%d" % pi
                    w_ = 512 - c0
                    P.add("act", lambda e: e.activation(out=Pt[pi][:, 0:w_], in_=banks_f[b][:, 0:w_], func=AF.Exp), r=[bk(b)], w=[kp])
                    BK.release(b)
                    if diag:
                        P.add("pool", lambda e: e.memset(Pt[pi][64:128, 0:64], 0.0), r=[kp], w=[kp])
                    P.add("pe", lambda e: e.matmul(banks_f[bo][0:65, c0:512], lhsT=V_all[:, kt, h * 65:(h + 1) * 65], rhs=Pt[pi][:, 0:w_],
                                                   start=(idx == 0), stop=(idx == nkt - 1)),
                          r=[kp, "V_all%d" % (kt // 4)], w=[bk(bo)])

                LOOK = 3
                for idx in range(min(LOOK, nkt)):
                    emit_s(idx)
                yield
                for idx in range(nkt):
                    emit_rest(idx)
                    if idx + LOOK < nkt:
                        emit_s(idx + LOOK)
                    yield
                ri = normalize_head(bo, 512)
                pend_m.append((ri, 512, otn[0][:, h, :], "otn0h%d" % h))
                yield

            def ca_qtile(j, h, qs, bo, i):
                kq = "cqT_b%d" % qs
                gi = 4 * j + i
                tmin = max(0, 4 - gi)
                pai = cnt["pa"] % 3
                cnt["pa"] += 1
                ba = BK.get(hold=True) if tmin <= 2 else None
                bb = BK.get(hold=True)

                def s_mm(t):
                    ktile = gi - 4 + t
                    if t <= 2:
                        dst = banks_f[ba][:, t * 128:(t + 1) * 128]
                        kb_ = bk(ba)
                    else:
                        dst = banks_f[bb][:, (t - 3) * 128:(t - 2) * 128]
                        kb_ = bk(bb)
                    P.add("pe", lambda e: e.matmul(dst, lhsT=ckT_all[:, h, ktile * 128:(ktile + 1) * 128],
                                                   rhs=cqT_b[qs][:, h, i * 128:(i + 1) * 128], start=True, stop=True),
                          r=["ckT_all%d" % (h // 4), kq], w=[kb_])

                def pv_mm(t):
                    ktile = gi - 4 + t
                    if t <= 2:
                        rhs = PA[pai][:, t * 128:(t + 1) * 128]
                        kr_ = "PA%d" % pai
                    else:
                        rhs = PBb[pai][:, (t - 3) * 128:(t - 2) * 128]
                        kr_ = "PBb%d" % pai
                    P.add("pe", lambda e: e.matmul(banks_f[bo][0:65, i * 128:(i + 1) * 128],
                                                   lhsT=cV_all[:, ktile, h * 65:(h + 1) * 65], rhs=rhs,
                                                   start=(t == tmin), stop=(t == 4)),
                          r=[kr_, "cV_all%d" % (ktile // 4)], w=[bk(bo)])

                for t in range(tmin, 5):
                    s_mm(t)
                yield
                if ba is not None:
                    a0 = tmin * 128
                    P.add("act", lambda e: e.activation(out=PA[pai][:, a0:384], in_=banks_f[ba][:, a0:384], func=AF.Exp,
                                                        bias=bfar[:, h:h + 1], scale=1.0), r=[bk(ba), "bfar"], w=["PA%d" % pai])
                    BK.release(ba)
                    if tmin == 0:
                        P.add("pool", lambda e: e.memset(PA[pai][0:64, 64:128], 0.0), r=["PA%d" % pai], w=["PA%d" % pai])
                b0 = 0 if tmin <= 3 else 128
                P.add("act", lambda e: e.activation(out=PB[pai][:, b0:256], in_=banks_f[bb][:, b0:256], func=AF.Exp), r=[bk(bb)], w=["PB%d" % pai])
                BK.release(bb)
                P.add("pool", lambda e: e.tensor_tensor(out=PBb[pai][:, b0:256], in0=PB[pai][:, b0:256],
                                                        in1=Ef[:, h, :, :].rearrange("p t q -> p (t q)")[:, b0:256], op=ALU.mult),
                      r=["PB%d" % pai, "Ef"], w=["PBb%d" % pai])
                yield
                for t in range(tmin, 5):
                    pv_mm(t)
                yield

            def ca_gen(s, j, h, qs):
                bo = BK.get(hold=True)
                for i in range(4):
                    yield from ca_qtile(j, h, qs, bo, i)
                ri = normalize_head(bo, 512)
                pend_c.append((ri, 512, otn[0][:, 8 + h, :], "otn0h%d" % (8 + h)))
                yield

            pend_m, pend_c = [], []

            def stream(gens, pend, delay):
                for g in gens:
                    n = 0
                    old = list(pend)
                    del pend[:]
                    for _ in g:
                        n += 1
                        yield
                        if n == delay and old:
                            for t in old:
                                normalize_tail(*t)
                            old = []
                            yield
                    if old:
                        for t in old:
                            normalize_tail(*t)
                        yield
                for t in pend:
                    normalize_tail(*t)
                del pend[:]
                yield

            def interleave(ga, gb, ra, rb):
                alive_a = alive_b = True
                while alive_a or alive_b:
                    for _ in range(ra):
                        if alive_a:
                            try:
                                next(ga)
                            except StopIteration:
                                alive_a = False
                    for _ in range(rb):
                        if alive_b:
                            try:
                                next(gb)
                            except StopIteration:
                                alive_b = False

            def load_seq(s):
                for hh in range(2):
                    P.add("sp", lambda e: e.dma_start(out=kT_all[:, 4 * hh:4 * hh + 4, :], in_=kT_d[s, 4 * hh:4 * hh + 4].rearrange("h d t -> d h t")),
                          r=["kT_d%d" % s], w=["kT_all%d" % hh], chan="ldk%d" % hh)
                    P.add("sp", lambda e: e.dma_start(out=ckT_all[:, 4 * hh:4 * hh + 4, :], in_=ckT_d[s, 4 * hh:4 * hh + 4].rearrange("h d t -> d h t")),
                          r=["ckT_d%d" % s], w=["ckT_all%d" % hh], chan="ldck%d" % hh)
                for q4 in range(4):
                    P.add("sp", lambda e: e.dma_start(out=V_all[:, 4 * q4:4 * q4 + 4, :], in_=V_d[s, 512 * q4:512 * q4 + 512, :].rearrange("(k p) c -> p k c", p=128)),
                          r=["V_d%d" % s], w=["V_all%d" % q4], chan="ldv%d" % q4)
                    P.add("sp", lambda e: e.dma_start(out=cV_all[:, 4 * q4:4 * q4 + 4, :], in_=cV_d[s, 512 * q4:512 * q4 + 512, :].rearrange("(k p) c -> p k c", p=128)),
                          r=["cV_d%d" % s], w=["cV_all%d" % q4], chan="ldcv%d" % q4)

            def load_seq_part(fn, *a):
                fn(*a)

            def load_q(s, j, qs):
                t0 = 512 * j
                P.add("sp", lambda e: e.dma_start(out=qT_b[qs][:], in_=qT_d[s, :, :, t0:t0 + 512].rearrange("h d t -> d h t")),
                      r=["qT_d%d" % s], w=["qT_b%d" % qs], chan="ldq%d" % qs)
                P.add("sp", lambda e: e.dma_start(out=cqT_b[qs][:], in_=cqT_d[s, :, :, t0:t0 + 512].rearrange("h d t -> d h t")),
                      r=["cqT_d%d" % s], w=["cqT_b%d" % qs], chan="ldcq%d" % qs)

            def attn_block(s, j, qs):
                t0 = 512 * j
                nb_ = s * 4 + j + 1
                if nb_ < NSEQ * 4:
                    load_q(nb_ // 4, nb_ % 4, 1 - qs)
                gm = stream([mla_gen(s, j, h, qs) for h in range(8)], pend_m, 3)
                gc = stream([ca_gen(s, j, h, qs) for h in range(8)], pend_c, 4)
                ra, rb = {0: (1, 2), 1: (3, 4), 2: (1, 1), 3: (3, 2)}[j]
                interleave(gm, gc, ra, rb)
                P.add("sp", lambda e: e.dma_start(out=otn_d[s, :, t0:t0 + 512].rearrange("(h d) t -> d h t", d=64), in_=otn[0][:]),
                      r=["otn0h%d" % hh for hh in range(16)], w=["otn_d%d_%d" % (s, j)], chan="stotn")

            def load_seq_safe(s):
                def ldk(hh):
                    P.add("sp", lambda e: e.dma_start(out=kT_all[:, 4 * hh:4 * hh + 4, :], in_=kT_d[s, 4 * hh:4 * hh + 4].rearrange("h d t -> d h t")),
                          r=["kT_d%d" % s], w=["kT_all%d" % hh], chan="ldk%d" % hh)
                    P.add("sp", lambda e: e.dma_start(out=ckT_all[:, 4 * hh:4 * hh + 4, :], in_=ckT_d[s, 4 * hh:4 * hh + 4].rearrange("h d t -> d h t")),
                          r=["ckT_d%d" % s], w=["ckT_all%d" % hh], chan="ldck%d" % hh)

                def ldv(q4):
                    P.add("sp", lambda e: e.dma_start(out=V_all[:, 4 * q4:4 * q4 + 4, :], in_=V_d[s, 512 * q4:512 * q4 + 512, :].rearrange("(k p) c -> p k c", p=128)),
                          r=["V_d%d" % s], w=["V_all%d" % q4], chan="ldv%d" % q4)
                    P.add("sp", lambda e: e.dma_start(out=cV_all[:, 4 * q4:4 * q4 + 4, :], in_=cV_d[s, 512 * q4:512 * q4 + 512, :].rearrange("(k p) c -> p k c", p=128)),
                          r=["cV_d%d" % s], w=["cV_all%d" % q4], chan="ldcv%d" % q4)
                ldk(0)
                ldv(0)
                ldk(1)
                ldv(1)
                ldv(2)
                ldv(3)

            blk = 0
            if stop not in ("setup", "A"):
                load_q(0, 0, 0)
            for s in range(NSEQ if stop not in ("setup", "A") else 0):
                load_seq_safe(s)
                for j in range(4):
                    attn_block(s, j, blk % 2)
                    blk += 1
            P.barrier()

        with ExitStack() as sc:
            TB = 256
            NTB = TB // 128
            wdn = sb(sc, "wdn", [128, NFF, D], BF16)
            wout = sb(sc, "wout", [128, 8, D], BF16)
            wup = sb(sc, "wup", [128, 8, 2 * D_FF], BF16)
            convp = sb(sc, "convp", [128, 4, 2 * NFF], F32)
            gab = sb(sc, "gab", [128, D], F32)
            gmb = sb(sc, "gmb", [128, D], F32)
            otb = sb(sc, "otb", [128, 8, TB], BF16)
            x1 = [sb(sc, "x1_%d" % i, [128, D], F32) for i in range(NTB)]
            xn2s = [sb(sc, "xn2_%d" % i, [128, D], BF16) for i in range(NTB)]
            tmph = [sb(sc, "tmph%d" % i, [128, 512], F32) for i in range(NTB)]
            hT2 = sb(sc, "hT2", [128, 8, TB + 2], BF16)
            gT = sb(sc, "gT", [128, NFF, TB], BF16)
            NUB = 3
            cg = [sb(sc, "cg%d" % i, [128, TB], F32) for i in range(NUB)]
            cv = [sb(sc, "cv%d" % i, [128, TB], F32) for i in range(NUB)]
            stC = sb(sc, "stC", [128, 4], F32)
            otile = sb(sc, "otile", [128, D], F32)
            P.tag = "wlC"
            P.add("sp", lambda e: e.dma_start(out=convp[:], in_=convp_d), w=["convp"], chan="small5", waitall=True)

            wl_cnt = {"n": 0}
            WUP_KEYS, WOUT_KEYS, WDN_KEYS = [], [], []

            def wl_piece(fn, keylist, name):
                n = wl_cnt["n"]
                wl_cnt["n"] += 1
                key = "%s_p%d" % (name, len(keylist))
                P.add("pool", fn, r=([wl_cnt["prev2"]] if n >= 2 else []), w=[key], chan="wlc%d" % (n % 2))
                wl_cnt["prev2"] = wl_cnt.get("prev1")
                wl_cnt["prev1"] = key
                keylist.append(key)

            def ldw(k):
                wl_piece(lambda e: e.dma_start(out=wup[:, :, k * 512:(k + 1) * 512],
                                               in_=wup_d[:, k * 512:(k + 1) * 512].rearrange("(j p) n -> p j n", p=128)), WUP_KEYS, "wup")

            def ldwo(k):
                wl_piece(lambda e: e.dma_start(out=wout[:, :, k * 512:(k + 1) * 512],
                                               in_=wout_d[:, k * 512:(k + 1) * 512].rearrange("(j p) n -> p j n", p=128)), WOUT_KEYS, "wout")

            def ldwd(k, jg):
                wl_piece(lambda e: e.dma_start(out=wdn[:, 11 * jg:11 * jg + 11, k * 512:(k + 1) * 512],
                                               in_=wdn_d[1408 * jg:1408 * jg + 1408, k * 512:(k + 1) * 512].rearrange("(j p) n -> p j n", p=128)), WDN_KEYS, "wdn")
            for k in range(2):
                ldwo(k)
            for k in range(11):
                ldw(k)
            for k in range(2):
                for jg in range(2):
                    ldwd(k, jg)
            P.tag = None
            ucnt = {"u": 0}
            HALO_KEYS = ["halo%d" % ch for ch in range(2 * NFF)]

            def seq_start(s):
                P.add("sp", lambda e: e.dma_start(out=gab[:], in_=mod_d[s, 2 * D:3 * D].partition_broadcast(128)), r=["mod_d"], w=["gab"], chan="ldga")
                P.add("sp", lambda e: e.dma_start(out=gmb[:], in_=mod_d[s, 5 * D:6 * D].partition_broadcast(128)), r=["mod_d"], w=["gmb"], chan="ldgm")

            def outproj_tile(s, t0, it):
                tk = t0 + it * 128
                kx1 = "x1_%d" % it
                kxs = [kx1 + "h0", kx1 + "h512"]
                ktm, kxn2, kst = "tmph%d" % it, "xn2_%d" % it, "stC%d" % it
                P.add("sp", lambda e: e.dma_start(out=x1[it][:], in_=x_d[s, tk:tk + 128, :]), w=kxs, chan="ldx%d" % it)
                bo0, bo1 = BK.get(hold=True), BK.get(hold=True)

                def mm(bb, n0):
                    for c in range(8):
                        P.add("pe", (lambda c: lambda e: e.matmul(banks_f[bb][:, 0:512], lhsT=otb[:, c, it * 128:(it + 1) * 128],
                                                                  rhs=wout[:, c, n0:n0 + 512], start=(c == 0), stop=(c == 7)))(c),
                              r=["otb"] + WOUT_KEYS, w=[bk(bb)])

                def epi(bb, n0):
                    P.add("dve", lambda e: e.tensor_tensor(out=tmph[it][:], in0=banks_f[bb][:, 0:512],
                                                           in1=gab[:, n0:n0 + 512], op=ALU.mult),
                          r=[bk(bb), "gab"], w=[ktm])
                    BK.release(bb)
                    P.add("pool", lambda e: e.tensor_tensor(out=x1[it][:, n0:n0 + 512], in0=x1[it][:, n0:n0 + 512],
                                                            in1=tmph[it][:], op=ALU.add),
                          r=[kx1 + "h%d" % n0, ktm], w=[kx1 + "h%d" % n0])
                mm(bo0, 0)
                mm(bo1, 512)
                yield
                epi(bo0, 0)
                yield
                epi(bo1, 512)
                yield
                P.add("act", lambda e: e.activation(out=xn2s[it][:], in_=x1[it][:], func=AF.Square, accum_out=stC[:, it:it + 1]),
                      r=kxs, w=[kst, kxn2])
                P.add("act", lambda e: e.activation(out=stC[:, it:it + 1], in_=stC[:, it:it + 1], func=AF.Sqrt, bias=float(D * EPS), scale=1.0),
                      r=[kst], w=[kst])
                yield
                P.add("dve", lambda e: e.reciprocal(out=stC[:, it:it + 1], in_=stC[:, it:it + 1]), r=[kst], w=[kst])
                P.add("act", lambda e: e.activation(out=xn2s[it][:], in_=x1[it][:], func=AF.Copy, scale=stC[:, it:it + 1]),
                      r=kxs + [kst], w=[kxn2])
                yield
                bt = BK.get(hold=True)
                for c in range(8):
                    P.add("pe", (lambda c: lambda e: e.transpose(out=banks_b[bt][:, c * 128:(c + 1) * 128],
                                                                 in_=xn2s[it][:, c * 128:(c + 1) * 128], identity=ident[:]))(c),
                          r=[kxn2, "ident"], w=[bk(bt)])
                yield
                tp3 = banks_b[bt][:, 0:1024].rearrange("p (c t) -> p c t", c=8)
                kh = "hT2_%d" % it
                P.add("dve", lambda e: e.tensor_tensor(out=hT2[:, :, 2 + it * 128:2 + (it + 1) * 128], in0=tp3,
                                                       in1=AB[:, s, 2, :].unsqueeze(2).to_broadcast([128, 8, 128]), op=ALU.mult),
                      r=[bk(bt), "AB"], w=[kh])
                BK.release(bt)
                P.add("pool", lambda e: e.tensor_tensor(out=hT2[:, :, 2 + it * 128:2 + (it + 1) * 128], in0=hT2[:, :, 2 + it * 128:2 + (it + 1) * 128],
                                                        in1=AB[:, s, 3, :].unsqueeze(2).to_broadcast([128, 8, 128]), op=ALU.add),
                      r=[kh, "AB"], w=[kh])
                yield

            def ffn_up(f, khs):
                ui = ucnt["u"] % NUB
                ucnt["u"] += 1
                bg, bv = BK.get(hold=True), BK.get(hold=True)

                def up_mm(bb, ch):
                    for k in range(8):
                        P.add("pe", (lambda k: lambda e: e.matmul(banks_f[bb][:, 0:TB + 2],
                                                                  lhsT=wup[:, k, ch * 128:(ch + 1) * 128], rhs=hT2[:, k, :],
                                                                  start=(k == 0), stop=(k == 7)))(k),
                              r=khs + ["hT2_halo"] + WUP_KEYS, w=[bk(bb)])
                up_mm(bg, f)
                up_mm(bv, NFF + f)
                kcg, kcv = "cg%d" % ui, "cv%d" % ui

                def tap2(bb, ch, cb, kcb):
                    P.add("act", lambda e: e.activation(out=cb[ui][:], in_=banks_f[bb][:, 2:TB + 2], func=AF.Identity,
                                                        scale=convp[:, 2, ch:ch + 1], bias=convp[:, 3, ch:ch + 1]),
                          r=[bk(bb), "convp"], w=[kcb])

                def tap(bb, ch, cb, kcb, j):
                    P.add("dve", lambda e: e.scalar_tensor_tensor(out=cb[ui][:], in0=banks_f[bb][:, j:TB + j], scalar=convp[:, j, ch:ch + 1],
                                                                  in1=cb[ui][:], op0=ALU.mult, op1=ALU.add),
                          r=[bk(bb), kcb, "convp"], w=[kcb])
                tap2(bg, f, cg, kcg)
                tap2(bv, NFF + f, cv, kcv)
                tap(bg, f, cg, kcg, 1)
                tap(bv, NFF + f, cv, kcv, 1)
                tap(bg, f, cg, kcg, 0)
                tap(bv, NFF + f, cv, kcv, 0)
                BK.release(bg)
                BK.release(bv)
                return (f, ui)

            def ffn_gate(f, ui):
                kcg, kcv = "cg%d" % ui, "cv%d" % ui
                P.add("act", lambda e: e.activation(out=cg[ui][:], in_=cg[ui][:], func=AF.Silu), r=[kcg], w=[kcg])
                P.add("pool", lambda e: e.tensor_tensor(out=gT[:, f, :], in0=cg[ui][:], in1=cv[ui][:], op=ALU.mult),
                      r=[kcg, kcv], w=["gT%d" % f])

            def down_tile(s, t0, it, kgs):
                tk = t0 + it * 128
                bd0, bd1 = BK.get(hold=True), BK.get(hold=True)

                def half(bb, n0):
                    for f in range(NFF):
                        P.add("pe", (lambda f: lambda e: e.matmul(banks_f[bb][:, 0:512], lhsT=gT[:, f, it * 128:(it + 1) * 128],
                                                                  rhs=wdn[:, f, n0:n0 + 512], start=(f == 0), stop=(f == NFF - 1)))(f),
                              r=kgs + WDN_KEYS, w=[bk(bb)])
                    P.add("dve", lambda e: e.tensor_tensor(out=otile[:, n0:n0 + 512], in0=banks_f[bb][:, 0:512],
                                                           in1=gmb[:, n0:n0 + 512], op=ALU.mult),
                          r=[bk(bb), "gmb"], w=["otileh%d" % n0])
                    P.add("pool", lambda e: e.tensor_tensor(out=otile[:, n0:n0 + 512], in0=otile[:, n0:n0 + 512],
                                                            in1=x1[it][:, n0:n0 + 512], op=ALU.add),
                          r=["otileh%d" % n0, "x1_%dh%d" % (it, n0)], w=["otileh%d" % n0])
                    BK.release(bb)
                half(bd0, 0)
                half(bd1, 512)
                P.add("sp", lambda e: e.dma_start(out=out_d[s, tk:tk + 128, :], in_=otile[:]),
                      r=["otileh0", "otileh512"], w=["out_d"], chan="stout")

            def ffn_block(s, tb):
                t0 = tb * TB
                jblk = t0 // 512
                P.add("sp", lambda e: e.dma_start(out=otb[:], in_=otn_d[s, :, t0:t0 + TB].rearrange("(c p) t -> p c t", p=128)),
                      r=["otn_d%d_%d" % (s, jblk)], w=["otb"], chan="ldot")
                if tb == 0:
                    P.add("pool", lambda e: e.memset(hT2[:, :, 0:2], 0.0), r=["hT2_%d" % (NTB - 1)], w=["hT2_halo"])
                else:
                    P.add("pool", lambda e: e.tensor_copy(out=hT2[:, :, 0:2], in_=hT2[:, :, TB:TB + 2]), r=["hT2_%d" % (NTB - 1)], w=["hT2_halo"])
                gens_ = [outproj_tile(s, t0, it) for it in range(NTB)]
                live = []
                pending_ = list(gens_)
                while pending_ or live:
                    if pending_:
                        live.append(pending_.pop(0))
                    for g_ in list(live):
                        try:
                            next(g_)
                        except StopIteration:
                            live.remove(g_)
                khs = ["hT2_%d" % it for it in range(NTB)]
                prev = None
                for f in range(NFF):
                    cur = ffn_up(f, khs)
                    if prev is not None:
                        ffn_gate(*prev)
                    prev = cur
                ffn_gate(*prev)
                kgs = ["gT%d" % f for f in range(NFF)]
                for it in range(NTB):
                    down_tile(s, t0, it, kgs)

            for s in range(NSEQ if stop not in ("setup", "A", "B") else 0):
                seq_start(s)
                for tb in range(S // TB):
                    ffn_block(s, tb)
            P.barrier()
            P.emit()
    return nc, P.stats


_CACHE = {}


def _feat_major(v):
    v = np.asarray(v, np.float32)
    return np.ascontiguousarray(v.reshape(-1, 128).T)


def _prepare(x, c, positions, w_ada, b_ada, g_attn_norm, w_in, g_q_latent, g_kv_latent, w_q_up, w_kv_up,
             g_mla_q, g_mla_k, g_ca_q, g_ca_k, rel_bias, w_out, g_mlp_norm, w_up, conv_w, conv_b, w_down):
    f = lambda a: np.ascontiguousarray(np.asarray(a))
    x = f(x); c = f(c); positions = f(positions)
    if "nc" not in _CACHE:
        _CACHE["nc"], _CACHE["stats"] = build_program(stop=_CACHE.get("stop"))
    nc = _CACHE["nc"]
    gfeat = np.concatenate([_feat_major(g_attn_norm[0]), _feat_major(g_mlp_norm[0]), _feat_major(g_q_latent[0]),
                            _feat_major(g_kv_latent[0])], axis=1).astype(np.float32)
    convp = np.zeros((128, 4, 2 * NFF), np.float32)
    for t in range(3):
        convp[:, t, :] = _feat_major(conv_w[0, t])
    convp[:, 3, :] = _feat_major(conv_b[0])
    grow = np.concatenate([np.asarray(g_mla_q[0]), np.asarray(g_mla_k[0]), np.asarray(g_ca_q[0]), np.asarray(g_ca_k[0])]).astype(np.float32)
    grow = np.ascontiguousarray(np.broadcast_to(grow[None, :], (128, 320)))
    rb = np.asarray(rel_bias[0], np.float32)
    kj = np.arange(128)[:, None]
    qi = np.arange(128)[None, :]
    bias34 = np.zeros((128, 8, 2, 128), np.float32)
    for ti, t in enumerate((3, 4)):
        idx = np.clip(128 * (4 - t) + qi - kj, -128, 128) + 128
        bias34[:, :, ti, :] = np.transpose(rb[:, idx], (1, 0, 2))
    bfar = np.ascontiguousarray(np.broadcast_to(rb[:, 256][None, :], (128, 8))).astype(np.float32)
    half = 16
    invf = np.power(np.float32(10000.0), -np.arange(half, dtype=np.float32) / np.float32(half)).astype(np.float32)
    invf = np.ascontiguousarray(np.broadcast_to(invf[None, :], (128, 16)))
    shared = {
        "w_ada": f(w_ada[0]), "w_in": f(w_in[0]), "w_q_up": f(w_q_up[0]), "w_kv_up": f(w_kv_up[0]), "w_out": f(w_out[0]),
        "w_up": f(w_up[0]), "w_down": f(w_down[0]), "gfeat": gfeat, "convp": convp, "grow": grow, "bias34": bias34,
        "bfar": bfar, "invf": invf,
    }
    in_maps = []
    for i in range(NCORES):
        b0 = NSEQ * i
        m = dict(shared)
        m["x"] = f(x[b0:b0 + NSEQ])
        m["cT"] = np.ascontiguousarray(c[b0:b0 + NSEQ].reshape(NSEQ, 8, 128).transpose(2, 1, 0)).astype(np.float32)
        pl = positions[b0:b0 + NSEQ].reshape(NSEQ, NT, 128).transpose(2, 0, 1).reshape(128, NSEQ * NT)
        m["posl"] = np.ascontiguousarray(pl).astype(np.int32)
        m["b_ada2"] = np.ascontiguousarray(np.broadcast_to(np.asarray(b_ada[0], np.float32)[None, :], (NSEQ, 6 * D)))
        in_maps.append(m)
    return nc, in_maps


def kernel(**inputs):
    nc, in_maps = _prepare(**inputs)
    res = run_bass_kernel_spmd(nc, in_maps, core_ids=list(range(NCORES)))
    out = np.concatenate([np.asarray(r["out"]) for r in res.results], axis=0)
    return out.astype(np.float32)
```
